# Optimizing a Trainium2 kernel written in Bass

```python
import math
import jax, jax.numpy as jnp
from jax import lax
import numpy as np

D_MODEL = 1024
BATCH = 8
SEQ = 4096
DEPTH = 2

GRID_W = 64
CTX_LEN = 256
EPS = 1e-6

A_W = D_MODEL // 4
A_GROUPS = 4
A_HORIZ = A_W // 2
G_HEADS = 4
G_DK = 128
G_DV = 128
QK_W = G_HEADS * G_DK
G_W = G_HEADS * G_DV
GDN_CHUNK = 64
C_GROUPS = 4
C_W = D_MODEL // 4
C_GD = C_W // C_GROUPS
C_CHUNK = 128
MIX_W = A_W + G_W + C_W
FFN_HIDDEN = ((8 * D_MODEL + 3 * 256 - 1) // (3 * 256)) * 256

OFF_A_B = 0
OFF_A_C = OFF_A_B + A_W
OFF_A_H = OFF_A_C + A_W
OFF_Q = OFF_A_H + A_W
OFF_K = OFF_Q + QK_W
OFF_V = OFF_K + QK_W
OFF_AB = OFF_V + G_W
OFF_Z = OFF_AB + 4 * G_HEADS
OFF_CU = OFF_Z + G_W
OFF_CV = OFF_CU + C_W
IN_COLS = OFF_CV + C_W

kernel_name = 'hybrid_conv_gdn_sgu_dit_block'


def _rmsnorm(a, g):
    a32 = a.astype(jnp.float32)
    y = a32 * lax.rsqrt(jnp.mean(a32 * a32, axis=-1, keepdims=True) + EPS) * g.astype(jnp.float32)
    return y.astype(a.dtype)


def _modulate(h, shift, scale):
    return h * (1 + scale) + shift


def _l2norm(a):
    a32 = a.astype(jnp.float32)
    return a32 * lax.rsqrt(jnp.sum(a32 * a32, axis=-1, keepdims=True) + EPS)


def _heads(a, dh):
    bn, t, _ = a.shape
    return a.reshape(bn, t, -1, dh).transpose(0, 2, 1, 3)


def _conv3_seq(a, w):
    ap = jnp.pad(a, ((0, 0), (1, 1), (0, 0)))
    return w[0] * ap[:, :-2] + w[1] * ap[:, 1:-1] + w[2] * ap[:, 2:]


def _conv3_grid(a, w):
    bn, t, ch = a.shape
    rows = t // GRID_W
    g = a.reshape(bn, rows, GRID_W, ch)
    wh, wv = w[:, :A_HORIZ], w[:, A_HORIZ:]
    gh = jnp.pad(g[..., :A_HORIZ], ((0, 0), (0, 0), (1, 1), (0, 0)))
    yh = wh[0] * gh[:, :, :-2] + wh[1] * gh[:, :, 1:-1] + wh[2] * gh[:, :, 2:]
    gv = jnp.pad(g[..., A_HORIZ:], ((0, 0), (1, 1), (0, 0), (0, 0)))
    yv = wv[0] * gv[:, :-2] + wv[1] * gv[:, 1:-1] + wv[2] * gv[:, 2:]
    return jnp.concatenate([yh, yv], axis=-1).reshape(bn, t, ch)


def _delta_update(s, w_c, u_c, kt_c, gl_c):
    v_new = u_c - jnp.einsum('bhck,bhkv->bhcv', w_c, s)
    s_new = s * jnp.exp(gl_c)[..., None, None] + jnp.einsum('bhck,bhcv->bhkv', kt_c, v_new)
    return v_new, s_new


def _gdn_scan(q, k, v, g, beta, s0):
    bn, h, t, _ = k.shape
    dv = v.shape[-1]
    n = t // GDN_CHUNK

    def chunks(a):
        a = a.astype(jnp.float32).reshape(bn, h, n, GDN_CHUNK, *a.shape[3:])
        return jnp.moveaxis(a, 2, 0)

    kc, vc, gc, bc = chunks(k), chunks(v), chunks(g), chunks(beta)
    gcum = jnp.cumsum(gc, axis=-1)
    idx = jnp.arange(GDN_CHUNK)
    causal = idx[:, None] >= idx[None, :]
    strict = idx[:, None] > idx[None, :]
    dmat = jnp.exp(jnp.where(causal, gcum[..., :, None] - gcum[..., None, :], -jnp.inf))
    kkt = jnp.einsum('nbhck,nbhdk->nbhcd', kc, kc)
    m = jnp.eye(GDN_CHUNK, dtype=jnp.float32) + jnp.where(strict, kkt * dmat * bc[..., None], 0.0)
    rhs = jnp.concatenate([vc * bc[..., None], kc * (bc * jnp.exp(gcum))[..., None]], axis=-1)
    sol = lax.linalg.triangular_solve(m, rhs, left_side=True, lower=True, unit_diagonal=True)
    u, w = sol[..., :dv], sol[..., dv:]
    g_last = gcum[..., -1]
    k_tail = kc * jnp.exp(g_last[..., None] - gcum)[..., None]
    s0 = s0.astype(jnp.float32)

    if q is None:
        def step_state(s, xs):
            w_c, u_c, kt_c, gl_c = xs
            _, s_new = _delta_update(s, w_c, u_c, kt_c, gl_c)
            return s_new, None
        s_fin, _ = lax.scan(step_state, s0, (w, u, k_tail, g_last))
        return None, s_fin

    qc = chunks(q)
    qk = jnp.einsum('nbhck,nbhdk->nbhcd', qc, kc) * dmat
    q_dec = qc * jnp.exp(gcum)[..., None]

    def step(s, xs):
        w_c, u_c, kt_c, gl_c, qd_c, qk_c = xs
        v_new, s_new = _delta_update(s, w_c, u_c, kt_c, gl_c)
        o = jnp.einsum('bhck,bhkv->bhcv', qd_c, s) + jnp.einsum('bhcd,bhdv->bhcv', qk_c, v_new)
        return s_new, o

    s_fin, o = lax.scan(step, s0, (w, u, k_tail, g_last, q_dec, qk))
    o = jnp.moveaxis(o, 0, 2).reshape(bn, h, t, dv)
    return o, s_fin


def _flip_t(a):
    return None if a is None else jnp.flip(a, axis=2)


def _gdn_bidir(q, k, v, g, beta, s0f, s0b):
    of, sf = _gdn_scan(q, k, v, g[0], beta[0], s0f)
    ob, sb = _gdn_scan(_flip_t(q), _flip_t(k), _flip_t(v), _flip_t(g[1]), _flip_t(beta[1]), s0b)
    o = None if q is None else of + _flip_t(ob)
    return o, sf, sb


def _gdn_kv(k_raw, v_raw, ab, conv_k, conv_v, a_log, dt_bias):
    bn, t, _ = k_raw.shape
    k = _l2norm(_heads(jax.nn.silu(_conv3_seq(k_raw, conv_k)), G_DK))
    v = _heads(jax.nn.silu(_conv3_seq(v_raw, conv_v)), G_DV).astype(jnp.float32)
    ab = ab.astype(jnp.float32).reshape(bn, t, 4, G_HEADS)
    beta = jax.nn.sigmoid(ab[:, :, :2])
    g = -jnp.exp(a_log.astype(jnp.float32)) * jax.nn.softplus(ab[:, :, 2:] + dt_bias.astype(jnp.float32))
    return k, v, jnp.transpose(g, (2, 0, 3, 1)), jnp.transpose(beta, (2, 0, 3, 1))


def _gated_out(o, z, gw):
    bn, t, _ = z.shape
    o = jnp.transpose(o, (0, 2, 1, 3))
    o = o * lax.rsqrt(jnp.mean(o * o, axis=-1, keepdims=True) + EPS) * gw.astype(jnp.float32)
    o = o * jax.nn.silu(z.astype(jnp.float32).reshape(bn, t, G_HEADS, G_DV))
    return o.reshape(bn, t, G_W).astype(z.dtype)


def _chunk_sgu(u, v, ln_g, ln_b, w_s, b_s):
    u = jax.nn.gelu(u, approximate=False)
    v32 = jax.nn.gelu(v, approximate=False).astype(jnp.float32)
    mu = jnp.mean(v32, axis=-1, keepdims=True)
    var = jnp.mean(jnp.square(v32 - mu), axis=-1, keepdims=True)
    v = ((v32 - mu) * lax.rsqrt(var + EPS) * ln_g.astype(jnp.float32) + ln_b.astype(jnp.float32)).astype(u.dtype)
    bn, t, _ = v.shape
    vc = v.reshape(bn, t // C_CHUNK, C_CHUNK, C_GROUPS, C_GD)
    mixed = jnp.einsum('gpq,bnqgc->bnpgc', w_s, vc) + b_s.T[None, None, :, :, None]
    return u * mixed.reshape(bn, t, C_W)


def _mixers(p, conv_fn, conv_a, conv_qkv, a_log, dt_bias, g_onorm, ln_g, ln_b, w_s, b_s, s0f, s0b):
    y_a = p[..., OFF_A_B:OFF_A_C] * conv_fn(p[..., OFF_A_C:OFF_A_H] * p[..., OFF_A_H:OFF_Q], conv_a)
    q = _l2norm(_heads(jax.nn.silu(_conv3_seq(p[..., OFF_Q:OFF_K], conv_qkv[:, :QK_W])), G_DK)) * (G_DK ** -0.5)
    k, v, g, beta = _gdn_kv(p[..., OFF_K:OFF_V], p[..., OFF_V:OFF_AB], p[..., OFF_AB:OFF_Z],
                            conv_qkv[:, QK_W:2 * QK_W], conv_qkv[:, 2 * QK_W:], a_log, dt_bias)
    o, sf, sb = _gdn_bidir(q, k, v, g, beta, s0f, s0b)
    y_b = _gated_out(o, p[..., OFF_Z:OFF_CU], g_onorm)
    y_c = _chunk_sgu(p[..., OFF_CU:OFF_CV], p[..., OFF_CV:IN_COLS], ln_g, ln_b, w_s, b_s)
    return jnp.concatenate([y_a, y_b, y_c], axis=-1), sf, sb


def _ffn_sublayer(s, shift, scale, gate, g_pre, g_post, w1, w2):
    h = _modulate(_rmsnorm(s, g_pre), shift, scale)
    gu = h @ w1
    y = (jax.nn.silu(gu[..., :FFN_HIDDEN]) * gu[..., FFN_HIDDEN:]) @ w2
    return s + gate * _rmsnorm(y, g_post)


def setup_inputs(seed: int = 0) -> dict:
    key = jax.random.key(seed)
    ks = jax.random.split(key, 24)
    L = DEPTH

    def nrm(k, shape, s):
        return jax.random.normal(k, shape, jnp.float32) * s

    dt = jnp.exp(jax.random.uniform(ks[14], (L, 2, G_HEADS), jnp.float32, math.log(1e-3), math.log(1e-1)))
    return {
        'x': nrm(ks[0], (BATCH, SEQ, D_MODEL), 1.0),
        'c': nrm(ks[1], (BATCH, D_MODEL), 1.0),
        'ctx': nrm(ks[2], (BATCH, CTX_LEN, D_MODEL), 1.0),
        'c_ctx': nrm(ks[3], (D_MODEL,), 1.0),
        'w_mod': nrm(ks[4], (L, D_MODEL, 6 * D_MODEL), 0.5 * D_MODEL ** -0.5),
        'b_mod': nrm(ks[5], (L, 6 * D_MODEL), 0.02),
        'g_pre_mix': 1.0 + nrm(ks[6], (L, D_MODEL), 0.02),
        'g_post_mix': 1.0 + nrm(ks[7], (L, D_MODEL), 0.02),
        'g_pre_ffn': 1.0 + nrm(ks[8], (L, D_MODEL), 0.02),
        'g_post_ffn': 1.0 + nrm(ks[9], (L, D_MODEL), 0.02),
        'w_in': nrm(ks[10], (L, D_MODEL, IN_COLS), D_MODEL ** -0.5),
        'conv_a': nrm(ks[11], (L, 3, A_W), 3 ** -0.5),
        'conv_qkv': nrm(ks[12], (L, 3, 2 * QK_W + G_W), 3 ** -0.5),
        'a_log': jnp.log(jax.random.uniform(ks[13], (L, 2, G_HEADS), jnp.float32, 1.0, 16.0)),
        'dt_bias': dt + jnp.log(-jnp.expm1(-dt)),
        'g_onorm': 1.0 + nrm(ks[15], (L, G_DV), 0.02),
        'ln_c_g': 1.0 + nrm(ks[16], (L, C_W), 0.02),
        'ln_c_b': nrm(ks[17], (L, C_W), 0.02),
        'w_s': nrm(ks[18], (L, C_GROUPS, C_CHUNK, C_CHUNK), C_CHUNK ** -0.5),
        'b_s': 1.0 + nrm(ks[19], (L, C_GROUPS, C_CHUNK), 0.1),
        'w_o': nrm(ks[20], (L, MIX_W, D_MODEL), MIX_W ** -0.5),
        'w_ffn_in': nrm(ks[21], (L, D_MODEL, 2 * FFN_HIDDEN), D_MODEL ** -0.5),
        'w_ffn_out': nrm(ks[22], (L, FFN_HIDDEN, D_MODEL), FFN_HIDDEN ** -0.5),
    }


def reference(x, c, ctx, c_ctx, w_mod, b_mod, g_pre_mix, g_post_mix, g_pre_ffn, g_post_ffn, w_in,
              conv_a, conv_qkv, a_log, dt_bias, g_onorm, ln_c_g, ln_c_b, w_s, b_s, w_o, w_ffn_in, w_ffn_out):
    bn = x.shape[0]
    s_zero = jnp.zeros((bn, G_HEADS, G_DK, G_DV), jnp.float32)
    for l in range(DEPTH):
        last = l == DEPTH - 1
        mx = jnp.split((jax.nn.silu(c) @ w_mod[l] + b_mod[l])[:, None, :], 6, axis=-1)
        mc = jnp.split(jax.nn.silu(c_ctx) @ w_mod[l] + b_mod[l], 6, axis=-1)

        hc = _modulate(_rmsnorm(ctx, g_pre_mix[l]), mc[0], mc[1])
        if last:
            pkv = hc @ w_in[l][:, OFF_K:OFF_Z]
            k_c, v_c, g_c, b_c = _gdn_kv(pkv[..., :QK_W], pkv[..., QK_W:QK_W + G_W], pkv[..., QK_W + G_W:],
                                         conv_qkv[l][:, QK_W:2 * QK_W], conv_qkv[l][:, 2 * QK_W:],
                                         a_log[l], dt_bias[l])
            _, sf, sb = _gdn_bidir(None, k_c, v_c, g_c, b_c, s_zero, s_zero)
        else:
            yc, sf, sb = _mixers(hc @ w_in[l], _conv3_seq, conv_a[l], conv_qkv[l], a_log[l], dt_bias[l],
                                 g_onorm[l], ln_c_g[l], ln_c_b[l], w_s[l], b_s[l], s_zero, s_zero)

        hx = _modulate(_rmsnorm(x, g_pre_mix[l]), mx[0], mx[1])
        yx, _, _ = _mixers(hx @ w_in[l], _conv3_grid, conv_a[l], conv_qkv[l], a_log[l], dt_bias[l],
                           g_onorm[l], ln_c_g[l], ln_c_b[l], w_s[l], b_s[l], sf, sb)
        x = x + mx[2] * _rmsnorm(yx @ w_o[l], g_post_mix[l])
        x = _ffn_sublayer(x, mx[3], mx[4], mx[5], g_pre_ffn[l], g_post_ffn[l], w_ffn_in[l], w_ffn_out[l])

        if not last:
            ctx = ctx + mc[2] * _rmsnorm(yc @ w_o[l], g_post_mix[l])
            ctx = _ffn_sublayer(ctx, mc[3], mc[4], mc[5], g_pre_ffn[l], g_post_ffn[l], w_ffn_in[l], w_ffn_out[l])
    return x
```

```python
import numpy as np
from contextlib import ExitStack
import concourse.bass as bass
import concourse.mybir as mybir
from concourse.bass_utils import run_bass_kernel_spmd

F32 = mybir.dt.float32
F32R = mybir.dt.float32r
BF16 = mybir.dt.bfloat16
AF = mybir.ActivationFunctionType
ALU = mybir.AluOpType
AX = mybir.AxisListType

ENGS = ("tensor", "vector", "scalar", "gpsimd", "sync")
SAME_ENGINE_SYNC = True


class Tile:
    def __init__(self, prog, h, name, dram=False):
        self.prog = prog
        self.h = h
        self.name = name
        self.dram = dram
        self.lastw = None
        self.readers = []
        prog.tiles[name] = self

    def __getitem__(self, idx):
        return self.h[idx]

    def ap(self):
        return self.h.ap() if self.dram else self.h[:]


class SubTile:
    def __init__(self, parent, c0, w, name):
        self.parent = parent
        self.c0 = c0
        self.w = w
        self.name = name
        self.lastw = None
        self.readers = []

    def __getitem__(self, idx):
        return self.parent.h[:, self.c0:self.c0 + self.w][idx]


class Op:
    __slots__ = ("eng", "fn", "deps", "waits", "sig", "sigval", "is_dma", "dsem", "dval", "idx", "is_load", "F")

    def __init__(self, eng, fn, is_dma=False):
        self.eng = eng
        self.fn = fn
        self.deps = []
        self.waits = []
        self.sig = False
        self.sigval = None
        self.is_dma = is_dma
        self.dsem = None
        self.dval = None


class EngProxy:
    def __init__(self, prog, eng):
        self.prog = prog
        self.eng = eng

    def __getattr__(self, meth):
        prog, eng = self.prog, self.eng

        def call(*args, **kwargs):
            reads, writes = [], []
            for k, v in kwargs.items():
                if isinstance(v, bass.AP):
                    t = prog.tiles.get(v.tensor.name)
                    if t is None:
                        continue
                    if getattr(t, "subs", None):
                        t = t.subs[(v.offset % t.rowlen) // t.subw]
                    if k in ("out", "accum_out") or (k == "ap" and meth == "memset"):
                        writes.append(t)
                    else:
                        reads.append(t)
            extra_r = kwargs.pop("_reads", [])
            extra_w = kwargs.pop("_writes", [])
            reads += extra_r
            writes += extra_w
            is_dma = meth in ("dma_start",)
            if meth == "matmul" and kwargs.get("start", True) is False:
                pass
            fn = lambda e, meth=meth, args=args, kwargs=kwargs: getattr(e, meth)(*args, **kwargs)
            return prog.record(eng, fn, reads, writes, is_dma)

        return call


class Prog:
    def __init__(self, nc):
        self.nc = nc
        self.es = ExitStack()
        self.tiles = {}
        self.csem = {e: self.es.enter_context(nc.semaphore("c_" + e)) for e in ENGS}
        self.cnt = {e: 0 for e in ENGS}
        self.known = {e: {} for e in ENGS}
        self.ops = []
        self.dma_sems = []
        self.ndma_sems = 24
        for i in range(self.ndma_sems):
            self.dma_sems.append([self.es.enter_context(nc.semaphore("d%d" % i)), 0, None])
        self.dma_rr = 0
        self.phase_es = None
        for e in ENGS:
            setattr(self, e[0] if e != "sync" else "sp", EngProxy(self, e))
        self.pe = EngProxy(self, "tensor")
        self.dve = EngProxy(self, "vector")
        self.act = EngProxy(self, "scalar")
        self.pool = EngProxy(self, "gpsimd")
        self.sp = EngProxy(self, "sync")
        self.uid = 0

    def sb(self, name, shape, dtype=F32):
        self.uid += 1
        name = "%s_%d" % (name, self.uid)
        h = self.phase_es.enter_context(self.nc.sbuf_tensor(name, list(shape), dtype))
        return Tile(self, h, name)

    def sbc(self, name, shape, dtype=F32):
        self.uid += 1
        name = "%s_%d" % (name, self.uid)
        h = self.es.enter_context(self.nc.sbuf_tensor(name, list(shape), dtype))
        return Tile(self, h, name)

    def sm(self, name, cols, depth=4):
        key = (name, cols)
        ring = self.rings.setdefault(key, [[], 0])
        if len(ring[0]) < depth:
            ring[0].append(self.sb(name, [128, cols]))
            return ring[0][-1]
        ring[1] += 1
        return ring[0][ring[1] % depth]

    def keep_begin(self):
        self.keep_es = ExitStack()

    def keep_end(self):
        self.keep_es.close()
        self.keep_es = None

    def sbk(self, name, shape, dtype=F32):
        self.uid += 1
        name = "%s_%d" % (name, self.uid)
        h = self.keep_es.enter_context(self.nc.sbuf_tensor(name, list(shape), dtype))
        return Tile(self, h, name)

    def ps(self, name, shape, dtype=F32):
        self.uid += 1
        name = "%s_%d" % (name, self.uid)
        h = self.phase_es.enter_context(self.nc.psum_tensor(name, list(shape), dtype))
        t = Tile(self, h, name)
        t.psum = True
        return t

    def psb(self, name, n, w, dtype=F32):
        rowlen = 512 if dtype == F32 else 1024
        assert n * w <= rowlen
        t = self.ps(name, [128, rowlen], dtype)
        return [SubTile(t, i * w, w, "%s.%d" % (t.name, i)) for i in range(n)]

    def dram(self, name, shape, dtype=F32, kind="Internal"):
        h = self.nc.dram_tensor(name, list(shape), dtype, kind=kind)
        return Tile(self, h, name, dram=True)

    def record(self, eng, fn, reads, writes, is_dma=False):
        op = Op(eng, fn, is_dma)
        op.idx = len(self.ops)
        op.is_load = is_dma and any(not getattr(t, "dram", False) for t in writes)
        deps = []
        for t in reads:
            if t.lastw is not None:
                deps.append(t.lastw)
            if getattr(t, "psum", False):
                deps.extend(rd for rd in t.readers if rd.eng != eng)
        for t in writes:
            if t.lastw is not None:
                deps.append(t.lastw)
            deps.extend(t.readers)
        for t in reads:
            t.readers.append(op)
        for t in writes:
            t.lastw = op
            t.readers = []
        seen = set()
        for d in deps:
            if id(d) in seen or d is op:
                continue
            seen.add(id(d))
            op.deps.append(d)
        if is_dma:
            slot = self.dma_sems[self.dma_rr % self.ndma_sems]
            self.dma_rr += 1
            prev = slot[2]
            if prev is not None:
                op.deps.append(prev)
            slot[1] += 16
            slot[2] = op
            op.dsem = slot[0]
            op.dval = slot[1]
        self.ops.append(op)
        return op

    def begin(self):
        self.phase_es = ExitStack()
        self.ops = []
        self.rings = {}

    def end(self, final=False):
        nc = self.nc
        ops = self.ops
        last = {e: None for e in ENGS}
        for op in ops:
            if not op.is_dma:
                last[op.eng] = op
        pend_dma = [s[2] for s in self.dma_sems if s[2] is not None]
        for op in ops:
            for d in op.deps:
                if d.is_dma:
                    continue
                if d.eng != op.eng or (SAME_ENGINE_SYNC and d.eng != "tensor") or op.is_dma:
                    d.sig = True
        for e in ENGS:
            if last[e] is not None:
                last[e].sig = True
        for op in ops:
            if op.is_dma:
                continue
            if op.sig:
                self.cnt[op.eng] += 1
                op.sigval = self.cnt[op.eng]
        sp_ops = [op for op in ops if op.eng == "sync"]
        spidx = {id(op): i for i, op in enumerate(sp_ops)}
        lastF = {e: -1 for e in ENGS}
        for op in ops:
            f = -1
            for d in op.deps:
                if id(d) in spidx:
                    f = max(f, spidx[id(d)])
                elif getattr(d, "F", None) is not None:
                    f = max(f, d.F)
            if op.eng != "sync":
                f = max(f, lastF[op.eng])
                lastF[op.eng] = f
            op.F = f
        keys = {}
        prev_load_key = -1.0
        for i, op in enumerate(sp_ops):
            if op.is_load:
                k = max(op.F + 0.5, prev_load_key)
                k = min(k, float(i))
                prev_load_key = k
                keys[id(op)] = k
            else:
                keys[id(op)] = float(i)
        sp_sorted = sorted(range(len(sp_ops)), key=lambda i: (keys[id(sp_ops[i])], i))
        sp_new = [sp_ops[i] for i in sp_sorted]
        per = {e: [op for op in ops if op.eng == e] for e in ENGS}
        per["sync"] = sp_new
        for e in ENGS:
            kn = self.known[e]
            for op in per[e]:
                for d in op.deps:
                    if d.is_dma:
                        key, val, sem = ("d", id(d.dsem)), d.dval, d.dsem
                    else:
                        if d.sigval is None:
                            continue
                        if d.eng == op.eng and not op.is_dma and (d.eng == "tensor" or not SAME_ENGINE_SYNC):
                            continue
                        key, val, sem = ("c", d.eng), d.sigval, self.csem[d.eng]
                    if kn.get(key, 0) >= val:
                        continue
                    kn[key] = val
                    op.waits.append((sem, val))
        bar = {}
        for e in ENGS:
            w = []
            kn = self.known[e]
            for e2 in ENGS:
                if last[e2] is None:
                    continue
                v = last[e2].sigval
                if kn.get(("c", e2), 0) < v:
                    kn[("c", e2)] = v
                    w.append((self.csem[e2], v))
            for d in pend_dma:
                key = ("d", id(d.dsem))
                if kn.get(key, 0) < d.dval:
                    kn[key] = d.dval
                    w.append((d.dsem, d.dval))
            bar[e] = w
        for s in self.dma_sems:
            s[2] = None

        with nc.Block() as block:
            def emit(e):
                def body(eng):
                    for op in per[e]:
                        for (sem, val) in op.waits:
                            eng.wait_ge(sem, val)
                        ins = op.fn(eng)
                        if op.is_dma:
                            ins.then_inc(op.dsem, 16)
                        elif op.sig:
                            ins.then_inc(self.csem[e], 1)
                    for (sem, val) in bar[e]:
                        eng.wait_ge(sem, val)
                return body
            block.tensor(emit("tensor"))
            block.vector(emit("vector"))
            block.scalar(emit("scalar"))
            block.gpsimd(emit("gpsimd"))
            block.sync(emit("sync"))
        for t in self.tiles.values():
            t.lastw = None
            t.readers = []
            for st in (getattr(t, "subs", None) or []):
                st.lastw = None
                st.readers = []
        self.phase_es.close()
        self.phase_es = None
        self.ops = []

    def close(self):
        self.es.close()


D = 1024
T = 4096
TC = 256
NT = TC + T
NTILE = NT // 128
DEPTH = 2
EPS = 1e-6
IN_COLS = 3344
OFF_A_B, OFF_A_C, OFF_A_H, OFF_Q, OFF_K, OFF_V = 0, 256, 512, 768, 1280, 1792
OFF_AB, OFF_Z, OFF_CU, OFF_CV = 2304, 2320, 2832, 3088
FFN_H = 2816
SCN_W = 8 * 512 + 32
GROUPS = [(0, 256, 0, 0)] + [(256 + 512 * i, 512, 1, 512 * i) for i in range(8)]


class Model:
    def __init__(self, nc, debug=False):
        self.nc = nc
        self.P = P = Prog(nc)
        self.debug = debug
        k_in = "ExternalInput"
        k_sc = "ExternalOutput" if debug else "Internal"
        self.xs_in = P.dram("xs", [NT, D], F32, k_in)
        self.ccT = P.dram("ccT", [128, 8, 2], F32, k_in)
        self.w_mod = P.dram("w_mod", [DEPTH, D, 6 * D], F32, k_in)
        self.b_mod = P.dram("b_mod", [DEPTH, 6 * D], F32, k_in)
        self.g_pre_mix = P.dram("g_pre_mix", [DEPTH, D], F32, k_in)
        self.g_post_mix = P.dram("g_post_mix", [DEPTH, D], F32, k_in)
        self.g_pre_ffn = P.dram("g_pre_ffn", [DEPTH, D], F32, k_in)
        self.g_post_ffn = P.dram("g_post_ffn", [DEPTH, D], F32, k_in)
        self.w_in = P.dram("w_in", [DEPTH, D, IN_COLS], F32, k_in)
        self.conv_a = P.dram("conv_a", [DEPTH, 3, 256], F32, k_in)
        self.conv_qkv = P.dram("conv_qkv", [DEPTH, 3, 1536], F32, k_in)
        self.a_log = P.dram("a_log", [DEPTH, 8], F32, k_in)
        self.dt_bias = P.dram("dt_bias", [DEPTH, 8], F32, k_in)
        self.g_onorm = P.dram("g_onorm", [DEPTH, 128], F32, k_in)
        self.ln_c_g = P.dram("ln_c_g", [DEPTH, 256], F32, k_in)
        self.ln_c_b = P.dram("ln_c_b", [DEPTH, 256], F32, k_in)
        self.w_s = P.dram("w_s", [DEPTH, 4, 128, 128], F32, k_in)
        self.b_s = P.dram("b_s", [DEPTH, 4, 128], F32, k_in)
        self.w_o = P.dram("w_o", [DEPTH, D, D], F32, k_in)
        self.w_ffn_in = P.dram("w_ffn_in", [DEPTH, D, 2 * FFN_H], F32, k_in)
        self.w_ffn_out = P.dram("w_ffn_out", [DEPTH, FFN_H, D], F32, k_in)
        self.out = P.dram("out", [T, D], F32, "ExternalOutput")
        self.XS = P.dram("XS", [NT, D], F32, k_sc)
        self.MB = P.dram("MB", [DEPTH, 6, 2, D], F32, k_sc)
        self.HT = [P.dram("HTc", [8, 128, TC + 2], BF16, k_sc), P.dram("HTx", [8, 128, T + 2], BF16, k_sc)]
        self.PFM = P.dram("PFM", [512, NT], F32, k_sc)
        self.YT = P.dram("YT", [1024, NT], BF16, k_sc)
        self.ZS = P.dram("ZS", [NT, 512], F32, k_sc)
        self.QS = P.dram("QS", [NT, 528], F32, k_sc)
        self.SCN = P.dram("SCN", [NTILE, 128, SCN_W], F32, k_sc)
        self.SC2 = P.dram("SC2", [NTILE, 128, 6 * 512 + 32], F32, k_sc)
        self.XB = P.dram("XB", [NT, D], F32, k_sc)
        self.OF = P.dram("OF", [NT, 512], F32, k_sc)
        self.OB = P.dram("OB", [NT, 512], F32, k_sc)

    def consts(self):
        P = self.P
        c = self.c = {}
        for nm in ("ones", "U0", "U1", "SU0", "SU1", "NM0", "NM1", "ident", "NU0", "NU1"):
            c[nm] = P.sbc(nm, [128, 128])
        c["ident_bf"] = P.sbc("ident_bf", [128, 128], BF16)
        P.begin()
        ones = c["ones"]
        zer = P.sb("zer", [128, 128])
        P.pool.memset(ap=ones[:], constant=1.0)
        P.pool.memset(ap=zer[:], constant=0.0)

        def sel(name, src, step, cm, base, op, fill):
            t = c[name]
            P.pool.affine_select(out=t[:], in_=src[:], pattern=[[step, 128]], compare_op=op,
                                 fill=fill, base=base, channel_multiplier=cm)
            return t
        sel("U0", ones, 1, -1, 0, ALU.is_ge, 0.0)
        sel("U1", ones, -1, 1, 0, ALU.is_ge, 0.0)
        sel("SU0", ones, 1, -1, -1, ALU.is_ge, 0.0)
        sel("SU1", ones, -1, 1, -1, ALU.is_ge, 0.0)
        sel("NM0", zer, 1, -1, 0, ALU.is_ge, -30000.0)
        sel("NM1", zer, -1, 1, 0, ALU.is_ge, -30000.0)
        sel("ident", ones, 1, -1, 0, ALU.is_equal, 0.0)
        for r in (0, 1):
            P.pool.tensor_scalar(out=c["NU%d" % r][:], in0=c["U%d" % r][:], scalar1=-1.0, scalar2=None, op0=ALU.mult)
        P.pool.tensor_copy(out=c["ident_bf"][:], in_=c["ident"][:])
        zb = P.sb("zb", [128, 8, 1], BF16)
        P.pool.memset(ap=zb[:], constant=0.0)
        for s, n in ((0, TC), (1, T)):
            v = self.HT[s].h.ap().rearrange("kc p t -> p kc t")
            P.sp.dma_start(out=v[:, :, 0:1], in_=zb[:], _writes=[self.HT[s]], allow_slow_non_contiguous=True)
            P.sp.dma_start(out=v[:, :, n + 1:n + 2], in_=zb[:], _writes=[self.HT[s]], allow_slow_non_contiguous=True)
        P.end()

    def mods(self, l):
        P = self.P
        P.begin()
        cT = P.sb("cT", [128, 8, 2])
        sT = P.sb("sT", [128, 8, 2])
        P.sp.dma_start(out=cT[:], in_=self.ccT.ap())
        P.act.activation(out=sT[:], in_=cT[:], func=AF.Silu)
        bm = P.sb("bm", [2, 6 * D])
        P.sp.dma_start(out=bm[:], in_=self.b_mod.h.ap()[l:l + 1, :].broadcast_to([2, 6 * D]))
        mods = P.sb("mods", [2, 6 * D])
        wbuf = [P.sb("wm%d" % i, [128, 8, 512]) for i in range(2)]
        pm = [P.ps("pm%d" % i, [2, 512]) for i in range(2)]
        wv = self.w_mod.h.ap()[l].rearrange("(kc p) n -> p kc n", p=128)
        for nb in range(12):
            wb = wbuf[nb % 2]
            P.sp.dma_start(out=wb[:], in_=wv[:, :, nb * 512:(nb + 1) * 512])
            pp = pm[nb % 2]
            for kc in range(8):
                P.pe.matmul(out=pp[:], lhsT=sT[:, kc, :], rhs=wb[:, kc, :], start=(kc == 0), stop=(kc == 7))
            P.dve.tensor_tensor(out=mods[:, nb * 512:(nb + 1) * 512], in0=pp[:], in1=bm[:, nb * 512:(nb + 1) * 512], op=ALU.add)
        gv = P.sb("gv", [2, 4, D])
        for i, g in enumerate((self.g_pre_mix, self.g_post_mix, self.g_pre_ffn, self.g_post_ffn)):
            P.sp.dma_start(out=gv[:, i, :], in_=g.h.ap()[l:l + 1, :].broadcast_to([2, D]))
        mb = P.sb("mb", [2, 6, D])
        m = lambda i: mods[:, i * D:(i + 1) * D]
        P.dve.tensor_copy(out=mb[:, 0, :], in_=m(0))
        P.dve.scalar_tensor_tensor(out=mb[:, 1, :], in0=m(1), scalar=1.0, in1=gv[:, 0, :], op0=ALU.add, op1=ALU.mult)
        P.dve.tensor_tensor(out=mb[:, 2, :], in0=m(2), in1=gv[:, 1, :], op=ALU.mult)
        P.dve.tensor_copy(out=mb[:, 3, :], in_=m(3))
        P.dve.scalar_tensor_tensor(out=mb[:, 4, :], in0=m(4), scalar=1.0, in1=gv[:, 2, :], op0=ALU.add, op1=ALU.mult)
        P.dve.tensor_tensor(out=mb[:, 5, :], in0=m(5), in1=gv[:, 3, :], op=ALU.mult)
        P.sp.dma_start(out=self.MB.h.ap()[l].rearrange("k s d -> s k d"), in_=mb[:], _writes=[self.MB])
        P.end()

    def load_mod(self, l, k, s, name):
        P = self.P
        t = P.sb(name, [128, D])
        P.sp.dma_start(out=t[:], in_=self.MB.h.ap()[l, k, s:s + 1, :].broadcast_to([128, D]), _reads=[self.MB])
        return t

    def rstd_chain(self, ss, n, scale, name, post=None):
        P = self.P
        t1 = P.sm(name + "a", n)
        t2 = P.sm(name + "b", n)
        t3 = P.sm(name + "c", n)
        P.dve.tensor_scalar(out=t1[:], in0=ss, scalar1=scale, scalar2=EPS, op0=ALU.mult, op1=ALU.add)
        P.act.activation(out=t2[:], in_=t1[:], func=AF.Sqrt)
        P.dve.reciprocal(out=t3[:], in_=t2[:])
        return t3

    def prenorm(self, l, ks, kg, src):
        P = self.P
        c = self.c
        P.begin()
        Sx = [self.load_mod(l, ks, 1, "Sc"), self.load_mod(l, ks, 0, "Sx")]
        Gx = [self.load_mod(l, kg, 1, "Gc"), self.load_mod(l, kg, 0, "Gx")]
        xt = [P.sb("xt%d" % i, [128, D]) for i in range(2)]
        junk = P.sb("junk", [128, D])
        h1 = [P.sb("h1%d" % i, [128, D]) for i in range(2)]
        hb = [P.sb("hb%d" % i, [128, D], BF16) for i in range(2)]
        pT = [P.ps("pT%d" % i, [128, D], BF16) for i in range(2)]
        hT = [P.sb("hT%d" % i, [128, 8, 512], BF16) for i in range(2)]
        it = 0
        for gi, (row0, ntok, s, t0) in enumerate(GROUPS):
            hTg = hT[gi % 2]
            for j in range(ntok // 128):
                b = it % 2
                it += 1
                r0 = row0 + j * 128
                P.sp.dma_start(out=xt[b][:], in_=src.h.ap()[r0:r0 + 128, :], _reads=[src])
                ss = P.sm("ss", 1)
                P.act.activation(out=junk[:], in_=xt[b][:], func=AF.Square, accum_out=ss[:])
                rstd = self.rstd_chain(ss[:], 1, 1.0 / D, "rs")
                P.dve.scalar_tensor_tensor(out=h1[b][:], in0=xt[b][:], scalar=rstd[:, 0:1], in1=Gx[s][:], op0=ALU.mult, op1=ALU.mult)
                P.pool.tensor_tensor(out=hb[b][:], in0=h1[b][:], in1=Sx[s][:], op=ALU.add)
                for kc in range(8):
                    P.pe.transpose(out=pT[b][:, kc * 128:(kc + 1) * 128], in_=hb[b][:, kc * 128:(kc + 1) * 128], identity=c["ident_bf"][:])
                P.act.copy(out=hTg[:, :, j * 128:(j + 1) * 128], in_=pT[b][:].rearrange("p (kc t) -> p kc t", kc=8))
            dst = self.HT[s].h.ap().rearrange("kc p t -> p kc t")[:, :, 1 + t0:1 + t0 + ntok]
            P.sp.dma_start(out=dst, in_=hTg[:, :, 0:ntok], _writes=[self.HT[s]])
        P.end()

    def proj(self, l):
        P = self.P
        c = self.c
        P.keep_begin()
        Wfm = P.sbk("Wfm", [128, 8, 1024], BF16)
        Wrest = P.sbk("Wrest", [128, 8, 784], BF16)
        Wqkv = [P.sbk("Wq%d" % j, [128, 8, 1536], BF16) for j in range(3)]
        P.begin()
        cw = P.sb("cw", [128, 3, 1536])
        P.sp.dma_start(out=cw[:], in_=self.conv_qkv.h.ap()[l:l + 1].broadcast_to([128, 3, 1536]))
        stg = [P.sb("stg%d" % i, [128, IN_COLS]) for i in range(2)]
        wv = self.w_in.h.ap()[l]
        for kc in range(8):
            st = stg[kc % 2]
            P.sp.dma_start(out=st[:], in_=wv[kc * 128:(kc + 1) * 128, :])
            P.act.copy(out=Wfm[:, kc, 0:768], in_=st[:, 0:768])
            P.act.copy(out=Wfm[:, kc, 768:1024], in_=st[:, OFF_CU:OFF_CV])
            P.act.copy(out=Wrest[:, kc, 0:512], in_=st[:, OFF_Z:OFF_CU])
            P.act.copy(out=Wrest[:, kc, 512:528], in_=st[:, OFF_AB:OFF_Z])
            P.act.copy(out=Wrest[:, kc, 528:784], in_=st[:, OFF_CV:IN_COLS])
            for j in range(3):
                eng = (P.dve, P.pool, P.dve)[j]
                eng.tensor_tensor(out=Wqkv[j][:, kc, :], in0=st[:, OFF_Q:OFF_AB], in1=cw[:, j, :], op=ALU.mult)
        P.end()

        P.begin()
        pfm = [P.ps("pfm%d" % i, [128, 512]) for i in range(2)]
        prot = [P.ps("prot%d" % i, [128, 512]) for i in range(3)]
        pr = P.ps("pr", [128, 512])
        pc = [P.ps("pc%d" % i, [128, 128]) for i in range(2)]
        wsl = P.sb("wsl", [128, 4, 128])
        wsT = P.sb("wsT", [128, 4, 128])
        P.sp.dma_start(out=wsl[:], in_=self.w_s.h.ap()[l].rearrange("g p q -> p g q"))
        for g in range(4):
            P.pe.transpose(out=pr[:, g * 128:(g + 1) * 128], in_=wsl[:, g, :], identity=c["ident"][:])
        P.dve.tensor_copy(out=wsT[:], in_=pr[:].rearrange("p (g q) -> p g q", g=4))
        Bs = P.sb("Bs", [128, 2, 128])
        for g in range(4):
            P.sp.dma_start(out=Bs[(g % 2) * 64:(g % 2) * 64 + 64, g // 2, :],
                           in_=self.b_s.h.ap()[l, g:g + 1, :].broadcast_to([64, 128]))
        lng = P.sb("lng", [128, 256])
        lnb = P.sb("lnb", [128, 256])
        P.sp.dma_start(out=lng[:], in_=self.ln_c_g.h.ap()[l:l + 1, :].broadcast_to([128, 256]))
        P.sp.dma_start(out=lnb[:], in_=self.ln_c_b.h.ap()[l:l + 1, :].broadcast_to([128, 256]))
        hTb = [P.sb("hTg%d" % i, [128, 8, 514], BF16) for i in range(2)]
        evb = [P.sb("evb%d" % i, [128, 512]) for i in range(2)]
        Csb = [P.sb("Csb%d" % i, [128, 512]) for i in range(2)]
        uT = P.sb("uT", [128, 2, 512])
        ycT = P.sb("ycT", [128, 2, 512], BF16)
        qsb = [P.sb("qsb%d" % i, [128, 512]) for i in range(2)]
        ksb = [P.sb("ksb%d" % i, [128, 512]) for i in range(2)]
        sq = [P.sb("sq%d" % i, [128, 512]) for i in range(2)]
        kvst = [P.sb("kvst%d" % i, [128, 2, 512]) for i in range(2)]
        qst = [P.sb("qst%d" % i, [128, 528]) for i in range(2)]
        zsb = [P.sb("zsb%d" % i, [128, 512]) for i in range(2)]
        cv = P.sb("cv", [128, 256])
        vn1 = P.sb("vn1", [128, 256])
        vn2 = P.sb("vn2", [128, 256])
        vn = P.sb("vn", [128, 256])
        tmpc = P.sb("tmpc", [128, 128])
        PFM = self.PFM.h.ap()
        it = 0
        rot = 0
        for gi, (row0, ntok, s, t0) in enumerate(GROUPS):
            hTg = hTb[gi % 2]
            src = self.HT[s].h.ap().rearrange("kc p t -> p kc t")[:, :, t0:t0 + ntok + 2]
            P.sp.dma_start(out=hTg[:, :, 0:ntok + 2], in_=src, _reads=[self.HT[s]])
            for cb in range(8):
                pf = pfm[cb % 2]
                for kc in range(8):
                    P.pe.matmul(out=pf[:, 0:ntok], lhsT=Wfm[:, kc, cb * 128:(cb + 1) * 128], rhs=hTg[:, kc, 1:1 + ntok],
                                start=(kc == 0), stop=(kc == 7))
                if cb < 2:
                    ev = evb[cb % 2]
                    P.act.copy(out=ev[:, 0:ntok], in_=pf[:, 0:ntok])
                    P.sp.dma_start(out=PFM[cb * 128:(cb + 1) * 128, row0:row0 + ntok], in_=ev[:, 0:ntok], _writes=[self.PFM])
                elif cb < 4:
                    P.act.copy(out=Csb[cb - 2][:, 0:ntok], in_=pf[:, 0:ntok])
                elif cb < 6:
                    ev = evb[cb % 2]
                    P.dve.tensor_tensor(out=ev[:, 0:ntok], in0=pf[:, 0:ntok], in1=Csb[cb - 4][:, 0:ntok], op=ALU.mult)
                    P.sp.dma_start(out=PFM[256 + (cb - 4) * 128:256 + (cb - 3) * 128, row0:row0 + ntok], in_=ev[:, 0:ntok], _writes=[self.PFM])
                else:
                    P.act.activation(out=uT[:, cb - 6, 0:ntok], in_=pf[:, 0:ntok], func=AF.Gelu)
            for j in range(ntok // 128):
                b = it % 2
                it += 1
                n = (row0 + j * 128) // 128
                off = j * 128
                r0 = row0 + off
                pq = []
                for which in range(4):
                    pp = prot[rot % 3]
                    rot += 1
                    pq.append(pp)
                    if which < 3:
                        first = True
                        for tap in range(3):
                            for kc in range(8):
                                P.pe.matmul(out=pp[:], lhsT=hTg[:, kc, off + tap:off + tap + 128],
                                            rhs=Wqkv[tap][:, kc, which * 512:(which + 1) * 512],
                                            start=first, stop=(tap == 2 and kc == 7))
                                first = False
                    else:
                        for kc in range(8):
                            P.pe.matmul(out=pp[:], lhsT=hTg[:, kc, off + 1:off + 129], rhs=Wrest[:, kc, 0:512],
                                        start=(kc == 0), stop=(kc == 7))
                    if which == 0:
                        P.act.activation(out=qsb[b][:], in_=pp[:], func=AF.Silu)
                    elif which == 1:
                        P.act.activation(out=ksb[b][:], in_=pp[:], func=AF.Silu)
                    elif which == 2:
                        P.act.activation(out=kvst[b][:, 1, :], in_=pp[:], func=AF.Silu)
                    else:
                        P.act.activation(out=zsb[b][:], in_=pp[:], func=AF.Silu)
                        P.sp.dma_start(out=self.ZS.h.ap()[r0:r0 + 128, :], in_=zsb[b][:], _writes=[self.ZS])
                for kc in range(8):
                    P.pe.matmul(out=pr[:, 0:272], lhsT=hTg[:, kc, off + 1:off + 129], rhs=Wrest[:, kc, 512:784],
                                start=(kc == 0), stop=(kc == 7))
                P.dve.tensor_copy(out=qst[b][:, 512:528], in_=pr[:, 0:16])
                P.act.activation(out=cv[:], in_=pr[:, 16:272], func=AF.Gelu)
                ss8 = P.sm("ss8", 8)
                P.pool.tensor_tensor(out=sq[0][:], in0=qsb[b][:], in1=qsb[b][:], op=ALU.mult)
                P.dve.tensor_reduce(out=ss8[:, 0:4], in_=sq[0][:].rearrange("p (h d) -> p h d", h=4), axis=AX.X, op=ALU.add)
                P.pool.tensor_tensor(out=sq[1][:], in0=ksb[b][:], in1=ksb[b][:], op=ALU.mult)
                P.dve.tensor_reduce(out=ss8[:, 4:8], in_=sq[1][:].rearrange("p (h d) -> p h d", h=4), axis=AX.X, op=ALU.add)
                rs = self.rstd_chain(ss8[:], 8, 1.0, "rsqk")
                rq = P.sm("rq", 4)
                P.dve.tensor_scalar(out=rq[:], in0=rs[:, 0:4], scalar1=128.0 ** -0.5, scalar2=None, op0=ALU.mult)
                P.dve.tensor_tensor(out=qst[b][:, 0:512].rearrange("p (h d) -> p h d", h=4),
                                    in0=qsb[b][:].rearrange("p (h d) -> p h d", h=4),
                                    in1=rq[:].unsqueeze(2).broadcast_to([128, 4, 128]), op=ALU.mult)
                P.dve.tensor_tensor(out=kvst[b][:, 0, :].rearrange("p (h d) -> p h d", h=4),
                                    in0=ksb[b][:].rearrange("p (h d) -> p h d", h=4),
                                    in1=rs[:, 4:8].unsqueeze(2).broadcast_to([128, 4, 128]), op=ALU.mult)
                dst = self.SCN.h.ap()[n][:, 512:2560].rearrange("p (a b) -> p a b", b=1024)[:, :, 0:512]
                P.sp.dma_start(out=dst, in_=kvst[b][:], _writes=[self.SCN])
                P.sp.dma_start(out=self.QS.h.ap()[r0:r0 + 128, :], in_=qst[b][:], _writes=[self.QS])
                st6 = P.sm("st6", 6)
                mv = P.sm("mv", 2)
                P.dve.bn_stats(out=st6[:], in_=cv[:])
                P.dve.bn_aggr(out=mv[:], in_=st6[:])
                rl = self.rstd_chain(mv[:, 1:2], 1, 1.0, "rln")
                P.dve.tensor_scalar(out=vn1[:], in0=cv[:], scalar1=mv[:, 0:1], scalar2=rl[:, 0:1], op0=ALU.subtract, op1=ALU.mult)
                P.pool.tensor_tensor(out=vn2[:], in0=vn1[:], in1=lng[:], op=ALU.mult)
                P.pool.tensor_tensor(out=vn[:], in0=vn2[:], in1=lnb[:], op=ALU.add)
                for g in range(4):
                    P.pe.matmul(out=pc[g // 2][(g % 2) * 64:(g % 2) * 64 + 64, :], lhsT=vn[:, g * 64:(g + 1) * 64],
                                rhs=wsT[:, g, :], start=True, stop=True)
                for ct in range(2):
                    P.dve.tensor_tensor(out=tmpc[:], in0=pc[ct][:], in1=Bs[:, ct, :], op=ALU.add)
                    P.pool.tensor_tensor(out=ycT[:, ct, off:off + 128], in0=tmpc[:], in1=uT[:, ct, off:off + 128], op=ALU.mult)
            dst = self.YT.h.ap()[768:1024, row0:row0 + ntok].rearrange("(ct p) t -> p ct t", p=128)
            P.sp.dma_start(out=dst, in_=ycT[:, :, 0:ntok], _writes=[self.YT])
        P.end()
        P.keep_end()

    def gdn_prep(self, l):
        P = self.P
        c = self.c
        P.begin()
        al = P.sb("al", [128, 8])
        dtb = P.sb("dtb", [128, 8])
        ea = P.sb("ea", [128, 8])
        nea = P.sb("nea", [128, 8])
        P.sp.dma_start(out=al[:], in_=self.a_log.h.ap()[l:l + 1, :].broadcast_to([128, 8]))
        P.sp.dma_start(out=dtb[:], in_=self.dt_bias.h.ap()[l:l + 1, :].broadcast_to([128, 8]))
        P.act.activation(out=ea[:], in_=al[:], func=AF.Exp)
        P.dve.tensor_scalar(out=nea[:], in0=ea[:], scalar1=-1.0, scalar2=None, op0=ALU.mult)
        ph = P.psb("pha", 4, 128) + P.psb("phb", 4, 128)
        pA = [P.psb("pA%d" % r, 4, 128) for r in range(2)]
        pB = [P.psb("pB%d" % r, 4, 128) for r in range(2)]
        pC = [P.psb("pC%d" % r, 4, 128) for r in range(2)]
        pg = SubTile(ph[4].parent, 0, 16, "pgv")
        GU = [P.sb("GU%d" % r, [128, 512]) for r in range(2)]
        kin = [P.sb("kin%d" % i, [128, 512]) for i in range(2)]
        qs = [P.sb("qs%d" % i, [128, 528]) for i in range(2)]
        okT = [P.sb("okT%d" % i, [128, 512]) for i in range(2)]
        oqT = [P.sb("oqT%d" % i, [128, 512]) for i in range(2)]
        oT2T = [[P.sb("oT2T%d_%d" % (i, r), [128, 512]) for r in range(2)] for i in range(2)]
        oQKT = [[P.sb("oQKT%d_%d" % (i, r), [128, 512]) for r in range(2)] for i in range(2)]
        scalb = [P.sb("scal%d" % i, [128, 32]) for i in range(2)]
        sm = {nm: P.sb(nm, [128, 8]) for nm in ("e1", "d1", "x2", "e2", "sp", "g", "dl", "et")}
        gsb = P.sb("gsb", [128, 16])
        U8 = [(r, h) for r in range(2) for h in range(4)]
        mk = lambda nm: {u: P.sb("%s%d%d" % (nm, u[0], u[1]), [128, 128]) for u in U8}
        Dt, E2, a1 = mk("Dt"), mk("E2"), mk("a1")
        Pp = [mk("Pp0"), mk("Pp1")]
        PT = [mk("PT0"), mk("PT1")]
        R = [mk("R0"), mk("R1")]
        SCN = self.SCN.h.ap()
        SC2 = self.SC2.h.ap()
        hs = lambda h: slice(h * 128, (h + 1) * 128)
        for n in range(NTILE):
            b = n % 2
            P.sp.dma_start(out=kin[b][:], in_=SCN[n][:, 512:1024], _reads=[self.SCN])
            P.sp.dma_start(out=qs[b][:], in_=self.QS.h.ap()[n * 128:(n + 1) * 128, :], _reads=[self.QS])
            scal = scalb[b]
            g = sm["g"]
            P.act.activation(out=sm["e1"][:], in_=qs[b][:, 512:520], func=AF.Exp, scale=-1.0)
            P.dve.tensor_scalar(out=sm["d1"][:], in0=sm["e1"][:], scalar1=1.0, scalar2=None, op0=ALU.add)
            P.dve.reciprocal(out=scal[:, 0:8], in_=sm["d1"][:])
            P.dve.tensor_tensor(out=sm["x2"][:], in0=qs[b][:, 520:528], in1=dtb[:], op=ALU.add)
            P.act.activation(out=sm["e2"][:], in_=sm["x2"][:], func=AF.Exp)
            P.act.activation(out=sm["sp"][:], in_=sm["e2"][:], func=AF.Ln, bias=1.0)
            P.dve.tensor_tensor(out=g[:], in0=sm["sp"][:], in1=nea[:], op=ALU.mult)
            P.pe.matmul(out=pg[:, 0:4], lhsT=c["U0"][:], rhs=g[:, 0:4], start=True, stop=True)
            P.pe.matmul(out=pg[:, 4:8], lhsT=c["U1"][:], rhs=g[:, 4:8], start=True, stop=True)
            P.pe.matmul(out=pg[:, 8:16], lhsT=c["ones"][:], rhs=g[:, 0:8], start=True, stop=True)
            P.dve.tensor_copy(out=gsb[:], in_=pg[:])
            P.act.activation(out=scal[:, 8:16], in_=gsb[:, 0:8], func=AF.Exp)
            P.dve.tensor_tensor(out=sm["dl"][:], in0=gsb[:, 8:16], in1=gsb[:, 0:8], op=ALU.subtract)
            P.act.activation(out=sm["et"][:], in_=sm["dl"][:], func=AF.Exp)
            P.dve.tensor_copy(out=scal[:, 16:24], in_=sm["et"][:])
            P.act.activation(out=scal[:, 24:32], in_=gsb[:, 8:16], func=AF.Exp)
            for h in range(4):
                P.pe.transpose(out=ph[h][:], in_=kin[b][:, hs(h)], identity=c["ident"][:])
            for h in range(4):
                P.pe.transpose(out=ph[4 + h][:], in_=qs[b][:, hs(h)], identity=c["ident"][:])
            for h in range(4):
                P.act.copy(out=okT[b][:, hs(h)], in_=ph[h][:])
            for h in range(4):
                P.dve.tensor_copy(out=oqT[b][:, hs(h)], in_=ph[4 + h][:])
            for h in range(4):
                P.pe.matmul(out=ph[h][:], lhsT=okT[b][:, hs(h)], rhs=okT[b][:, hs(h)], start=True, stop=True)
            for h in range(4):
                P.pe.matmul(out=ph[4 + h][:], lhsT=okT[b][:, hs(h)], rhs=oqT[b][:, hs(h)], start=True, stop=True)
            for (r, h) in U8:
                idx = r * 4 + h
                P.act.activation(out=GU[r][:, hs(h)], in_=c["U%d" % r][:], func=AF.Copy, scale=g[:, idx:idx + 1])
            for r in range(2):
                P.pe.matmul(out=pA[r][0].parent[:], lhsT=c["ones"][:], rhs=GU[r][:], start=True, stop=True)
            for (r, h) in U8:
                idx = r * 4 + h
                P.dve.scalar_tensor_tensor(out=Dt[(r, h)][:], in0=pA[r][h][:], scalar=gsb[:, idx:idx + 1], in1=c["NM%d" % r][:],
                                           op0=ALU.subtract, op1=ALU.add)
            for u in U8:
                P.act.activation(out=E2[u][:], in_=Dt[u][:], func=AF.Exp)
            for (r, h) in U8:
                u = (r, h)
                P.dve.tensor_tensor(out=oQKT[b][r][:, hs(h)], in0=ph[4 + h][:], in1=E2[u][:], op=ALU.mult)
            for (r, h) in U8:
                u = (r, h)
                idx = r * 4 + h
                P.dve.scalar_tensor_tensor(out=a1[u][:], in0=ph[h][:], scalar=scal[:, idx:idx + 1], in1=E2[u][:],
                                           op0=ALU.mult, op1=ALU.mult)
                P.pool.tensor_tensor(out=Pp[0][u][:], in0=a1[u][:], in1=c["SU%d" % r][:], op=ALU.mult)
                P.dve.scalar_tensor_tensor(out=R[0][u][:], in0=Pp[0][u][:], scalar=-1.0, in1=c["ident"][:],
                                           op0=ALU.mult, op1=ALU.add)
            for (r, h) in U8:
                P.pe.transpose(out=pB[r][h][:], in_=Pp[0][(r, h)][:], identity=c["ident"][:])
            for (r, h) in U8:
                (P.act.copy if r == 0 else P.dve.tensor_copy)(out=PT[0][(r, h)][:], in_=pB[r][h][:])
            def level_stages(r, lvl):
                cur, nxt = lvl % 2, 1 - (lvl % 2)
                def s1():
                    for h in range(4):
                        u = (r, h)
                        if lvl < 5:
                            P.pe.matmul(out=pA[r][h][:], lhsT=PT[cur][u][:], rhs=Pp[cur][u][:], start=True, stop=True)
                        P.pe.matmul(out=pB[r][h][:], lhsT=Pp[cur][u][:], rhs=PT[cur][u][:], start=True, stop=True)
                def s2():
                    if lvl < 5:
                        for h in range(4):
                            (P.act.copy if r == 0 else P.dve.tensor_copy)(out=Pp[nxt][(r, h)][:], in_=pA[r][h][:])
                    for h in range(4):
                        (P.dve.tensor_copy if r == 0 else P.act.copy)(out=PT[nxt][(r, h)][:], in_=pB[r][h][:])
                def s3():
                    for h in range(4):
                        u = (r, h)
                        P.pe.matmul(out=pC[r][h][:], lhsT=PT[nxt][u][:], rhs=R[cur][u][:], start=True, stop=True)
                def s4():
                    for h in range(4):
                        u = (r, h)
                        dst = R[nxt][u][:] if lvl < 5 else oT2T[b][r][:, hs(h)]
                        P.dve.tensor_tensor(out=dst, in0=pC[r][h][:], in1=R[cur][u][:], op=ALU.add)
                return [s1, s2, s3, s4]
            stA = [st for lvl in range(6) for st in level_stages(0, lvl)]
            stB = [st for lvl in range(6) for st in level_stages(1, lvl)]
            for i in range(len(stA) + 1):
                if i < len(stA):
                    stA[i]()
                if i >= 1:
                    stB[i - 1]()
            w = [self.SC2]
            P.sp.dma_start(out=SC2[n][:, 0:512], in_=okT[b][:], _writes=w)
            P.sp.dma_start(out=SC2[n][:, 512:1024], in_=oqT[b][:], _writes=w)
            for r in range(2):
                P.sp.dma_start(out=SC2[n][:, 1024 + r * 512:1536 + r * 512], in_=oT2T[b][r][:], _writes=w)
                P.sp.dma_start(out=SC2[n][:, 2048 + r * 512:2560 + r * 512], in_=oQKT[b][r][:], _writes=w)
            P.sp.dma_start(out=SC2[n][:, 3072:3104], in_=scal[:], _writes=w)
        P.end()

    def gdn_scan(self, l):
        P = self.P
        P.begin()
        SCN = self.SCN.h.ap()
        SC2 = self.SC2.h.ap()
        Forder = list(range(NTILE))
        Border = [1, 0] + list(range(NTILE - 1, 1, -1))
        names = ("kT", "k", "qT", "v", "T2T", "QKT")
        col0 = {"kT": 0, "k": 512, "qT": 1024, "v": 1536}
        inb = [[{nm: P.sb("i%s%d%d" % (nm, r, i), [128, 512]) for nm in names} for i in range(2)] for r in range(2)]
        scb = [[P.sb("isc%d%d" % (r, i), [128, 32]) for i in range(2)] for r in range(2)]
        ngc = [P.sb("ngc%d" % r, [128, 8]) for r in range(2)]
        S = [[[P.sb("S%d%d%d" % (r, h, i), [128, 128]) for i in range(2)] for h in range(4)] for r in range(2)]
        pa = [P.psb("pa%d" % r, 4, 128) for r in range(2)]
        po = [P.psb("po%d" % r, 4, 128) for r in range(2)]
        rr = [[P.sb("rr%d%d" % (r, h), [128, 128]) for h in range(4)] for r in range(2)]
        vn = [[P.sb("vn%d%d" % (r, h), [128, 128]) for h in range(4)] for r in range(2)]
        vt = [[P.sb("vt%d%d" % (r, h), [128, 128]) for h in range(4)] for r in range(2)]
        t1 = [[P.sb("t1%d%d" % (r, h), [128, 128]) for h in range(4)] for r in range(2)]
        oo = [[P.sb("oo%d%d" % (r, i), [128, 512]) for i in range(2)] for r in range(2)]
        for r in range(2):
            for h in range(4):
                P.pool.memset(ap=S[r][h][0][:], constant=0.0)
        units = [(r, h) for r in range(2) for h in range(4)]
        cur = 0
        for i in range(NTILE):
            b = i % 2
            nxt = 1 - cur
            tl = (Forder[i], Border[i])
            for r in range(2):
                n = tl[r]
                d = inb[r][b]
                for nm in ("k", "v"):
                    P.sp.dma_start(out=d[nm][:], in_=SCN[n][:, col0[nm]:col0[nm] + 512], _reads=[self.SCN])
                P.sp.dma_start(out=d["kT"][:], in_=SC2[n][:, 0:512], _reads=[self.SC2])
                P.sp.dma_start(out=d["qT"][:], in_=SC2[n][:, 512:1024], _reads=[self.SC2])
                P.sp.dma_start(out=d["T2T"][:], in_=SC2[n][:, 1024 + r * 512:1536 + r * 512], _reads=[self.SC2])
                P.sp.dma_start(out=d["QKT"][:], in_=SC2[n][:, 2048 + r * 512:2560 + r * 512], _reads=[self.SC2])
                P.sp.dma_start(out=scb[r][b][:], in_=SC2[n][:, 3072:3104], _reads=[self.SC2])
                P.pool.tensor_scalar(out=ngc[r][:], in0=scb[r][b][:, 8:16], scalar1=-1.0, scalar2=None, op0=ALU.mult)
            hs = lambda h: slice(h * 128, (h + 1) * 128)

            def scan_stages(r):
                d = inb[r][b]
                sc = scb[r][b]
                def t1_():
                    for h in range(4):
                        P.pe.matmul(out=pa[r][h][:], lhsT=d["kT"][:, hs(h)], rhs=S[r][h][cur][:], start=True, stop=True)
                    for h in range(4):
                        P.pe.matmul(out=po[r][h][:], lhsT=d["qT"][:, hs(h)], rhs=S[r][h][cur][:], start=True, stop=True)
                def t2_():
                    for h in range(4):
                        idx = r * 4 + h
                        P.dve.scalar_tensor_tensor(out=rr[r][h][:], in0=pa[r][h][:], scalar=ngc[r][:, idx:idx + 1], in1=d["v"][:, hs(h)],
                                                   op0=ALU.mult, op1=ALU.add)
                    for h in range(4):
                        idx = r * 4 + h
                        P.act.activation(out=t1[r][h][:], in_=po[r][h][:], func=AF.Copy, scale=sc[:, 8 + idx:9 + idx])
                def t3_():
                    for h in range(4):
                        P.pe.matmul(out=pa[r][h][:], lhsT=d["T2T"][:, hs(h)], rhs=rr[r][h][:], start=True, stop=True)
                def t4_():
                    for h in range(4):
                        idx = r * 4 + h
                        P.act.activation(out=vn[r][h][:], in_=pa[r][h][:], func=AF.Copy, scale=sc[:, idx:idx + 1])
                    for h in range(4):
                        idx = r * 4 + h
                        P.pool.tensor_scalar(out=vt[r][h][:], in0=vn[r][h][:], scalar1=sc[:, 16 + idx:17 + idx], scalar2=None, op0=ALU.mult)
                def t5_():
                    for h in range(4):
                        P.pe.matmul(out=pa[r][h][:], lhsT=d["k"][:, hs(h)], rhs=vt[r][h][:], start=True, stop=True)
                    for h in range(4):
                        P.pe.matmul(out=po[r][h][:], lhsT=d["QKT"][:, hs(h)], rhs=vn[r][h][:], start=True, stop=True)
                def t6_():
                    for h in range(4):
                        idx = r * 4 + h
                        P.dve.scalar_tensor_tensor(out=S[r][h][nxt][:], in0=S[r][h][cur][:], scalar=sc[:, 24 + idx:25 + idx],
                                                   in1=pa[r][h][:], op0=ALU.mult, op1=ALU.add)
                    for h in range(4):
                        P.dve.tensor_tensor(out=oo[r][b][:, hs(h)], in0=po[r][h][:], in1=t1[r][h][:], op=ALU.add)
                return [t1_, t2_, t3_, t4_, t5_, t6_]
            sA, sB = scan_stages(0), scan_stages(1)
            for k_ in range(len(sA) + 1):
                if k_ < len(sA):
                    sA[k_]()
                if k_ >= 1:
                    sB[k_ - 1]()
            for r in range(2):
                n = tl[r]
                dst = (self.OF, self.OB)[r]
                P.sp.dma_start(out=dst.h.ap()[n * 128:(n + 1) * 128, :], in_=oo[r][b][:], _writes=[dst])
            cur = nxt
        P.end()

    def mixer_a(self, l):
        P = self.P
        P.begin()
        PFM = self.PFM.h.ap()
        YT = self.YT.h.ap()
        cwa = P.sb("cwa", [128, 2, 3])
        for ct in range(2):
            P.sp.dma_start(out=cwa[:, ct, :], in_=self.conv_a.h.ap()[l][:, ct * 128:(ct + 1) * 128].rearrange("j c -> c j"),
                           allow_slow_non_contiguous=True)
        ca = [P.sb("ca%d" % i, [128, T]) for i in range(2)]
        Bt = [P.sb("Bt%d" % i, [128, T]) for i in range(2)]
        acc = [P.sb("acc%d" % i, [128, T]) for i in range(2)]
        ya = [P.sb("ya%d" % i, [128, T], BF16) for i in range(2)]
        it = 0
        for s, (c0, n) in enumerate(((0, TC), (TC, T))):
            if s == 0 and l == DEPTH - 1:
                continue
            for ct in range(2):
                b = it % 2
                it += 1
                P.sp.dma_start(out=ca[b][:, 0:n], in_=PFM[256 + ct * 128:256 + (ct + 1) * 128, c0:c0 + n], _reads=[self.PFM])
                P.sp.dma_start(out=Bt[b][:, 0:n], in_=PFM[ct * 128:(ct + 1) * 128, c0:c0 + n], _reads=[self.PFM])
                w = lambda j: cwa[:, ct, j:j + 1]
                P.pool.tensor_scalar(out=acc[b][:, 0:n], in0=ca[b][:, 0:n], scalar1=w(1), scalar2=None, op0=ALU.mult)
                if s == 0:
                    sh = [(acc[b][:, 1:n], ca[b][:, 0:n - 1], 0), (acc[b][:, 0:n - 1], ca[b][:, 1:n], 2)]
                elif ct == 0:
                    av = acc[b][:, 0:n].rearrange("p (r c) -> p r c", c=64)
                    cv = ca[b][:, 0:n].rearrange("p (r c) -> p r c", c=64)
                    sh = [(av[:, :, 1:64], cv[:, :, 0:63], 0), (av[:, :, 0:63], cv[:, :, 1:64], 2)]
                else:
                    sh = [(acc[b][:, 64:n], ca[b][:, 0:n - 64], 0), (acc[b][:, 0:n - 64], ca[b][:, 64:n], 2)]
                for (dst, src, j) in sh:
                    P.dve.scalar_tensor_tensor(out=dst, in0=src, scalar=w(j), in1=dst, op0=ALU.mult, op1=ALU.add)
                P.pool.tensor_tensor(out=ya[b][:, 0:n], in0=acc[b][:, 0:n], in1=Bt[b][:, 0:n], op=ALU.mult)
                P.sp.dma_start(out=YT[ct * 128:(ct + 1) * 128, c0:c0 + n], in_=ya[b][:, 0:n], _writes=[self.YT])
        P.end()

    def post_norm_residual(self, py, xt, G, tmp, xo, ssq, junk):
        P = self.P
        for hf in range(2):
            P.act.activation(out=junk[:, 0:512], in_=py[hf][:], func=AF.Square, accum_out=ssq[:, hf:hf + 1])
        ss = P.sm("pss", 1)
        P.dve.tensor_tensor(out=ss[:], in0=ssq[:, 0:1], in1=ssq[:, 1:2], op=ALU.add)
        rstd = self.rstd_chain(ss[:], 1, 1.0 / D, "prs")
        for hf in range(2):
            P.dve.scalar_tensor_tensor(out=tmp[:, hf * 512:(hf + 1) * 512], in0=py[hf][:], scalar=rstd[:, 0:1],
                                       in1=G[:, hf * 512:(hf + 1) * 512], op0=ALU.mult, op1=ALU.mult)
        P.pool.tensor_tensor(out=xo[:], in0=tmp[:], in1=xt[:], op=ALU.add)

    def mix_out(self, l, src):
        P = self.P
        c = self.c
        last = l == DEPTH - 1
        P.keep_begin()
        Wo = P.sbk("Wo", [128, 8, D], BF16)
        P.begin()
        stg = [P.sb("stgo%d" % i, [128, D]) for i in range(2)]
        for kc in range(8):
            P.sp.dma_start(out=stg[kc % 2][:], in_=self.w_o.h.ap()[l][kc * 128:(kc + 1) * 128, :])
            (P.act.copy if kc % 2 else P.dve.tensor_copy)(out=Wo[:, kc, :], in_=stg[kc % 2][:])
        P.end()
        P.begin()
        gon = P.sb("gon", [128, 128])
        P.sp.dma_start(out=gon[:], in_=self.g_onorm.h.ap()[l:l + 1, :].broadcast_to([128, 128]))
        G = [self.load_mod(l, 2, 1, "Gpc"), self.load_mod(l, 2, 0, "Gpx")]
        of = [P.sb("of%d" % i, [128, 512]) for i in range(2)]
        ob = [P.sb("ob%d" % i, [128, 512]) for i in range(2)]
        zs = [P.sb("zs%d" % i, [128, 512]) for i in range(2)]
        xt = [P.sb("xt%d" % i, [128, D]) for i in range(2)]
        yac = [P.sb("yac%d" % i, [128, 4, 128], BF16) for i in range(2)]
        o = P.sb("o", [128, 512])
        sq = P.sb("sq", [128, 512])
        y1 = P.sb("y1", [128, 512])
        y2 = P.sb("y2", [128, 512])
        yb = P.sb("yb", [128, 512], BF16)
        ybT = [P.sb("ybT%d" % i, [128, 4, 128], BF16) for i in range(2)]
        tmp = P.sb("tmp", [128, D])
        junk = P.sb("junk", [128, 512])
        xo = [P.sb("xo%d" % i, [128, D]) for i in range(2)]
        pT = P.ps("pT", [128, 1024], BF16)
        py = [[P.ps("py%d%d" % (i, hf), [128, 512]) for hf in range(2)] for i in range(2)]
        YT = self.YT.h.ap()
        h4 = lambda ap: ap.rearrange("p (h d) -> p h d", h=4)
        it = 0
        for n in range(2 if last else 0, NTILE):
            b = it % 2
            it += 1
            s = 0 if n < 2 else 1
            r0 = n * 128
            P.sp.dma_start(out=of[b][:], in_=self.OF.h.ap()[r0:r0 + 128, :], _reads=[self.OF])
            P.sp.dma_start(out=ob[b][:], in_=self.OB.h.ap()[r0:r0 + 128, :], _reads=[self.OB])
            P.sp.dma_start(out=zs[b][:], in_=self.ZS.h.ap()[r0:r0 + 128, :], _reads=[self.ZS])
            P.sp.dma_start(out=xt[b][:], in_=src.h.ap()[r0:r0 + 128, :], _reads=[src])
            P.sp.dma_start(out=yac[b][:, 0:2, :], in_=YT[0:256, r0:r0 + 128].rearrange("(ct p) t -> p ct t", p=128), _reads=[self.YT])
            P.sp.dma_start(out=yac[b][:, 2:4, :], in_=YT[768:1024, r0:r0 + 128].rearrange("(ct p) t -> p ct t", p=128), _reads=[self.YT])
            P.pool.tensor_tensor(out=o[:], in0=of[b][:], in1=ob[b][:], op=ALU.add)
            P.pool.tensor_tensor(out=sq[:], in0=o[:], in1=o[:], op=ALU.mult)
            ss4 = P.sm("ss4", 4)
            P.dve.tensor_reduce(out=ss4[:], in_=h4(sq[:]), axis=AX.X, op=ALU.add)
            rs = self.rstd_chain(ss4[:], 4, 1.0 / 128, "rso")
            P.dve.tensor_tensor(out=h4(y1[:]), in0=h4(o[:]), in1=rs[:].unsqueeze(2).broadcast_to([128, 4, 128]), op=ALU.mult)
            P.pool.tensor_tensor(out=h4(y2[:]), in0=h4(y1[:]), in1=gon[:].unsqueeze(1).broadcast_to([128, 4, 128]), op=ALU.mult)
            P.dve.tensor_tensor(out=yb[:], in0=y2[:], in1=zs[b][:], op=ALU.mult)
            for h in range(4):
                P.pe.transpose(out=pT[:, h * 128:(h + 1) * 128], in_=yb[:, h * 128:(h + 1) * 128], identity=c["ident_bf"][:])
            P.act.copy(out=ybT[b][:], in_=pT[:, 0:512].rearrange("p (h t) -> p h t", h=4))
            lhs = [yac[b][:, 0, :], yac[b][:, 1, :]] + [ybT[b][:, h, :] for h in range(4)] + [yac[b][:, 2, :], yac[b][:, 3, :]]
            for hf in range(2):
                for kc in range(8):
                    P.pe.matmul(out=py[b][hf][:], lhsT=lhs[kc], rhs=Wo[:, kc, hf * 512:(hf + 1) * 512], start=(kc == 0), stop=(kc == 7))
            ssq = P.sm("ssq", 2)
            self.post_norm_residual(py[b], xt[b], G[s], tmp, xo[b], ssq, junk)
            P.sp.dma_start(out=self.XS.h.ap()[r0:r0 + 128, :], in_=xo[b][:], _writes=[self.XS])
        P.end()
        P.keep_end()

    def ffn(self, l):
        P = self.P
        c = self.c
        last = l == DEPTH - 1
        NJ = FFN_H // 128
        P.keep_begin()
        W1 = P.sbk("W1", [128, 8, 2 * FFN_H], BF16)
        W2 = P.sbk("W2", [128, NJ, D], BF16)
        P.begin()
        stg = [P.sb("stgf%d" % i, [128, 2 * FFN_H]) for i in range(2)]
        w1v = self.w_ffn_in.h.ap()[l]
        engs = (P.act.copy, P.dve.tensor_copy, P.pool.tensor_copy, P.act.copy)
        for kc in range(8):
            st = stg[kc % 2]
            P.sp.dma_start(out=st[:], in_=w1v[kc * 128:(kc + 1) * 128, :])
            for q in range(4):
                engs[q](out=W1[:, kc, q * 1408:(q + 1) * 1408], in_=st[:, q * 1408:(q + 1) * 1408])
        w2v = self.w_ffn_out.h.ap()[l].rearrange("(j p) n -> p j n", p=128)
        for jj, (j0, j1) in enumerate(((0, 5), (5, 10), (10, 15), (15, 20), (20, 22))):
            st = stg[jj % 2]
            nj = j1 - j0
            P.sp.dma_start(out=st[:, 0:nj * D].rearrange("p (j n) -> p j n", n=D), in_=w2v[:, j0:j1, :])
            for q in range(nj):
                engs[q % 3](out=W2[:, j0 + q, :], in_=st[:, q * D:(q + 1) * D])
        P.end()

        P.begin()
        S = [P.sb("Sf", [128, D]), None]
        Gm = [P.sb("Gf", [128, D]), None]
        Gp = [P.sb("Gpf", [128, D]), None]
        MBv = self.MB.h.ap()

        def load_mods(s):
            ms = 1 - s
            for t, k in ((S[0], 3), (Gm[0], 4), (Gp[0], 5)):
                P.sp.dma_start(out=t[:], in_=MBv[l, k, ms:ms + 1, :].broadcast_to([128, D]), _reads=[self.MB])
        xt = [[P.sb("xt%d%d" % (i, t), [128, D]) for t in range(2)] for i in range(2)]
        h1 = P.sb("h1", [128, D])
        hb = P.sb("hb", [128, D], BF16)
        hT = [P.sb("hT%d" % i, [128, 8, 256], BF16) for i in range(2)]
        sg = [P.sb("sg%d" % i, [128, 256]) for i in range(2)]
        aT = [P.sb("aT%d" % i, [128, 256], BF16) for i in range(3)]
        tmp = P.sb("tmp", [128, D])
        xo = [P.sb("xo%d" % i, [128, D]) for i in range(2)]
        pT = P.ps("pT", [128, 1024], BF16)
        pg = P.ps("pg", [128, 512])
        pu = P.ps("pu", [128, 512])
        py = [[P.ps("py%d%d" % (t, hf), [128, 512]) for hf in range(2)] for t in range(2)]
        cur_s = None
        groups = list(range(1 if last else 0, NT // 256))
        for gi, g in enumerate(groups):
            b = gi % 2
            s = 0 if g == 0 else 1
            if s != cur_s:
                load_mods(s)
                cur_s = s
            row0 = g * 256
            for t in range(2):
                r0 = row0 + t * 128
                P.sp.dma_start(out=xt[b][t][:], in_=self.XS.h.ap()[r0:r0 + 128, :], _reads=[self.XS])
                ss = P.sm("ss", 1)
                P.act.activation(out=tmp[:], in_=xt[b][t][:], func=AF.Square, accum_out=ss[:])
                rstd = self.rstd_chain(ss[:], 1, 1.0 / D, "rsf")
                P.dve.scalar_tensor_tensor(out=h1[:], in0=xt[b][t][:], scalar=rstd[:, 0:1], in1=Gm[0][:], op0=ALU.mult, op1=ALU.mult)
                P.pool.tensor_tensor(out=hb[:], in0=h1[:], in1=S[0][:], op=ALU.add)
                for kc in range(8):
                    P.pe.transpose(out=pT[:, kc * 128:(kc + 1) * 128], in_=hb[:, kc * 128:(kc + 1) * 128], identity=c["ident_bf"][:])
                P.act.copy(out=hT[b][:, :, t * 128:(t + 1) * 128], in_=pT[:].rearrange("p (kc t) -> p kc t", kc=8))

            def second(j):
                a = aT[j % 3]
                for t in range(2):
                    for hf in range(2):
                        P.pe.matmul(out=py[t][hf][:], lhsT=a[:, t * 128:(t + 1) * 128], rhs=W2[:, j, hf * 512:(hf + 1) * 512],
                                    start=(j == 0), stop=(j == NJ - 1))
            for j in range(NJ):
                for kc in range(8):
                    P.pe.matmul(out=pg[:, 0:256], lhsT=W1[:, kc, j * 128:(j + 1) * 128], rhs=hT[b][:, kc, :], start=(kc == 0), stop=(kc == 7))
                for kc in range(8):
                    P.pe.matmul(out=pu[:, 0:256], lhsT=W1[:, kc, FFN_H + j * 128:FFN_H + (j + 1) * 128], rhs=hT[b][:, kc, :], start=(kc == 0), stop=(kc == 7))
                if j > 0:
                    second(j - 1)
                P.act.activation(out=sg[j % 2][:], in_=pg[:, 0:256], func=AF.Silu)
                P.dve.tensor_tensor(out=aT[j % 3][:], in0=pu[:, 0:256], in1=sg[j % 2][:], op=ALU.mult)
            second(NJ - 1)
            for t in range(2):
                r0 = row0 + t * 128
                ssq = P.sm("ssqf", 2)
                self.post_norm_residual(py[t], xt[b][t], Gp[0], tmp, xo[t], ssq, h1)
                if last:
                    P.sp.dma_start(out=self.out.h.ap()[r0 - TC:r0 - TC + 128, :], in_=xo[t][:], _writes=[self.out])
                else:
                    P.sp.dma_start(out=self.XB.h.ap()[r0:r0 + 128, :], in_=xo[t][:], _writes=[self.XB])
        P.end()
        P.keep_end()

    def forward(self, upto=None):
        self.consts()
        for l in range(DEPTH):
            src = self.xs_in if l == 0 else self.XB
            self.mods(l)
            self.prenorm(l, 0, 1, src)
            self.proj(l)
            self.gdn_prep(l)
            self.gdn_scan(l)
            self.mixer_a(l)
            self.mix_out(l, src)
            self.ffn(l)
        self.P.close()


W_NAMES = ["w_mod", "b_mod", "g_pre_mix", "g_post_mix", "g_pre_ffn", "g_post_ffn", "w_in", "conv_a", "conv_qkv",
           "a_log", "dt_bias", "g_onorm", "ln_c_g", "ln_c_b", "w_s", "b_s", "w_o", "w_ffn_in", "w_ffn_out"]


def make_in_maps(inputs, cores=range(8)):
    f = lambda a: np.ascontiguousarray(np.asarray(a, dtype=np.float32))
    shared = {n: f(inputs[n]) for n in W_NAMES}
    shared["a_log"] = shared["a_log"].reshape(DEPTH, 8)
    shared["dt_bias"] = shared["dt_bias"].reshape(DEPTH, 8)
    x, c, ctx, c_ctx = f(inputs["x"]), f(inputs["c"]), f(inputs["ctx"]), f(inputs["c_ctx"])
    maps = []
    for b in cores:
        m = dict(shared)
        m["xs"] = np.ascontiguousarray(np.concatenate([ctx[b], x[b]], axis=0))
        cc = np.stack([c[b], c_ctx], axis=0)
        m["ccT"] = np.ascontiguousarray(cc.reshape(2, 8, 128).transpose(2, 1, 0))
        maps.append(m)
    return maps


_CACHE = {}


def kernel(**inputs):
    if "nc" not in _CACHE:
        nc = bass.Bass("TRN2", target_bir_lowering=False)
        Model(nc).forward()
        _CACHE["nc"] = nc
    nc = _CACHE["nc"]
    maps = make_in_maps(inputs)
    res = run_bass_kernel_spmd(nc, maps, core_ids=list(range(8)))
    return np.stack([np.asarray(r["out"], dtype=np.float32) for r in res.results], axis=0)
```

```python
import numpy as np
from contextlib import ExitStack
import concourse.bass as bass
import concourse.mybir as mybir
from concourse.bass_utils import run_bass_kernel_spmd

F32 = mybir.dt.float32
F32R = mybir.dt.float32r
BF16 = mybir.dt.bfloat16
AF = mybir.ActivationFunctionType
ALU = mybir.AluOpType
AX = mybir.AxisListType

ENGS = ("tensor", "vector", "scalar", "gpsimd", "sync")
SAME_ENGINE_SYNC = True


class Tile:
    def __init__(self, prog, h, name, dram=False):
        self.prog = prog
        self.h = h
        self.name = name
        self.dram = dram
        self.lastw = None
        self.readers = []
        prog.tiles[name] = self

    def __getitem__(self, idx):
        return self.h[idx]

    def ap(self):
        return self.h.ap() if self.dram else self.h[:]


class SubTile:
    def __init__(self, parent, c0, w, name):
        self.parent = parent
        self.c0 = c0
        self.w = w
        self.name = name
        self.lastw = None
        self.readers = []

    def __getitem__(self, idx):
        return self.parent.h[:, self.c0:self.c0 + self.w][idx]


class Op:
    __slots__ = ("eng", "fn", "deps", "waits", "sig", "sigval", "is_dma", "dsem", "dval", "idx", "is_load", "F")

    def __init__(self, eng, fn, is_dma=False):
        self.eng = eng
        self.fn = fn
        self.deps = []
        self.waits = []
        self.sig = False
        self.sigval = None
        self.is_dma = is_dma
        self.dsem = None
        self.dval = None


class EngProxy:
    def __init__(self, prog, eng):
        self.prog = prog
        self.eng = eng

    def __getattr__(self, meth):
        prog, eng = self.prog, self.eng

        def call(*args, **kwargs):
            reads, writes = [], []
            for k, v in kwargs.items():
                if isinstance(v, bass.AP):
                    t = prog.tiles.get(v.tensor.name)
                    if t is None:
                        continue
                    if getattr(t, "subs", None):
                        t = t.subs[(v.offset % t.rowlen) // t.subw]
                    if k in ("out", "accum_out") or (k == "ap" and meth == "memset"):
                        writes.append(t)
                    else:
                        reads.append(t)
            extra_r = kwargs.pop("_reads", [])
            extra_w = kwargs.pop("_writes", [])
            reads += extra_r
            writes += extra_w
            is_dma = meth in ("dma_start",)
            if meth == "matmul" and kwargs.get("start", True) is False:
                pass
            fn = lambda e, meth=meth, args=args, kwargs=kwargs: getattr(e, meth)(*args, **kwargs)
            return prog.record(eng, fn, reads, writes, is_dma)

        return call


class Prog:
    def __init__(self, nc):
        self.nc = nc
        self.es = ExitStack()
        self.tiles = {}
        self.csem = {e: self.es.enter_context(nc.semaphore("c_" + e)) for e in ENGS}
        self.cnt = {e: 0 for e in ENGS}
        self.known = {e: {} for e in ENGS}
        self.ops = []
        self.dma_sems = []
        self.ndma_sems = 24
        for i in range(self.ndma_sems):
            self.dma_sems.append([self.es.enter_context(nc.semaphore("d%d" % i)), 0, None])
        self.dma_rr = 0
        self.phase_es = None
        for e in ENGS:
            setattr(self, e[0] if e != "sync" else "sp", EngProxy(self, e))
        self.pe = EngProxy(self, "tensor")
        self.dve = EngProxy(self, "vector")
        self.act = EngProxy(self, "scalar")
        self.pool = EngProxy(self, "gpsimd")
        self.sp = EngProxy(self, "sync")
        self.uid = 0

    def sb(self, name, shape, dtype=F32):
        self.uid += 1
        name = "%s_%d" % (name, self.uid)
        h = self.phase_es.enter_context(self.nc.sbuf_tensor(name, list(shape), dtype))
        return Tile(self, h, name)

    def sbc(self, name, shape, dtype=F32):
        self.uid += 1
        name = "%s_%d" % (name, self.uid)
        h = self.es.enter_context(self.nc.sbuf_tensor(name, list(shape), dtype))
        return Tile(self, h, name)

    def sm(self, name, cols, depth=4):
        key = (name, cols)
        ring = self.rings.setdefault(key, [[], 0])
        if len(ring[0]) < depth:
            ring[0].append(self.sb(name, [128, cols]))
            return ring[0][-1]
        ring[1] += 1
        return ring[0][ring[1] % depth]

    def keep_begin(self):
        self.keep_es = ExitStack()

    def keep_end(self):
        self.keep_es.close()
        self.keep_es = None

    def sbk(self, name, shape, dtype=F32):
        self.uid += 1
        name = "%s_%d" % (name, self.uid)
        h = self.keep_es.enter_context(self.nc.sbuf_tensor(name, list(shape), dtype))
        return Tile(self, h, name)

    def ps(self, name, shape, dtype=F32):
        self.uid += 1
        name = "%s_%d" % (name, self.uid)
        h = self.phase_es.enter_context(self.nc.psum_tensor(name, list(shape), dtype))
        t = Tile(self, h, name)
        t.psum = True
        return t

    def psb(self, name, n, w, dtype=F32):
        rowlen = 512 if dtype == F32 else 1024
        assert n * w <= rowlen
        t = self.ps(name, [128, rowlen], dtype)
        return [SubTile(t, i * w, w, "%s.%d" % (t.name, i)) for i in range(n)]

    def dram(self, name, shape, dtype=F32, kind="Internal"):
        h = self.nc.dram_tensor(name, list(shape), dtype, kind=kind)
        return Tile(self, h, name, dram=True)

    def record(self, eng, fn, reads, writes, is_dma=False):
        op = Op(eng, fn, is_dma)
        op.idx = len(self.ops)
        op.is_load = is_dma and any(not getattr(t, "dram", False) for t in writes)
        deps = []
        for t in reads:
            if t.lastw is not None:
                deps.append(t.lastw)
            if getattr(t, "psum", False):
                deps.extend(rd for rd in t.readers if rd.eng != eng)
        for t in writes:
            if t.lastw is not None:
                deps.append(t.lastw)
            deps.extend(t.readers)
        for t in reads:
            t.readers.append(op)
        for t in writes:
            t.lastw = op
            t.readers = []
        seen = set()
        for d in deps:
            if id(d) in seen or d is op:
                continue
            seen.add(id(d))
            op.deps.append(d)
        if is_dma:
            slot = self.dma_sems[self.dma_rr % self.ndma_sems]
            self.dma_rr += 1
            prev = slot[2]
            if prev is not None:
                op.deps.append(prev)
            slot[1] += 16
            slot[2] = op
            op.dsem = slot[0]
            op.dval = slot[1]
        self.ops.append(op)
        return op

    def begin(self):
        self.phase_es = ExitStack()
        self.ops = []
        self.rings = {}

    def end(self, final=False):
        nc = self.nc
        ops = self.ops
        last = {e: None for e in ENGS}
        for op in ops:
            if not op.is_dma:
                last[op.eng] = op
        pend_dma = [s[2] for s in self.dma_sems if s[2] is not None]
        for op in ops:
            for d in op.deps:
                if d.is_dma:
                    continue
                if d.eng != op.eng or (SAME_ENGINE_SYNC and d.eng != "tensor") or op.is_dma:
                    d.sig = True
        for e in ENGS:
            if last[e] is not None:
                last[e].sig = True
        for op in ops:
            if op.is_dma:
                continue
            if op.sig:
                self.cnt[op.eng] += 1
                op.sigval = self.cnt[op.eng]
        sp_ops = [op for op in ops if op.eng == "sync"]
        spidx = {id(op): i for i, op in enumerate(sp_ops)}
        lastF = {e: -1 for e in ENGS}
        for op in ops:
            f = -1
            for d in op.deps:
                if id(d) in spidx:
                    f = max(f, spidx[id(d)])
                elif getattr(d, "F", None) is not None:
                    f = max(f, d.F)
            if op.eng != "sync":
                f = max(f, lastF[op.eng])
                lastF[op.eng] = f
            op.F = f
        keys = {}
        prev_load_key = -1.0
        for i, op in enumerate(sp_ops):
            if op.is_load:
                k = max(op.F + 0.5, prev_load_key)
                k = min(k, float(i))
                prev_load_key = k
                keys[id(op)] = k
            else:
                keys[id(op)] = float(i)
        sp_sorted = sorted(range(len(sp_ops)), key=lambda i: (keys[id(sp_ops[i])], i))
        sp_new = [sp_ops[i] for i in sp_sorted]
        per = {e: [op for op in ops if op.eng == e] for e in ENGS}
        per["sync"] = sp_new
        for e in ENGS:
            kn = self.known[e]
            for op in per[e]:
                for d in op.deps:
                    if d.is_dma:
                        key, val, sem = ("d", id(d.dsem)), d.dval, d.dsem
                    else:
                        if d.sigval is None:
                            continue
                        if d.eng == op.eng and not op.is_dma and (d.eng == "tensor" or not SAME_ENGINE_SYNC):
                            continue
                        key, val, sem = ("c", d.eng), d.sigval, self.csem[d.eng]
                    if kn.get(key, 0) >= val:
                        continue
                    kn[key] = val
                    op.waits.append((sem, val))
        bar = {}
        for e in ENGS:
            w = []
            kn = self.known[e]
            for e2 in ENGS:
                if last[e2] is None:
                    continue
                v = last[e2].sigval
                if kn.get(("c", e2), 0) < v:
                    kn[("c", e2)] = v
                    w.append((self.csem[e2], v))
            for d in pend_dma:
                key = ("d", id(d.dsem))
                if kn.get(key, 0) < d.dval:
                    kn[key] = d.dval
                    w.append((d.dsem, d.dval))
            bar[e] = w
        for s in self.dma_sems:
            s[2] = None

        with nc.Block() as block:
            def emit(e):
                def body(eng):
                    for op in per[e]:
                        for (sem, val) in op.waits:
                            eng.wait_ge(sem, val)
                        ins = op.fn(eng)
                        if op.is_dma:
                            ins.then_inc(op.dsem, 16)
                        elif op.sig:
                            ins.then_inc(self.csem[e], 1)
                    for (sem, val) in bar[e]:
                        eng.wait_ge(sem, val)
                return body
            block.tensor(emit("tensor"))
            block.vector(emit("vector"))
            block.scalar(emit("scalar"))
            block.gpsimd(emit("gpsimd"))
            block.sync(emit("sync"))
        for t in self.tiles.values():
            t.lastw = None
            t.readers = []
            for st in (getattr(t, "subs", None) or []):
                st.lastw = None
                st.readers = []
        self.phase_es.close()
        self.phase_es = None
        self.ops = []

    def close(self):
        self.es.close()


D = 1024
T = 4096
TC = 256
NT = TC + T
NTILE = NT // 128
DEPTH = 2
EPS = 1e-6
IN_COLS = 3344
OFF_A_B, OFF_A_C, OFF_A_H, OFF_Q, OFF_K, OFF_V = 0, 256, 512, 768, 1280, 1792
OFF_AB, OFF_Z, OFF_CU, OFF_CV = 2304, 2320, 2832, 3088
FFN_H = 2816
SCN_W = 8 * 512 + 32
GROUPS = [(0, 256, 0, 0)] + [(256 + 512 * i, 512, 1, 512 * i) for i in range(8)]


class Model:
    def __init__(self, nc, debug=False):
        self.nc = nc
        self.P = P = Prog(nc)
        self.debug = debug
        k_in = "ExternalInput"
        k_sc = "ExternalOutput" if debug else "Internal"
        self.xs_in = P.dram("xs", [NT, D], F32, k_in)
        self.ccT = P.dram("ccT", [128, 8, 2], F32, k_in)
        self.w_mod = P.dram("w_mod", [DEPTH, D, 6 * D], F32, k_in)
        self.b_mod = P.dram("b_mod", [DEPTH, 6 * D], F32, k_in)
        self.g_pre_mix = P.dram("g_pre_mix", [DEPTH, D], F32, k_in)
        self.g_post_mix = P.dram("g_post_mix", [DEPTH, D], F32, k_in)
        self.g_pre_ffn = P.dram("g_pre_ffn", [DEPTH, D], F32, k_in)
        self.g_post_ffn = P.dram("g_post_ffn", [DEPTH, D], F32, k_in)
        self.w_in = P.dram("w_in", [DEPTH, D, IN_COLS], F32, k_in)
        self.conv_a = P.dram("conv_a", [DEPTH, 3, 256], F32, k_in)
        self.conv_qkv = P.dram("conv_qkv", [DEPTH, 3, 1536], F32, k_in)
        self.a_log = P.dram("a_log", [DEPTH, 8], F32, k_in)
        self.dt_bias = P.dram("dt_bias", [DEPTH, 8], F32, k_in)
        self.g_onorm = P.dram("g_onorm", [DEPTH, 128], F32, k_in)
        self.ln_c_g = P.dram("ln_c_g", [DEPTH, 256], F32, k_in)
        self.ln_c_b = P.dram("ln_c_b", [DEPTH, 256], F32, k_in)
        self.w_s = P.dram("w_s", [DEPTH, 4, 128, 128], F32, k_in)
        self.b_s = P.dram("b_s", [DEPTH, 4, 128], F32, k_in)
        self.w_o = P.dram("w_o", [DEPTH, D, D], F32, k_in)
        self.w_ffn_in = P.dram("w_ffn_in", [DEPTH, D, 2 * FFN_H], F32, k_in)
        self.w_ffn_out = P.dram("w_ffn_out", [DEPTH, FFN_H, D], F32, k_in)
        self.out = P.dram("out", [T, D], F32, "ExternalOutput")
        self.XS = P.dram("XS", [NT, D], F32, k_sc)
        self.MB = P.dram("MB", [DEPTH, 6, 2, D], F32, k_sc)
        self.HT = [P.dram("HTc", [8, 128, TC + 2], BF16, k_sc), P.dram("HTx", [8, 128, T + 2], BF16, k_sc)]
        self.PFM = P.dram("PFM", [512, NT], F32, k_sc)
        self.YT = P.dram("YT", [1024, NT], BF16, k_sc)
        self.ZS = P.dram("ZS", [NT, 512], F32, k_sc)
        self.QS = P.dram("QS", [NT, 528], F32, k_sc)
        self.SCN = P.dram("SCN", [NTILE, 128, SCN_W], F32, k_sc)
        self.SC2 = P.dram("SC2", [NTILE, 128, 6 * 512 + 32], F32, k_sc)
        self.XB = P.dram("XB", [NT, D], F32, k_sc)
        self.OF = P.dram("OF", [NT, 512], F32, k_sc)
        self.OB = P.dram("OB", [NT, 512], F32, k_sc)

    def consts(self):
        P = self.P
        c = self.c = {}
        for nm in ("ones", "U0", "U1", "SU0", "SU1", "NM0", "NM1", "ident", "NU0", "NU1"):
            c[nm] = P.sbc(nm, [128, 128])
        c["ident_bf"] = P.sbc("ident_bf", [128, 128], BF16)
        P.begin()
        ones = c["ones"]
        zer = P.sb("zer", [128, 128])
        P.pool.memset(ap=ones[:], constant=1.0)
        P.pool.memset(ap=zer[:], constant=0.0)

        def sel(name, src, step, cm, base, op, fill):
            t = c[name]
            P.pool.affine_select(out=t[:], in_=src[:], pattern=[[step, 128]], compare_op=op,
                                 fill=fill, base=base, channel_multiplier=cm)
            return t
        sel("U0", ones, 1, -1, 0, ALU.is_ge, 0.0)
        sel("U1", ones, -1, 1, 0, ALU.is_ge, 0.0)
        sel("SU0", ones, 1, -1, -1, ALU.is_ge, 0.0)
        sel("SU1", ones, -1, 1, -1, ALU.is_ge, 0.0)
        sel("NM0", zer, 1, -1, 0, ALU.is_ge, -30000.0)
        sel("NM1", zer, -1, 1, 0, ALU.is_ge, -30000.0)
        sel("ident", ones, 1, -1, 0, ALU.is_equal, 0.0)
        for r in (0, 1):
            P.pool.tensor_scalar(out=c["NU%d" % r][:], in0=c["U%d" % r][:], scalar1=-1.0, scalar2=None, op0=ALU.mult)
        P.pool.tensor_copy(out=c["ident_bf"][:], in_=c["ident"][:])
        zb = P.sb("zb", [128, 8, 1], BF16)
        P.pool.memset(ap=zb[:], constant=0.0)
        for s, n in ((0, TC), (1, T)):
            v = self.HT[s].h.ap().rearrange("kc p t -> p kc t")
            P.sp.dma_start(out=v[:, :, 0:1], in_=zb[:], _writes=[self.HT[s]], allow_slow_non_contiguous=True)
            P.sp.dma_start(out=v[:, :, n + 1:n + 2], in_=zb[:], _writes=[self.HT[s]], allow_slow_non_contiguous=True)
        P.end()

    def mods(self, l):
        P = self.P
        P.begin()
        cT = P.sb("cT", [128, 8, 2])
        sT = P.sb("sT", [128, 8, 2])
        P.sp.dma_start(out=cT[:], in_=self.ccT.ap())
        P.act.activation(out=sT[:], in_=cT[:], func=AF.Silu)
        bm = P.sb("bm", [2, 6 * D])
        P.sp.dma_start(out=bm[:], in_=self.b_mod.h.ap()[l:l + 1, :].broadcast_to([2, 6 * D]))
        mods = P.sb("mods", [2, 6 * D])
        wbuf = [P.sb("wm%d" % i, [128, 8, 512]) for i in range(2)]
        pm = [P.ps("pm%d" % i, [2, 512]) for i in range(2)]
        wv = self.w_mod.h.ap()[l].rearrange("(kc p) n -> p kc n", p=128)
        for nb in range(12):
            wb = wbuf[nb % 2]
            P.sp.dma_start(out=wb[:], in_=wv[:, :, nb * 512:(nb + 1) * 512])
            pp = pm[nb % 2]
            for kc in range(8):
                P.pe.matmul(out=pp[:], lhsT=sT[:, kc, :], rhs=wb[:, kc, :], start=(kc == 0), stop=(kc == 7))
            P.dve.tensor_tensor(out=mods[:, nb * 512:(nb + 1) * 512], in0=pp[:], in1=bm[:, nb * 512:(nb + 1) * 512], op=ALU.add)
        gv = P.sb("gv", [2, 4, D])
        for i, g in enumerate((self.g_pre_mix, self.g_post_mix, self.g_pre_ffn, self.g_post_ffn)):
            P.sp.dma_start(out=gv[:, i, :], in_=g.h.ap()[l:l + 1, :].broadcast_to([2, D]))
        mb = P.sb("mb", [2, 6, D])
        m = lambda i: mods[:, i * D:(i + 1) * D]
        P.dve.tensor_copy(out=mb[:, 0, :], in_=m(0))
        P.dve.scalar_tensor_tensor(out=mb[:, 1, :], in0=m(1), scalar=1.0, in1=gv[:, 0, :], op0=ALU.add, op1=ALU.mult)
        P.dve.tensor_tensor(out=mb[:, 2, :], in0=m(2), in1=gv[:, 1, :], op=ALU.mult)
        P.dve.tensor_copy(out=mb[:, 3, :], in_=m(3))
        P.dve.scalar_tensor_tensor(out=mb[:, 4, :], in0=m(4), scalar=1.0, in1=gv[:, 2, :], op0=ALU.add, op1=ALU.mult)
        P.dve.tensor_tensor(out=mb[:, 5, :], in0=m(5), in1=gv[:, 3, :], op=ALU.mult)
        P.sp.dma_start(out=self.MB.h.ap()[l].rearrange("k s d -> s k d"), in_=mb[:], _writes=[self.MB])
        P.end()

    def load_mod(self, l, k, s, name):
        P = self.P
        t = P.sb(name, [128, D])
        P.sp.dma_start(out=t[:], in_=self.MB.h.ap()[l, k, s:s + 1, :].broadcast_to([128, D]), _reads=[self.MB])
        return t

    def rstd_chain(self, ss, n, scale, name, post=None):
        P = self.P
        t1 = P.sm(name + "a", n)
        t2 = P.sm(name + "b", n)
        t3 = P.sm(name + "c", n)
        P.dve.tensor_scalar(out=t1[:], in0=ss, scalar1=scale, scalar2=EPS, op0=ALU.mult, op1=ALU.add)
        P.act.activation(out=t2[:], in_=t1[:], func=AF.Sqrt)
        P.dve.reciprocal(out=t3[:], in_=t2[:])
        return t3

    def prenorm(self, l, ks, kg, src):
        P = self.P
        c = self.c
        P.begin()
        Sx = [self.load_mod(l, ks, 1, "Sc"), self.load_mod(l, ks, 0, "Sx")]
        Gx = [self.load_mod(l, kg, 1, "Gc"), self.load_mod(l, kg, 0, "Gx")]
        xt = [P.sb("xt%d" % i, [128, D]) for i in range(2)]
        junk = P.sb("junk", [128, D])
        h1 = [P.sb("h1%d" % i, [128, D]) for i in range(2)]
        hb = [P.sb("hb%d" % i, [128, D], BF16) for i in range(2)]
        pT = [P.ps("pT%d" % i, [128, D], BF16) for i in range(2)]
        hT = [P.sb("hT%d" % i, [128, 8, 512], BF16) for i in range(2)]
        it = 0
        for gi, (row0, ntok, s, t0) in enumerate(GROUPS):
            hTg = hT[gi % 2]
            for j in range(ntok // 128):
                b = it % 2
                it += 1
                r0 = row0 + j * 128
                P.sp.dma_start(out=xt[b][:], in_=src.h.ap()[r0:r0 + 128, :], _reads=[src])
                ss = P.sm("ss", 1)
                P.act.activation(out=junk[:], in_=xt[b][:], func=AF.Square, accum_out=ss[:])
                rstd = self.rstd_chain(ss[:], 1, 1.0 / D, "rs")
                P.dve.scalar_tensor_tensor(out=h1[b][:], in0=xt[b][:], scalar=rstd[:, 0:1], in1=Gx[s][:], op0=ALU.mult, op1=ALU.mult)
                P.pool.tensor_tensor(out=hb[b][:], in0=h1[b][:], in1=Sx[s][:], op=ALU.add)
                for kc in range(8):
                    P.pe.transpose(out=pT[b][:, kc * 128:(kc + 1) * 128], in_=hb[b][:, kc * 128:(kc + 1) * 128], identity=c["ident_bf"][:])
                P.act.copy(out=hTg[:, :, j * 128:(j + 1) * 128], in_=pT[b][:].rearrange("p (kc t) -> p kc t", kc=8))
            dst = self.HT[s].h.ap().rearrange("kc p t -> p kc t")[:, :, 1 + t0:1 + t0 + ntok]
            P.sp.dma_start(out=dst, in_=hTg[:, :, 0:ntok], _writes=[self.HT[s]])
        P.end()

    def proj(self, l):
        P = self.P
        c = self.c
        P.keep_begin()
        Wfm = P.sbk("Wfm", [128, 8, 1024], BF16)
        Wrest = P.sbk("Wrest", [128, 8, 784], BF16)
        Wqkv = [P.sbk("Wq%d" % j, [128, 8, 1536], BF16) for j in range(3)]
        P.begin()
        cw = P.sb("cw", [128, 3, 1536])
        P.sp.dma_start(out=cw[:], in_=self.conv_qkv.h.ap()[l:l + 1].broadcast_to([128, 3, 1536]))
        stg = [P.sb("stg%d" % i, [128, IN_COLS]) for i in range(2)]
        wv = self.w_in.h.ap()[l]
        for kc in range(8):
            st = stg[kc % 2]
            P.sp.dma_start(out=st[:], in_=wv[kc * 128:(kc + 1) * 128, :])
            P.act.copy(out=Wfm[:, kc, 0:768], in_=st[:, 0:768])
            P.act.copy(out=Wfm[:, kc, 768:1024], in_=st[:, OFF_CU:OFF_CV])
            P.act.copy(out=Wrest[:, kc, 0:512], in_=st[:, OFF_Z:OFF_CU])
            P.act.copy(out=Wrest[:, kc, 512:528], in_=st[:, OFF_AB:OFF_Z])
            P.act.copy(out=Wrest[:, kc, 528:784], in_=st[:, OFF_CV:IN_COLS])
            for j in range(3):
                eng = (P.dve, P.pool, P.dve)[j]
                eng.tensor_tensor(out=Wqkv[j][:, kc, :], in0=st[:, OFF_Q:OFF_AB], in1=cw[:, j, :], op=ALU.mult)
        P.end()

        P.begin()
        pfm = [P.ps("pfm%d" % i, [128, 512]) for i in range(2)]
        prot = [P.ps("prot%d" % i, [128, 512]) for i in range(3)]
        pr = P.ps("pr", [128, 512])
        pc = [P.ps("pc%d" % i, [128, 128]) for i in range(2)]
        wsl = P.sb("wsl", [128, 4, 128])
        wsT = P.sb("wsT", [128, 4, 128])
        P.sp.dma_start(out=wsl[:], in_=self.w_s.h.ap()[l].rearrange("g p q -> p g q"))
        for g in range(4):
            P.pe.transpose(out=pr[:, g * 128:(g + 1) * 128], in_=wsl[:, g, :], identity=c["ident"][:])
        P.dve.tensor_copy(out=wsT[:], in_=pr[:].rearrange("p (g q) -> p g q", g=4))
        Bs = P.sb("Bs", [128, 2, 128])
        for g in range(4):
            P.sp.dma_start(out=Bs[(g % 2) * 64:(g % 2) * 64 + 64, g // 2, :],
                           in_=self.b_s.h.ap()[l, g:g + 1, :].broadcast_to([64, 128]))
        lng = P.sb("lng", [128, 256])
        lnb = P.sb("lnb", [128, 256])
        P.sp.dma_start(out=lng[:], in_=self.ln_c_g.h.ap()[l:l + 1, :].broadcast_to([128, 256]))
        P.sp.dma_start(out=lnb[:], in_=self.ln_c_b.h.ap()[l:l + 1, :].broadcast_to([128, 256]))
        hTb = [P.sb("hTg%d" % i, [128, 8, 514], BF16) for i in range(2)]
        evb = [P.sb("evb%d" % i, [128, 512]) for i in range(2)]
        Csb = [P.sb("Csb%d" % i, [128, 512]) for i in range(2)]
        uT = P.sb("uT", [128, 2, 512])
        ycT = P.sb("ycT", [128, 2, 512], BF16)
        qsb = [P.sb("qsb%d" % i, [128, 512]) for i in range(2)]
        ksb = [P.sb("ksb%d" % i, [128, 512]) for i in range(2)]
        sq = [P.sb("sq%d" % i, [128, 512]) for i in range(2)]
        kvst = [P.sb("kvst%d" % i, [128, 2, 512]) for i in range(2)]
        qst = [P.sb("qst%d" % i, [128, 528]) for i in range(2)]
        zsb = [P.sb("zsb%d" % i, [128, 512]) for i in range(2)]
        cv = P.sb("cv", [128, 256])
        vn1 = P.sb("vn1", [128, 256])
        vn2 = P.sb("vn2", [128, 256])
        vn = P.sb("vn", [128, 256])
        tmpc = P.sb("tmpc", [128, 128])
        PFM = self.PFM.h.ap()
        it = 0
        rot = 0
        for gi, (row0, ntok, s, t0) in enumerate(GROUPS):
            hTg = hTb[gi % 2]
            src = self.HT[s].h.ap().rearrange("kc p t -> p kc t")[:, :, t0:t0 + ntok + 2]
            P.sp.dma_start(out=hTg[:, :, 0:ntok + 2], in_=src, _reads=[self.HT[s]])
            for cb in range(8):
                pf = pfm[cb % 2]
                for kc in range(8):
                    P.pe.matmul(out=pf[:, 0:ntok], lhsT=Wfm[:, kc, cb * 128:(cb + 1) * 128], rhs=hTg[:, kc, 1:1 + ntok],
                                start=(kc == 0), stop=(kc == 7))
                if cb < 2:
                    ev = evb[cb % 2]
                    P.act.copy(out=ev[:, 0:ntok], in_=pf[:, 0:ntok])
                    P.sp.dma_start(out=PFM[cb * 128:(cb + 1) * 128, row0:row0 + ntok], in_=ev[:, 0:ntok], _writes=[self.PFM])
                elif cb < 4:
                    P.act.copy(out=Csb[cb - 2][:, 0:ntok], in_=pf[:, 0:ntok])
                elif cb < 6:
                    ev = evb[cb % 2]
                    P.dve.tensor_tensor(out=ev[:, 0:ntok], in0=pf[:, 0:ntok], in1=Csb[cb - 4][:, 0:ntok], op=ALU.mult)
                    P.sp.dma_start(out=PFM[256 + (cb - 4) * 128:256 + (cb - 3) * 128, row0:row0 + ntok], in_=ev[:, 0:ntok], _writes=[self.PFM])
                else:
                    P.act.activation(out=uT[:, cb - 6, 0:ntok], in_=pf[:, 0:ntok], func=AF.Gelu)
            for j in range(ntok // 128):
                b = it % 2
                it += 1
                n = (row0 + j * 128) // 128
                off = j * 128
                r0 = row0 + off
                pq = []
                for which in range(4):
                    pp = prot[rot % 3]
                    rot += 1
                    pq.append(pp)
                    if which < 3:
                        first = True
                        for tap in range(3):
                            for kc in range(8):
                                P.pe.matmul(out=pp[:], lhsT=hTg[:, kc, off + tap:off + tap + 128],
                                            rhs=Wqkv[tap][:, kc, which * 512:(which + 1) * 512],
                                            start=first, stop=(tap == 2 and kc == 7))
                                first = False
                    else:
                        for kc in range(8):
                            P.pe.matmul(out=pp[:], lhsT=hTg[:, kc, off + 1:off + 129], rhs=Wrest[:, kc, 0:512],
                                        start=(kc == 0), stop=(kc == 7))
                    if which == 0:
                        P.act.activation(out=qsb[b][:], in_=pp[:], func=AF.Silu)
                    elif which == 1:
                        P.act.activation(out=ksb[b][:], in_=pp[:], func=AF.Silu)
                    elif which == 2:
                        P.act.activation(out=kvst[b][:, 1, :], in_=pp[:], func=AF.Silu)
                    else:
                        P.act.activation(out=zsb[b][:], in_=pp[:], func=AF.Silu)
                        P.sp.dma_start(out=self.ZS.h.ap()[r0:r0 + 128, :], in_=zsb[b][:], _writes=[self.ZS])
                for kc in range(8):
                    P.pe.matmul(out=pr[:, 0:272], lhsT=hTg[:, kc, off + 1:off + 129], rhs=Wrest[:, kc, 512:784],
                                start=(kc == 0), stop=(kc == 7))
                P.dve.tensor_copy(out=qst[b][:, 512:528], in_=pr[:, 0:16])
                P.act.activation(out=cv[:], in_=pr[:, 16:272], func=AF.Gelu)
                ss8 = P.sm("ss8", 8)
                P.pool.tensor_tensor(out=sq[0][:], in0=qsb[b][:], in1=qsb[b][:], op=ALU.mult)
                P.dve.tensor_reduce(out=ss8[:, 0:4], in_=sq[0][:].rearrange("p (h d) -> p h d", h=4), axis=AX.X, op=ALU.add)
                P.pool.tensor_tensor(out=sq[1][:], in0=ksb[b][:], in1=ksb[b][:], op=ALU.mult)
                P.dve.tensor_reduce(out=ss8[:, 4:8], in_=sq[1][:].rearrange("p (h d) -> p h d", h=4), axis=AX.X, op=ALU.add)
                rs = self.rstd_chain(ss8[:], 8, 1.0, "rsqk")
                rq = P.sm("rq", 4)
                P.dve.tensor_scalar(out=rq[:], in0=rs[:, 0:4], scalar1=128.0 ** -0.5, scalar2=None, op0=ALU.mult)
                P.dve.tensor_tensor(out=qst[b][:, 0:512].rearrange("p (h d) -> p h d", h=4),
                                    in0=qsb[b][:].rearrange("p (h d) -> p h d", h=4),
                                    in1=rq[:].unsqueeze(2).broadcast_to([128, 4, 128]), op=ALU.mult)
                P.dve.tensor_tensor(out=kvst[b][:, 0, :].rearrange("p (h d) -> p h d", h=4),
                                    in0=ksb[b][:].rearrange("p (h d) -> p h d", h=4),
                                    in1=rs[:, 4:8].unsqueeze(2).broadcast_to([128, 4, 128]), op=ALU.mult)
                dst = self.SCN.h.ap()[n][:, 512:2560].rearrange("p (a b) -> p a b", b=1024)[:, :, 0:512]
                P.sp.dma_start(out=dst, in_=kvst[b][:], _writes=[self.SCN])
                P.sp.dma_start(out=self.QS.h.ap()[r0:r0 + 128, :], in_=qst[b][:], _writes=[self.QS])
                st6 = P.sm("st6", 6)
                mv = P.sm("mv", 2)
                P.dve.bn_stats(out=st6[:], in_=cv[:])
                P.dve.bn_aggr(out=mv[:], in_=st6[:])
                rl = self.rstd_chain(mv[:, 1:2], 1, 1.0, "rln")
                P.dve.tensor_scalar(out=vn1[:], in0=cv[:], scalar1=mv[:, 0:1], scalar2=rl[:, 0:1], op0=ALU.subtract, op1=ALU.mult)
                P.pool.tensor_tensor(out=vn2[:], in0=vn1[:], in1=lng[:], op=ALU.mult)
                P.pool.tensor_tensor(out=vn[:], in0=vn2[:], in1=lnb[:], op=ALU.add)
                for g in range(4):
                    P.pe.matmul(out=pc[g // 2][(g % 2) * 64:(g % 2) * 64 + 64, :], lhsT=vn[:, g * 64:(g + 1) * 64],
                                rhs=wsT[:, g, :], start=True, stop=True)
                for ct in range(2):
                    P.dve.tensor_tensor(out=tmpc[:], in0=pc[ct][:], in1=Bs[:, ct, :], op=ALU.add)
                    P.pool.tensor_tensor(out=ycT[:, ct, off:off + 128], in0=tmpc[:], in1=uT[:, ct, off:off + 128], op=ALU.mult)
            dst = self.YT.h.ap()[768:1024, row0:row0 + ntok].rearrange("(ct p) t -> p ct t", p=128)
            P.sp.dma_start(out=dst, in_=ycT[:, :, 0:ntok], _writes=[self.YT])
        P.end()
        P.keep_end()

    def gdn_prep(self, l):
        P = self.P
        c = self.c
        P.begin()
        al = P.sb("al", [128, 8])
        dtb = P.sb("dtb", [128, 8])
        ea = P.sb("ea", [128, 8])
        nea = P.sb("nea", [128, 8])
        P.sp.dma_start(out=al[:], in_=self.a_log.h.ap()[l:l + 1, :].broadcast_to([128, 8]))
        P.sp.dma_start(out=dtb[:], in_=self.dt_bias.h.ap()[l:l + 1, :].broadcast_to([128, 8]))
        P.act.activation(out=ea[:], in_=al[:], func=AF.Exp)
        P.dve.tensor_scalar(out=nea[:], in0=ea[:], scalar1=-1.0, scalar2=None, op0=ALU.mult)
        ph = P.psb("pha", 4, 128) + P.psb("phb", 4, 128)
        pA = [P.psb("pA%d" % r, 4, 128) for r in range(2)]
        pB = [P.psb("pB%d" % r, 4, 128) for r in range(2)]
        pC = [P.psb("pC%d" % r, 4, 128) for r in range(2)]
        pg = SubTile(ph[4].parent, 0, 16, "pgv")
        GU = [P.sb("GU%d" % r, [128, 512]) for r in range(2)]
        kin = [P.sb("kin%d" % i, [128, 512]) for i in range(2)]
        qs = [P.sb("qs%d" % i, [128, 528]) for i in range(2)]
        okT = [P.sb("okT%d" % i, [128, 512]) for i in range(2)]
        oqT = [P.sb("oqT%d" % i, [128, 512]) for i in range(2)]
        oT2T = [[P.sb("oT2T%d_%d" % (i, r), [128, 512]) for r in range(2)] for i in range(2)]
        oQKT = [[P.sb("oQKT%d_%d" % (i, r), [128, 512]) for r in range(2)] for i in range(2)]
        scalb = [P.sb("scal%d" % i, [128, 32]) for i in range(2)]
        sm = {nm: P.sb(nm, [128, 8]) for nm in ("e1", "d1", "x2", "e2", "sp", "g", "dl", "et")}
        gsb = P.sb("gsb", [128, 16])
        U8 = [(r, h) for r in range(2) for h in range(4)]
        mk = lambda nm: {u: P.sb("%s%d%d" % (nm, u[0], u[1]), [128, 128]) for u in U8}
        Dt, E2, a1 = mk("Dt"), mk("E2"), mk("a1")
        Pp = [mk("Pp0"), mk("Pp1")]
        PT = [mk("PT0"), mk("PT1")]
        R = [mk("R0"), mk("R1")]
        SCN = self.SCN.h.ap()
        SC2 = self.SC2.h.ap()
        hs = lambda h: slice(h * 128, (h + 1) * 128)
        for n in range(NTILE):
            b = n % 2
            P.sp.dma_start(out=kin[b][:], in_=SCN[n][:, 512:1024], _reads=[self.SCN])
            P.sp.dma_start(out=qs[b][:], in_=self.QS.h.ap()[n * 128:(n + 1) * 128, :], _reads=[self.QS])
            scal = scalb[b]
            g = sm["g"]
            P.act.activation(out=sm["e1"][:], in_=qs[b][:, 512:520], func=AF.Exp, scale=-1.0)
            P.dve.tensor_scalar(out=sm["d1"][:], in0=sm["e1"][:], scalar1=1.0, scalar2=None, op0=ALU.add)
            P.dve.reciprocal(out=scal[:, 0:8], in_=sm["d1"][:])
            P.dve.tensor_tensor(out=sm["x2"][:], in0=qs[b][:, 520:528], in1=dtb[:], op=ALU.add)
            P.act.activation(out=sm["e2"][:], in_=sm["x2"][:], func=AF.Exp)
            P.act.activation(out=sm["sp"][:], in_=sm["e2"][:], func=AF.Ln, bias=1.0)
            P.dve.tensor_tensor(out=g[:], in0=sm["sp"][:], in1=nea[:], op=ALU.mult)
            P.pe.matmul(out=pg[:, 0:4], lhsT=c["U0"][:], rhs=g[:, 0:4], start=True, stop=True)
            P.pe.matmul(out=pg[:, 4:8], lhsT=c["U1"][:], rhs=g[:, 4:8], start=True, stop=True)
            P.pe.matmul(out=pg[:, 8:16], lhsT=c["ones"][:], rhs=g[:, 0:8], start=True, stop=True)
            P.dve.tensor_copy(out=gsb[:], in_=pg[:])
            P.act.activation(out=scal[:, 8:16], in_=gsb[:, 0:8], func=AF.Exp)
            P.dve.tensor_tensor(out=sm["dl"][:], in0=gsb[:, 8:16], in1=gsb[:, 0:8], op=ALU.subtract)
            P.act.activation(out=sm["et"][:], in_=sm["dl"][:], func=AF.Exp)
            P.dve.tensor_copy(out=scal[:, 16:24], in_=sm["et"][:])
            P.act.activation(out=scal[:, 24:32], in_=gsb[:, 8:16], func=AF.Exp)
            for h in range(4):
                P.pe.transpose(out=ph[h][:], in_=kin[b][:, hs(h)], identity=c["ident"][:])
            for h in range(4):
                P.pe.transpose(out=ph[4 + h][:], in_=qs[b][:, hs(h)], identity=c["ident"][:])
            for h in range(4):
                P.act.copy(out=okT[b][:, hs(h)], in_=ph[h][:])
            for h in range(4):
                P.dve.tensor_copy(out=oqT[b][:, hs(h)], in_=ph[4 + h][:])
            for h in range(4):
                P.pe.matmul(out=ph[h][:], lhsT=okT[b][:, hs(h)], rhs=okT[b][:, hs(h)], start=True, stop=True)
            for h in range(4):
                P.pe.matmul(out=ph[4 + h][:], lhsT=okT[b][:, hs(h)], rhs=oqT[b][:, hs(h)], start=True, stop=True)
            for (r, h) in U8:
                idx = r * 4 + h
                P.act.activation(out=GU[r][:, hs(h)], in_=c["U%d" % r][:], func=AF.Copy, scale=g[:, idx:idx + 1])
            for r in range(2):
                P.pe.matmul(out=pA[r][0].parent[:], lhsT=c["ones"][:], rhs=GU[r][:], start=True, stop=True)
            for (r, h) in U8:
                idx = r * 4 + h
                P.dve.scalar_tensor_tensor(out=Dt[(r, h)][:], in0=pA[r][h][:], scalar=gsb[:, idx:idx + 1], in1=c["NM%d" % r][:],
                                           op0=ALU.subtract, op1=ALU.add)
            for u in U8:
                P.act.activation(out=E2[u][:], in_=Dt[u][:], func=AF.Exp)
            for (r, h) in U8:
                u = (r, h)
                P.dve.tensor_tensor(out=oQKT[b][r][:, hs(h)], in0=ph[4 + h][:], in1=E2[u][:], op=ALU.mult)
            for (r, h) in U8:
                u = (r, h)
                idx = r * 4 + h
                P.dve.scalar_tensor_tensor(out=a1[u][:], in0=ph[h][:], scalar=scal[:, idx:idx + 1], in1=E2[u][:],
                                           op0=ALU.mult, op1=ALU.mult)
                P.pool.tensor_tensor(out=Pp[0][u][:], in0=a1[u][:], in1=c["SU%d" % r][:], op=ALU.mult)
                P.dve.scalar_tensor_tensor(out=R[0][u][:], in0=Pp[0][u][:], scalar=-1.0, in1=c["ident"][:],
                                           op0=ALU.mult, op1=ALU.add)
            for (r, h) in U8:
                P.pe.transpose(out=pB[r][h][:], in_=Pp[0][(r, h)][:], identity=c["ident"][:])
            for (r, h) in U8:
                (P.act.copy if r == 0 else P.dve.tensor_copy)(out=PT[0][(r, h)][:], in_=pB[r][h][:])
            def level_stages(r, lvl):
                cur, nxt = lvl % 2, 1 - (lvl % 2)
                def s1():
                    for h in range(4):
                        u = (r, h)
                        if lvl < 5:
                            P.pe.matmul(out=pA[r][h][:], lhsT=PT[cur][u][:], rhs=Pp[cur][u][:], start=True, stop=True)
                        P.pe.matmul(out=pB[r][h][:], lhsT=Pp[cur][u][:], rhs=PT[cur][u][:], start=True, stop=True)
                def s2():
                    if lvl < 5:
                        for h in range(4):
                            (P.act.copy if r == 0 else P.dve.tensor_copy)(out=Pp[nxt][(r, h)][:], in_=pA[r][h][:])
                    for h in range(4):
                        (P.dve.tensor_copy if r == 0 else P.act.copy)(out=PT[nxt][(r, h)][:], in_=pB[r][h][:])
                def s3():
                    for h in range(4):
                        u = (r, h)
                        P.pe.matmul(out=pC[r][h][:], lhsT=PT[nxt][u][:], rhs=R[cur][u][:], start=True, stop=True)
                def s4():
                    for h in range(4):
                        u = (r, h)
                        dst = R[nxt][u][:] if lvl < 5 else oT2T[b][r][:, hs(h)]
                        P.dve.tensor_tensor(out=dst, in0=pC[r][h][:], in1=R[cur][u][:], op=ALU.add)
                return [s1, s2, s3, s4]
            stA = [st for lvl in range(6) for st in level_stages(0, lvl)]
            stB = [st for lvl in range(6) for st in level_stages(1, lvl)]
            for i in range(len(stA) + 1):
                if i < len(stA):
                    stA[i]()
                if i >= 1:
                    stB[i - 1]()
            w = [self.SC2]
            P.sp.dma_start(out=SC2[n][:, 0:512], in_=okT[b][:], _writes=w)
            P.sp.dma_start(out=SC2[n][:, 512:1024], in_=oqT[b][:], _writes=w)
            for r in range(2):
                P.sp.dma_start(out=SC2[n][:, 1024 + r * 512:1536 + r * 512], in_=oT2T[b][r][:], _writes=w)
                P.sp.dma_start(out=SC2[n][:, 2048 + r * 512:2560 + r * 512], in_=oQKT[b][r][:], _writes=w)
            P.sp.dma_start(out=SC2[n][:, 3072:3104], in_=scal[:], _writes=w)
        P.end()

    def gdn_scan(self, l):
        P = self.P
        P.begin()
        SCN = self.SCN.h.ap()
        SC2 = self.SC2.h.ap()
        Forder = list(range(NTILE))
        Border = [1, 0] + list(range(NTILE - 1, 1, -1))
        names = ("kT", "k", "qT", "v", "T2T", "QKT")
        col0 = {"kT": 0, "k": 512, "qT": 1024, "v": 1536}
        inb = [[{nm: P.sb("i%s%d%d" % (nm, r, i), [128, 512]) for nm in names} for i in range(2)] for r in range(2)]
        scb = [[P.sb("isc%d%d" % (r, i), [128, 32]) for i in range(2)] for r in range(2)]
        ngc = [P.sb("ngc%d" % r, [128, 8]) for r in range(2)]
        S = [[[P.sb("S%d%d%d" % (r, h, i), [128, 128]) for i in range(2)] for h in range(4)] for r in range(2)]
        pa = [P.psb("pa%d" % r, 4, 128) for r in range(2)]
        po = [P.psb("po%d" % r, 4, 128) for r in range(2)]
        rr = [[P.sb("rr%d%d" % (r, h), [128, 128]) for h in range(4)] for r in range(2)]
        vn = [[P.sb("vn%d%d" % (r, h), [128, 128]) for h in range(4)] for r in range(2)]
        vt = [[P.sb("vt%d%d" % (r, h), [128, 128]) for h in range(4)] for r in range(2)]
        t1 = [[P.sb("t1%d%d" % (r, h), [128, 128]) for h in range(4)] for r in range(2)]
        oo = [[P.sb("oo%d%d" % (r, i), [128, 512]) for i in range(2)] for r in range(2)]
        for r in range(2):
            for h in range(4):
                P.pool.memset(ap=S[r][h][0][:], constant=0.0)
        units = [(r, h) for r in range(2) for h in range(4)]
        cur = 0
        for i in range(NTILE):
            b = i % 2
            nxt = 1 - cur
            tl = (Forder[i], Border[i])
            for r in range(2):
                n = tl[r]
                d = inb[r][b]
                for nm in ("k", "v"):
                    P.sp.dma_start(out=d[nm][:], in_=SCN[n][:, col0[nm]:col0[nm] + 512], _reads=[self.SCN])
                P.sp.dma_start(out=d["kT"][:], in_=SC2[n][:, 0:512], _reads=[self.SC2])
                P.sp.dma_start(out=d["qT"][:], in_=SC2[n][:, 512:1024], _reads=[self.SC2])
                P.sp.dma_start(out=d["T2T"][:], in_=SC2[n][:, 1024 + r * 512:1536 + r * 512], _reads=[self.SC2])
                P.sp.dma_start(out=d["QKT"][:], in_=SC2[n][:, 2048 + r * 512:2560 + r * 512], _reads=[self.SC2])
                P.sp.dma_start(out=scb[r][b][:], in_=SC2[n][:, 3072:3104], _reads=[self.SC2])
                P.pool.tensor_scalar(out=ngc[r][:], in0=scb[r][b][:, 8:16], scalar1=-1.0, scalar2=None, op0=ALU.mult)
            hs = lambda h: slice(h * 128, (h + 1) * 128)

            def scan_stages(r):
                d = inb[r][b]
                sc = scb[r][b]
                def t1_():
                    for h in range(4):
                        P.pe.matmul(out=pa[r][h][:], lhsT=d["kT"][:, hs(h)], rhs=S[r][h][cur][:], start=True, stop=True)
                    for h in range(4):
                        P.pe.matmul(out=po[r][h][:], lhsT=d["qT"][:, hs(h)], rhs=S[r][h][cur][:], start=True, stop=True)
                def t2_():
                    for h in range(4):
                        idx = r * 4 + h
                        P.dve.scalar_tensor_tensor(out=rr[r][h][:], in0=pa[r][h][:], scalar=ngc[r][:, idx:idx + 1], in1=d["v"][:, hs(h)],
                                                   op0=ALU.mult, op1=ALU.add)
                    for h in range(4):
                        idx = r * 4 + h
                        P.act.activation(out=t1[r][h][:], in_=po[r][h][:], func=AF.Copy, scale=sc[:, 8 + idx:9 + idx])
                def t3_():
                    for h in range(4):
                        P.pe.matmul(out=pa[r][h][:], lhsT=d["T2T"][:, hs(h)], rhs=rr[r][h][:], start=True, stop=True)
                def t4_():
                    for h in range(4):
                        idx = r * 4 + h
                        P.act.activation(out=vn[r][h][:], in_=pa[r][h][:], func=AF.Copy, scale=sc[:, idx:idx + 1])
                    for h in range(4):
                        idx = r * 4 + h
                        P.pool.tensor_scalar(out=vt[r][h][:], in0=vn[r][h][:], scalar1=sc[:, 16 + idx:17 + idx], scalar2=None, op0=ALU.mult)
                def t5_():
                    for h in range(4):
                        P.pe.matmul(out=pa[r][h][:], lhsT=d["k"][:, hs(h)], rhs=vt[r][h][:], start=True, stop=True)
                    for h in range(4):
                        P.pe.matmul(out=po[r][h][:], lhsT=d["QKT"][:, hs(h)], rhs=vn[r][h][:], start=True, stop=True)
                def t6_():
                    for h in range(4):
                        idx = r * 4 + h
                        P.dve.scalar_tensor_tensor(out=S[r][h][nxt][:], in0=S[r][h][cur][:], scalar=sc[:, 24 + idx:25 + idx],
                                                   in1=pa[r][h][:], op0=ALU.mult, op1=ALU.add)
                    for h in range(4):
                        P.dve.tensor_tensor(out=oo[r][b][:, hs(h)], in0=po[r][h][:], in1=t1[r][h][:], op=ALU.add)
                return [t1_, t2_, t3_, t4_, t5_, t6_]
            sA, sB = scan_stages(0), scan_stages(1)
            for k_ in range(len(sA) + 1):
                if k_ < len(sA):
                    sA[k_]()
                if k_ >= 1:
                    sB[k_ - 1]()
            for r in range(2):
                n = tl[r]
                dst = (self.OF, self.OB)[r]
                P.sp.dma_start(out=dst.h.ap()[n * 128:(n + 1) * 128, :], in_=oo[r][b][:], _writes=[dst])
            cur = nxt
        P.end()

    def mixer_a(self, l):
        P = self.P
        P.begin()
        PFM = self.PFM.h.ap()
        YT = self.YT.h.ap()
        cwa = P.sb("cwa", [128, 2, 3])
        for ct in range(2):
            P.sp.dma_start(out=cwa[:, ct, :], in_=self.conv_a.h.ap()[l][:, ct * 128:(ct + 1) * 128].rearrange("j c -> c j"),
                           allow_slow_non_contiguous=True)
        ca = [P.sb("ca%d" % i, [128, T]) for i in range(2)]
        Bt = [P.sb("Bt%d" % i, [128, T]) for i in range(2)]
        acc = [P.sb("acc%d" % i, [128, T]) for i in range(2)]
        ya = [P.sb("ya%d" % i, [128, T], BF16) for i in range(2)]
        it = 0
        for s, (c0, n) in enumerate(((0, TC), (TC, T))):
            if s == 0 and l == DEPTH - 1:
                continue
            for ct in range(2):
                b = it % 2
                it += 1
                P.sp.dma_start(out=ca[b][:, 0:n], in_=PFM[256 + ct * 128:256 + (ct + 1) * 128, c0:c0 + n], _reads=[self.PFM])
                P.sp.dma_start(out=Bt[b][:, 0:n], in_=PFM[ct * 128:(ct + 1) * 128, c0:c0 + n], _reads=[self.PFM])
                w = lambda j: cwa[:, ct, j:j + 1]
                P.pool.tensor_scalar(out=acc[b][:, 0:n], in0=ca[b][:, 0:n], scalar1=w(1), scalar2=None, op0=ALU.mult)
                if s == 0:
                    sh = [(acc[b][:, 1:n], ca[b][:, 0:n - 1], 0), (acc[b][:, 0:n - 1], ca[b][:, 1:n], 2)]
                elif ct == 0:
                    av = acc[b][:, 0:n].rearrange("p (r c) -> p r c", c=64)
                    cv = ca[b][:, 0:n].rearrange("p (r c) -> p r c", c=64)
                    sh = [(av[:, :, 1:64], cv[:, :, 0:63], 0), (av[:, :, 0:63], cv[:, :, 1:64], 2)]
                else:
                    sh = [(acc[b][:, 64:n], ca[b][:, 0:n - 64], 0), (acc[b][:, 0:n - 64], ca[b][:, 64:n], 2)]
                for (dst, src, j) in sh:
                    P.dve.scalar_tensor_tensor(out=dst, in0=src, scalar=w(j), in1=dst, op0=ALU.mult, op1=ALU.add)
                P.pool.tensor_tensor(out=ya[b][:, 0:n], in0=acc[b][:, 0:n], in1=Bt[b][:, 0:n], op=ALU.mult)
                P.sp.dma_start(out=YT[ct * 128:(ct + 1) * 128, c0:c0 + n], in_=ya[b][:, 0:n], _writes=[self.YT])
        P.end()

    def post_norm_residual(self, py, xt, G, tmp, xo, ssq, junk):
        P = self.P
        for hf in range(2):
            P.act.activation(out=junk[:, 0:512], in_=py[hf][:], func=AF.Square, accum_out=ssq[:, hf:hf + 1])
        ss = P.sm("pss", 1)
        P.dve.tensor_tensor(out=ss[:], in0=ssq[:, 0:1], in1=ssq[:, 1:2], op=ALU.add)
        rstd = self.rstd_chain(ss[:], 1, 1.0 / D, "prs")
        for hf in range(2):
            P.dve.scalar_tensor_tensor(out=tmp[:, hf * 512:(hf + 1) * 512], in0=py[hf][:], scalar=rstd[:, 0:1],
                                       in1=G[:, hf * 512:(hf + 1) * 512], op0=ALU.mult, op1=ALU.mult)
        P.pool.tensor_tensor(out=xo[:], in0=tmp[:], in1=xt[:], op=ALU.add)

    def mix_out(self, l, src):
        P = self.P
        c = self.c
        last = l == DEPTH - 1
        P.keep_begin()
        Wo = P.sbk("Wo", [128, 8, D], BF16)
        P.begin()
        stg = [P.sb("stgo%d" % i, [128, D]) for i in range(2)]
        for kc in range(8):
            P.sp.dma_start(out=stg[kc % 2][:], in_=self.w_o.h.ap()[l][kc * 128:(kc + 1) * 128, :])
            (P.act.copy if kc % 2 else P.dve.tensor_copy)(out=Wo[:, kc, :], in_=stg[kc % 2][:])
        P.end()
        P.begin()
        gon = P.sb("gon", [128, 128])
        P.sp.dma_start(out=gon[:], in_=self.g_onorm.h.ap()[l:l + 1, :].broadcast_to([128, 128]))
        G = [self.load_mod(l, 2, 1, "Gpc"), self.load_mod(l, 2, 0, "Gpx")]
        of = [P.sb("of%d" % i, [128, 512]) for i in range(2)]
        ob = [P.sb("ob%d" % i, [128, 512]) for i in range(2)]
        zs = [P.sb("zs%d" % i, [128, 512]) for i in range(2)]
        xt = [P.sb("xt%d" % i, [128, D]) for i in range(2)]
        yac = [P.sb("yac%d" % i, [128, 4, 128], BF16) for i in range(2)]
        o = P.sb("o", [128, 512])
        sq = P.sb("sq", [128, 512])
        y1 = P.sb("y1", [128, 512])
        y2 = P.sb("y2", [128, 512])
        yb = P.sb("yb", [128, 512], BF16)
        ybT = [P.sb("ybT%d" % i, [128, 4, 128], BF16) for i in range(2)]
        tmp = P.sb("tmp", [128, D])
        junk = P.sb("junk", [128, 512])
        xo = [P.sb("xo%d" % i, [128, D]) for i in range(2)]
        pT = P.ps("pT", [128, 1024], BF16)
        py = [[P.ps("py%d%d" % (i, hf), [128, 512]) for hf in range(2)] for i in range(2)]
        YT = self.YT.h.ap()
        h4 = lambda ap: ap.rearrange("p (h d) -> p h d", h=4)
        it = 0
        for n in range(2 if last else 0, NTILE):
            b = it % 2
            it += 1
            s = 0 if n < 2 else 1
            r0 = n * 128
            P.sp.dma_start(out=of[b][:], in_=self.OF.h.ap()[r0:r0 + 128, :], _reads=[self.OF])
            P.sp.dma_start(out=ob[b][:], in_=self.OB.h.ap()[r0:r0 + 128, :], _reads=[self.OB])
            P.sp.dma_start(out=zs[b][:], in_=self.ZS.h.ap()[r0:r0 + 128, :], _reads=[self.ZS])
            P.sp.dma_start(out=xt[b][:], in_=src.h.ap()[r0:r0 + 128, :], _reads=[src])
            P.sp.dma_start(out=yac[b][:, 0:2, :], in_=YT[0:256, r0:r0 + 128].rearrange("(ct p) t -> p ct t", p=128), _reads=[self.YT])
            P.sp.dma_start(out=yac[b][:, 2:4, :], in_=YT[768:1024, r0:r0 + 128].rearrange("(ct p) t -> p ct t", p=128), _reads=[self.YT])
            P.pool.tensor_tensor(out=o[:], in0=of[b][:], in1=ob[b][:], op=ALU.add)
            P.pool.tensor_tensor(out=sq[:], in0=o[:], in1=o[:], op=ALU.mult)
            ss4 = P.sm("ss4", 4)
            P.dve.tensor_reduce(out=ss4[:], in_=h4(sq[:]), axis=AX.X, op=ALU.add)
            rs = self.rstd_chain(ss4[:], 4, 1.0 / 128, "rso")
            P.dve.tensor_tensor(out=h4(y1[:]), in0=h4(o[:]), in1=rs[:].unsqueeze(2).broadcast_to([128, 4, 128]), op=ALU.mult)
            P.pool.tensor_tensor(out=h4(y2[:]), in0=h4(y1[:]), in1=gon[:].unsqueeze(1).broadcast_to([128, 4, 128]), op=ALU.mult)
            P.dve.tensor_tensor(out=yb[:], in0=y2[:], in1=zs[b][:], op=ALU.mult)
            for h in range(4):
                P.pe.transpose(out=pT[:, h * 128:(h + 1) * 128], in_=yb[:, h * 128:(h + 1) * 128], identity=c["ident_bf"][:])
            P.act.copy(out=ybT[b][:], in_=pT[:, 0:512].rearrange("p (h t) -> p h t", h=4))
            lhs = [yac[b][:, 0, :], yac[b][:, 1, :]] + [ybT[b][:, h, :] for h in range(4)] + [yac[b][:, 2, :], yac[b][:, 3, :]]
            for hf in range(2):
                for kc in range(8):
                    P.pe.matmul(out=py[b][hf][:], lhsT=lhs[kc], rhs=Wo[:, kc, hf * 512:(hf + 1) * 512], start=(kc == 0), stop=(kc == 7))
            ssq = P.sm("ssq", 2)
            self.post_norm_residual(py[b], xt[b], G[s], tmp, xo[b], ssq, junk)
            P.sp.dma_start(out=self.XS.h.ap()[r0:r0 + 128, :], in_=xo[b][:], _writes=[self.XS])
        P.end()
        P.keep_end()

    def ffn(self, l):
        P = self.P
        c = self.c
        last = l == DEPTH - 1
        NJ = FFN_H // 128
        P.keep_begin()
        W1 = P.sbk("W1", [128, 8, 2 * FFN_H], BF16)
        W2 = P.sbk("W2", [128, NJ, D], BF16)
        P.begin()
        stg = [P.sb("stgf%d" % i, [128, 2 * FFN_H]) for i in range(2)]
        w1v = self.w_ffn_in.h.ap()[l]
        engs = (P.act.copy, P.dve.tensor_copy, P.pool.tensor_copy, P.act.copy)
        for kc in range(8):
            st = stg[kc % 2]
            P.sp.dma_start(out=st[:], in_=w1v[kc * 128:(kc + 1) * 128, :])
            for q in range(4):
                engs[q](out=W1[:, kc, q * 1408:(q + 1) * 1408], in_=st[:, q * 1408:(q + 1) * 1408])
        w2v = self.w_ffn_out.h.ap()[l].rearrange("(j p) n -> p j n", p=128)
        for jj, (j0, j1) in enumerate(((0, 5), (5, 10), (10, 15), (15, 20), (20, 22))):
            st = stg[jj % 2]
            nj = j1 - j0
            P.sp.dma_start(out=st[:, 0:nj * D].rearrange("p (j n) -> p j n", n=D), in_=w2v[:, j0:j1, :])
            for q in range(nj):
                engs[q % 3](out=W2[:, j0 + q, :], in_=st[:, q * D:(q + 1) * D])
        P.end()

        P.begin()
        S = [P.sb("Sf", [128, D]), None]
        Gm = [P.sb("Gf", [128, D]), None]
        Gp = [P.sb("Gpf", [128, D]), None]
        MBv = self.MB.h.ap()

        def load_mods(s):
            ms = 1 - s
            for t, k in ((S[0], 3), (Gm[0], 4), (Gp[0], 5)):
                P.sp.dma_start(out=t[:], in_=MBv[l, k, ms:ms + 1, :].broadcast_to([128, D]), _reads=[self.MB])
        xt = [[P.sb("xt%d%d" % (i, t), [128, D]) for t in range(2)] for i in range(2)]
        h1 = P.sb("h1", [128, D])
        hb = P.sb("hb", [128, D], BF16)
        hT = [P.sb("hT%d" % i, [128, 8, 256], BF16) for i in range(2)]
        sg = [P.sb("sg%d" % i, [128, 256]) for i in range(2)]
        tmp = P.sb("tmp", [128, D])
        xo = [P.sb("xo%d" % i, [128, D]) for i in range(2)]
        pT = P.ps("pT", [128, 1024], BF16)
        pg = P.ps("pg", [128, 512])
        pu = P.ps("pu", [128, 512])
        py = [[P.ps("py%d%d" % (t, hf), [128, 512]) for hf in range(2)] for t in range(2)]
        groups = list(range(1 if last else 0, NT // 256))
        LAG = 3
        aT = [P.sb("aTr%d" % i, [128, 256], BF16) for i in range(LAG + 2)]
        state = {"s": None}

        def pre_elem(gi, t):
            g = groups[gi]
            b = gi % 2
            s_ = 0 if g == 0 else 1
            if s_ != state["s"]:
                load_mods(s_)
                state["s"] = s_
            r0 = g * 256 + t * 128
            P.sp.dma_start(out=xt[b][t][:], in_=self.XS.h.ap()[r0:r0 + 128, :], _reads=[self.XS])
            ss = P.sm("ss", 1)
            P.act.activation(out=tmp[:], in_=xt[b][t][:], func=AF.Square, accum_out=ss[:])
            rstd = self.rstd_chain(ss[:], 1, 1.0 / D, "rsf")
            P.dve.scalar_tensor_tensor(out=h1[:], in0=xt[b][t][:], scalar=rstd[:, 0:1], in1=Gm[0][:], op0=ALU.mult, op1=ALU.mult)
            P.pool.tensor_tensor(out=hb[:], in0=h1[:], in1=S[0][:], op=ALU.add)

        def pre_tr(gi, t):
            b = gi % 2
            for kc in range(8):
                P.pe.transpose(out=pT[:, kc * 128:(kc + 1) * 128], in_=hb[:, kc * 128:(kc + 1) * 128], identity=c["ident_bf"][:])
            P.act.copy(out=hT[b][:, :, t * 128:(t + 1) * 128], in_=pT[:].rearrange("p (kc t) -> p kc t", kc=8))

        def second(j):
            a = aT[j % (LAG + 2)]
            for t in range(2):
                for hf in range(2):
                    P.pe.matmul(out=py[t][hf][:], lhsT=a[:, t * 128:(t + 1) * 128], rhs=W2[:, j, hf * 512:(hf + 1) * 512],
                                start=(j == 0), stop=(j == NJ - 1))

        def post(gi):
            g = groups[gi]
            b = gi % 2
            for t in range(2):
                r0 = g * 256 + t * 128
                ssq = P.sm("ssqf", 2)
                self.post_norm_residual(py[t], xt[b][t], Gp[0], tmp, xo[t], ssq, h1)
                if last:
                    P.sp.dma_start(out=self.out.h.ap()[r0 - TC:r0 - TC + 128, :], in_=xo[t][:], _writes=[self.out])
                else:
                    P.sp.dma_start(out=self.XB.h.ap()[r0:r0 + 128, :], in_=xo[t][:], _writes=[self.XB])

        for t in range(2):
            pre_elem(0, t)
            pre_tr(0, t)
        for gi, g in enumerate(groups):
            b = gi % 2
            nxt_ok = gi + 1 < len(groups)
            same_mods = nxt_ok and ((0 if groups[gi + 1] == 0 else 1) == state["s"])
            for j in range(NJ):
                for kc in range(8):
                    P.pe.matmul(out=pg[:, 0:256], lhsT=W1[:, kc, j * 128:(j + 1) * 128], rhs=hT[b][:, kc, :], start=(kc == 0), stop=(kc == 7))
                for kc in range(8):
                    P.pe.matmul(out=pu[:, 0:256], lhsT=W1[:, kc, FFN_H + j * 128:FFN_H + (j + 1) * 128], rhs=hT[b][:, kc, :], start=(kc == 0), stop=(kc == 7))
                if j >= LAG:
                    second(j - LAG)
                P.act.activation(out=sg[j % 2][:], in_=pg[:, 0:256], func=AF.Silu)
                P.dve.tensor_tensor(out=aT[j % (LAG + 2)][:], in0=pu[:, 0:256], in1=sg[j % 2][:], op=ALU.mult)
                if same_mods:
                    if j == 5:
                        pre_elem(gi + 1, 0)
                    elif j == 10:
                        pre_tr(gi + 1, 0)
                    elif j == 12:
                        pre_elem(gi + 1, 1)
                    elif j == 17:
                        pre_tr(gi + 1, 1)
            for j in range(NJ - LAG, NJ):
                second(j)
            post(gi)
            if nxt_ok and not same_mods:
                for t in range(2):
                    pre_elem(gi + 1, t)
                    pre_tr(gi + 1, t)
        P.end()
        P.keep_end()

    def forward(self, upto=None):
        self.consts()
        for l in range(DEPTH):
            src = self.xs_in if l == 0 else self.XB
            self.mods(l)
            self.prenorm(l, 0, 1, src)
            self.proj(l)
            self.gdn_prep(l)
            self.gdn_scan(l)
            self.mixer_a(l)
            self.mix_out(l, src)
            self.ffn(l)
        self.P.close()


W_NAMES = ["w_mod", "b_mod", "g_pre_mix", "g_post_mix", "g_pre_ffn", "g_post_ffn", "w_in", "conv_a", "conv_qkv",
           "a_log", "dt_bias", "g_onorm", "ln_c_g", "ln_c_b", "w_s", "b_s", "w_o", "w_ffn_in", "w_ffn_out"]


def make_in_maps(inputs, cores=range(8)):
    f = lambda a: np.ascontiguousarray(np.asarray(a, dtype=np.float32))
    shared = {n: f(inputs[n]) for n in W_NAMES}
    shared["a_log"] = shared["a_log"].reshape(DEPTH, 8)
    shared["dt_bias"] = shared["dt_bias"].reshape(DEPTH, 8)
    x, c, ctx, c_ctx = f(inputs["x"]), f(inputs["c"]), f(inputs["ctx"]), f(inputs["c_ctx"])
    maps = []
    for b in cores:
        m = dict(shared)
        m["xs"] = np.ascontiguousarray(np.concatenate([ctx[b], x[b]], axis=0))
        cc = np.stack([c[b], c_ctx], axis=0)
        m["ccT"] = np.ascontiguousarray(cc.reshape(2, 8, 128).transpose(2, 1, 0))
        maps.append(m)
    return maps


_CACHE = {}


def kernel(**inputs):
    if "nc" not in _CACHE:
        nc = bass.Bass("TRN2", target_bir_lowering=False)
        Model(nc).forward()
        _CACHE["nc"] = nc
    nc = _CACHE["nc"]
    maps = make_in_maps(inputs)
    res = run_bass_kernel_spmd(nc, maps, core_ids=list(range(8)))
    return np.stack([np.asarray(r["out"], dtype=np.float32) for r in res.results], axis=0)
```

```python
import numpy as np
from contextlib import ExitStack
import concourse.bass as bass
import concourse.mybir as mybir
from concourse.bass_utils import run_bass_kernel_spmd

F32 = mybir.dt.float32
F32R = mybir.dt.float32r
BF16 = mybir.dt.bfloat16
AF = mybir.ActivationFunctionType
ALU = mybir.AluOpType
AX = mybir.AxisListType

ENGS = ("tensor", "vector", "scalar", "gpsimd", "sync")
SAME_ENGINE_SYNC = True


class Tile:
    def __init__(self, prog, h, name, dram=False):
        self.prog = prog
        self.h = h
        self.name = name
        self.dram = dram
        self.lastw = None
        self.readers = []
        prog.tiles[name] = self

    def __getitem__(self, idx):
        return self.h[idx]

    def ap(self):
        return self.h.ap() if self.dram else self.h[:]


class SubTile:
    def __init__(self, parent, c0, w, name):
        self.parent = parent
        self.c0 = c0
        self.w = w
        self.name = name
        self.lastw = None
        self.readers = []

    def __getitem__(self, idx):
        return self.parent.h[:, self.c0:self.c0 + self.w][idx]


class Op:
    __slots__ = ("eng", "fn", "deps", "waits", "sig", "sigval", "is_dma", "dsem", "dval", "idx", "is_load", "F")

    def __init__(self, eng, fn, is_dma=False):
        self.eng = eng
        self.fn = fn
        self.deps = []
        self.waits = []
        self.sig = False
        self.sigval = None
        self.is_dma = is_dma
        self.dsem = None
        self.dval = None


class EngProxy:
    def __init__(self, prog, eng):
        self.prog = prog
        self.eng = eng

    def __getattr__(self, meth):
        prog, eng = self.prog, self.eng

        def call(*args, **kwargs):
            reads, writes = [], []
            for k, v in kwargs.items():
                if isinstance(v, bass.AP):
                    t = prog.tiles.get(v.tensor.name)
                    if t is None:
                        continue
                    if getattr(t, "subs", None):
                        t = t.subs[(v.offset % t.rowlen) // t.subw]
                    if k in ("out", "accum_out") or (k == "ap" and meth == "memset"):
                        writes.append(t)
                    else:
                        reads.append(t)
            extra_r = kwargs.pop("_reads", [])
            extra_w = kwargs.pop("_writes", [])
            reads += extra_r
            writes += extra_w
            is_dma = meth in ("dma_start",)
            if meth == "matmul" and kwargs.get("start", True) is False:
                pass
            fn = lambda e, meth=meth, args=args, kwargs=kwargs: getattr(e, meth)(*args, **kwargs)
            return prog.record(eng, fn, reads, writes, is_dma)

        return call


class Prog:
    def __init__(self, nc):
        self.nc = nc
        self.es = ExitStack()
        self.tiles = {}
        self.csem = {e: self.es.enter_context(nc.semaphore("c_" + e)) for e in ENGS}
        self.cnt = {e: 0 for e in ENGS}
        self.known = {e: {} for e in ENGS}
        self.ops = []
        self.dma_sems = []
        self.ndma_sems = 24
        for i in range(self.ndma_sems):
            self.dma_sems.append([self.es.enter_context(nc.semaphore("d%d" % i)), 0, None])
        self.dma_rr = 0
        self.phase_es = None
        for e in ENGS:
            setattr(self, e[0] if e != "sync" else "sp", EngProxy(self, e))
        self.pe = EngProxy(self, "tensor")
        self.dve = EngProxy(self, "vector")
        self.act = EngProxy(self, "scalar")
        self.pool = EngProxy(self, "gpsimd")
        self.sp = EngProxy(self, "sync")
        self.uid = 0

    def sb(self, name, shape, dtype=F32):
        self.uid += 1
        name = "%s_%d" % (name, self.uid)
        h = self.phase_es.enter_context(self.nc.sbuf_tensor(name, list(shape), dtype))
        return Tile(self, h, name)

    def sbc(self, name, shape, dtype=F32):
        self.uid += 1
        name = "%s_%d" % (name, self.uid)
        h = self.es.enter_context(self.nc.sbuf_tensor(name, list(shape), dtype))
        return Tile(self, h, name)

    def sm(self, name, cols, depth=4):
        key = (name, cols)
        ring = self.rings.setdefault(key, [[], 0])
        if len(ring[0]) < depth:
            ring[0].append(self.sb(name, [128, cols]))
            return ring[0][-1]
        ring[1] += 1
        return ring[0][ring[1] % depth]

    def keep_begin(self):
        self.keep_es = ExitStack()

    def keep_end(self):
        self.keep_es.close()
        self.keep_es = None

    def sbk(self, name, shape, dtype=F32):
        self.uid += 1
        name = "%s_%d" % (name, self.uid)
        h = self.keep_es.enter_context(self.nc.sbuf_tensor(name, list(shape), dtype))
        return Tile(self, h, name)

    def ps(self, name, shape, dtype=F32):
        self.uid += 1
        name = "%s_%d" % (name, self.uid)
        h = self.phase_es.enter_context(self.nc.psum_tensor(name, list(shape), dtype))
        t = Tile(self, h, name)
        t.psum = True
        return t

    def psb(self, name, n, w, dtype=F32):
        rowlen = 512 if dtype == F32 else 1024
        assert n * w <= rowlen
        t = self.ps(name, [128, rowlen], dtype)
        return [SubTile(t, i * w, w, "%s.%d" % (t.name, i)) for i in range(n)]

    def dram(self, name, shape, dtype=F32, kind="Internal"):
        h = self.nc.dram_tensor(name, list(shape), dtype, kind=kind)
        return Tile(self, h, name, dram=True)

    def record(self, eng, fn, reads, writes, is_dma=False):
        op = Op(eng, fn, is_dma)
        op.idx = len(self.ops)
        op.is_load = is_dma and any(not getattr(t, "dram", False) for t in writes)
        deps = []
        for t in reads:
            if t.lastw is not None:
                deps.append(t.lastw)
            if getattr(t, "psum", False):
                deps.extend(rd for rd in t.readers if rd.eng != eng)
        for t in writes:
            if t.lastw is not None:
                deps.append(t.lastw)
            deps.extend(t.readers)
        for t in reads:
            t.readers.append(op)
        for t in writes:
            t.lastw = op
            t.readers = []
        seen = set()
        for d in deps:
            if id(d) in seen or d is op:
                continue
            seen.add(id(d))
            op.deps.append(d)
        if is_dma:
            slot = self.dma_sems[self.dma_rr % self.ndma_sems]
            self.dma_rr += 1
            prev = slot[2]
            if prev is not None:
                op.deps.append(prev)
            slot[1] += 16
            slot[2] = op
            op.dsem = slot[0]
            op.dval = slot[1]
        self.ops.append(op)
        return op

    def begin(self):
        self.phase_es = ExitStack()
        self.ops = []
        self.rings = {}

    def end(self, final=False):
        nc = self.nc
        ops = self.ops
        last = {e: None for e in ENGS}
        for op in ops:
            if not op.is_dma:
                last[op.eng] = op
        pend_dma = [s[2] for s in self.dma_sems if s[2] is not None]
        for op in ops:
            for d in op.deps:
                if d.is_dma:
                    continue
                if d.eng != op.eng or (SAME_ENGINE_SYNC and d.eng != "tensor") or op.is_dma:
                    d.sig = True
        for e in ENGS:
            if last[e] is not None:
                last[e].sig = True
        for op in ops:
            if op.is_dma:
                continue
            if op.sig:
                self.cnt[op.eng] += 1
                op.sigval = self.cnt[op.eng]
        sp_ops = [op for op in ops if op.eng == "sync"]
        spidx = {id(op): i for i, op in enumerate(sp_ops)}
        lastF = {e: -1 for e in ENGS}
        for op in ops:
            f = -1
            for d in op.deps:
                if id(d) in spidx:
                    f = max(f, spidx[id(d)])
                elif getattr(d, "F", None) is not None:
                    f = max(f, d.F)
            if op.eng != "sync":
                f = max(f, lastF[op.eng])
                lastF[op.eng] = f
            op.F = f
        keys = {}
        prev_load_key = -1.0
        for i, op in enumerate(sp_ops):
            if op.is_load:
                k = max(op.F + 0.5, prev_load_key)
                k = min(k, float(i))
                prev_load_key = k
                keys[id(op)] = k
            else:
                keys[id(op)] = float(i)
        sp_sorted = sorted(range(len(sp_ops)), key=lambda i: (keys[id(sp_ops[i])], i))
        sp_new = [sp_ops[i] for i in sp_sorted]
        per = {e: [op for op in ops if op.eng == e] for e in ENGS}
        per["sync"] = sp_new
        for e in ENGS:
            kn = self.known[e]
            for op in per[e]:
                for d in op.deps:
                    if d.is_dma:
                        key, val, sem = ("d", id(d.dsem)), d.dval, d.dsem
                    else:
                        if d.sigval is None:
                            continue
                        if d.eng == op.eng and not op.is_dma and (d.eng == "tensor" or not SAME_ENGINE_SYNC):
                            continue
                        key, val, sem = ("c", d.eng), d.sigval, self.csem[d.eng]
                    if kn.get(key, 0) >= val:
                        continue
                    kn[key] = val
                    op.waits.append((sem, val))
        bar = {}
        for e in ENGS:
            w = []
            kn = self.known[e]
            for e2 in ENGS:
                if last[e2] is None:
                    continue
                v = last[e2].sigval
                if kn.get(("c", e2), 0) < v:
                    kn[("c", e2)] = v
                    w.append((self.csem[e2], v))
            for d in pend_dma:
                key = ("d", id(d.dsem))
                if kn.get(key, 0) < d.dval:
                    kn[key] = d.dval
                    w.append((d.dsem, d.dval))
            bar[e] = w
        for s in self.dma_sems:
            s[2] = None

        with nc.Block() as block:
            def emit(e):
                def body(eng):
                    for op in per[e]:
                        for (sem, val) in op.waits:
                            eng.wait_ge(sem, val)
                        ins = op.fn(eng)
                        if op.is_dma:
                            ins.then_inc(op.dsem, 16)
                        elif op.sig:
                            ins.then_inc(self.csem[e], 1)
                    for (sem, val) in bar[e]:
                        eng.wait_ge(sem, val)
                return body
            block.tensor(emit("tensor"))
            block.vector(emit("vector"))
            block.scalar(emit("scalar"))
            block.gpsimd(emit("gpsimd"))
            block.sync(emit("sync"))
        for t in self.tiles.values():
            t.lastw = None
            t.readers = []
            for st in (getattr(t, "subs", None) or []):
                st.lastw = None
                st.readers = []
        self.phase_es.close()
        self.phase_es = None
        self.ops = []

    def close(self):
        self.es.close()


D = 1024
T = 4096
TC = 256
NT = TC + T
NTILE = NT // 128
DEPTH = 2
EPS = 1e-6
IN_COLS = 3344
OFF_A_B, OFF_A_C, OFF_A_H, OFF_Q, OFF_K, OFF_V = 0, 256, 512, 768, 1280, 1792
OFF_AB, OFF_Z, OFF_CU, OFF_CV = 2304, 2320, 2832, 3088
FFN_H = 2816
SCN_W = 8 * 512 + 32
GROUPS = [(0, 256, 0, 0)] + [(256 + 512 * i, 512, 1, 512 * i) for i in range(8)]


class Model:
    def __init__(self, nc, debug=False):
        self.nc = nc
        self.P = P = Prog(nc)
        self.debug = debug
        k_in = "ExternalInput"
        k_sc = "ExternalOutput" if debug else "Internal"
        self.xs_in = P.dram("xs", [NT, D], F32, k_in)
        self.ccT = P.dram("ccT", [128, 8, 2], F32, k_in)
        self.w_mod = P.dram("w_mod", [DEPTH, D, 6 * D], F32, k_in)
        self.b_mod = P.dram("b_mod", [DEPTH, 6 * D], F32, k_in)
        self.g_pre_mix = P.dram("g_pre_mix", [DEPTH, D], F32, k_in)
        self.g_post_mix = P.dram("g_post_mix", [DEPTH, D], F32, k_in)
        self.g_pre_ffn = P.dram("g_pre_ffn", [DEPTH, D], F32, k_in)
        self.g_post_ffn = P.dram("g_post_ffn", [DEPTH, D], F32, k_in)
        self.w_in = P.dram("w_in", [DEPTH, D, IN_COLS], F32, k_in)
        self.conv_a = P.dram("conv_a", [DEPTH, 3, 256], F32, k_in)
        self.conv_qkv = P.dram("conv_qkv", [DEPTH, 3, 1536], F32, k_in)
        self.a_log = P.dram("a_log", [DEPTH, 8], F32, k_in)
        self.dt_bias = P.dram("dt_bias", [DEPTH, 8], F32, k_in)
        self.g_onorm = P.dram("g_onorm", [DEPTH, 128], F32, k_in)
        self.ln_c_g = P.dram("ln_c_g", [DEPTH, 256], F32, k_in)
        self.ln_c_b = P.dram("ln_c_b", [DEPTH, 256], F32, k_in)
        self.w_s = P.dram("w_s", [DEPTH, 4, 128, 128], F32, k_in)
        self.b_s = P.dram("b_s", [DEPTH, 4, 128], F32, k_in)
        self.w_o = P.dram("w_o", [DEPTH, D, D], F32, k_in)
        self.w_ffn_in = P.dram("w_ffn_in", [DEPTH, D, 2 * FFN_H], F32, k_in)
        self.w_ffn_out = P.dram("w_ffn_out", [DEPTH, FFN_H, D], F32, k_in)
        self.out = P.dram("out", [T, D], F32, "ExternalOutput")
        self.XS = P.dram("XS", [NT, D], F32, k_sc)
        self.MB = P.dram("MB", [DEPTH, 6, 2, D], F32, k_sc)
        self.HT = [P.dram("HTc", [8, 128, TC + 2], BF16, k_sc), P.dram("HTx", [8, 128, T + 2], BF16, k_sc)]
        self.PFM = P.dram("PFM", [512, NT], F32, k_sc)
        self.YT = P.dram("YT", [1024, NT], BF16, k_sc)
        self.ZS = P.dram("ZS", [NT, 512], F32, k_sc)
        self.QS = P.dram("QS", [NT, 528], F32, k_sc)
        self.SCN = P.dram("SCN", [NTILE, 128, SCN_W], F32, k_sc)
        self.SC2 = P.dram("SC2", [NTILE, 128, 6 * 512 + 32], F32, k_sc)
        self.XB = P.dram("XB", [NT, D], F32, k_sc)
        self.OF = P.dram("OF", [NT, 512], F32, k_sc)
        self.OB = P.dram("OB", [NT, 512], F32, k_sc)

    def consts(self):
        P = self.P
        c = self.c = {}
        for nm in ("ones", "U0", "U1", "SU0", "SU1", "NM0", "NM1", "ident", "NU0", "NU1"):
            c[nm] = P.sbc(nm, [128, 128])
        c["ident_bf"] = P.sbc("ident_bf", [128, 128], BF16)
        P.begin()
        ones = c["ones"]
        zer = P.sb("zer", [128, 128])
        P.pool.memset(ap=ones[:], constant=1.0)
        P.pool.memset(ap=zer[:], constant=0.0)

        def sel(name, src, step, cm, base, op, fill):
            t = c[name]
            P.pool.affine_select(out=t[:], in_=src[:], pattern=[[step, 128]], compare_op=op,
                                 fill=fill, base=base, channel_multiplier=cm)
            return t
        sel("U0", ones, 1, -1, 0, ALU.is_ge, 0.0)
        sel("U1", ones, -1, 1, 0, ALU.is_ge, 0.0)
        sel("SU0", ones, 1, -1, -1, ALU.is_ge, 0.0)
        sel("SU1", ones, -1, 1, -1, ALU.is_ge, 0.0)
        sel("NM0", zer, 1, -1, 0, ALU.is_ge, -30000.0)
        sel("NM1", zer, -1, 1, 0, ALU.is_ge, -30000.0)
        sel("ident", ones, 1, -1, 0, ALU.is_equal, 0.0)
        for r in (0, 1):
            P.pool.tensor_scalar(out=c["NU%d" % r][:], in0=c["U%d" % r][:], scalar1=-1.0, scalar2=None, op0=ALU.mult)
        P.pool.tensor_copy(out=c["ident_bf"][:], in_=c["ident"][:])
        zb = P.sb("zb", [128, 8, 1], BF16)
        P.pool.memset(ap=zb[:], constant=0.0)
        for s, n in ((0, TC), (1, T)):
            v = self.HT[s].h.ap().rearrange("kc p t -> p kc t")
            P.sp.dma_start(out=v[:, :, 0:1], in_=zb[:], _writes=[self.HT[s]], allow_slow_non_contiguous=True)
            P.sp.dma_start(out=v[:, :, n + 1:n + 2], in_=zb[:], _writes=[self.HT[s]], allow_slow_non_contiguous=True)
        P.end()

    def mods(self, l):
        P = self.P
        P.begin()
        cT = P.sb("cT", [128, 8, 2])
        sT = P.sb("sT", [128, 8, 2])
        P.sp.dma_start(out=cT[:], in_=self.ccT.ap())
        P.act.activation(out=sT[:], in_=cT[:], func=AF.Silu)
        bm = P.sb("bm", [2, 6 * D])
        P.sp.dma_start(out=bm[:], in_=self.b_mod.h.ap()[l:l + 1, :].broadcast_to([2, 6 * D]))
        mods = P.sb("mods", [2, 6 * D])
        wbuf = [P.sb("wm%d" % i, [128, 8, 512]) for i in range(2)]
        pm = [P.ps("pm%d" % i, [2, 512]) for i in range(2)]
        wv = self.w_mod.h.ap()[l].rearrange("(kc p) n -> p kc n", p=128)
        for nb in range(12):
            wb = wbuf[nb % 2]
            P.sp.dma_start(out=wb[:], in_=wv[:, :, nb * 512:(nb + 1) * 512])
            pp = pm[nb % 2]
            for kc in range(8):
                P.pe.matmul(out=pp[:], lhsT=sT[:, kc, :], rhs=wb[:, kc, :], start=(kc == 0), stop=(kc == 7))
            P.dve.tensor_tensor(out=mods[:, nb * 512:(nb + 1) * 512], in0=pp[:], in1=bm[:, nb * 512:(nb + 1) * 512], op=ALU.add)
        gv = P.sb("gv", [2, 4, D])
        for i, g in enumerate((self.g_pre_mix, self.g_post_mix, self.g_pre_ffn, self.g_post_ffn)):
            P.sp.dma_start(out=gv[:, i, :], in_=g.h.ap()[l:l + 1, :].broadcast_to([2, D]))
        mb = P.sb("mb", [2, 6, D])
        m = lambda i: mods[:, i * D:(i + 1) * D]
        P.dve.tensor_copy(out=mb[:, 0, :], in_=m(0))
        P.dve.scalar_tensor_tensor(out=mb[:, 1, :], in0=m(1), scalar=1.0, in1=gv[:, 0, :], op0=ALU.add, op1=ALU.mult)
        P.dve.tensor_tensor(out=mb[:, 2, :], in0=m(2), in1=gv[:, 1, :], op=ALU.mult)
        P.dve.tensor_copy(out=mb[:, 3, :], in_=m(3))
        P.dve.scalar_tensor_tensor(out=mb[:, 4, :], in0=m(4), scalar=1.0, in1=gv[:, 2, :], op0=ALU.add, op1=ALU.mult)
        P.dve.tensor_tensor(out=mb[:, 5, :], in0=m(5), in1=gv[:, 3, :], op=ALU.mult)
        P.sp.dma_start(out=self.MB.h.ap()[l].rearrange("k s d -> s k d"), in_=mb[:], _writes=[self.MB])
        P.end()

    def load_mod(self, l, k, s, name):
        P = self.P
        t = P.sb(name, [128, D])
        P.sp.dma_start(out=t[:], in_=self.MB.h.ap()[l, k, s:s + 1, :].broadcast_to([128, D]), _reads=[self.MB])
        return t

    def rstd_chain(self, ss, n, scale, name, post=None):
        P = self.P
        t1 = P.sm(name + "a", n)
        t2 = P.sm(name + "b", n)
        t3 = P.sm(name + "c", n)
        P.dve.tensor_scalar(out=t1[:], in0=ss, scalar1=scale, scalar2=EPS, op0=ALU.mult, op1=ALU.add)
        P.act.activation(out=t2[:], in_=t1[:], func=AF.Sqrt)
        P.dve.reciprocal(out=t3[:], in_=t2[:])
        return t3

    def prenorm(self, l, ks, kg, src):
        P = self.P
        c = self.c
        P.begin()
        Sx = [self.load_mod(l, ks, 1, "Sc"), self.load_mod(l, ks, 0, "Sx")]
        Gx = [self.load_mod(l, kg, 1, "Gc"), self.load_mod(l, kg, 0, "Gx")]
        xt = [P.sb("xt%d" % i, [128, D]) for i in range(2)]
        junk = P.sb("junk", [128, D])
        h1 = [P.sb("h1%d" % i, [128, D]) for i in range(2)]
        hb = [P.sb("hb%d" % i, [128, D], BF16) for i in range(2)]
        pT = [P.ps("pT%d" % i, [128, D], BF16) for i in range(2)]
        hT = [P.sb("hT%d" % i, [128, 8, 512], BF16) for i in range(2)]
        it = 0
        for gi, (row0, ntok, s, t0) in enumerate(GROUPS):
            hTg = hT[gi % 2]
            for j in range(ntok // 128):
                b = it % 2
                it += 1
                r0 = row0 + j * 128
                P.sp.dma_start(out=xt[b][:], in_=src.h.ap()[r0:r0 + 128, :], _reads=[src])
                ss = P.sm("ss", 1)
                P.act.activation(out=junk[:], in_=xt[b][:], func=AF.Square, accum_out=ss[:])
                rstd = self.rstd_chain(ss[:], 1, 1.0 / D, "rs")
                P.dve.scalar_tensor_tensor(out=h1[b][:], in0=xt[b][:], scalar=rstd[:, 0:1], in1=Gx[s][:], op0=ALU.mult, op1=ALU.mult)
                P.pool.tensor_tensor(out=hb[b][:], in0=h1[b][:], in1=Sx[s][:], op=ALU.add)
                for kc in range(8):
                    P.pe.transpose(out=pT[b][:, kc * 128:(kc + 1) * 128], in_=hb[b][:, kc * 128:(kc + 1) * 128], identity=c["ident_bf"][:])
                P.act.copy(out=hTg[:, :, j * 128:(j + 1) * 128], in_=pT[b][:].rearrange("p (kc t) -> p kc t", kc=8))
            dst = self.HT[s].h.ap().rearrange("kc p t -> p kc t")[:, :, 1 + t0:1 + t0 + ntok]
            P.sp.dma_start(out=dst, in_=hTg[:, :, 0:ntok], _writes=[self.HT[s]])
        P.end()

    def proj(self, l):
        P = self.P
        c = self.c
        P.keep_begin()
        Wfm = P.sbk("Wfm", [128, 8, 1024], BF16)
        Wrest = P.sbk("Wrest", [128, 8, 784], BF16)
        Wqkv = [P.sbk("Wq%d" % j, [128, 8, 1536], BF16) for j in range(3)]
        P.begin()
        cw = P.sb("cw", [128, 3, 1536])
        P.sp.dma_start(out=cw[:], in_=self.conv_qkv.h.ap()[l:l + 1].broadcast_to([128, 3, 1536]))
        stg = [P.sb("stg%d" % i, [128, IN_COLS]) for i in range(2)]
        wv = self.w_in.h.ap()[l]
        for kc in range(8):
            st = stg[kc % 2]
            P.sp.dma_start(out=st[:], in_=wv[kc * 128:(kc + 1) * 128, :])
            P.act.copy(out=Wfm[:, kc, 0:768], in_=st[:, 0:768])
            P.act.copy(out=Wfm[:, kc, 768:1024], in_=st[:, OFF_CU:OFF_CV])
            P.act.copy(out=Wrest[:, kc, 0:512], in_=st[:, OFF_Z:OFF_CU])
            P.act.copy(out=Wrest[:, kc, 512:528], in_=st[:, OFF_AB:OFF_Z])
            P.act.copy(out=Wrest[:, kc, 528:784], in_=st[:, OFF_CV:IN_COLS])
            for j in range(3):
                eng = (P.dve, P.pool, P.dve)[j]
                eng.tensor_tensor(out=Wqkv[j][:, kc, :], in0=st[:, OFF_Q:OFF_AB], in1=cw[:, j, :], op=ALU.mult)
        P.end()

        P.begin()
        pfm = [P.ps("pfm%d" % i, [128, 512]) for i in range(2)]
        prot = [P.ps("prot%d" % i, [128, 512]) for i in range(3)]
        pr = P.ps("pr", [128, 512])
        pc = [P.ps("pc%d" % i, [128, 128]) for i in range(2)]
        wsl = P.sb("wsl", [128, 4, 128])
        wsT = P.sb("wsT", [128, 4, 128])
        P.sp.dma_start(out=wsl[:], in_=self.w_s.h.ap()[l].rearrange("g p q -> p g q"))
        for g in range(4):
            P.pe.transpose(out=pr[:, g * 128:(g + 1) * 128], in_=wsl[:, g, :], identity=c["ident"][:])
        P.dve.tensor_copy(out=wsT[:], in_=pr[:].rearrange("p (g q) -> p g q", g=4))
        Bs = P.sb("Bs", [128, 2, 128])
        for g in range(4):
            P.sp.dma_start(out=Bs[(g % 2) * 64:(g % 2) * 64 + 64, g // 2, :],
                           in_=self.b_s.h.ap()[l, g:g + 1, :].broadcast_to([64, 128]))
        lng = P.sb("lng", [128, 256])
        lnb = P.sb("lnb", [128, 256])
        P.sp.dma_start(out=lng[:], in_=self.ln_c_g.h.ap()[l:l + 1, :].broadcast_to([128, 256]))
        P.sp.dma_start(out=lnb[:], in_=self.ln_c_b.h.ap()[l:l + 1, :].broadcast_to([128, 256]))
        hTb = [P.sb("hTg%d" % i, [128, 8, 514], BF16) for i in range(2)]
        evb = [P.sb("evb%d" % i, [128, 512]) for i in range(2)]
        Csb = [P.sb("Csb%d" % i, [128, 512]) for i in range(2)]
        uT = P.sb("uT", [128, 2, 512])
        ycT = P.sb("ycT", [128, 2, 512], BF16)
        qsb = [P.sb("qsb%d" % i, [128, 512]) for i in range(2)]
        ksb = [P.sb("ksb%d" % i, [128, 512]) for i in range(2)]
        sq = [P.sb("sq%d" % i, [128, 512]) for i in range(2)]
        kvst = [P.sb("kvst%d" % i, [128, 2, 512]) for i in range(2)]
        qst = [P.sb("qst%d" % i, [128, 528]) for i in range(2)]
        zsb = [P.sb("zsb%d" % i, [128, 512]) for i in range(2)]
        cv = P.sb("cv", [128, 256])
        vn1 = P.sb("vn1", [128, 256])
        vn2 = P.sb("vn2", [128, 256])
        vn = P.sb("vn", [128, 256])
        tmpc = P.sb("tmpc", [128, 128])
        PFM = self.PFM.h.ap()
        it = 0
        rot = 0
        for gi, (row0, ntok, s, t0) in enumerate(GROUPS):
            hTg = hTb[gi % 2]
            src = self.HT[s].h.ap().rearrange("kc p t -> p kc t")[:, :, t0:t0 + ntok + 2]
            P.sp.dma_start(out=hTg[:, :, 0:ntok + 2], in_=src, _reads=[self.HT[s]])
            for cb in range(8):
                pf = pfm[cb % 2]
                for kc in range(8):
                    P.pe.matmul(out=pf[:, 0:ntok], lhsT=Wfm[:, kc, cb * 128:(cb + 1) * 128], rhs=hTg[:, kc, 1:1 + ntok],
                                start=(kc == 0), stop=(kc == 7))
                if cb < 2:
                    ev = evb[cb % 2]
                    P.act.copy(out=ev[:, 0:ntok], in_=pf[:, 0:ntok])
                    P.sp.dma_start(out=PFM[cb * 128:(cb + 1) * 128, row0:row0 + ntok], in_=ev[:, 0:ntok], _writes=[self.PFM])
                elif cb < 4:
                    P.act.copy(out=Csb[cb - 2][:, 0:ntok], in_=pf[:, 0:ntok])
                elif cb < 6:
                    ev = evb[cb % 2]
                    P.dve.tensor_tensor(out=ev[:, 0:ntok], in0=pf[:, 0:ntok], in1=Csb[cb - 4][:, 0:ntok], op=ALU.mult)
                    P.sp.dma_start(out=PFM[256 + (cb - 4) * 128:256 + (cb - 3) * 128, row0:row0 + ntok], in_=ev[:, 0:ntok], _writes=[self.PFM])
                else:
                    P.act.activation(out=uT[:, cb - 6, 0:ntok], in_=pf[:, 0:ntok], func=AF.Gelu)
            for j in range(ntok // 128):
                b = it % 2
                it += 1
                n = (row0 + j * 128) // 128
                off = j * 128
                r0 = row0 + off
                pq = []
                for which in range(4):
                    pp = prot[rot % 3]
                    rot += 1
                    pq.append(pp)
                    if which < 3:
                        first = True
                        for tap in range(3):
                            for kc in range(8):
                                P.pe.matmul(out=pp[:], lhsT=hTg[:, kc, off + tap:off + tap + 128],
                                            rhs=Wqkv[tap][:, kc, which * 512:(which + 1) * 512],
                                            start=first, stop=(tap == 2 and kc == 7))
                                first = False
                    else:
                        for kc in range(8):
                            P.pe.matmul(out=pp[:], lhsT=hTg[:, kc, off + 1:off + 129], rhs=Wrest[:, kc, 0:512],
                                        start=(kc == 0), stop=(kc == 7))
                    if which == 0:
                        P.act.activation(out=qsb[b][:], in_=pp[:], func=AF.Silu)
                    elif which == 1:
                        P.act.activation(out=ksb[b][:], in_=pp[:], func=AF.Silu)
                    elif which == 2:
                        P.act.activation(out=kvst[b][:, 1, :], in_=pp[:], func=AF.Silu)
                    else:
                        P.act.activation(out=zsb[b][:], in_=pp[:], func=AF.Silu)
                        P.sp.dma_start(out=self.ZS.h.ap()[r0:r0 + 128, :], in_=zsb[b][:], _writes=[self.ZS])
                for kc in range(8):
                    P.pe.matmul(out=pr[:, 0:272], lhsT=hTg[:, kc, off + 1:off + 129], rhs=Wrest[:, kc, 512:784],
                                start=(kc == 0), stop=(kc == 7))
                P.dve.tensor_copy(out=qst[b][:, 512:528], in_=pr[:, 0:16])
                P.act.activation(out=cv[:], in_=pr[:, 16:272], func=AF.Gelu)
                ss8 = P.sm("ss8", 8)
                P.pool.tensor_tensor(out=sq[0][:], in0=qsb[b][:], in1=qsb[b][:], op=ALU.mult)
                P.dve.tensor_reduce(out=ss8[:, 0:4], in_=sq[0][:].rearrange("p (h d) -> p h d", h=4), axis=AX.X, op=ALU.add)
                P.pool.tensor_tensor(out=sq[1][:], in0=ksb[b][:], in1=ksb[b][:], op=ALU.mult)
                P.dve.tensor_reduce(out=ss8[:, 4:8], in_=sq[1][:].rearrange("p (h d) -> p h d", h=4), axis=AX.X, op=ALU.add)
                rs = self.rstd_chain(ss8[:], 8, 1.0, "rsqk")
                rq = P.sm("rq", 4)
                P.dve.tensor_scalar(out=rq[:], in0=rs[:, 0:4], scalar1=128.0 ** -0.5, scalar2=None, op0=ALU.mult)
                P.dve.tensor_tensor(out=qst[b][:, 0:512].rearrange("p (h d) -> p h d", h=4),
                                    in0=qsb[b][:].rearrange("p (h d) -> p h d", h=4),
                                    in1=rq[:].unsqueeze(2).broadcast_to([128, 4, 128]), op=ALU.mult)
                P.dve.tensor_tensor(out=kvst[b][:, 0, :].rearrange("p (h d) -> p h d", h=4),
                                    in0=ksb[b][:].rearrange("p (h d) -> p h d", h=4),
                                    in1=rs[:, 4:8].unsqueeze(2).broadcast_to([128, 4, 128]), op=ALU.mult)
                dst = self.SCN.h.ap()[n][:, 512:2560].rearrange("p (a b) -> p a b", b=1024)[:, :, 0:512]
                P.sp.dma_start(out=dst, in_=kvst[b][:], _writes=[self.SCN])
                P.sp.dma_start(out=self.QS.h.ap()[r0:r0 + 128, :], in_=qst[b][:], _writes=[self.QS])
                st6 = P.sm("st6", 6)
                mv = P.sm("mv", 2)
                P.dve.bn_stats(out=st6[:], in_=cv[:])
                P.dve.bn_aggr(out=mv[:], in_=st6[:])
                rl = self.rstd_chain(mv[:, 1:2], 1, 1.0, "rln")
                P.dve.tensor_scalar(out=vn1[:], in0=cv[:], scalar1=mv[:, 0:1], scalar2=rl[:, 0:1], op0=ALU.subtract, op1=ALU.mult)
                P.pool.tensor_tensor(out=vn2[:], in0=vn1[:], in1=lng[:], op=ALU.mult)
                P.pool.tensor_tensor(out=vn[:], in0=vn2[:], in1=lnb[:], op=ALU.add)
                for g in range(4):
                    P.pe.matmul(out=pc[g // 2][(g % 2) * 64:(g % 2) * 64 + 64, :], lhsT=vn[:, g * 64:(g + 1) * 64],
                                rhs=wsT[:, g, :], start=True, stop=True)
                for ct in range(2):
                    P.dve.tensor_tensor(out=tmpc[:], in0=pc[ct][:], in1=Bs[:, ct, :], op=ALU.add)
                    P.pool.tensor_tensor(out=ycT[:, ct, off:off + 128], in0=tmpc[:], in1=uT[:, ct, off:off + 128], op=ALU.mult)
            dst = self.YT.h.ap()[768:1024, row0:row0 + ntok].rearrange("(ct p) t -> p ct t", p=128)
            P.sp.dma_start(out=dst, in_=ycT[:, :, 0:ntok], _writes=[self.YT])
        P.end()
        P.keep_end()

    def gdn_prep(self, l):
        P = self.P
        c = self.c
        P.begin()
        al = P.sb("al", [128, 8])
        dtb = P.sb("dtb", [128, 8])
        ea = P.sb("ea", [128, 8])
        nea = P.sb("nea", [128, 8])
        P.sp.dma_start(out=al[:], in_=self.a_log.h.ap()[l:l + 1, :].broadcast_to([128, 8]))
        P.sp.dma_start(out=dtb[:], in_=self.dt_bias.h.ap()[l:l + 1, :].broadcast_to([128, 8]))
        P.act.activation(out=ea[:], in_=al[:], func=AF.Exp)
        P.dve.tensor_scalar(out=nea[:], in0=ea[:], scalar1=-1.0, scalar2=None, op0=ALU.mult)
        ph = P.psb("pha", 4, 128) + P.psb("phb", 4, 128)
        pA = [P.psb("pA%d" % r, 4, 128) for r in range(2)]
        pB = [P.psb("pB%d" % r, 4, 128) for r in range(2)]
        pD = [P.psb("pD%d" % r, 4, 128) for r in range(2)]
        pg = SubTile(ph[4].parent, 0, 16, "pgv")
        GU = [P.sb("GU%d" % r, [128, 512]) for r in range(2)]
        kin = [P.sb("kin%d" % i, [128, 512]) for i in range(2)]
        qs = [P.sb("qs%d" % i, [128, 528]) for i in range(2)]
        okT = [P.sb("okT%d" % i, [128, 512]) for i in range(2)]
        oqT = [P.sb("oqT%d" % i, [128, 512]) for i in range(2)]
        oT2T = [[P.sb("oT2T%d_%d" % (i, r), [128, 512]) for r in range(2)] for i in range(2)]
        oQKT = [[P.sb("oQKT%d_%d" % (i, r), [128, 512]) for r in range(2)] for i in range(2)]
        scalb = [P.sb("scal%d" % i, [128, 32]) for i in range(2)]
        sm = {nm: P.sb(nm, [128, 8]) for nm in ("e1", "d1", "x2", "e2", "sp", "g", "dl", "et")}
        gsb = P.sb("gsb", [128, 16])
        U8 = [(r, h) for r in range(2) for h in range(4)]
        mk = lambda nm: {u: P.sb("%s%d%d" % (nm, u[0], u[1]), [128, 128]) for u in U8}
        Dt, E2, a1 = mk("Dt"), mk("E2"), mk("a1")
        Pp = [[mk("Pp%d%d" % (q, i)) for i in range(2)] for q in range(2)]
        PT = [[mk("PT%d%d" % (q, i)) for i in range(2)] for q in range(2)]
        R = [[mk("R%d%d" % (q, i)) for i in range(2)] for q in range(2)]
        SCN = self.SCN.h.ap()
        SC2 = self.SC2.h.ap()
        hs = lambda h: slice(h * 128, (h + 1) * 128)

        def stage_a(n):
            b = n % 2
            scal = scalb[b]
            g = sm["g"]
            Pp0, PT0, R0 = Pp[b][0], PT[b][0], R[b][0]

            def a_load():
                P.sp.dma_start(out=kin[b][:], in_=SCN[n][:, 512:1024], _reads=[self.SCN])
                P.sp.dma_start(out=qs[b][:], in_=self.QS.h.ap()[n * 128:(n + 1) * 128, :], _reads=[self.QS])

            def a_gates1():
                P.act.activation(out=sm["e1"][:], in_=qs[b][:, 512:520], func=AF.Exp, scale=-1.0)
                P.dve.tensor_scalar(out=sm["d1"][:], in0=sm["e1"][:], scalar1=1.0, scalar2=None, op0=ALU.add)
                P.dve.reciprocal(out=scal[:, 0:8], in_=sm["d1"][:])
                P.dve.tensor_tensor(out=sm["x2"][:], in0=qs[b][:, 520:528], in1=dtb[:], op=ALU.add)
                P.act.activation(out=sm["e2"][:], in_=sm["x2"][:], func=AF.Exp)
                P.act.activation(out=sm["sp"][:], in_=sm["e2"][:], func=AF.Ln, bias=1.0)
                P.dve.tensor_tensor(out=g[:], in0=sm["sp"][:], in1=nea[:], op=ALU.mult)

            def a_gates2():
                P.pe.matmul(out=pg[:, 0:4], lhsT=c["U0"][:], rhs=g[:, 0:4], start=True, stop=True)
                P.pe.matmul(out=pg[:, 4:8], lhsT=c["U1"][:], rhs=g[:, 4:8], start=True, stop=True)
                P.pe.matmul(out=pg[:, 8:16], lhsT=c["ones"][:], rhs=g[:, 0:8], start=True, stop=True)

            def a_gates3():
                P.dve.tensor_copy(out=gsb[:], in_=pg[:])
                P.act.activation(out=scal[:, 8:16], in_=gsb[:, 0:8], func=AF.Exp)
                P.dve.tensor_tensor(out=sm["dl"][:], in0=gsb[:, 8:16], in1=gsb[:, 0:8], op=ALU.subtract)
                P.act.activation(out=sm["et"][:], in_=sm["dl"][:], func=AF.Exp)
                P.dve.tensor_copy(out=scal[:, 16:24], in_=sm["et"][:])
                P.act.activation(out=scal[:, 24:32], in_=gsb[:, 8:16], func=AF.Exp)
                for (r, h) in U8:
                    idx = r * 4 + h
                    P.act.activation(out=GU[r][:, hs(h)], in_=c["U%d" % r][:], func=AF.Copy, scale=g[:, idx:idx + 1])

            def a_tr():
                for h in range(4):
                    P.pe.transpose(out=ph[h][:], in_=kin[b][:, hs(h)], identity=c["ident"][:])
                for h in range(4):
                    P.pe.transpose(out=ph[4 + h][:], in_=qs[b][:, hs(h)], identity=c["ident"][:])

            def a_trc():
                for h in range(4):
                    P.act.copy(out=okT[b][:, hs(h)], in_=ph[h][:])
                for h in range(4):
                    P.dve.tensor_copy(out=oqT[b][:, hs(h)], in_=ph[4 + h][:])

            def a_kk():
                for h in range(4):
                    P.pe.matmul(out=ph[h][:], lhsT=okT[b][:, hs(h)], rhs=okT[b][:, hs(h)], start=True, stop=True)
                for h in range(4):
                    P.pe.matmul(out=ph[4 + h][:], lhsT=okT[b][:, hs(h)], rhs=oqT[b][:, hs(h)], start=True, stop=True)
                for r in range(2):
                    P.pe.matmul(out=pD[r][0].parent[:], lhsT=c["ones"][:], rhs=GU[r][:], start=True, stop=True)

            def a_dt():
                for (r, h) in U8:
                    idx = r * 4 + h
                    P.dve.scalar_tensor_tensor(out=Dt[(r, h)][:], in0=pD[r][h][:], scalar=gsb[:, idx:idx + 1], in1=c["NM%d" % r][:],
                                               op0=ALU.subtract, op1=ALU.add)
                for u in U8:
                    P.act.activation(out=E2[u][:], in_=Dt[u][:], func=AF.Exp)

            def a_n():
                for (r, h) in U8:
                    u = (r, h)
                    idx = r * 4 + h
                    P.dve.tensor_tensor(out=oQKT[b][r][:, hs(h)], in0=ph[4 + h][:], in1=E2[u][:], op=ALU.mult)
                    P.dve.scalar_tensor_tensor(out=a1[u][:], in0=ph[h][:], scalar=scal[:, idx:idx + 1], in1=E2[u][:],
                                               op0=ALU.mult, op1=ALU.mult)
                    P.pool.tensor_tensor(out=Pp0[u][:], in0=a1[u][:], in1=c["SU%d" % r][:], op=ALU.mult)
                for u in U8:
                    P.pool.tensor_tensor(out=R0[u][:], in0=c["ident"][:], in1=Pp0[u][:], op=ALU.subtract)

            def a_nt():
                for (r, h) in U8:
                    P.pe.transpose(out=pD[r][h][:], in_=Pp0[(r, h)][:], identity=c["ident"][:])

            def a_ntc():
                for (r, h) in U8:
                    (P.act.copy if r == 0 else P.dve.tensor_copy)(out=PT0[(r, h)][:], in_=pD[r][h][:])
            return [a_load, a_gates1, a_gates2, a_gates3, a_tr, a_trc, a_kk, a_dt, a_n, a_nt, a_ntc]

        def level_stages(n, r, lvl):
            b = n % 2
            cur, nxt = lvl % 2, 1 - (lvl % 2)
            Pc, Pn, Tc, Tn, Rc, Rn = Pp[b][cur], Pp[b][nxt], PT[b][cur], PT[b][nxt], R[b][cur], R[b][nxt]

            def s1():
                for h in range(4):
                    u = (r, h)
                    if lvl < 5:
                        P.pe.matmul(out=pA[r][h][:], lhsT=Tc[u][:], rhs=Pc[u][:], start=True, stop=True)
                    P.pe.matmul(out=pB[r][h][:], lhsT=Pc[u][:], rhs=Tc[u][:], start=True, stop=True)

            def s2():
                if lvl < 5:
                    for h in range(4):
                        (P.act.copy if r == 0 else P.dve.tensor_copy)(out=Pn[(r, h)][:], in_=pA[r][h][:])
                for h in range(4):
                    (P.dve.tensor_copy if r == 0 else P.act.copy)(out=Tn[(r, h)][:], in_=pB[r][h][:])

            def s3():
                for h in range(4):
                    u = (r, h)
                    P.pe.matmul(out=pA[r][h][:], lhsT=Tn[u][:], rhs=Rc[u][:], start=True, stop=True)

            def s4():
                for h in range(4):
                    u = (r, h)
                    dst = Rn[u][:] if lvl < 5 else oT2T[b][r][:, hs(h)]
                    P.dve.tensor_tensor(out=dst, in0=pA[r][h][:], in1=Rc[u][:], op=ALU.add)
            return [s1, s2, s3, s4]

        def stores(n):
            b = n % 2
            w = [self.SC2]
            P.sp.dma_start(out=SC2[n][:, 0:512], in_=okT[b][:], _writes=w)
            P.sp.dma_start(out=SC2[n][:, 512:1024], in_=oqT[b][:], _writes=w)
            for r in range(2):
                P.sp.dma_start(out=SC2[n][:, 1024 + r * 512:1536 + r * 512], in_=oT2T[b][r][:], _writes=w)
                P.sp.dma_start(out=SC2[n][:, 2048 + r * 512:2560 + r * 512], in_=oQKT[b][r][:], _writes=w)
            P.sp.dma_start(out=SC2[n][:, 3072:3104], in_=scalb[b][:], _writes=w)

        for f in stage_a(0):
            f()
        A_AT = {0: 0, 1: 1, 2: 4, 4: 5, 8: 2, 10: 3, 14: 6, 16: 7, 18: 8, 22: 9, 23: 10}
        for n in range(NTILE):
            stA = [st for lvl in range(6) for st in level_stages(n, 0, lvl)]
            stB = [st for lvl in range(6) for st in level_stages(n, 1, lvl)]
            nxtA = stage_a(n + 1) if n + 1 < NTILE else None
            for i in range(len(stA) + 1):
                if i < len(stA):
                    stA[i]()
                if i >= 1:
                    stB[i - 1]()
                if nxtA is not None and i in A_AT:
                    nxtA[A_AT[i]]()
            stores(n)
        P.end()

    def gdn_scan(self, l):
        P = self.P
        P.begin()
        SCN = self.SCN.h.ap()
        SC2 = self.SC2.h.ap()
        Forder = list(range(NTILE))
        Border = [1, 0] + list(range(NTILE - 1, 1, -1))
        names = ("kT", "k", "qT", "v", "T2T", "QKT")
        col0 = {"kT": 0, "k": 512, "qT": 1024, "v": 1536}
        inb = [[{nm: P.sb("i%s%d%d" % (nm, r, i), [128, 512]) for nm in names} for i in range(2)] for r in range(2)]
        scb = [[P.sb("isc%d%d" % (r, i), [128, 32]) for i in range(2)] for r in range(2)]
        ngc = [P.sb("ngc%d" % r, [128, 8]) for r in range(2)]
        S = [[[P.sb("S%d%d%d" % (r, h, i), [128, 128]) for i in range(2)] for h in range(4)] for r in range(2)]
        pa = [P.psb("pa%d" % r, 4, 128) for r in range(2)]
        po = [P.psb("po%d" % r, 4, 128) for r in range(2)]
        rr = [[P.sb("rr%d%d" % (r, h), [128, 128]) for h in range(4)] for r in range(2)]
        vn = [[P.sb("vn%d%d" % (r, h), [128, 128]) for h in range(4)] for r in range(2)]
        vt = [[P.sb("vt%d%d" % (r, h), [128, 128]) for h in range(4)] for r in range(2)]
        t1 = [[P.sb("t1%d%d" % (r, h), [128, 128]) for h in range(4)] for r in range(2)]
        oo = [[P.sb("oo%d%d" % (r, i), [128, 512]) for i in range(2)] for r in range(2)]
        for r in range(2):
            for h in range(4):
                P.pool.memset(ap=S[r][h][0][:], constant=0.0)
        units = [(r, h) for r in range(2) for h in range(4)]
        cur = 0
        for i in range(NTILE):
            b = i % 2
            nxt = 1 - cur
            tl = (Forder[i], Border[i])
            for r in range(2):
                n = tl[r]
                d = inb[r][b]
                for nm in ("k", "v"):
                    P.sp.dma_start(out=d[nm][:], in_=SCN[n][:, col0[nm]:col0[nm] + 512], _reads=[self.SCN])
                P.sp.dma_start(out=d["kT"][:], in_=SC2[n][:, 0:512], _reads=[self.SC2])
                P.sp.dma_start(out=d["qT"][:], in_=SC2[n][:, 512:1024], _reads=[self.SC2])
                P.sp.dma_start(out=d["T2T"][:], in_=SC2[n][:, 1024 + r * 512:1536 + r * 512], _reads=[self.SC2])
                P.sp.dma_start(out=d["QKT"][:], in_=SC2[n][:, 2048 + r * 512:2560 + r * 512], _reads=[self.SC2])
                P.sp.dma_start(out=scb[r][b][:], in_=SC2[n][:, 3072:3104], _reads=[self.SC2])
                P.pool.tensor_scalar(out=ngc[r][:], in0=scb[r][b][:, 8:16], scalar1=-1.0, scalar2=None, op0=ALU.mult)
            hs = lambda h: slice(h * 128, (h + 1) * 128)

            def scan_stages(r):
                d = inb[r][b]
                sc = scb[r][b]
                def t1_():
                    for h in range(4):
                        P.pe.matmul(out=pa[r][h][:], lhsT=d["kT"][:, hs(h)], rhs=S[r][h][cur][:], start=True, stop=True)
                    for h in range(4):
                        P.pe.matmul(out=po[r][h][:], lhsT=d["qT"][:, hs(h)], rhs=S[r][h][cur][:], start=True, stop=True)
                def t2_():
                    for h in range(4):
                        idx = r * 4 + h
                        P.dve.scalar_tensor_tensor(out=rr[r][h][:], in0=pa[r][h][:], scalar=ngc[r][:, idx:idx + 1], in1=d["v"][:, hs(h)],
                                                   op0=ALU.mult, op1=ALU.add)
                    for h in range(4):
                        idx = r * 4 + h
                        P.act.activation(out=t1[r][h][:], in_=po[r][h][:], func=AF.Copy, scale=sc[:, 8 + idx:9 + idx])
                def t3_():
                    for h in range(4):
                        P.pe.matmul(out=pa[r][h][:], lhsT=d["T2T"][:, hs(h)], rhs=rr[r][h][:], start=True, stop=True)
                def t4_():
                    for h in range(4):
                        idx = r * 4 + h
                        P.act.activation(out=vn[r][h][:], in_=pa[r][h][:], func=AF.Copy, scale=sc[:, idx:idx + 1])
                    for h in range(4):
                        idx = r * 4 + h
                        P.pool.tensor_scalar(out=vt[r][h][:], in0=vn[r][h][:], scalar1=sc[:, 16 + idx:17 + idx], scalar2=None, op0=ALU.mult)
                def t5_():
                    for h in range(4):
                        P.pe.matmul(out=pa[r][h][:], lhsT=d["k"][:, hs(h)], rhs=vt[r][h][:], start=True, stop=True)
                    for h in range(4):
                        P.pe.matmul(out=po[r][h][:], lhsT=d["QKT"][:, hs(h)], rhs=vn[r][h][:], start=True, stop=True)
                def t6_():
                    for h in range(4):
                        idx = r * 4 + h
                        P.dve.scalar_tensor_tensor(out=S[r][h][nxt][:], in0=S[r][h][cur][:], scalar=sc[:, 24 + idx:25 + idx],
                                                   in1=pa[r][h][:], op0=ALU.mult, op1=ALU.add)
                    for h in range(4):
                        P.dve.tensor_tensor(out=oo[r][b][:, hs(h)], in0=po[r][h][:], in1=t1[r][h][:], op=ALU.add)
                return [t1_, t2_, t3_, t4_, t5_, t6_]
            sA, sB = scan_stages(0), scan_stages(1)
            for k_ in range(len(sA) + 1):
                if k_ < len(sA):
                    sA[k_]()
                if k_ >= 1:
                    sB[k_ - 1]()
            for r in range(2):
                n = tl[r]
                dst = (self.OF, self.OB)[r]
                P.sp.dma_start(out=dst.h.ap()[n * 128:(n + 1) * 128, :], in_=oo[r][b][:], _writes=[dst])
            cur = nxt
        P.end()

    def mixer_a(self, l):
        P = self.P
        P.begin()
        PFM = self.PFM.h.ap()
        YT = self.YT.h.ap()
        cwa = P.sb("cwa", [128, 2, 3])
        for ct in range(2):
            P.sp.dma_start(out=cwa[:, ct, :], in_=self.conv_a.h.ap()[l][:, ct * 128:(ct + 1) * 128].rearrange("j c -> c j"),
                           allow_slow_non_contiguous=True)
        ca = [P.sb("ca%d" % i, [128, T]) for i in range(2)]
        Bt = [P.sb("Bt%d" % i, [128, T]) for i in range(2)]
        acc = [P.sb("acc%d" % i, [128, T]) for i in range(2)]
        ya = [P.sb("ya%d" % i, [128, T], BF16) for i in range(2)]
        it = 0
        for s, (c0, n) in enumerate(((0, TC), (TC, T))):
            if s == 0 and l == DEPTH - 1:
                continue
            for ct in range(2):
                b = it % 2
                it += 1
                P.sp.dma_start(out=ca[b][:, 0:n], in_=PFM[256 + ct * 128:256 + (ct + 1) * 128, c0:c0 + n], _reads=[self.PFM])
                P.sp.dma_start(out=Bt[b][:, 0:n], in_=PFM[ct * 128:(ct + 1) * 128, c0:c0 + n], _reads=[self.PFM])
                w = lambda j: cwa[:, ct, j:j + 1]
                P.pool.tensor_scalar(out=acc[b][:, 0:n], in0=ca[b][:, 0:n], scalar1=w(1), scalar2=None, op0=ALU.mult)
                if s == 0:
                    sh = [(acc[b][:, 1:n], ca[b][:, 0:n - 1], 0), (acc[b][:, 0:n - 1], ca[b][:, 1:n], 2)]
                elif ct == 0:
                    av = acc[b][:, 0:n].rearrange("p (r c) -> p r c", c=64)
                    cv = ca[b][:, 0:n].rearrange("p (r c) -> p r c", c=64)
                    sh = [(av[:, :, 1:64], cv[:, :, 0:63], 0), (av[:, :, 0:63], cv[:, :, 1:64], 2)]
                else:
                    sh = [(acc[b][:, 64:n], ca[b][:, 0:n - 64], 0), (acc[b][:, 0:n - 64], ca[b][:, 64:n], 2)]
                for (dst, src, j) in sh:
                    P.dve.scalar_tensor_tensor(out=dst, in0=src, scalar=w(j), in1=dst, op0=ALU.mult, op1=ALU.add)
                P.pool.tensor_tensor(out=ya[b][:, 0:n], in0=acc[b][:, 0:n], in1=Bt[b][:, 0:n], op=ALU.mult)
                P.sp.dma_start(out=YT[ct * 128:(ct + 1) * 128, c0:c0 + n], in_=ya[b][:, 0:n], _writes=[self.YT])
        P.end()

    def post_norm_residual(self, py, xt, G, tmp, xo, ssq, junk):
        P = self.P
        for hf in range(2):
            P.act.activation(out=junk[:, 0:512], in_=py[hf][:], func=AF.Square, accum_out=ssq[:, hf:hf + 1])
        ss = P.sm("pss", 1)
        P.dve.tensor_tensor(out=ss[:], in0=ssq[:, 0:1], in1=ssq[:, 1:2], op=ALU.add)
        rstd = self.rstd_chain(ss[:], 1, 1.0 / D, "prs")
        for hf in range(2):
            P.dve.scalar_tensor_tensor(out=tmp[:, hf * 512:(hf + 1) * 512], in0=py[hf][:], scalar=rstd[:, 0:1],
                                       in1=G[:, hf * 512:(hf + 1) * 512], op0=ALU.mult, op1=ALU.mult)
        P.pool.tensor_tensor(out=xo[:], in0=tmp[:], in1=xt[:], op=ALU.add)

    def mix_out(self, l, src):
        P = self.P
        c = self.c
        last = l == DEPTH - 1
        P.keep_begin()
        Wo = P.sbk("Wo", [128, 8, D], BF16)
        P.begin()
        stg = [P.sb("stgo%d" % i, [128, D]) for i in range(2)]
        for kc in range(8):
            P.sp.dma_start(out=stg[kc % 2][:], in_=self.w_o.h.ap()[l][kc * 128:(kc + 1) * 128, :])
            (P.act.copy if kc % 2 else P.dve.tensor_copy)(out=Wo[:, kc, :], in_=stg[kc % 2][:])
        P.end()
        P.begin()
        gon = P.sb("gon", [128, 128])
        P.sp.dma_start(out=gon[:], in_=self.g_onorm.h.ap()[l:l + 1, :].broadcast_to([128, 128]))
        G = [self.load_mod(l, 2, 1, "Gpc"), self.load_mod(l, 2, 0, "Gpx")]
        of = [P.sb("of%d" % i, [128, 512]) for i in range(2)]
        ob = [P.sb("ob%d" % i, [128, 512]) for i in range(2)]
        zs = [P.sb("zs%d" % i, [128, 512]) for i in range(2)]
        xt = [P.sb("xt%d" % i, [128, D]) for i in range(2)]
        yac = [P.sb("yac%d" % i, [128, 4, 128], BF16) for i in range(2)]
        o = P.sb("o", [128, 512])
        sq = P.sb("sq", [128, 512])
        y1 = P.sb("y1", [128, 512])
        y2 = P.sb("y2", [128, 512])
        yb = P.sb("yb", [128, 512], BF16)
        ybT = [P.sb("ybT%d" % i, [128, 4, 128], BF16) for i in range(2)]
        tmp = P.sb("tmp", [128, D])
        junk = P.sb("junk", [128, 512])
        xo = [P.sb("xo%d" % i, [128, D]) for i in range(2)]
        pT = P.ps("pT", [128, 1024], BF16)
        py = [[P.ps("py%d%d" % (i, hf), [128, 512]) for hf in range(2)] for i in range(2)]
        YT = self.YT.h.ap()
        h4 = lambda ap: ap.rearrange("p (h d) -> p h d", h=4)
        it = 0
        for n in range(2 if last else 0, NTILE):
            b = it % 2
            it += 1
            s = 0 if n < 2 else 1
            r0 = n * 128
            P.sp.dma_start(out=of[b][:], in_=self.OF.h.ap()[r0:r0 + 128, :], _reads=[self.OF])
            P.sp.dma_start(out=ob[b][:], in_=self.OB.h.ap()[r0:r0 + 128, :], _reads=[self.OB])
            P.sp.dma_start(out=zs[b][:], in_=self.ZS.h.ap()[r0:r0 + 128, :], _reads=[self.ZS])
            P.sp.dma_start(out=xt[b][:], in_=src.h.ap()[r0:r0 + 128, :], _reads=[src])
            P.sp.dma_start(out=yac[b][:, 0:2, :], in_=YT[0:256, r0:r0 + 128].rearrange("(ct p) t -> p ct t", p=128), _reads=[self.YT])
            P.sp.dma_start(out=yac[b][:, 2:4, :], in_=YT[768:1024, r0:r0 + 128].rearrange("(ct p) t -> p ct t", p=128), _reads=[self.YT])
            P.pool.tensor_tensor(out=o[:], in0=of[b][:], in1=ob[b][:], op=ALU.add)
            P.pool.tensor_tensor(out=sq[:], in0=o[:], in1=o[:], op=ALU.mult)
            ss4 = P.sm("ss4", 4)
            P.dve.tensor_reduce(out=ss4[:], in_=h4(sq[:]), axis=AX.X, op=ALU.add)
            rs = self.rstd_chain(ss4[:], 4, 1.0 / 128, "rso")
            P.dve.tensor_tensor(out=h4(y1[:]), in0=h4(o[:]), in1=rs[:].unsqueeze(2).broadcast_to([128, 4, 128]), op=ALU.mult)
            P.pool.tensor_tensor(out=h4(y2[:]), in0=h4(y1[:]), in1=gon[:].unsqueeze(1).broadcast_to([128, 4, 128]), op=ALU.mult)
            P.dve.tensor_tensor(out=yb[:], in0=y2[:], in1=zs[b][:], op=ALU.mult)
            for h in range(4):
                P.pe.transpose(out=pT[:, h * 128:(h + 1) * 128], in_=yb[:, h * 128:(h + 1) * 128], identity=c["ident_bf"][:])
            P.act.copy(out=ybT[b][:], in_=pT[:, 0:512].rearrange("p (h t) -> p h t", h=4))
            lhs = [yac[b][:, 0, :], yac[b][:, 1, :]] + [ybT[b][:, h, :] for h in range(4)] + [yac[b][:, 2, :], yac[b][:, 3, :]]
            for hf in range(2):
                for kc in range(8):
                    P.pe.matmul(out=py[b][hf][:], lhsT=lhs[kc], rhs=Wo[:, kc, hf * 512:(hf + 1) * 512], start=(kc == 0), stop=(kc == 7))
            ssq = P.sm("ssq", 2)
            self.post_norm_residual(py[b], xt[b], G[s], tmp, xo[b], ssq, junk)
            P.sp.dma_start(out=self.XS.h.ap()[r0:r0 + 128, :], in_=xo[b][:], _writes=[self.XS])
        P.end()
        P.keep_end()

    def ffn(self, l):
        P = self.P
        c = self.c
        last = l == DEPTH - 1
        NJ = FFN_H // 128
        P.keep_begin()
        W1 = P.sbk("W1", [128, 8, 2 * FFN_H], BF16)
        W2 = P.sbk("W2", [128, NJ, D], BF16)
        P.begin()
        stg = [P.sb("stgf%d" % i, [128, 2 * FFN_H]) for i in range(2)]
        w1v = self.w_ffn_in.h.ap()[l]
        engs = (P.act.copy, P.dve.tensor_copy, P.pool.tensor_copy, P.act.copy)
        for kc in range(8):
            st = stg[kc % 2]
            P.sp.dma_start(out=st[:], in_=w1v[kc * 128:(kc + 1) * 128, :])
            for q in range(4):
                engs[q](out=W1[:, kc, q * 1408:(q + 1) * 1408], in_=st[:, q * 1408:(q + 1) * 1408])
        w2v = self.w_ffn_out.h.ap()[l].rearrange("(j p) n -> p j n", p=128)
        for jj, (j0, j1) in enumerate(((0, 5), (5, 10), (10, 15), (15, 20), (20, 22))):
            st = stg[jj % 2]
            nj = j1 - j0
            P.sp.dma_start(out=st[:, 0:nj * D].rearrange("p (j n) -> p j n", n=D), in_=w2v[:, j0:j1, :])
            for q in range(nj):
                engs[q % 3](out=W2[:, j0 + q, :], in_=st[:, q * D:(q + 1) * D])
        P.end()

        P.begin()
        S = [P.sb("Sf", [128, D]), None]
        Gm = [P.sb("Gf", [128, D]), None]
        Gp = [P.sb("Gpf", [128, D]), None]
        MBv = self.MB.h.ap()

        def load_mods(s):
            ms = 1 - s
            for t, k in ((S[0], 3), (Gm[0], 4), (Gp[0], 5)):
                P.sp.dma_start(out=t[:], in_=MBv[l, k, ms:ms + 1, :].broadcast_to([128, D]), _reads=[self.MB])
        xt = [[P.sb("xt%d%d" % (i, t), [128, D]) for t in range(2)] for i in range(2)]
        h1 = P.sb("h1", [128, D])
        hb = P.sb("hb", [128, D], BF16)
        hT = [P.sb("hT%d" % i, [128, 8, 256], BF16) for i in range(2)]
        sg = [P.sb("sg%d" % i, [128, 256]) for i in range(2)]
        tmp = P.sb("tmp", [128, D])
        xo = [P.sb("xo%d" % i, [128, D]) for i in range(2)]
        pT = P.ps("pT", [128, 1024], BF16)
        pg = P.ps("pg", [128, 512])
        pu = P.ps("pu", [128, 512])
        py = [[P.ps("py%d%d" % (t, hf), [128, 512]) for hf in range(2)] for t in range(2)]
        groups = list(range(1 if last else 0, NT // 256))
        LAG = 3
        aT = [P.sb("aTr%d" % i, [128, 256], BF16) for i in range(LAG + 2)]
        state = {"s": None}

        def pre_elem(gi, t):
            g = groups[gi]
            b = gi % 2
            s_ = 0 if g == 0 else 1
            if s_ != state["s"]:
                load_mods(s_)
                state["s"] = s_
            r0 = g * 256 + t * 128
            P.sp.dma_start(out=xt[b][t][:], in_=self.XS.h.ap()[r0:r0 + 128, :], _reads=[self.XS])
            ss = P.sm("ss", 1)
            P.act.activation(out=tmp[:], in_=xt[b][t][:], func=AF.Square, accum_out=ss[:])
            rstd = self.rstd_chain(ss[:], 1, 1.0 / D, "rsf")
            P.dve.scalar_tensor_tensor(out=h1[:], in0=xt[b][t][:], scalar=rstd[:, 0:1], in1=Gm[0][:], op0=ALU.mult, op1=ALU.mult)
            P.pool.tensor_tensor(out=hb[:], in0=h1[:], in1=S[0][:], op=ALU.add)

        def pre_tr(gi, t):
            b = gi % 2
            for kc in range(8):
                P.pe.transpose(out=pT[:, kc * 128:(kc + 1) * 128], in_=hb[:, kc * 128:(kc + 1) * 128], identity=c["ident_bf"][:])
            P.act.copy(out=hT[b][:, :, t * 128:(t + 1) * 128], in_=pT[:].rearrange("p (kc t) -> p kc t", kc=8))

        def second(j):
            a = aT[j % (LAG + 2)]
            for t in range(2):
                for hf in range(2):
                    P.pe.matmul(out=py[t][hf][:], lhsT=a[:, t * 128:(t + 1) * 128], rhs=W2[:, j, hf * 512:(hf + 1) * 512],
                                start=(j == 0), stop=(j == NJ - 1))

        def post(gi):
            g = groups[gi]
            b = gi % 2
            for t in range(2):
                r0 = g * 256 + t * 128
                ssq = P.sm("ssqf", 2)
                self.post_norm_residual(py[t], xt[b][t], Gp[0], tmp, xo[t], ssq, h1)
                if last:
                    P.sp.dma_start(out=self.out.h.ap()[r0 - TC:r0 - TC + 128, :], in_=xo[t][:], _writes=[self.out])
                else:
                    P.sp.dma_start(out=self.XB.h.ap()[r0:r0 + 128, :], in_=xo[t][:], _writes=[self.XB])

        for t in range(2):
            pre_elem(0, t)
            pre_tr(0, t)
        for gi, g in enumerate(groups):
            b = gi % 2
            nxt_ok = gi + 1 < len(groups)
            same_mods = nxt_ok and ((0 if groups[gi + 1] == 0 else 1) == state["s"])
            for j in range(NJ):
                for kc in range(8):
                    P.pe.matmul(out=pg[:, 0:256], lhsT=W1[:, kc, j * 128:(j + 1) * 128], rhs=hT[b][:, kc, :], start=(kc == 0), stop=(kc == 7))
                for kc in range(8):
                    P.pe.matmul(out=pu[:, 0:256], lhsT=W1[:, kc, FFN_H + j * 128:FFN_H + (j + 1) * 128], rhs=hT[b][:, kc, :], start=(kc == 0), stop=(kc == 7))
                if j >= LAG:
                    second(j - LAG)
                P.act.activation(out=sg[j % 2][:], in_=pg[:, 0:256], func=AF.Silu)
                P.dve.tensor_tensor(out=aT[j % (LAG + 2)][:], in0=pu[:, 0:256], in1=sg[j % 2][:], op=ALU.mult)
                if same_mods:
                    if j == 5:
                        pre_elem(gi + 1, 0)
                    elif j == 10:
                        pre_tr(gi + 1, 0)
                    elif j == 12:
                        pre_elem(gi + 1, 1)
                    elif j == 17:
                        pre_tr(gi + 1, 1)
            for j in range(NJ - LAG, NJ):
                second(j)
            post(gi)
            if nxt_ok and not same_mods:
                for t in range(2):
                    pre_elem(gi + 1, t)
                    pre_tr(gi + 1, t)
        P.end()
        P.keep_end()

    def forward(self, upto=None):
        self.consts()
        for l in range(DEPTH):
            src = self.xs_in if l == 0 else self.XB
            self.mods(l)
            self.prenorm(l, 0, 1, src)
            self.proj(l)
            self.gdn_prep(l)
            self.gdn_scan(l)
            self.mixer_a(l)
            self.mix_out(l, src)
            self.ffn(l)
        self.P.close()


W_NAMES = ["w_mod", "b_mod", "g_pre_mix", "g_post_mix", "g_pre_ffn", "g_post_ffn", "w_in", "conv_a", "conv_qkv",
           "a_log", "dt_bias", "g_onorm", "ln_c_g", "ln_c_b", "w_s", "b_s", "w_o", "w_ffn_in", "w_ffn_out"]


def make_in_maps(inputs, cores=range(8)):
    f = lambda a: np.ascontiguousarray(np.asarray(a, dtype=np.float32))
    shared = {n: f(inputs[n]) for n in W_NAMES}
    shared["a_log"] = shared["a_log"].reshape(DEPTH, 8)
    shared["dt_bias"] = shared["dt_bias"].reshape(DEPTH, 8)
    x, c, ctx, c_ctx = f(inputs["x"]), f(inputs["c"]), f(inputs["ctx"]), f(inputs["c_ctx"])
    maps = []
    for b in cores:
        m = dict(shared)
        m["xs"] = np.ascontiguousarray(np.concatenate([ctx[b], x[b]], axis=0))
        cc = np.stack([c[b], c_ctx], axis=0)
        m["ccT"] = np.ascontiguousarray(cc.reshape(2, 8, 128).transpose(2, 1, 0))
        maps.append(m)
    return maps


_CACHE = {}


def kernel(**inputs):
    if "nc" not in _CACHE:
        nc = bass.Bass("TRN2", target_bir_lowering=False)
        Model(nc).forward()
        _CACHE["nc"] = nc
    nc = _CACHE["nc"]
    maps = make_in_maps(inputs)
    res = run_bass_kernel_spmd(nc, maps, core_ids=list(range(8)))
    return np.stack([np.asarray(r["out"], dtype=np.float32) for r in res.results], axis=0)
```

```python
import numpy as np
from contextlib import ExitStack
import concourse.bass as bass
import concourse.mybir as mybir
from concourse.bass_utils import run_bass_kernel_spmd

F32 = mybir.dt.float32
F32R = mybir.dt.float32r
BF16 = mybir.dt.bfloat16
AF = mybir.ActivationFunctionType
ALU = mybir.AluOpType
AX = mybir.AxisListType

ENGS = ("tensor", "vector", "scalar", "gpsimd", "sync")
SAME_ENGINE_SYNC = True


class Tile:
    def __init__(self, prog, h, name, dram=False):
        self.prog = prog
        self.h = h
        self.name = name
        self.dram = dram
        self.lastw = None
        self.readers = []
        prog.tiles[name] = self

    def __getitem__(self, idx):
        return self.h[idx]

    def ap(self):
        return self.h.ap() if self.dram else self.h[:]


class SubTile:
    def __init__(self, parent, c0, w, name):
        self.parent = parent
        self.c0 = c0
        self.w = w
        self.name = name
        self.lastw = None
        self.readers = []

    def __getitem__(self, idx):
        return self.parent.h[:, self.c0:self.c0 + self.w][idx]


class Op:
    __slots__ = ("eng", "fn", "deps", "waits", "sig", "sigval", "is_dma", "dsem", "dval", "idx", "is_load", "F")

    def __init__(self, eng, fn, is_dma=False):
        self.eng = eng
        self.fn = fn
        self.deps = []
        self.waits = []
        self.sig = False
        self.sigval = None
        self.is_dma = is_dma
        self.dsem = None
        self.dval = None


class EngProxy:
    def __init__(self, prog, eng):
        self.prog = prog
        self.eng = eng

    def __getattr__(self, meth):
        prog, eng = self.prog, self.eng

        def call(*args, **kwargs):
            reads, writes = [], []
            for k, v in kwargs.items():
                if isinstance(v, bass.AP):
                    t = prog.tiles.get(v.tensor.name)
                    if t is None:
                        continue
                    if getattr(t, "subs", None):
                        t = t.subs[(v.offset % t.rowlen) // t.subw]
                    if k in ("out", "accum_out") or (k == "ap" and meth == "memset"):
                        writes.append(t)
                    else:
                        reads.append(t)
            extra_r = kwargs.pop("_reads", [])
            extra_w = kwargs.pop("_writes", [])
            reads += extra_r
            writes += extra_w
            is_dma = meth in ("dma_start",)
            if meth == "matmul" and kwargs.get("start", True) is False:
                pass
            fn = lambda e, meth=meth, args=args, kwargs=kwargs: getattr(e, meth)(*args, **kwargs)
            return prog.record(eng, fn, reads, writes, is_dma)

        return call


class Prog:
    def __init__(self, nc):
        self.nc = nc
        self.es = ExitStack()
        self.tiles = {}
        self.csem = {e: self.es.enter_context(nc.semaphore("c_" + e)) for e in ENGS}
        self.cnt = {e: 0 for e in ENGS}
        self.known = {e: {} for e in ENGS}
        self.ops = []
        self.dma_sems = []
        self.ndma_sems = 24
        for i in range(self.ndma_sems):
            self.dma_sems.append([self.es.enter_context(nc.semaphore("d%d" % i)), 0, None])
        self.dma_rr = 0
        self.phase_es = None
        for e in ENGS:
            setattr(self, e[0] if e != "sync" else "sp", EngProxy(self, e))
        self.pe = EngProxy(self, "tensor")
        self.dve = EngProxy(self, "vector")
        self.act = EngProxy(self, "scalar")
        self.pool = EngProxy(self, "gpsimd")
        self.sp = EngProxy(self, "sync")
        self.uid = 0

    def sb(self, name, shape, dtype=F32):
        self.uid += 1
        name = "%s_%d" % (name, self.uid)
        h = self.phase_es.enter_context(self.nc.sbuf_tensor(name, list(shape), dtype))
        return Tile(self, h, name)

    def sbc(self, name, shape, dtype=F32):
        self.uid += 1
        name = "%s_%d" % (name, self.uid)
        h = self.es.enter_context(self.nc.sbuf_tensor(name, list(shape), dtype))
        return Tile(self, h, name)

    def sm(self, name, cols, depth=4):
        key = (name, cols)
        ring = self.rings.setdefault(key, [[], 0])
        if len(ring[0]) < depth:
            ring[0].append(self.sb(name, [128, cols]))
            return ring[0][-1]
        ring[1] += 1
        return ring[0][ring[1] % depth]

    def keep_begin(self):
        self.keep_es = ExitStack()

    def keep_end(self):
        self.keep_es.close()
        self.keep_es = None

    def sbk(self, name, shape, dtype=F32):
        self.uid += 1
        name = "%s_%d" % (name, self.uid)
        h = self.keep_es.enter_context(self.nc.sbuf_tensor(name, list(shape), dtype))
        return Tile(self, h, name)

    def ps(self, name, shape, dtype=F32):
        self.uid += 1
        name = "%s_%d" % (name, self.uid)
        h = self.phase_es.enter_context(self.nc.psum_tensor(name, list(shape), dtype))
        t = Tile(self, h, name)
        t.psum = True
        return t

    def psb(self, name, n, w, dtype=F32):
        rowlen = 512 if dtype == F32 else 1024
        assert n * w <= rowlen
        t = self.ps(name, [128, rowlen], dtype)
        return [SubTile(t, i * w, w, "%s.%d" % (t.name, i)) for i in range(n)]

    def dram(self, name, shape, dtype=F32, kind="Internal"):
        h = self.nc.dram_tensor(name, list(shape), dtype, kind=kind)
        return Tile(self, h, name, dram=True)

    def record(self, eng, fn, reads, writes, is_dma=False):
        op = Op(eng, fn, is_dma)
        op.idx = len(self.ops)
        op.is_load = is_dma and any(not getattr(t, "dram", False) for t in writes)
        deps = []
        for t in reads:
            if t.lastw is not None:
                deps.append(t.lastw)
            if getattr(t, "psum", False):
                deps.extend(rd for rd in t.readers if rd.eng != eng)
        for t in writes:
            if t.lastw is not None:
                deps.append(t.lastw)
            deps.extend(t.readers)
        for t in reads:
            t.readers.append(op)
        for t in writes:
            t.lastw = op
            t.readers = []
        seen = set()
        for d in deps:
            if id(d) in seen or d is op:
                continue
            seen.add(id(d))
            op.deps.append(d)
        if is_dma:
            slot = self.dma_sems[self.dma_rr % self.ndma_sems]
            self.dma_rr += 1
            prev = slot[2]
            if prev is not None:
                op.deps.append(prev)
            slot[1] += 16
            slot[2] = op
            op.dsem = slot[0]
            op.dval = slot[1]
        self.ops.append(op)
        return op

    def begin(self):
        self.phase_es = ExitStack()
        self.ops = []
        self.rings = {}

    def end(self, final=False):
        nc = self.nc
        ops = self.ops
        last = {e: None for e in ENGS}
        for op in ops:
            if not op.is_dma:
                last[op.eng] = op
        pend_dma = [s[2] for s in self.dma_sems if s[2] is not None]
        for op in ops:
            for d in op.deps:
                if d.is_dma:
                    continue
                if d.eng != op.eng or (SAME_ENGINE_SYNC and d.eng != "tensor") or op.is_dma:
                    d.sig = True
        for e in ENGS:
            if last[e] is not None:
                last[e].sig = True
        for op in ops:
            if op.is_dma:
                continue
            if op.sig:
                self.cnt[op.eng] += 1
                op.sigval = self.cnt[op.eng]
        sp_ops = [op for op in ops if op.eng == "sync"]
        spidx = {id(op): i for i, op in enumerate(sp_ops)}
        lastF = {e: -1 for e in ENGS}
        for op in ops:
            f = -1
            for d in op.deps:
                if id(d) in spidx:
                    f = max(f, spidx[id(d)])
                elif getattr(d, "F", None) is not None:
                    f = max(f, d.F)
            if op.eng != "sync":
                f = max(f, lastF[op.eng])
                lastF[op.eng] = f
            op.F = f
        keys = {}
        prev_load_key = -1.0
        for i, op in enumerate(sp_ops):
            if op.is_load:
                k = max(op.F + 0.5, prev_load_key)
                k = min(k, float(i))
                prev_load_key = k
                keys[id(op)] = k
            else:
                keys[id(op)] = float(i)
        sp_sorted = sorted(range(len(sp_ops)), key=lambda i: (keys[id(sp_ops[i])], i))
        sp_new = [sp_ops[i] for i in sp_sorted]
        per = {e: [op for op in ops if op.eng == e] for e in ENGS}
        per["sync"] = sp_new
        for e in ENGS:
            kn = self.known[e]
            for op in per[e]:
                for d in op.deps:
                    if d.is_dma:
                        key, val, sem = ("d", id(d.dsem)), d.dval, d.dsem
                    else:
                        if d.sigval is None:
                            continue
                        if d.eng == op.eng and not op.is_dma and (d.eng == "tensor" or not SAME_ENGINE_SYNC):
                            continue
                        key, val, sem = ("c", d.eng), d.sigval, self.csem[d.eng]
                    if kn.get(key, 0) >= val:
                        continue
                    kn[key] = val
                    op.waits.append((sem, val))
        bar = {}
        for e in ENGS:
            w = []
            kn = self.known[e]
            for e2 in ENGS:
                if last[e2] is None:
                    continue
                v = last[e2].sigval
                if kn.get(("c", e2), 0) < v:
                    kn[("c", e2)] = v
                    w.append((self.csem[e2], v))
            for d in pend_dma:
                key = ("d", id(d.dsem))
                if kn.get(key, 0) < d.dval:
                    kn[key] = d.dval
                    w.append((d.dsem, d.dval))
            bar[e] = w
        for s in self.dma_sems:
            s[2] = None

        with nc.Block() as block:
            def emit(e):
                def body(eng):
                    for op in per[e]:
                        for (sem, val) in op.waits:
                            eng.wait_ge(sem, val)
                        ins = op.fn(eng)
                        if op.is_dma:
                            ins.then_inc(op.dsem, 16)
                        elif op.sig:
                            ins.then_inc(self.csem[e], 1)
                    for (sem, val) in bar[e]:
                        eng.wait_ge(sem, val)
                return body
            block.tensor(emit("tensor"))
            block.vector(emit("vector"))
            block.scalar(emit("scalar"))
            block.gpsimd(emit("gpsimd"))
            block.sync(emit("sync"))
        for t in self.tiles.values():
            t.lastw = None
            t.readers = []
            for st in (getattr(t, "subs", None) or []):
                st.lastw = None
                st.readers = []
        self.phase_es.close()
        self.phase_es = None
        self.ops = []

    def close(self):
        self.es.close()


D = 1024
T = 4096
TC = 256
NT = TC + T
NTILE = NT // 128
DEPTH = 2
EPS = 1e-6
IN_COLS = 3344
OFF_A_B, OFF_A_C, OFF_A_H, OFF_Q, OFF_K, OFF_V = 0, 256, 512, 768, 1280, 1792
OFF_AB, OFF_Z, OFF_CU, OFF_CV = 2304, 2320, 2832, 3088
FFN_H = 2816
SCN_W = 8 * 512 + 32
GROUPS = [(0, 256, 0, 0)] + [(256 + 512 * i, 512, 1, 512 * i) for i in range(8)]


class Model:
    def __init__(self, nc, debug=False):
        self.nc = nc
        self.P = P = Prog(nc)
        self.debug = debug
        k_in = "ExternalInput"
        k_sc = "ExternalOutput" if debug else "Internal"
        self.xs_in = P.dram("xs", [NT, D], F32, k_in)
        self.ccT = P.dram("ccT", [128, 8, 2], F32, k_in)
        self.w_mod = P.dram("w_mod", [DEPTH, D, 6 * D], F32, k_in)
        self.b_mod = P.dram("b_mod", [DEPTH, 6 * D], F32, k_in)
        self.g_pre_mix = P.dram("g_pre_mix", [DEPTH, D], F32, k_in)
        self.g_post_mix = P.dram("g_post_mix", [DEPTH, D], F32, k_in)
        self.g_pre_ffn = P.dram("g_pre_ffn", [DEPTH, D], F32, k_in)
        self.g_post_ffn = P.dram("g_post_ffn", [DEPTH, D], F32, k_in)
        self.w_in = P.dram("w_in", [DEPTH, D, IN_COLS], F32, k_in)
        self.conv_a = P.dram("conv_a", [DEPTH, 3, 256], F32, k_in)
        self.conv_qkv = P.dram("conv_qkv", [DEPTH, 3, 1536], F32, k_in)
        self.a_log = P.dram("a_log", [DEPTH, 8], F32, k_in)
        self.dt_bias = P.dram("dt_bias", [DEPTH, 8], F32, k_in)
        self.g_onorm = P.dram("g_onorm", [DEPTH, 128], F32, k_in)
        self.ln_c_g = P.dram("ln_c_g", [DEPTH, 256], F32, k_in)
        self.ln_c_b = P.dram("ln_c_b", [DEPTH, 256], F32, k_in)
        self.w_s = P.dram("w_s", [DEPTH, 4, 128, 128], F32, k_in)
        self.b_s = P.dram("b_s", [DEPTH, 4, 128], F32, k_in)
        self.w_o = P.dram("w_o", [DEPTH, D, D], F32, k_in)
        self.w_ffn_in = P.dram("w_ffn_in", [DEPTH, D, 2 * FFN_H], F32, k_in)
        self.w_ffn_out = P.dram("w_ffn_out", [DEPTH, FFN_H, D], F32, k_in)
        self.out = P.dram("out", [T, D], F32, "ExternalOutput")
        self.XS = P.dram("XS", [NT, D], F32, k_sc)
        self.MB = P.dram("MB", [DEPTH, 6, 2, D], F32, k_sc)
        self.HT = [P.dram("HTc", [8, 128, TC + 2], BF16, k_sc), P.dram("HTx", [8, 128, T + 2], BF16, k_sc)]
        self.PFM = P.dram("PFM", [512, NT], F32, k_sc)
        self.YT = P.dram("YT", [1024, NT], BF16, k_sc)
        self.ZS = P.dram("ZS", [NT, 512], F32, k_sc)
        self.QS = P.dram("QS", [NT, 528], F32, k_sc)
        self.SCN = P.dram("SCN", [NTILE, 128, SCN_W], F32, k_sc)
        self.SC2 = P.dram("SC2", [NTILE, 128, 6 * 512 + 32], F32, k_sc)
        self.XB = P.dram("XB", [NT, D], F32, k_sc)
        self.OF = P.dram("OF", [NT, 512], F32, k_sc)
        self.OB = P.dram("OB", [NT, 512], F32, k_sc)

    def consts(self):
        P = self.P
        c = self.c = {}
        for nm in ("ones", "U0", "U1", "SU0", "SU1", "NM0", "NM1", "ident", "NU0", "NU1"):
            c[nm] = P.sbc(nm, [128, 128])
        c["ident_bf"] = P.sbc("ident_bf", [128, 128], BF16)
        P.begin()
        ones = c["ones"]
        zer = P.sb("zer", [128, 128])
        P.pool.memset(ap=ones[:], constant=1.0)
        P.pool.memset(ap=zer[:], constant=0.0)

        def sel(name, src, step, cm, base, op, fill):
            t = c[name]
            P.pool.affine_select(out=t[:], in_=src[:], pattern=[[step, 128]], compare_op=op,
                                 fill=fill, base=base, channel_multiplier=cm)
            return t
        sel("U0", ones, 1, -1, 0, ALU.is_ge, 0.0)
        sel("U1", ones, -1, 1, 0, ALU.is_ge, 0.0)
        sel("SU0", ones, 1, -1, -1, ALU.is_ge, 0.0)
        sel("SU1", ones, -1, 1, -1, ALU.is_ge, 0.0)
        sel("NM0", zer, 1, -1, 0, ALU.is_ge, -30000.0)
        sel("NM1", zer, -1, 1, 0, ALU.is_ge, -30000.0)
        sel("ident", ones, 1, -1, 0, ALU.is_equal, 0.0)
        for r in (0, 1):
            P.pool.tensor_scalar(out=c["NU%d" % r][:], in0=c["U%d" % r][:], scalar1=-1.0, scalar2=None, op0=ALU.mult)
        P.pool.tensor_copy(out=c["ident_bf"][:], in_=c["ident"][:])
        zb = P.sb("zb", [128, 8, 1], BF16)
        P.pool.memset(ap=zb[:], constant=0.0)
        for s, n in ((0, TC), (1, T)):
            v = self.HT[s].h.ap().rearrange("kc p t -> p kc t")
            P.sp.dma_start(out=v[:, :, 0:1], in_=zb[:], _writes=[self.HT[s]], allow_slow_non_contiguous=True)
            P.sp.dma_start(out=v[:, :, n + 1:n + 2], in_=zb[:], _writes=[self.HT[s]], allow_slow_non_contiguous=True)
        P.end()

    def mods(self, l):
        P = self.P
        P.begin()
        cT = P.sb("cT", [128, 8, 2])
        sT = P.sb("sT", [128, 8, 2])
        P.sp.dma_start(out=cT[:], in_=self.ccT.ap())
        P.act.activation(out=sT[:], in_=cT[:], func=AF.Silu)
        bm = P.sb("bm", [2, 6 * D])
        P.sp.dma_start(out=bm[:], in_=self.b_mod.h.ap()[l:l + 1, :].broadcast_to([2, 6 * D]))
        mods = P.sb("mods", [2, 6 * D])
        wbuf = [P.sb("wm%d" % i, [128, 8, 512]) for i in range(2)]
        pm = [P.ps("pm%d" % i, [2, 512]) for i in range(2)]
        wv = self.w_mod.h.ap()[l].rearrange("(kc p) n -> p kc n", p=128)
        for nb in range(12):
            wb = wbuf[nb % 2]
            P.sp.dma_start(out=wb[:], in_=wv[:, :, nb * 512:(nb + 1) * 512])
            pp = pm[nb % 2]
            for kc in range(8):
                P.pe.matmul(out=pp[:], lhsT=sT[:, kc, :], rhs=wb[:, kc, :], start=(kc == 0), stop=(kc == 7))
            P.dve.tensor_tensor(out=mods[:, nb * 512:(nb + 1) * 512], in0=pp[:], in1=bm[:, nb * 512:(nb + 1) * 512], op=ALU.add)
        gv = P.sb("gv", [2, 4, D])
        for i, g in enumerate((self.g_pre_mix, self.g_post_mix, self.g_pre_ffn, self.g_post_ffn)):
            P.sp.dma_start(out=gv[:, i, :], in_=g.h.ap()[l:l + 1, :].broadcast_to([2, D]))
        mb = P.sb("mb", [2, 6, D])
        m = lambda i: mods[:, i * D:(i + 1) * D]
        P.dve.tensor_copy(out=mb[:, 0, :], in_=m(0))
        P.dve.scalar_tensor_tensor(out=mb[:, 1, :], in0=m(1), scalar=1.0, in1=gv[:, 0, :], op0=ALU.add, op1=ALU.mult)
        P.dve.tensor_tensor(out=mb[:, 2, :], in0=m(2), in1=gv[:, 1, :], op=ALU.mult)
        P.dve.tensor_copy(out=mb[:, 3, :], in_=m(3))
        P.dve.scalar_tensor_tensor(out=mb[:, 4, :], in0=m(4), scalar=1.0, in1=gv[:, 2, :], op0=ALU.add, op1=ALU.mult)
        P.dve.tensor_tensor(out=mb[:, 5, :], in0=m(5), in1=gv[:, 3, :], op=ALU.mult)
        P.sp.dma_start(out=self.MB.h.ap()[l].rearrange("k s d -> s k d"), in_=mb[:], _writes=[self.MB])
        P.end()

    def load_mod(self, l, k, s, name):
        P = self.P
        t = P.sb(name, [128, D])
        P.sp.dma_start(out=t[:], in_=self.MB.h.ap()[l, k, s:s + 1, :].broadcast_to([128, D]), _reads=[self.MB])
        return t

    def rstd_chain(self, ss, n, scale, name, post=None):
        P = self.P
        t1 = P.sm(name + "a", n)
        t2 = P.sm(name + "b", n)
        t3 = P.sm(name + "c", n)
        P.dve.tensor_scalar(out=t1[:], in0=ss, scalar1=scale, scalar2=EPS, op0=ALU.mult, op1=ALU.add)
        P.act.activation(out=t2[:], in_=t1[:], func=AF.Sqrt)
        P.dve.reciprocal(out=t3[:], in_=t2[:])
        return t3

    def prenorm(self, l, ks, kg, src):
        P = self.P
        c = self.c
        P.begin()
        Sx = [self.load_mod(l, ks, 1, "Sc"), self.load_mod(l, ks, 0, "Sx")]
        Gx = [self.load_mod(l, kg, 1, "Gc"), self.load_mod(l, kg, 0, "Gx")]
        xt = [P.sb("xt%d" % i, [128, D]) for i in range(2)]
        junk = P.sb("junk", [128, D])
        h1 = [P.sb("h1%d" % i, [128, D]) for i in range(2)]
        hb = [P.sb("hb%d" % i, [128, D], BF16) for i in range(2)]
        pT = [P.ps("pT%d" % i, [128, D], BF16) for i in range(2)]
        hT = [P.sb("hT%d" % i, [128, 8, 512], BF16) for i in range(2)]
        tiles = []
        for gi, (row0, ntok, s, t0) in enumerate(GROUPS):
            for j in range(ntok // 128):
                tiles.append((gi, row0, ntok, s, t0, j))

        def front(i):
            gi, row0, ntok, s, t0, j = tiles[i]
            b = i % 2
            r0 = row0 + j * 128
            P.sp.dma_start(out=xt[b][:], in_=src.h.ap()[r0:r0 + 128, :], _reads=[src])
            ss = P.sm("ss", 1)
            P.act.activation(out=junk[:], in_=xt[b][:], func=AF.Square, accum_out=ss[:])
            rstd = self.rstd_chain(ss[:], 1, 1.0 / D, "rs")
            P.dve.scalar_tensor_tensor(out=h1[b][:], in0=xt[b][:], scalar=rstd[:, 0:1], in1=Gx[s][:], op0=ALU.mult, op1=ALU.mult)
            P.pool.tensor_tensor(out=hb[b][:], in0=h1[b][:], in1=Sx[s][:], op=ALU.add)

        def back(i):
            gi, row0, ntok, s, t0, j = tiles[i]
            b = i % 2
            hTg = hT[gi % 2]
            for kc in range(8):
                P.pe.transpose(out=pT[b][:, kc * 128:(kc + 1) * 128], in_=hb[b][:, kc * 128:(kc + 1) * 128], identity=c["ident_bf"][:])
            P.act.copy(out=hTg[:, :, j * 128:(j + 1) * 128], in_=pT[b][:].rearrange("p (kc t) -> p kc t", kc=8))
            if j == ntok // 128 - 1:
                dst = self.HT[s].h.ap().rearrange("kc p t -> p kc t")[:, :, 1 + t0:1 + t0 + ntok]
                P.sp.dma_start(out=dst, in_=hTg[:, :, 0:ntok], _writes=[self.HT[s]])
        front(0)
        for i in range(len(tiles)):
            if i + 1 < len(tiles):
                front(i + 1)
            back(i)
        P.end()

    def proj(self, l):
        P = self.P
        c = self.c
        P.keep_begin()
        Wfm = P.sbk("Wfm", [128, 8, 1024], BF16)
        Wrest = P.sbk("Wrest", [128, 8, 784], BF16)
        Wqkv = [P.sbk("Wq%d" % j, [128, 8, 1536], BF16) for j in range(3)]
        P.begin()
        cw = P.sb("cw", [128, 3, 1536])
        P.sp.dma_start(out=cw[:], in_=self.conv_qkv.h.ap()[l:l + 1].broadcast_to([128, 3, 1536]))
        stg = [P.sb("stg%d" % i, [128, IN_COLS]) for i in range(2)]
        wv = self.w_in.h.ap()[l]
        for kc in range(8):
            st = stg[kc % 2]
            P.sp.dma_start(out=st[:], in_=wv[kc * 128:(kc + 1) * 128, :])
            P.act.copy(out=Wfm[:, kc, 0:768], in_=st[:, 0:768])
            P.act.copy(out=Wfm[:, kc, 768:1024], in_=st[:, OFF_CU:OFF_CV])
            P.act.copy(out=Wrest[:, kc, 0:512], in_=st[:, OFF_Z:OFF_CU])
            P.act.copy(out=Wrest[:, kc, 512:528], in_=st[:, OFF_AB:OFF_Z])
            P.act.copy(out=Wrest[:, kc, 528:784], in_=st[:, OFF_CV:IN_COLS])
            for j in range(3):
                eng = (P.dve, P.pool, P.dve)[j]
                eng.tensor_tensor(out=Wqkv[j][:, kc, :], in0=st[:, OFF_Q:OFF_AB], in1=cw[:, j, :], op=ALU.mult)
        P.end()

        P.begin()
        pfm = [P.ps("pfm%d" % i, [128, 512]) for i in range(2)]
        prot = [P.ps("prot%d" % i, [128, 512]) for i in range(3)]
        pr = P.ps("pr", [128, 512])
        pc = [P.ps("pc%d" % i, [128, 128]) for i in range(2)]
        wsl = P.sb("wsl", [128, 4, 128])
        wsT = P.sb("wsT", [128, 4, 128])
        P.sp.dma_start(out=wsl[:], in_=self.w_s.h.ap()[l].rearrange("g p q -> p g q"))
        for g in range(4):
            P.pe.transpose(out=pr[:, g * 128:(g + 1) * 128], in_=wsl[:, g, :], identity=c["ident"][:])
        P.dve.tensor_copy(out=wsT[:], in_=pr[:].rearrange("p (g q) -> p g q", g=4))
        Bs = P.sb("Bs", [128, 2, 128])
        for g in range(4):
            P.sp.dma_start(out=Bs[(g % 2) * 64:(g % 2) * 64 + 64, g // 2, :],
                           in_=self.b_s.h.ap()[l, g:g + 1, :].broadcast_to([64, 128]))
        lng = P.sb("lng", [128, 256])
        lnb = P.sb("lnb", [128, 256])
        P.sp.dma_start(out=lng[:], in_=self.ln_c_g.h.ap()[l:l + 1, :].broadcast_to([128, 256]))
        P.sp.dma_start(out=lnb[:], in_=self.ln_c_b.h.ap()[l:l + 1, :].broadcast_to([128, 256]))
        hTb = [P.sb("hTg%d" % i, [128, 8, 514], BF16) for i in range(2)]
        evb = [P.sb("evb%d" % i, [128, 512]) for i in range(2)]
        Csb = [P.sb("Csb%d" % i, [128, 512]) for i in range(2)]
        uT = P.sb("uT", [128, 2, 512])
        ycT = P.sb("ycT", [128, 2, 512], BF16)
        qsb = [P.sb("qsb%d" % i, [128, 512]) for i in range(2)]
        ksb = [P.sb("ksb%d" % i, [128, 512]) for i in range(2)]
        sq = [P.sb("sq%d" % i, [128, 512]) for i in range(2)]
        kvst = [P.sb("kvst%d" % i, [128, 2, 512]) for i in range(2)]
        qst = [P.sb("qst%d" % i, [128, 528]) for i in range(2)]
        zsb = [P.sb("zsb%d" % i, [128, 512]) for i in range(2)]
        cv = P.sb("cv", [128, 256])
        vn1 = P.sb("vn1", [128, 256])
        vn2 = P.sb("vn2", [128, 256])
        vn = P.sb("vn", [128, 256])
        tmpc = P.sb("tmpc", [128, 128])
        PFM = self.PFM.h.ap()
        it = 0
        rot = 0
        for gi, (row0, ntok, s, t0) in enumerate(GROUPS):
            hTg = hTb[gi % 2]
            src = self.HT[s].h.ap().rearrange("kc p t -> p kc t")[:, :, t0:t0 + ntok + 2]
            P.sp.dma_start(out=hTg[:, :, 0:ntok + 2], in_=src, _reads=[self.HT[s]])
            for cb in range(8):
                pf = pfm[cb % 2]
                for kc in range(8):
                    P.pe.matmul(out=pf[:, 0:ntok], lhsT=Wfm[:, kc, cb * 128:(cb + 1) * 128], rhs=hTg[:, kc, 1:1 + ntok],
                                start=(kc == 0), stop=(kc == 7))
                if cb < 2:
                    ev = evb[cb % 2]
                    P.act.copy(out=ev[:, 0:ntok], in_=pf[:, 0:ntok])
                    P.sp.dma_start(out=PFM[cb * 128:(cb + 1) * 128, row0:row0 + ntok], in_=ev[:, 0:ntok], _writes=[self.PFM])
                elif cb < 4:
                    P.act.copy(out=Csb[cb - 2][:, 0:ntok], in_=pf[:, 0:ntok])
                elif cb < 6:
                    ev = evb[cb % 2]
                    P.dve.tensor_tensor(out=ev[:, 0:ntok], in0=pf[:, 0:ntok], in1=Csb[cb - 4][:, 0:ntok], op=ALU.mult)
                    P.sp.dma_start(out=PFM[256 + (cb - 4) * 128:256 + (cb - 3) * 128, row0:row0 + ntok], in_=ev[:, 0:ntok], _writes=[self.PFM])
                else:
                    P.act.activation(out=uT[:, cb - 6, 0:ntok], in_=pf[:, 0:ntok], func=AF.Gelu)
            for j in range(ntok // 128):
                b = it % 2
                it += 1
                n = (row0 + j * 128) // 128
                off = j * 128
                r0 = row0 + off
                pq = []
                for which in range(4):
                    pp = prot[rot % 3]
                    rot += 1
                    pq.append(pp)
                    if which < 3:
                        first = True
                        for tap in range(3):
                            for kc in range(8):
                                P.pe.matmul(out=pp[:], lhsT=hTg[:, kc, off + tap:off + tap + 128],
                                            rhs=Wqkv[tap][:, kc, which * 512:(which + 1) * 512],
                                            start=first, stop=(tap == 2 and kc == 7))
                                first = False
                    else:
                        for kc in range(8):
                            P.pe.matmul(out=pp[:], lhsT=hTg[:, kc, off + 1:off + 129], rhs=Wrest[:, kc, 0:512],
                                        start=(kc == 0), stop=(kc == 7))
                    if which == 0:
                        P.act.activation(out=qsb[b][:], in_=pp[:], func=AF.Silu)
                    elif which == 1:
                        P.act.activation(out=ksb[b][:], in_=pp[:], func=AF.Silu)
                    elif which == 2:
                        P.act.activation(out=kvst[b][:, 1, :], in_=pp[:], func=AF.Silu)
                    else:
                        P.act.activation(out=zsb[b][:], in_=pp[:], func=AF.Silu)
                        P.sp.dma_start(out=self.ZS.h.ap()[r0:r0 + 128, :], in_=zsb[b][:], _writes=[self.ZS])
                for kc in range(8):
                    P.pe.matmul(out=pr[:, 0:272], lhsT=hTg[:, kc, off + 1:off + 129], rhs=Wrest[:, kc, 512:784],
                                start=(kc == 0), stop=(kc == 7))
                P.dve.tensor_copy(out=qst[b][:, 512:528], in_=pr[:, 0:16])
                P.act.activation(out=cv[:], in_=pr[:, 16:272], func=AF.Gelu)
                ss8 = P.sm("ss8", 8)
                P.pool.tensor_tensor(out=sq[0][:], in0=qsb[b][:], in1=qsb[b][:], op=ALU.mult)
                P.dve.tensor_reduce(out=ss8[:, 0:4], in_=sq[0][:].rearrange("p (h d) -> p h d", h=4), axis=AX.X, op=ALU.add)
                P.pool.tensor_tensor(out=sq[1][:], in0=ksb[b][:], in1=ksb[b][:], op=ALU.mult)
                P.dve.tensor_reduce(out=ss8[:, 4:8], in_=sq[1][:].rearrange("p (h d) -> p h d", h=4), axis=AX.X, op=ALU.add)
                rs = self.rstd_chain(ss8[:], 8, 1.0, "rsqk")
                rq = P.sm("rq", 4)
                P.dve.tensor_scalar(out=rq[:], in0=rs[:, 0:4], scalar1=128.0 ** -0.5, scalar2=None, op0=ALU.mult)
                P.dve.tensor_tensor(out=qst[b][:, 0:512].rearrange("p (h d) -> p h d", h=4),
                                    in0=qsb[b][:].rearrange("p (h d) -> p h d", h=4),
                                    in1=rq[:].unsqueeze(2).broadcast_to([128, 4, 128]), op=ALU.mult)
                P.dve.tensor_tensor(out=kvst[b][:, 0, :].rearrange("p (h d) -> p h d", h=4),
                                    in0=ksb[b][:].rearrange("p (h d) -> p h d", h=4),
                                    in1=rs[:, 4:8].unsqueeze(2).broadcast_to([128, 4, 128]), op=ALU.mult)
                dst = self.SCN.h.ap()[n][:, 512:2560].rearrange("p (a b) -> p a b", b=1024)[:, :, 0:512]
                P.sp.dma_start(out=dst, in_=kvst[b][:], _writes=[self.SCN])
                P.sp.dma_start(out=self.QS.h.ap()[r0:r0 + 128, :], in_=qst[b][:], _writes=[self.QS])
                st6 = P.sm("st6", 6)
                mv = P.sm("mv", 2)
                P.dve.bn_stats(out=st6[:], in_=cv[:])
                P.dve.bn_aggr(out=mv[:], in_=st6[:])
                rl = self.rstd_chain(mv[:, 1:2], 1, 1.0, "rln")
                P.dve.tensor_scalar(out=vn1[:], in0=cv[:], scalar1=mv[:, 0:1], scalar2=rl[:, 0:1], op0=ALU.subtract, op1=ALU.mult)
                P.pool.tensor_tensor(out=vn2[:], in0=vn1[:], in1=lng[:], op=ALU.mult)
                P.pool.tensor_tensor(out=vn[:], in0=vn2[:], in1=lnb[:], op=ALU.add)
                for g in range(4):
                    P.pe.matmul(out=pc[g // 2][(g % 2) * 64:(g % 2) * 64 + 64, :], lhsT=vn[:, g * 64:(g + 1) * 64],
                                rhs=wsT[:, g, :], start=True, stop=True)
                for ct in range(2):
                    P.dve.tensor_tensor(out=tmpc[:], in0=pc[ct][:], in1=Bs[:, ct, :], op=ALU.add)
                    P.pool.tensor_tensor(out=ycT[:, ct, off:off + 128], in0=tmpc[:], in1=uT[:, ct, off:off + 128], op=ALU.mult)
            dst = self.YT.h.ap()[768:1024, row0:row0 + ntok].rearrange("(ct p) t -> p ct t", p=128)
            P.sp.dma_start(out=dst, in_=ycT[:, :, 0:ntok], _writes=[self.YT])
        P.end()
        P.keep_end()

    def gdn_prep(self, l):
        P = self.P
        c = self.c
        P.begin()
        al = P.sb("al", [128, 8])
        dtb = P.sb("dtb", [128, 8])
        ea = P.sb("ea", [128, 8])
        nea = P.sb("nea", [128, 8])
        P.sp.dma_start(out=al[:], in_=self.a_log.h.ap()[l:l + 1, :].broadcast_to([128, 8]))
        P.sp.dma_start(out=dtb[:], in_=self.dt_bias.h.ap()[l:l + 1, :].broadcast_to([128, 8]))
        P.act.activation(out=ea[:], in_=al[:], func=AF.Exp)
        P.dve.tensor_scalar(out=nea[:], in0=ea[:], scalar1=-1.0, scalar2=None, op0=ALU.mult)
        ph = P.psb("pha", 4, 128) + P.psb("phb", 4, 128)
        pA = [P.psb("pA%d" % r, 4, 128) for r in range(2)]
        pB = [P.psb("pB%d" % r, 4, 128) for r in range(2)]
        pD = [P.psb("pD%d" % r, 4, 128) for r in range(2)]
        pg = SubTile(ph[4].parent, 0, 16, "pgv")
        GU = [P.sb("GU%d" % r, [128, 512]) for r in range(2)]
        kin = [P.sb("kin%d" % i, [128, 512]) for i in range(2)]
        qs = [P.sb("qs%d" % i, [128, 528]) for i in range(2)]
        okT = [P.sb("okT%d" % i, [128, 512]) for i in range(2)]
        oqT = [P.sb("oqT%d" % i, [128, 512]) for i in range(2)]
        oT2T = [[P.sb("oT2T%d_%d" % (i, r), [128, 512]) for r in range(2)] for i in range(2)]
        oQKT = [[P.sb("oQKT%d_%d" % (i, r), [128, 512]) for r in range(2)] for i in range(2)]
        scalb = [P.sb("scal%d" % i, [128, 32]) for i in range(2)]
        sm = {nm: P.sb(nm, [128, 8]) for nm in ("e1", "d1", "x2", "e2", "sp", "g", "dl", "et")}
        gsb = P.sb("gsb", [128, 16])
        U8 = [(r, h) for r in range(2) for h in range(4)]
        mk = lambda nm: {u: P.sb("%s%d%d" % (nm, u[0], u[1]), [128, 128]) for u in U8}
        Dt, E2, a1 = mk("Dt"), mk("E2"), mk("a1")
        Pp = [[mk("Pp%d%d" % (q, i)) for i in range(2)] for q in range(2)]
        PT = [[mk("PT%d%d" % (q, i)) for i in range(2)] for q in range(2)]
        R = [[mk("R%d%d" % (q, i)) for i in range(2)] for q in range(2)]
        SCN = self.SCN.h.ap()
        SC2 = self.SC2.h.ap()
        hs = lambda h: slice(h * 128, (h + 1) * 128)

        def stage_a(n):
            b = n % 2
            scal = scalb[b]
            g = sm["g"]
            Pp0, PT0, R0 = Pp[b][0], PT[b][0], R[b][0]

            def a_load():
                P.sp.dma_start(out=kin[b][:], in_=SCN[n][:, 512:1024], _reads=[self.SCN])
                P.sp.dma_start(out=qs[b][:], in_=self.QS.h.ap()[n * 128:(n + 1) * 128, :], _reads=[self.QS])

            def a_gates1():
                P.act.activation(out=sm["e1"][:], in_=qs[b][:, 512:520], func=AF.Exp, scale=-1.0)
                P.dve.tensor_scalar(out=sm["d1"][:], in0=sm["e1"][:], scalar1=1.0, scalar2=None, op0=ALU.add)
                P.dve.reciprocal(out=scal[:, 0:8], in_=sm["d1"][:])
                P.dve.tensor_tensor(out=sm["x2"][:], in0=qs[b][:, 520:528], in1=dtb[:], op=ALU.add)
                P.act.activation(out=sm["e2"][:], in_=sm["x2"][:], func=AF.Exp)
                P.act.activation(out=sm["sp"][:], in_=sm["e2"][:], func=AF.Ln, bias=1.0)
                P.dve.tensor_tensor(out=g[:], in0=sm["sp"][:], in1=nea[:], op=ALU.mult)

            def a_gates2():
                P.pe.matmul(out=pg[:, 0:4], lhsT=c["U0"][:], rhs=g[:, 0:4], start=True, stop=True)
                P.pe.matmul(out=pg[:, 4:8], lhsT=c["U1"][:], rhs=g[:, 4:8], start=True, stop=True)
                P.pe.matmul(out=pg[:, 8:16], lhsT=c["ones"][:], rhs=g[:, 0:8], start=True, stop=True)

            def a_gates3():
                P.dve.tensor_copy(out=gsb[:], in_=pg[:])
                P.act.activation(out=scal[:, 8:16], in_=gsb[:, 0:8], func=AF.Exp)
                P.dve.tensor_tensor(out=sm["dl"][:], in0=gsb[:, 8:16], in1=gsb[:, 0:8], op=ALU.subtract)
                P.act.activation(out=sm["et"][:], in_=sm["dl"][:], func=AF.Exp)
                P.dve.tensor_copy(out=scal[:, 16:24], in_=sm["et"][:])
                P.act.activation(out=scal[:, 24:32], in_=gsb[:, 8:16], func=AF.Exp)
                for (r, h) in U8:
                    idx = r * 4 + h
                    P.act.activation(out=GU[r][:, hs(h)], in_=c["U%d" % r][:], func=AF.Copy, scale=g[:, idx:idx + 1])

            def a_tr():
                for h in range(4):
                    P.pe.transpose(out=ph[h][:], in_=kin[b][:, hs(h)], identity=c["ident"][:])
                for h in range(4):
                    P.pe.transpose(out=ph[4 + h][:], in_=qs[b][:, hs(h)], identity=c["ident"][:])

            def a_trc():
                for h in range(4):
                    P.act.copy(out=okT[b][:, hs(h)], in_=ph[h][:])
                for h in range(4):
                    P.dve.tensor_copy(out=oqT[b][:, hs(h)], in_=ph[4 + h][:])

            def a_kk():
                for h in range(4):
                    P.pe.matmul(out=ph[h][:], lhsT=okT[b][:, hs(h)], rhs=okT[b][:, hs(h)], start=True, stop=True)
                for h in range(4):
                    P.pe.matmul(out=ph[4 + h][:], lhsT=okT[b][:, hs(h)], rhs=oqT[b][:, hs(h)], start=True, stop=True)
                for r in range(2):
                    P.pe.matmul(out=pD[r][0].parent[:], lhsT=c["ones"][:], rhs=GU[r][:], start=True, stop=True)

            def a_dt():
                for (r, h) in U8:
                    idx = r * 4 + h
                    P.dve.scalar_tensor_tensor(out=Dt[(r, h)][:], in0=pD[r][h][:], scalar=gsb[:, idx:idx + 1], in1=c["NM%d" % r][:],
                                               op0=ALU.subtract, op1=ALU.add)
                for u in U8:
                    P.act.activation(out=E2[u][:], in_=Dt[u][:], func=AF.Exp)

            def a_n():
                for (r, h) in U8:
                    u = (r, h)
                    idx = r * 4 + h
                    P.dve.tensor_tensor(out=oQKT[b][r][:, hs(h)], in0=ph[4 + h][:], in1=E2[u][:], op=ALU.mult)
                    P.dve.scalar_tensor_tensor(out=a1[u][:], in0=ph[h][:], scalar=scal[:, idx:idx + 1], in1=E2[u][:],
                                               op0=ALU.mult, op1=ALU.mult)
                    P.pool.tensor_tensor(out=Pp0[u][:], in0=a1[u][:], in1=c["SU%d" % r][:], op=ALU.mult)
                for u in U8:
                    P.pool.tensor_tensor(out=R0[u][:], in0=c["ident"][:], in1=Pp0[u][:], op=ALU.subtract)

            def a_nt():
                for (r, h) in U8:
                    P.pe.transpose(out=pD[r][h][:], in_=Pp0[(r, h)][:], identity=c["ident"][:])

            def a_ntc():
                for (r, h) in U8:
                    (P.act.copy if r == 0 else P.dve.tensor_copy)(out=PT0[(r, h)][:], in_=pD[r][h][:])
            return [a_load, a_gates1, a_gates2, a_gates3, a_tr, a_trc, a_kk, a_dt, a_n, a_nt, a_ntc]

        def level_stages(n, r, lvl):
            b = n % 2
            cur, nxt = lvl % 2, 1 - (lvl % 2)
            Pc, Pn, Tc, Tn, Rc, Rn = Pp[b][cur], Pp[b][nxt], PT[b][cur], PT[b][nxt], R[b][cur], R[b][nxt]

            def s1():
                for h in range(4):
                    u = (r, h)
                    if lvl < 5:
                        P.pe.matmul(out=pA[r][h][:], lhsT=Tc[u][:], rhs=Pc[u][:], start=True, stop=True)
                    P.pe.matmul(out=pB[r][h][:], lhsT=Pc[u][:], rhs=Tc[u][:], start=True, stop=True)

            def s2():
                if lvl < 5:
                    for h in range(4):
                        (P.act.copy if r == 0 else P.dve.tensor_copy)(out=Pn[(r, h)][:], in_=pA[r][h][:])
                for h in range(4):
                    (P.dve.tensor_copy if r == 0 else P.act.copy)(out=Tn[(r, h)][:], in_=pB[r][h][:])

            def s3():
                for h in range(4):
                    u = (r, h)
                    P.pe.matmul(out=pA[r][h][:], lhsT=Tn[u][:], rhs=Rc[u][:], start=True, stop=True)

            def s4():
                for h in range(4):
                    u = (r, h)
                    dst = Rn[u][:] if lvl < 5 else oT2T[b][r][:, hs(h)]
                    P.dve.tensor_tensor(out=dst, in0=pA[r][h][:], in1=Rc[u][:], op=ALU.add)
            return [s1, s2, s3, s4]

        def stores(n):
            b = n % 2
            w = [self.SC2]
            P.sp.dma_start(out=SC2[n][:, 0:512], in_=okT[b][:], _writes=w)
            P.sp.dma_start(out=SC2[n][:, 512:1024], in_=oqT[b][:], _writes=w)
            for r in range(2):
                P.sp.dma_start(out=SC2[n][:, 1024 + r * 512:1536 + r * 512], in_=oT2T[b][r][:], _writes=w)
                P.sp.dma_start(out=SC2[n][:, 2048 + r * 512:2560 + r * 512], in_=oQKT[b][r][:], _writes=w)
            P.sp.dma_start(out=SC2[n][:, 3072:3104], in_=scalb[b][:], _writes=w)

        for f in stage_a(0):
            f()
        A_AT = {0: 0, 1: 1, 2: 4, 4: 5, 8: 2, 10: 3, 14: 6, 16: 7, 18: 8, 22: 9, 23: 10}
        for n in range(NTILE):
            stA = [st for lvl in range(6) for st in level_stages(n, 0, lvl)]
            stB = [st for lvl in range(6) for st in level_stages(n, 1, lvl)]
            nxtA = stage_a(n + 1) if n + 1 < NTILE else None
            for i in range(len(stA) + 1):
                if i < len(stA):
                    stA[i]()
                if i >= 1:
                    stB[i - 1]()
                if nxtA is not None and i in A_AT:
                    nxtA[A_AT[i]]()
            stores(n)
        P.end()

    def gdn_scan(self, l):
        P = self.P
        P.begin()
        SCN = self.SCN.h.ap()
        SC2 = self.SC2.h.ap()
        Forder = list(range(NTILE))
        Border = [1, 0] + list(range(NTILE - 1, 1, -1))
        names = ("kT", "k", "qT", "v", "T2T", "QKT")
        col0 = {"kT": 0, "k": 512, "qT": 1024, "v": 1536}
        inb = [[{nm: P.sb("i%s%d%d" % (nm, r, i), [128, 512]) for nm in names} for i in range(2)] for r in range(2)]
        scb = [[P.sb("isc%d%d" % (r, i), [128, 32]) for i in range(2)] for r in range(2)]
        ngc = [P.sb("ngc%d" % r, [128, 8]) for r in range(2)]
        S = [[[P.sb("S%d%d%d" % (r, h, i), [128, 128]) for i in range(2)] for h in range(4)] for r in range(2)]
        pa = [P.psb("pa%d" % r, 4, 128) for r in range(2)]
        po = [P.psb("po%d" % r, 4, 128) for r in range(2)]
        rr = [[P.sb("rr%d%d" % (r, h), [128, 128]) for h in range(4)] for r in range(2)]
        vn = [[P.sb("vn%d%d" % (r, h), [128, 128]) for h in range(4)] for r in range(2)]
        vt = [[P.sb("vt%d%d" % (r, h), [128, 128]) for h in range(4)] for r in range(2)]
        t1 = [[P.sb("t1%d%d" % (r, h), [128, 128]) for h in range(4)] for r in range(2)]
        oo = [[P.sb("oo%d%d" % (r, i), [128, 512]) for i in range(2)] for r in range(2)]
        for r in range(2):
            for h in range(4):
                P.pool.memset(ap=S[r][h][0][:], constant=0.0)
        units = [(r, h) for r in range(2) for h in range(4)]
        cur = 0
        for i in range(NTILE):
            b = i % 2
            nxt = 1 - cur
            tl = (Forder[i], Border[i])
            for r in range(2):
                n = tl[r]
                d = inb[r][b]
                for nm in ("k", "v"):
                    P.sp.dma_start(out=d[nm][:], in_=SCN[n][:, col0[nm]:col0[nm] + 512], _reads=[self.SCN])
                P.sp.dma_start(out=d["kT"][:], in_=SC2[n][:, 0:512], _reads=[self.SC2])
                P.sp.dma_start(out=d["qT"][:], in_=SC2[n][:, 512:1024], _reads=[self.SC2])
                P.sp.dma_start(out=d["T2T"][:], in_=SC2[n][:, 1024 + r * 512:1536 + r * 512], _reads=[self.SC2])
                P.sp.dma_start(out=d["QKT"][:], in_=SC2[n][:, 2048 + r * 512:2560 + r * 512], _reads=[self.SC2])
                P.sp.dma_start(out=scb[r][b][:], in_=SC2[n][:, 3072:3104], _reads=[self.SC2])
                P.pool.tensor_scalar(out=ngc[r][:], in0=scb[r][b][:, 8:16], scalar1=-1.0, scalar2=None, op0=ALU.mult)
            hs = lambda h: slice(h * 128, (h + 1) * 128)

            def scan_stages(r):
                d = inb[r][b]
                sc = scb[r][b]
                def t1_():
                    for h in range(4):
                        P.pe.matmul(out=pa[r][h][:], lhsT=d["kT"][:, hs(h)], rhs=S[r][h][cur][:], start=True, stop=True)
                    for h in range(4):
                        P.pe.matmul(out=po[r][h][:], lhsT=d["qT"][:, hs(h)], rhs=S[r][h][cur][:], start=True, stop=True)
                def t2_():
                    for h in range(4):
                        idx = r * 4 + h
                        P.dve.scalar_tensor_tensor(out=rr[r][h][:], in0=pa[r][h][:], scalar=ngc[r][:, idx:idx + 1], in1=d["v"][:, hs(h)],
                                                   op0=ALU.mult, op1=ALU.add)
                    for h in range(4):
                        idx = r * 4 + h
                        P.act.activation(out=t1[r][h][:], in_=po[r][h][:], func=AF.Copy, scale=sc[:, 8 + idx:9 + idx])
                def t3_():
                    for h in range(4):
                        P.pe.matmul(out=pa[r][h][:], lhsT=d["T2T"][:, hs(h)], rhs=rr[r][h][:], start=True, stop=True)
                def t4_():
                    for h in range(4):
                        idx = r * 4 + h
                        P.act.activation(out=vn[r][h][:], in_=pa[r][h][:], func=AF.Copy, scale=sc[:, idx:idx + 1])
                    for h in range(4):
                        idx = r * 4 + h
                        P.pool.tensor_scalar(out=vt[r][h][:], in0=vn[r][h][:], scalar1=sc[:, 16 + idx:17 + idx], scalar2=None, op0=ALU.mult)
                def t5_():
                    for h in range(4):
                        P.pe.matmul(out=pa[r][h][:], lhsT=d["k"][:, hs(h)], rhs=vt[r][h][:], start=True, stop=True)
                    for h in range(4):
                        P.pe.matmul(out=po[r][h][:], lhsT=d["QKT"][:, hs(h)], rhs=vn[r][h][:], start=True, stop=True)
                def t6_():
                    for h in range(4):
                        idx = r * 4 + h
                        P.dve.scalar_tensor_tensor(out=S[r][h][nxt][:], in0=S[r][h][cur][:], scalar=sc[:, 24 + idx:25 + idx],
                                                   in1=pa[r][h][:], op0=ALU.mult, op1=ALU.add)
                    for h in range(4):
                        P.dve.tensor_tensor(out=oo[r][b][:, hs(h)], in0=po[r][h][:], in1=t1[r][h][:], op=ALU.add)
                return [t1_, t2_, t3_, t4_, t5_, t6_]
            sA, sB = scan_stages(0), scan_stages(1)
            for k_ in range(len(sA) + 1):
                if k_ < len(sA):
                    sA[k_]()
                if k_ >= 1:
                    sB[k_ - 1]()
            for r in range(2):
                n = tl[r]
                dst = (self.OF, self.OB)[r]
                P.sp.dma_start(out=dst.h.ap()[n * 128:(n + 1) * 128, :], in_=oo[r][b][:], _writes=[dst])
            cur = nxt
        P.end()

    def mixer_a(self, l):
        P = self.P
        P.begin()
        PFM = self.PFM.h.ap()
        YT = self.YT.h.ap()
        cwa = P.sb("cwa", [128, 2, 3])
        for ct in range(2):
            P.sp.dma_start(out=cwa[:, ct, :], in_=self.conv_a.h.ap()[l][:, ct * 128:(ct + 1) * 128].rearrange("j c -> c j"),
                           allow_slow_non_contiguous=True)
        ca = [P.sb("ca%d" % i, [128, T]) for i in range(2)]
        Bt = [P.sb("Bt%d" % i, [128, T]) for i in range(2)]
        acc = [P.sb("acc%d" % i, [128, T]) for i in range(2)]
        ya = [P.sb("ya%d" % i, [128, T], BF16) for i in range(2)]
        it = 0
        for s, (c0, n) in enumerate(((0, TC), (TC, T))):
            if s == 0 and l == DEPTH - 1:
                continue
            for ct in range(2):
                b = it % 2
                it += 1
                P.sp.dma_start(out=ca[b][:, 0:n], in_=PFM[256 + ct * 128:256 + (ct + 1) * 128, c0:c0 + n], _reads=[self.PFM])
                P.sp.dma_start(out=Bt[b][:, 0:n], in_=PFM[ct * 128:(ct + 1) * 128, c0:c0 + n], _reads=[self.PFM])
                w = lambda j: cwa[:, ct, j:j + 1]
                P.pool.tensor_scalar(out=acc[b][:, 0:n], in0=ca[b][:, 0:n], scalar1=w(1), scalar2=None, op0=ALU.mult)
                if s == 0:
                    sh = [(acc[b][:, 1:n], ca[b][:, 0:n - 1], 0), (acc[b][:, 0:n - 1], ca[b][:, 1:n], 2)]
                elif ct == 0:
                    av = acc[b][:, 0:n].rearrange("p (r c) -> p r c", c=64)
                    cv = ca[b][:, 0:n].rearrange("p (r c) -> p r c", c=64)
                    sh = [(av[:, :, 1:64], cv[:, :, 0:63], 0), (av[:, :, 0:63], cv[:, :, 1:64], 2)]
                else:
                    sh = [(acc[b][:, 64:n], ca[b][:, 0:n - 64], 0), (acc[b][:, 0:n - 64], ca[b][:, 64:n], 2)]
                for (dst, src, j) in sh:
                    P.dve.scalar_tensor_tensor(out=dst, in0=src, scalar=w(j), in1=dst, op0=ALU.mult, op1=ALU.add)
                P.pool.tensor_tensor(out=ya[b][:, 0:n], in0=acc[b][:, 0:n], in1=Bt[b][:, 0:n], op=ALU.mult)
                P.sp.dma_start(out=YT[ct * 128:(ct + 1) * 128, c0:c0 + n], in_=ya[b][:, 0:n], _writes=[self.YT])
        P.end()

    def post_norm_residual(self, py, xt, G, tmp, xo, ssq, junk):
        P = self.P
        for hf in range(2):
            P.act.activation(out=junk[:, 0:512], in_=py[hf][:], func=AF.Square, accum_out=ssq[:, hf:hf + 1])
        ss = P.sm("pss", 1)
        P.dve.tensor_tensor(out=ss[:], in0=ssq[:, 0:1], in1=ssq[:, 1:2], op=ALU.add)
        rstd = self.rstd_chain(ss[:], 1, 1.0 / D, "prs")
        for hf in range(2):
            P.dve.scalar_tensor_tensor(out=tmp[:, hf * 512:(hf + 1) * 512], in0=py[hf][:], scalar=rstd[:, 0:1],
                                       in1=G[:, hf * 512:(hf + 1) * 512], op0=ALU.mult, op1=ALU.mult)
        P.pool.tensor_tensor(out=xo[:], in0=tmp[:], in1=xt[:], op=ALU.add)

    def mix_out(self, l, src):
        P = self.P
        c = self.c
        last = l == DEPTH - 1
        P.keep_begin()
        Wo = P.sbk("Wo", [128, 8, D], BF16)
        P.begin()
        stg = [P.sb("stgo%d" % i, [128, D]) for i in range(2)]
        for kc in range(8):
            P.sp.dma_start(out=stg[kc % 2][:], in_=self.w_o.h.ap()[l][kc * 128:(kc + 1) * 128, :])
            (P.act.copy if kc % 2 else P.dve.tensor_copy)(out=Wo[:, kc, :], in_=stg[kc % 2][:])
        P.end()
        P.begin()
        gon = P.sb("gon", [128, 128])
        P.sp.dma_start(out=gon[:], in_=self.g_onorm.h.ap()[l:l + 1, :].broadcast_to([128, 128]))
        G = [self.load_mod(l, 2, 1, "Gpc"), self.load_mod(l, 2, 0, "Gpx")]
        of = [P.sb("of%d" % i, [128, 512]) for i in range(2)]
        ob = [P.sb("ob%d" % i, [128, 512]) for i in range(2)]
        zs = [P.sb("zs%d" % i, [128, 512]) for i in range(2)]
        xt = [P.sb("xt%d" % i, [128, D]) for i in range(2)]
        yac = [P.sb("yac%d" % i, [128, 4, 128], BF16) for i in range(2)]
        o = P.sb("o", [128, 512])
        sq = P.sb("sq", [128, 512])
        y1 = P.sb("y1", [128, 512])
        y2 = P.sb("y2", [128, 512])
        yb = P.sb("yb", [128, 512], BF16)
        ybT = [P.sb("ybT%d" % i, [128, 4, 128], BF16) for i in range(2)]
        tmp = P.sb("tmp", [128, D])
        junk = P.sb("junk", [128, 512])
        xo = [P.sb("xo%d" % i, [128, D]) for i in range(2)]
        pT = P.ps("pT", [128, 1024], BF16)
        py = [[P.ps("py%d%d" % (i, hf), [128, 512]) for hf in range(2)] for i in range(2)]
        YT = self.YT.h.ap()
        h4 = lambda ap: ap.rearrange("p (h d) -> p h d", h=4)
        tl = list(range(2 if last else 0, NTILE))

        def front(i):
            n = tl[i]
            b = i % 2
            r0 = n * 128
            P.sp.dma_start(out=of[b][:], in_=self.OF.h.ap()[r0:r0 + 128, :], _reads=[self.OF])
            P.sp.dma_start(out=ob[b][:], in_=self.OB.h.ap()[r0:r0 + 128, :], _reads=[self.OB])
            P.sp.dma_start(out=zs[b][:], in_=self.ZS.h.ap()[r0:r0 + 128, :], _reads=[self.ZS])
            P.sp.dma_start(out=xt[b][:], in_=src.h.ap()[r0:r0 + 128, :], _reads=[src])
            P.sp.dma_start(out=yac[b][:, 0:2, :], in_=YT[0:256, r0:r0 + 128].rearrange("(ct p) t -> p ct t", p=128), _reads=[self.YT])
            P.sp.dma_start(out=yac[b][:, 2:4, :], in_=YT[768:1024, r0:r0 + 128].rearrange("(ct p) t -> p ct t", p=128), _reads=[self.YT])
            P.pool.tensor_tensor(out=o[:], in0=of[b][:], in1=ob[b][:], op=ALU.add)
            P.pool.tensor_tensor(out=sq[:], in0=o[:], in1=o[:], op=ALU.mult)
            ss4 = P.sm("ss4", 4)
            P.dve.tensor_reduce(out=ss4[:], in_=h4(sq[:]), axis=AX.X, op=ALU.add)
            rs = self.rstd_chain(ss4[:], 4, 1.0 / 128, "rso")
            P.dve.tensor_tensor(out=h4(y1[:]), in0=h4(o[:]), in1=rs[:].unsqueeze(2).broadcast_to([128, 4, 128]), op=ALU.mult)
            P.pool.tensor_tensor(out=h4(y2[:]), in0=h4(y1[:]), in1=gon[:].unsqueeze(1).broadcast_to([128, 4, 128]), op=ALU.mult)
            P.dve.tensor_tensor(out=yb[:], in0=y2[:], in1=zs[b][:], op=ALU.mult)
            for h in range(4):
                P.pe.transpose(out=pT[:, h * 128:(h + 1) * 128], in_=yb[:, h * 128:(h + 1) * 128], identity=c["ident_bf"][:])
            P.act.copy(out=ybT[b][:], in_=pT[:, 0:512].rearrange("p (h t) -> p h t", h=4))
            lhs = [yac[b][:, 0, :], yac[b][:, 1, :]] + [ybT[b][:, h, :] for h in range(4)] + [yac[b][:, 2, :], yac[b][:, 3, :]]
            for hf in range(2):
                for kc in range(8):
                    P.pe.matmul(out=py[b][hf][:], lhsT=lhs[kc], rhs=Wo[:, kc, hf * 512:(hf + 1) * 512], start=(kc == 0), stop=(kc == 7))

        def back(i):
            n = tl[i]
            b = i % 2
            s_ = 0 if n < 2 else 1
            r0 = n * 128
            ssq = P.sm("ssq", 2)
            self.post_norm_residual(py[b], xt[b], G[s_], tmp, xo[b], ssq, junk)
            P.sp.dma_start(out=self.XS.h.ap()[r0:r0 + 128, :], in_=xo[b][:], _writes=[self.XS])
        front(0)
        for i in range(len(tl)):
            if i + 1 < len(tl):
                front(i + 1)
            back(i)
        P.end()
        P.keep_end()

    def ffn(self, l):
        P = self.P
        c = self.c
        last = l == DEPTH - 1
        NJ = FFN_H // 128
        P.keep_begin()
        W1 = P.sbk("W1", [128, 8, 2 * FFN_H], BF16)
        W2 = P.sbk("W2", [128, NJ, D], BF16)
        P.begin()
        stg = [P.sb("stgf%d" % i, [128, 2 * FFN_H]) for i in range(2)]
        w1v = self.w_ffn_in.h.ap()[l]
        engs = (P.act.copy, P.dve.tensor_copy, P.pool.tensor_copy, P.act.copy)
        for kc in range(8):
            st = stg[kc % 2]
            P.sp.dma_start(out=st[:], in_=w1v[kc * 128:(kc + 1) * 128, :])
            for q in range(4):
                engs[q](out=W1[:, kc, q * 1408:(q + 1) * 1408], in_=st[:, q * 1408:(q + 1) * 1408])
        w2v = self.w_ffn_out.h.ap()[l].rearrange("(j p) n -> p j n", p=128)
        for jj, (j0, j1) in enumerate(((0, 5), (5, 10), (10, 15), (15, 20), (20, 22))):
            st = stg[jj % 2]
            nj = j1 - j0
            P.sp.dma_start(out=st[:, 0:nj * D].rearrange("p (j n) -> p j n", n=D), in_=w2v[:, j0:j1, :])
            for q in range(nj):
                engs[q % 3](out=W2[:, j0 + q, :], in_=st[:, q * D:(q + 1) * D])
        P.end()

        P.begin()
        S = [P.sb("Sf", [128, D]), None]
        Gm = [P.sb("Gf", [128, D]), None]
        Gp = [P.sb("Gpf", [128, D]), None]
        MBv = self.MB.h.ap()

        def load_mods(s):
            ms = 1 - s
            for t, k in ((S[0], 3), (Gm[0], 4), (Gp[0], 5)):
                P.sp.dma_start(out=t[:], in_=MBv[l, k, ms:ms + 1, :].broadcast_to([128, D]), _reads=[self.MB])
        xt = [[P.sb("xt%d%d" % (i, t), [128, D]) for t in range(2)] for i in range(2)]
        h1 = P.sb("h1", [128, D])
        hb = P.sb("hb", [128, D], BF16)
        hT = [P.sb("hT%d" % i, [128, 8, 256], BF16) for i in range(2)]
        sg = [P.sb("sg%d" % i, [128, 256]) for i in range(2)]
        tmp = P.sb("tmp", [128, D])
        xo = [P.sb("xo%d" % i, [128, D]) for i in range(2)]
        pT = P.ps("pT", [128, 1024], BF16)
        pg = P.ps("pg", [128, 512])
        pu = P.ps("pu", [128, 512])
        py = [[P.ps("py%d%d" % (t, hf), [128, 512]) for hf in range(2)] for t in range(2)]
        groups = list(range(1 if last else 0, NT // 256))
        LAG = 3
        aT = [P.sb("aTr%d" % i, [128, 256], BF16) for i in range(LAG + 2)]
        state = {"s": None}

        def pre_elem(gi, t):
            g = groups[gi]
            b = gi % 2
            s_ = 0 if g == 0 else 1
            if s_ != state["s"]:
                load_mods(s_)
                state["s"] = s_
            r0 = g * 256 + t * 128
            P.sp.dma_start(out=xt[b][t][:], in_=self.XS.h.ap()[r0:r0 + 128, :], _reads=[self.XS])
            ss = P.sm("ss", 1)
            P.act.activation(out=tmp[:], in_=xt[b][t][:], func=AF.Square, accum_out=ss[:])
            rstd = self.rstd_chain(ss[:], 1, 1.0 / D, "rsf")
            P.dve.scalar_tensor_tensor(out=h1[:], in0=xt[b][t][:], scalar=rstd[:, 0:1], in1=Gm[0][:], op0=ALU.mult, op1=ALU.mult)
            P.pool.tensor_tensor(out=hb[:], in0=h1[:], in1=S[0][:], op=ALU.add)

        def pre_tr(gi, t):
            b = gi % 2
            for kc in range(8):
                P.pe.transpose(out=pT[:, kc * 128:(kc + 1) * 128], in_=hb[:, kc * 128:(kc + 1) * 128], identity=c["ident_bf"][:])
            P.act.copy(out=hT[b][:, :, t * 128:(t + 1) * 128], in_=pT[:].rearrange("p (kc t) -> p kc t", kc=8))

        def second(j):
            a = aT[j % (LAG + 2)]
            for t in range(2):
                for hf in range(2):
                    P.pe.matmul(out=py[t][hf][:], lhsT=a[:, t * 128:(t + 1) * 128], rhs=W2[:, j, hf * 512:(hf + 1) * 512],
                                start=(j == 0), stop=(j == NJ - 1))

        def post(gi):
            g = groups[gi]
            b = gi % 2
            for t in range(2):
                r0 = g * 256 + t * 128
                ssq = P.sm("ssqf", 2)
                self.post_norm_residual(py[t], xt[b][t], Gp[0], tmp, xo[t], ssq, h1)
                if last:
                    P.sp.dma_start(out=self.out.h.ap()[r0 - TC:r0 - TC + 128, :], in_=xo[t][:], _writes=[self.out])
                else:
                    P.sp.dma_start(out=self.XB.h.ap()[r0:r0 + 128, :], in_=xo[t][:], _writes=[self.XB])

        for t in range(2):
            pre_elem(0, t)
            pre_tr(0, t)
        for gi, g in enumerate(groups):
            b = gi % 2
            nxt_ok = gi + 1 < len(groups)
            same_mods = nxt_ok and ((0 if groups[gi + 1] == 0 else 1) == state["s"])
            for j in range(NJ):
                for kc in range(8):
                    P.pe.matmul(out=pg[:, 0:256], lhsT=W1[:, kc, j * 128:(j + 1) * 128], rhs=hT[b][:, kc, :], start=(kc == 0), stop=(kc == 7))
                for kc in range(8):
                    P.pe.matmul(out=pu[:, 0:256], lhsT=W1[:, kc, FFN_H + j * 128:FFN_H + (j + 1) * 128], rhs=hT[b][:, kc, :], start=(kc == 0), stop=(kc == 7))
                if j >= LAG:
                    second(j - LAG)
                P.act.activation(out=sg[j % 2][:], in_=pg[:, 0:256], func=AF.Silu)
                P.dve.tensor_tensor(out=aT[j % (LAG + 2)][:], in0=pu[:, 0:256], in1=sg[j % 2][:], op=ALU.mult)
                if same_mods:
                    if j == 5:
                        pre_elem(gi + 1, 0)
                    elif j == 10:
                        pre_tr(gi + 1, 0)
                    elif j == 12:
                        pre_elem(gi + 1, 1)
                    elif j == 17:
                        pre_tr(gi + 1, 1)
            for j in range(NJ - LAG, NJ):
                second(j)
            post(gi)
            if nxt_ok and not same_mods:
                for t in range(2):
                    pre_elem(gi + 1, t)
                    pre_tr(gi + 1, t)
        P.end()
        P.keep_end()

    def forward(self, upto=None):
        self.consts()
        for l in range(DEPTH):
            src = self.xs_in if l == 0 else self.XB
            self.mods(l)
            self.prenorm(l, 0, 1, src)
            self.proj(l)
            self.gdn_prep(l)
            self.gdn_scan(l)
            self.mixer_a(l)
            self.mix_out(l, src)
            self.ffn(l)
        self.P.close()


W_NAMES = ["w_mod", "b_mod", "g_pre_mix", "g_post_mix", "g_pre_ffn", "g_post_ffn", "w_in", "conv_a", "conv_qkv",
           "a_log", "dt_bias", "g_onorm", "ln_c_g", "ln_c_b", "w_s", "b_s", "w_o", "w_ffn_in", "w_ffn_out"]


def make_in_maps(inputs, cores=range(8)):
    f = lambda a: np.ascontiguousarray(np.asarray(a, dtype=np.float32))
    shared = {n: f(inputs[n]) for n in W_NAMES}
    shared["a_log"] = shared["a_log"].reshape(DEPTH, 8)
    shared["dt_bias"] = shared["dt_bias"].reshape(DEPTH, 8)
    x, c, ctx, c_ctx = f(inputs["x"]), f(inputs["c"]), f(inputs["ctx"]), f(inputs["c_ctx"])
    maps = []
    for b in cores:
        m = dict(shared)
        m["xs"] = np.ascontiguousarray(np.concatenate([ctx[b], x[b]], axis=0))
        cc = np.stack([c[b], c_ctx], axis=0)
        m["ccT"] = np.ascontiguousarray(cc.reshape(2, 8, 128).transpose(2, 1, 0))
        maps.append(m)
    return maps


_CACHE = {}


def kernel(**inputs):
    if "nc" not in _CACHE:
        nc = bass.Bass("TRN2", target_bir_lowering=False)
        Model(nc).forward()
        _CACHE["nc"] = nc
    nc = _CACHE["nc"]
    maps = make_in_maps(inputs)
    res = run_bass_kernel_spmd(nc, maps, core_ids=list(range(8)))
    return np.stack([np.asarray(r["out"], dtype=np.float32) for r in res.results], axis=0)
```

```python
import numpy as np
from contextlib import ExitStack
import concourse.bass as bass
import concourse.mybir as mybir
from concourse.bass_utils import run_bass_kernel_spmd

F32 = mybir.dt.float32
F32R = mybir.dt.float32r
BF16 = mybir.dt.bfloat16
AF = mybir.ActivationFunctionType
ALU = mybir.AluOpType
AX = mybir.AxisListType

ENGS = ("tensor", "vector", "scalar", "gpsimd", "sync")
SAME_ENGINE_SYNC = True


class Tile:
    def __init__(self, prog, h, name, dram=False):
        self.prog = prog
        self.h = h
        self.name = name
        self.dram = dram
        self.lastw = None
        self.readers = []
        prog.tiles[name] = self

    def __getitem__(self, idx):
        return self.h[idx]

    def ap(self):
        return self.h.ap() if self.dram else self.h[:]


class SubTile:
    def __init__(self, parent, c0, w, name):
        self.parent = parent
        self.c0 = c0
        self.w = w
        self.name = name
        self.lastw = None
        self.readers = []

    def __getitem__(self, idx):
        return self.parent.h[:, self.c0:self.c0 + self.w][idx]


class Op:
    __slots__ = ("eng", "fn", "deps", "waits", "sig", "sigval", "is_dma", "dsem", "dval", "idx", "is_load", "F")

    def __init__(self, eng, fn, is_dma=False):
        self.eng = eng
        self.fn = fn
        self.deps = []
        self.waits = []
        self.sig = False
        self.sigval = None
        self.is_dma = is_dma
        self.dsem = None
        self.dval = None


class EngProxy:
    def __init__(self, prog, eng):
        self.prog = prog
        self.eng = eng

    def __getattr__(self, meth):
        prog, eng = self.prog, self.eng

        def call(*args, **kwargs):
            reads, writes = [], []
            for k, v in kwargs.items():
                if isinstance(v, bass.AP):
                    t = prog.tiles.get(v.tensor.name)
                    if t is None:
                        continue
                    if getattr(t, "subs", None):
                        t = t.subs[(v.offset % t.rowlen) // t.subw]
                    if k in ("out", "accum_out") or (k == "ap" and meth == "memset"):
                        writes.append(t)
                    else:
                        reads.append(t)
            extra_r = kwargs.pop("_reads", [])
            extra_w = kwargs.pop("_writes", [])
            reads += extra_r
            writes += extra_w
            is_dma = meth in ("dma_start",)
            if meth == "matmul" and kwargs.get("start", True) is False:
                pass
            fn = lambda e, meth=meth, args=args, kwargs=kwargs: getattr(e, meth)(*args, **kwargs)
            return prog.record(eng, fn, reads, writes, is_dma)

        return call


class Prog:
    def __init__(self, nc):
        self.nc = nc
        self.es = ExitStack()
        self.tiles = {}
        self.csem = {e: self.es.enter_context(nc.semaphore("c_" + e)) for e in ENGS}
        self.cnt = {e: 0 for e in ENGS}
        self.known = {e: {} for e in ENGS}
        self.ops = []
        self.dma_sems = []
        self.ndma_sems = 24
        for i in range(self.ndma_sems):
            self.dma_sems.append([self.es.enter_context(nc.semaphore("d%d" % i)), 0, None])
        self.dma_rr = 0
        self.phase_es = None
        for e in ENGS:
            setattr(self, e[0] if e != "sync" else "sp", EngProxy(self, e))
        self.pe = EngProxy(self, "tensor")
        self.dve = EngProxy(self, "vector")
        self.act = EngProxy(self, "scalar")
        self.pool = EngProxy(self, "gpsimd")
        self.sp = EngProxy(self, "sync")
        self.uid = 0

    def sb(self, name, shape, dtype=F32):
        self.uid += 1
        name = "%s_%d" % (name, self.uid)
        h = self.phase_es.enter_context(self.nc.sbuf_tensor(name, list(shape), dtype))
        return Tile(self, h, name)

    def sbc(self, name, shape, dtype=F32):
        self.uid += 1
        name = "%s_%d" % (name, self.uid)
        h = self.es.enter_context(self.nc.sbuf_tensor(name, list(shape), dtype))
        return Tile(self, h, name)

    def sm(self, name, cols, depth=4):
        key = (name, cols)
        ring = self.rings.setdefault(key, [[], 0])
        if len(ring[0]) < depth:
            ring[0].append(self.sb(name, [128, cols]))
            return ring[0][-1]
        ring[1] += 1
        return ring[0][ring[1] % depth]

    def keep_begin(self):
        self.keep_es = ExitStack()

    def keep_end(self):
        self.keep_es.close()
        self.keep_es = None

    def sbk(self, name, shape, dtype=F32):
        self.uid += 1
        name = "%s_%d" % (name, self.uid)
        h = self.keep_es.enter_context(self.nc.sbuf_tensor(name, list(shape), dtype))
        return Tile(self, h, name)

    def ps(self, name, shape, dtype=F32):
        self.uid += 1
        name = "%s_%d" % (name, self.uid)
        h = self.phase_es.enter_context(self.nc.psum_tensor(name, list(shape), dtype))
        t = Tile(self, h, name)
        t.psum = True
        return t

    def psb(self, name, n, w, dtype=F32):
        rowlen = 512 if dtype == F32 else 1024
        assert n * w <= rowlen
        t = self.ps(name, [128, rowlen], dtype)
        return [SubTile(t, i * w, w, "%s.%d" % (t.name, i)) for i in range(n)]

    def dram(self, name, shape, dtype=F32, kind="Internal"):
        h = self.nc.dram_tensor(name, list(shape), dtype, kind=kind)
        return Tile(self, h, name, dram=True)

    def record(self, eng, fn, reads, writes, is_dma=False):
        op = Op(eng, fn, is_dma)
        op.idx = len(self.ops)
        op.is_load = is_dma and any(not getattr(t, "dram", False) for t in writes)
        deps = []
        for t in reads:
            if t.lastw is not None:
                deps.append(t.lastw)
            if getattr(t, "psum", False):
                deps.extend(rd for rd in t.readers if rd.eng != eng)
        for t in writes:
            if t.lastw is not None:
                deps.append(t.lastw)
            deps.extend(t.readers)
        for t in reads:
            t.readers.append(op)
        for t in writes:
            t.lastw = op
            t.readers = []
        seen = set()
        for d in deps:
            if id(d) in seen or d is op:
                continue
            seen.add(id(d))
            op.deps.append(d)
        if is_dma:
            slot = self.dma_sems[self.dma_rr % self.ndma_sems]
            self.dma_rr += 1
            prev = slot[2]
            if prev is not None:
                op.deps.append(prev)
            slot[1] += 16
            slot[2] = op
            op.dsem = slot[0]
            op.dval = slot[1]
        self.ops.append(op)
        return op

    def begin(self):
        self.phase_es = ExitStack()
        self.ops = []
        self.rings = {}

    def end(self, final=False):
        nc = self.nc
        ops = self.ops
        last = {e: None for e in ENGS}
        for op in ops:
            if not op.is_dma:
                last[op.eng] = op
        pend_dma = [s[2] for s in self.dma_sems if s[2] is not None]
        for op in ops:
            for d in op.deps:
                if d.is_dma:
                    continue
                if d.eng != op.eng or (SAME_ENGINE_SYNC and d.eng != "tensor") or op.is_dma:
                    d.sig = True
        for e in ENGS:
            if last[e] is not None:
                last[e].sig = True
        for op in ops:
            if op.is_dma:
                continue
            if op.sig:
                self.cnt[op.eng] += 1
                op.sigval = self.cnt[op.eng]
        sp_ops = [op for op in ops if op.eng == "sync"]
        spidx = {id(op): i for i, op in enumerate(sp_ops)}
        lastF = {e: -1 for e in ENGS}
        for op in ops:
            f = -1
            for d in op.deps:
                if id(d) in spidx:
                    f = max(f, spidx[id(d)])
                elif getattr(d, "F", None) is not None:
                    f = max(f, d.F)
            if op.eng != "sync":
                f = max(f, lastF[op.eng])
                lastF[op.eng] = f
            op.F = f
        keys = {}
        prev_load_key = -1.0
        for i, op in enumerate(sp_ops):
            if op.is_load:
                k = max(op.F + 0.5, prev_load_key)
                k = min(k, float(i))
                prev_load_key = k
                keys[id(op)] = k
            else:
                keys[id(op)] = float(i)
        sp_sorted = sorted(range(len(sp_ops)), key=lambda i: (keys[id(sp_ops[i])], i))
        sp_new = [sp_ops[i] for i in sp_sorted]
        per = {e: [op for op in ops if op.eng == e] for e in ENGS}
        per["sync"] = sp_new
        for e in ENGS:
            kn = self.known[e]
            for op in per[e]:
                for d in op.deps:
                    if d.is_dma:
                        key, val, sem = ("d", id(d.dsem)), d.dval, d.dsem
                    else:
                        if d.sigval is None:
                            continue
                        if d.eng == op.eng and not op.is_dma and (d.eng == "tensor" or not SAME_ENGINE_SYNC):
                            continue
                        key, val, sem = ("c", d.eng), d.sigval, self.csem[d.eng]
                    if kn.get(key, 0) >= val:
                        continue
                    kn[key] = val
                    op.waits.append((sem, val))
        bar = {}
        for e in ENGS:
            w = []
            kn = self.known[e]
            for e2 in ENGS:
                if last[e2] is None:
                    continue
                v = last[e2].sigval
                if kn.get(("c", e2), 0) < v:
                    kn[("c", e2)] = v
                    w.append((self.csem[e2], v))
            for d in pend_dma:
                key = ("d", id(d.dsem))
                if kn.get(key, 0) < d.dval:
                    kn[key] = d.dval
                    w.append((d.dsem, d.dval))
            bar[e] = w
        for s in self.dma_sems:
            s[2] = None

        with nc.Block() as block:
            def emit(e):
                def body(eng):
                    for op in per[e]:
                        for (sem, val) in op.waits:
                            eng.wait_ge(sem, val)
                        ins = op.fn(eng)
                        if op.is_dma:
                            ins.then_inc(op.dsem, 16)
                        elif op.sig:
                            ins.then_inc(self.csem[e], 1)
                    for (sem, val) in bar[e]:
                        eng.wait_ge(sem, val)
                return body
            block.tensor(emit("tensor"))
            block.vector(emit("vector"))
            block.scalar(emit("scalar"))
            block.gpsimd(emit("gpsimd"))
            block.sync(emit("sync"))
        for t in self.tiles.values():
            t.lastw = None
            t.readers = []
            for st in (getattr(t, "subs", None) or []):
                st.lastw = None
                st.readers = []
        self.phase_es.close()
        self.phase_es = None
        self.ops = []

    def close(self):
        self.es.close()


D = 1024
T = 4096
TC = 256
NT = TC + T
NTILE = NT // 128
DEPTH = 2
EPS = 1e-6
IN_COLS = 3344
OFF_A_B, OFF_A_C, OFF_A_H, OFF_Q, OFF_K, OFF_V = 0, 256, 512, 768, 1280, 1792
OFF_AB, OFF_Z, OFF_CU, OFF_CV = 2304, 2320, 2832, 3088
FFN_H = 2816
SCN_W = 8 * 512 + 32
GROUPS = [(0, 256, 0, 0)] + [(256 + 512 * i, 512, 1, 512 * i) for i in range(8)]


class Model:
    def __init__(self, nc, debug=False):
        self.nc = nc
        self.P = P = Prog(nc)
        self.debug = debug
        k_in = "ExternalInput"
        k_sc = "ExternalOutput" if debug else "Internal"
        self.xs_in = P.dram("xs", [NT, D], F32, k_in)
        self.ccT = P.dram("ccT", [128, 8, 2], F32, k_in)
        self.w_mod = P.dram("w_mod", [DEPTH, D, 6 * D], F32, k_in)
        self.b_mod = P.dram("b_mod", [DEPTH, 6 * D], F32, k_in)
        self.g_pre_mix = P.dram("g_pre_mix", [DEPTH, D], F32, k_in)
        self.g_post_mix = P.dram("g_post_mix", [DEPTH, D], F32, k_in)
        self.g_pre_ffn = P.dram("g_pre_ffn", [DEPTH, D], F32, k_in)
        self.g_post_ffn = P.dram("g_post_ffn", [DEPTH, D], F32, k_in)
        self.w_in = P.dram("w_in", [DEPTH, D, IN_COLS], F32, k_in)
        self.conv_a = P.dram("conv_a", [DEPTH, 3, 256], F32, k_in)
        self.conv_qkv = P.dram("conv_qkv", [DEPTH, 3, 1536], F32, k_in)
        self.a_log = P.dram("a_log", [DEPTH, 8], F32, k_in)
        self.dt_bias = P.dram("dt_bias", [DEPTH, 8], F32, k_in)
        self.g_onorm = P.dram("g_onorm", [DEPTH, 128], F32, k_in)
        self.ln_c_g = P.dram("ln_c_g", [DEPTH, 256], F32, k_in)
        self.ln_c_b = P.dram("ln_c_b", [DEPTH, 256], F32, k_in)
        self.w_s = P.dram("w_s", [DEPTH, 4, 128, 128], F32, k_in)
        self.b_s = P.dram("b_s", [DEPTH, 4, 128], F32, k_in)
        self.w_o = P.dram("w_o", [DEPTH, D, D], F32, k_in)
        self.w_ffn_in = P.dram("w_ffn_in", [DEPTH, D, 2 * FFN_H], F32, k_in)
        self.w_ffn_out = P.dram("w_ffn_out", [DEPTH, FFN_H, D], F32, k_in)
        self.out = P.dram("out", [T, D], F32, "ExternalOutput")
        self.XS = P.dram("XS", [NT, D], F32, k_sc)
        self.MB = P.dram("MB", [DEPTH, 6, 2, D], F32, k_sc)
        self.HT = [P.dram("HTc", [8, 128, TC + 2], BF16, k_sc), P.dram("HTx", [8, 128, T + 2], BF16, k_sc)]
        self.PFM = P.dram("PFM", [512, NT], F32, k_sc)
        self.YT = P.dram("YT", [1024, NT], BF16, k_sc)
        self.ZS = P.dram("ZS", [NT, 512], F32, k_sc)
        self.QS = P.dram("QS", [NT, 528], F32, k_sc)
        self.SCN = P.dram("SCN", [NTILE, 128, SCN_W], F32, k_sc)
        self.SC2 = P.dram("SC2", [NTILE, 128, 6 * 512 + 32], F32, k_sc)
        self.XB = P.dram("XB", [NT, D], F32, k_sc)
        self.OF = P.dram("OF", [NT, 512], F32, k_sc)
        self.OB = P.dram("OB", [NT, 512], F32, k_sc)

    def consts(self):
        P = self.P
        c = self.c = {}
        for nm in ("ones", "U0", "U1", "SU0", "SU1", "NM0", "NM1", "ident", "NU0", "NU1"):
            c[nm] = P.sbc(nm, [128, 128])
        c["ident_bf"] = P.sbc("ident_bf", [128, 128], BF16)
        P.begin()
        ones = c["ones"]
        zer = P.sb("zer", [128, 128])
        P.pool.memset(ap=ones[:], constant=1.0)
        P.pool.memset(ap=zer[:], constant=0.0)

        def sel(name, src, step, cm, base, op, fill):
            t = c[name]
            P.pool.affine_select(out=t[:], in_=src[:], pattern=[[step, 128]], compare_op=op,
                                 fill=fill, base=base, channel_multiplier=cm)
            return t
        sel("U0", ones, 1, -1, 0, ALU.is_ge, 0.0)
        sel("U1", ones, -1, 1, 0, ALU.is_ge, 0.0)
        sel("SU0", ones, 1, -1, -1, ALU.is_ge, 0.0)
        sel("SU1", ones, -1, 1, -1, ALU.is_ge, 0.0)
        sel("NM0", zer, 1, -1, 0, ALU.is_ge, -30000.0)
        sel("NM1", zer, -1, 1, 0, ALU.is_ge, -30000.0)
        sel("ident", ones, 1, -1, 0, ALU.is_equal, 0.0)
        for r in (0, 1):
            P.pool.tensor_scalar(out=c["NU%d" % r][:], in0=c["U%d" % r][:], scalar1=-1.0, scalar2=None, op0=ALU.mult)
        P.pool.tensor_copy(out=c["ident_bf"][:], in_=c["ident"][:])
        zb = P.sb("zb", [128, 8, 1], BF16)
        P.pool.memset(ap=zb[:], constant=0.0)
        for s, n in ((0, TC), (1, T)):
            v = self.HT[s].h.ap().rearrange("kc p t -> p kc t")
            P.sp.dma_start(out=v[:, :, 0:1], in_=zb[:], _writes=[self.HT[s]], allow_slow_non_contiguous=True)
            P.sp.dma_start(out=v[:, :, n + 1:n + 2], in_=zb[:], _writes=[self.HT[s]], allow_slow_non_contiguous=True)
        P.end()

    def mods(self, l):
        P = self.P
        P.begin()
        cT = P.sb("cT", [128, 8, 2])
        sT = P.sb("sT", [128, 8, 2])
        P.sp.dma_start(out=cT[:], in_=self.ccT.ap())
        P.act.activation(out=sT[:], in_=cT[:], func=AF.Silu)
        bm = P.sb("bm", [2, 6 * D])
        P.sp.dma_start(out=bm[:], in_=self.b_mod.h.ap()[l:l + 1, :].broadcast_to([2, 6 * D]))
        mods = P.sb("mods", [2, 6 * D])
        wbuf = [P.sb("wm%d" % i, [128, 8, 512]) for i in range(2)]
        pm = [P.ps("pm%d" % i, [2, 512]) for i in range(2)]
        wv = self.w_mod.h.ap()[l].rearrange("(kc p) n -> p kc n", p=128)
        for nb in range(12):
            wb = wbuf[nb % 2]
            P.sp.dma_start(out=wb[:], in_=wv[:, :, nb * 512:(nb + 1) * 512])
            pp = pm[nb % 2]
            for kc in range(8):
                P.pe.matmul(out=pp[:], lhsT=sT[:, kc, :], rhs=wb[:, kc, :], start=(kc == 0), stop=(kc == 7))
            P.dve.tensor_tensor(out=mods[:, nb * 512:(nb + 1) * 512], in0=pp[:], in1=bm[:, nb * 512:(nb + 1) * 512], op=ALU.add)
        gv = P.sb("gv", [2, 4, D])
        for i, g in enumerate((self.g_pre_mix, self.g_post_mix, self.g_pre_ffn, self.g_post_ffn)):
            P.sp.dma_start(out=gv[:, i, :], in_=g.h.ap()[l:l + 1, :].broadcast_to([2, D]))
        mb = P.sb("mb", [2, 6, D])
        m = lambda i: mods[:, i * D:(i + 1) * D]
        P.dve.tensor_copy(out=mb[:, 0, :], in_=m(0))
        P.dve.scalar_tensor_tensor(out=mb[:, 1, :], in0=m(1), scalar=1.0, in1=gv[:, 0, :], op0=ALU.add, op1=ALU.mult)
        P.dve.tensor_tensor(out=mb[:, 2, :], in0=m(2), in1=gv[:, 1, :], op=ALU.mult)
        P.dve.tensor_copy(out=mb[:, 3, :], in_=m(3))
        P.dve.scalar_tensor_tensor(out=mb[:, 4, :], in0=m(4), scalar=1.0, in1=gv[:, 2, :], op0=ALU.add, op1=ALU.mult)
        P.dve.tensor_tensor(out=mb[:, 5, :], in0=m(5), in1=gv[:, 3, :], op=ALU.mult)
        P.sp.dma_start(out=self.MB.h.ap()[l].rearrange("k s d -> s k d"), in_=mb[:], _writes=[self.MB])
        P.end()

    def load_mod(self, l, k, s, name):
        P = self.P
        t = P.sb(name, [128, D])
        P.sp.dma_start(out=t[:], in_=self.MB.h.ap()[l, k, s:s + 1, :].broadcast_to([128, D]), _reads=[self.MB])
        return t

    def rstd_chain(self, ss, n, scale, name, post=None):
        P = self.P
        t1 = P.sm(name + "a", n)
        t2 = P.sm(name + "b", n)
        t3 = P.sm(name + "c", n)
        P.dve.tensor_scalar(out=t1[:], in0=ss, scalar1=scale, scalar2=EPS, op0=ALU.mult, op1=ALU.add)
        P.act.activation(out=t2[:], in_=t1[:], func=AF.Sqrt)
        P.dve.reciprocal(out=t3[:], in_=t2[:])
        return t3

    def prenorm(self, l, ks, kg, src):
        P = self.P
        c = self.c
        P.begin()
        Sx = [self.load_mod(l, ks, 1, "Sc"), self.load_mod(l, ks, 0, "Sx")]
        Gx = [self.load_mod(l, kg, 1, "Gc"), self.load_mod(l, kg, 0, "Gx")]
        xt = [P.sb("xt%d" % i, [128, D]) for i in range(2)]
        junk = P.sb("junk", [128, D])
        h1 = [P.sb("h1%d" % i, [128, D]) for i in range(2)]
        hb = [P.sb("hb%d" % i, [128, D], BF16) for i in range(2)]
        pT = [P.ps("pT%d" % i, [128, D], BF16) for i in range(2)]
        hT = [P.sb("hT%d" % i, [128, 8, 512], BF16) for i in range(2)]
        tiles = []
        for gi, (row0, ntok, s, t0) in enumerate(GROUPS):
            for j in range(ntok // 128):
                tiles.append((gi, row0, ntok, s, t0, j))

        def front(i):
            gi, row0, ntok, s, t0, j = tiles[i]
            b = i % 2
            r0 = row0 + j * 128
            P.sp.dma_start(out=xt[b][:], in_=src.h.ap()[r0:r0 + 128, :], _reads=[src])
            ss = P.sm("ss", 1)
            P.act.activation(out=junk[:], in_=xt[b][:], func=AF.Square, accum_out=ss[:])
            rstd = self.rstd_chain(ss[:], 1, 1.0 / D, "rs")
            P.dve.scalar_tensor_tensor(out=h1[b][:], in0=xt[b][:], scalar=rstd[:, 0:1], in1=Gx[s][:], op0=ALU.mult, op1=ALU.mult)
            P.pool.tensor_tensor(out=hb[b][:], in0=h1[b][:], in1=Sx[s][:], op=ALU.add)

        def back(i):
            gi, row0, ntok, s, t0, j = tiles[i]
            b = i % 2
            hTg = hT[gi % 2]
            for kc in range(8):
                P.pe.transpose(out=pT[b][:, kc * 128:(kc + 1) * 128], in_=hb[b][:, kc * 128:(kc + 1) * 128], identity=c["ident_bf"][:])
            P.act.copy(out=hTg[:, :, j * 128:(j + 1) * 128], in_=pT[b][:].rearrange("p (kc t) -> p kc t", kc=8))
            if j == ntok // 128 - 1:
                dst = self.HT[s].h.ap().rearrange("kc p t -> p kc t")[:, :, 1 + t0:1 + t0 + ntok]
                P.sp.dma_start(out=dst, in_=hTg[:, :, 0:ntok], _writes=[self.HT[s]])
        front(0)
        for i in range(len(tiles)):
            if i + 1 < len(tiles):
                front(i + 1)
            back(i)
        P.end()

    def proj(self, l):
        P = self.P
        c = self.c
        P.keep_begin()
        Wfm = P.sbk("Wfm", [128, 8, 1024], BF16)
        Wrest = P.sbk("Wrest", [128, 8, 784], BF16)
        Wqkv = [P.sbk("Wq%d" % j, [128, 8, 1536], BF16) for j in range(3)]
        P.begin()
        cw = P.sb("cw", [128, 3, 1536])
        P.sp.dma_start(out=cw[:], in_=self.conv_qkv.h.ap()[l:l + 1].broadcast_to([128, 3, 1536]))
        stg = [P.sb("stg%d" % i, [128, IN_COLS]) for i in range(2)]
        wv = self.w_in.h.ap()[l]
        for kc in range(8):
            st = stg[kc % 2]
            P.sp.dma_start(out=st[:], in_=wv[kc * 128:(kc + 1) * 128, :])
            P.act.copy(out=Wfm[:, kc, 0:768], in_=st[:, 0:768])
            P.act.copy(out=Wfm[:, kc, 768:1024], in_=st[:, OFF_CU:OFF_CV])
            P.act.copy(out=Wrest[:, kc, 0:512], in_=st[:, OFF_Z:OFF_CU])
            P.act.copy(out=Wrest[:, kc, 512:528], in_=st[:, OFF_AB:OFF_Z])
            P.act.copy(out=Wrest[:, kc, 528:784], in_=st[:, OFF_CV:IN_COLS])
            for j in range(3):
                eng = (P.dve, P.pool, P.dve)[j]
                eng.tensor_tensor(out=Wqkv[j][:, kc, :], in0=st[:, OFF_Q:OFF_AB], in1=cw[:, j, :], op=ALU.mult)
        P.end()

        P.begin()
        pfm = [P.ps("pfm%d" % i, [128, 512]) for i in range(2)]
        prot = [P.ps("prot%d" % i, [128, 512]) for i in range(3)]
        pr = P.ps("pr", [128, 512])
        pc = [P.ps("pc%d" % i, [128, 128]) for i in range(2)]
        wsl = P.sb("wsl", [128, 4, 128])
        wsT = P.sb("wsT", [128, 4, 128])
        P.sp.dma_start(out=wsl[:], in_=self.w_s.h.ap()[l].rearrange("g p q -> p g q"))
        for g in range(4):
            P.pe.transpose(out=pr[:, g * 128:(g + 1) * 128], in_=wsl[:, g, :], identity=c["ident"][:])
        P.dve.tensor_copy(out=wsT[:], in_=pr[:].rearrange("p (g q) -> p g q", g=4))
        Bs = P.sb("Bs", [128, 2, 128])
        for g in range(4):
            P.sp.dma_start(out=Bs[(g % 2) * 64:(g % 2) * 64 + 64, g // 2, :],
                           in_=self.b_s.h.ap()[l, g:g + 1, :].broadcast_to([64, 128]))
        lng = P.sb("lng", [128, 256])
        lnb = P.sb("lnb", [128, 256])
        P.sp.dma_start(out=lng[:], in_=self.ln_c_g.h.ap()[l:l + 1, :].broadcast_to([128, 256]))
        P.sp.dma_start(out=lnb[:], in_=self.ln_c_b.h.ap()[l:l + 1, :].broadcast_to([128, 256]))
        hTb = [P.sb("hTg%d" % i, [128, 8, 514], BF16) for i in range(2)]
        evb = [P.sb("evb%d" % i, [128, 512]) for i in range(2)]
        Csb = [P.sb("Csb%d" % i, [128, 512]) for i in range(2)]
        uTb = [P.sb("uT%d" % i, [128, 2, 512]) for i in range(2)]
        ycTb = [P.sb("ycT%d" % i, [128, 2, 512], BF16) for i in range(2)]
        qsb = [P.sb("qsb%d" % i, [128, 512]) for i in range(2)]
        ksb = [P.sb("ksb%d" % i, [128, 512]) for i in range(2)]
        sq = [P.sb("sq%d" % i, [128, 512]) for i in range(2)]
        kvst = [P.sb("kvst%d" % i, [128, 2, 512]) for i in range(2)]
        qst = [P.sb("qst%d" % i, [128, 528]) for i in range(2)]
        zsb = [P.sb("zsb%d" % i, [128, 512]) for i in range(2)]
        cv = P.sb("cv", [128, 256])
        vn1 = P.sb("vn1", [128, 256])
        vn2 = P.sb("vn2", [128, 256])
        vnb = [P.sb("vn%d" % i, [128, 256]) for i in range(2)]
        pending = []
        tmpc = P.sb("tmpc", [128, 128])
        PFM = self.PFM.h.ap()
        it = 0
        rot = 0
        for gi, (row0, ntok, s, t0) in enumerate(GROUPS):
            hTg = hTb[gi % 2]
            uT = uTb[gi % 2]
            ycT = ycTb[gi % 2]
            src = self.HT[s].h.ap().rearrange("kc p t -> p kc t")[:, :, t0:t0 + ntok + 2]
            P.sp.dma_start(out=hTg[:, :, 0:ntok + 2], in_=src, _reads=[self.HT[s]])
            for cb in range(8):
                pf = pfm[cb % 2]
                for kc in range(8):
                    P.pe.matmul(out=pf[:, 0:ntok], lhsT=Wfm[:, kc, cb * 128:(cb + 1) * 128], rhs=hTg[:, kc, 1:1 + ntok],
                                start=(kc == 0), stop=(kc == 7))
                if cb < 2:
                    ev = evb[cb % 2]
                    P.act.copy(out=ev[:, 0:ntok], in_=pf[:, 0:ntok])
                    P.sp.dma_start(out=PFM[cb * 128:(cb + 1) * 128, row0:row0 + ntok], in_=ev[:, 0:ntok], _writes=[self.PFM])
                elif cb < 4:
                    P.act.copy(out=Csb[cb - 2][:, 0:ntok], in_=pf[:, 0:ntok])
                elif cb < 6:
                    ev = evb[cb % 2]
                    P.dve.tensor_tensor(out=ev[:, 0:ntok], in0=pf[:, 0:ntok], in1=Csb[cb - 4][:, 0:ntok], op=ALU.mult)
                    P.sp.dma_start(out=PFM[256 + (cb - 4) * 128:256 + (cb - 3) * 128, row0:row0 + ntok], in_=ev[:, 0:ntok], _writes=[self.PFM])
                else:
                    P.act.activation(out=uT[:, cb - 6, 0:ntok], in_=pf[:, 0:ntok], func=AF.Gelu)
            for j in range(ntok // 128):
                b = it % 2
                it += 1
                n = (row0 + j * 128) // 128
                off = j * 128
                r0 = row0 + off
                pq = []
                for which in range(4):
                    pp = prot[rot % 3]
                    rot += 1
                    pq.append(pp)
                    if which < 3:
                        first = True
                        for tap in range(3):
                            for kc in range(8):
                                P.pe.matmul(out=pp[:], lhsT=hTg[:, kc, off + tap:off + tap + 128],
                                            rhs=Wqkv[tap][:, kc, which * 512:(which + 1) * 512],
                                            start=first, stop=(tap == 2 and kc == 7))
                                first = False
                    else:
                        for kc in range(8):
                            P.pe.matmul(out=pp[:], lhsT=hTg[:, kc, off + 1:off + 129], rhs=Wrest[:, kc, 0:512],
                                        start=(kc == 0), stop=(kc == 7))
                    if which == 0:
                        P.act.activation(out=qsb[b][:], in_=pp[:], func=AF.Silu)
                    elif which == 1:
                        P.act.activation(out=ksb[b][:], in_=pp[:], func=AF.Silu)
                    elif which == 2:
                        P.act.activation(out=kvst[b][:, 1, :], in_=pp[:], func=AF.Silu)
                    else:
                        P.act.activation(out=zsb[b][:], in_=pp[:], func=AF.Silu)
                        P.sp.dma_start(out=self.ZS.h.ap()[r0:r0 + 128, :], in_=zsb[b][:], _writes=[self.ZS])
                while pending:
                    pending.pop(0)()
                vn = vnb[it % 2]
                for kc in range(8):
                    P.pe.matmul(out=pr[:, 0:272], lhsT=hTg[:, kc, off + 1:off + 129], rhs=Wrest[:, kc, 512:784],
                                start=(kc == 0), stop=(kc == 7))
                P.dve.tensor_copy(out=qst[b][:, 512:528], in_=pr[:, 0:16])
                P.act.activation(out=cv[:], in_=pr[:, 16:272], func=AF.Gelu)
                ss8 = P.sm("ss8", 8)
                P.pool.tensor_tensor(out=sq[0][:], in0=qsb[b][:], in1=qsb[b][:], op=ALU.mult)
                P.dve.tensor_reduce(out=ss8[:, 0:4], in_=sq[0][:].rearrange("p (h d) -> p h d", h=4), axis=AX.X, op=ALU.add)
                P.pool.tensor_tensor(out=sq[1][:], in0=ksb[b][:], in1=ksb[b][:], op=ALU.mult)
                P.dve.tensor_reduce(out=ss8[:, 4:8], in_=sq[1][:].rearrange("p (h d) -> p h d", h=4), axis=AX.X, op=ALU.add)
                rs = self.rstd_chain(ss8[:], 8, 1.0, "rsqk")
                rq = P.sm("rq", 4)
                P.dve.tensor_scalar(out=rq[:], in0=rs[:, 0:4], scalar1=128.0 ** -0.5, scalar2=None, op0=ALU.mult)
                P.dve.tensor_tensor(out=qst[b][:, 0:512].rearrange("p (h d) -> p h d", h=4),
                                    in0=qsb[b][:].rearrange("p (h d) -> p h d", h=4),
                                    in1=rq[:].unsqueeze(2).broadcast_to([128, 4, 128]), op=ALU.mult)
                P.dve.tensor_tensor(out=kvst[b][:, 0, :].rearrange("p (h d) -> p h d", h=4),
                                    in0=ksb[b][:].rearrange("p (h d) -> p h d", h=4),
                                    in1=rs[:, 4:8].unsqueeze(2).broadcast_to([128, 4, 128]), op=ALU.mult)
                dst = self.SCN.h.ap()[n][:, 512:2560].rearrange("p (a b) -> p a b", b=1024)[:, :, 0:512]
                P.sp.dma_start(out=dst, in_=kvst[b][:], _writes=[self.SCN])
                P.sp.dma_start(out=self.QS.h.ap()[r0:r0 + 128, :], in_=qst[b][:], _writes=[self.QS])
                st6 = P.sm("st6", 6)
                mv = P.sm("mv", 2)
                P.dve.bn_stats(out=st6[:], in_=cv[:])
                P.dve.bn_aggr(out=mv[:], in_=st6[:])
                rl = self.rstd_chain(mv[:, 1:2], 1, 1.0, "rln")
                P.dve.tensor_scalar(out=vn1[:], in0=cv[:], scalar1=mv[:, 0:1], scalar2=rl[:, 0:1], op0=ALU.subtract, op1=ALU.mult)
                P.pool.tensor_tensor(out=vn2[:], in0=vn1[:], in1=lng[:], op=ALU.mult)
                P.pool.tensor_tensor(out=vn[:], in0=vn2[:], in1=lnb[:], op=ALU.add)
                def tail(vn=vn, uT=uT, ycT=ycT, off=off, lastj=(j == ntok // 128 - 1), row0=row0, ntok=ntok):
                    for g in range(4):
                        P.pe.matmul(out=pc[g // 2][(g % 2) * 64:(g % 2) * 64 + 64, :], lhsT=vn[:, g * 64:(g + 1) * 64],
                                    rhs=wsT[:, g, :], start=True, stop=True)
                    for ct in range(2):
                        P.dve.tensor_tensor(out=tmpc[:], in0=pc[ct][:], in1=Bs[:, ct, :], op=ALU.add)
                        P.pool.tensor_tensor(out=ycT[:, ct, off:off + 128], in0=tmpc[:], in1=uT[:, ct, off:off + 128], op=ALU.mult)
                    if lastj:
                        dst = self.YT.h.ap()[768:1024, row0:row0 + ntok].rearrange("(ct p) t -> p ct t", p=128)
                        P.sp.dma_start(out=dst, in_=ycT[:, :, 0:ntok], _writes=[self.YT])
                pending.append(tail)
        while pending:
            pending.pop(0)()
        P.end()
        P.keep_end()

    def gdn_prep(self, l):
        P = self.P
        c = self.c
        P.begin()
        al = P.sb("al", [128, 8])
        dtb = P.sb("dtb", [128, 8])
        ea = P.sb("ea", [128, 8])
        nea = P.sb("nea", [128, 8])
        P.sp.dma_start(out=al[:], in_=self.a_log.h.ap()[l:l + 1, :].broadcast_to([128, 8]))
        P.sp.dma_start(out=dtb[:], in_=self.dt_bias.h.ap()[l:l + 1, :].broadcast_to([128, 8]))
        P.act.activation(out=ea[:], in_=al[:], func=AF.Exp)
        P.dve.tensor_scalar(out=nea[:], in0=ea[:], scalar1=-1.0, scalar2=None, op0=ALU.mult)
        ph = P.psb("pha", 4, 128) + P.psb("phb", 4, 128)
        pA = [P.psb("pA%d" % r, 4, 128) for r in range(2)]
        pB = [P.psb("pB%d" % r, 4, 128) for r in range(2)]
        pD = [P.psb("pD%d" % r, 4, 128) for r in range(2)]
        pg = SubTile(ph[4].parent, 0, 16, "pgv")
        GU = [P.sb("GU%d" % r, [128, 512]) for r in range(2)]
        kin = [P.sb("kin%d" % i, [128, 512]) for i in range(2)]
        qs = [P.sb("qs%d" % i, [128, 528]) for i in range(2)]
        okT = [P.sb("okT%d" % i, [128, 512]) for i in range(2)]
        oqT = [P.sb("oqT%d" % i, [128, 512]) for i in range(2)]
        oT2T = [[P.sb("oT2T%d_%d" % (i, r), [128, 512]) for r in range(2)] for i in range(2)]
        oQKT = [[P.sb("oQKT%d_%d" % (i, r), [128, 512]) for r in range(2)] for i in range(2)]
        scalb = [P.sb("scal%d" % i, [128, 32]) for i in range(2)]
        sm = {nm: P.sb(nm, [128, 8]) for nm in ("e1", "d1", "x2", "e2", "sp", "g", "dl", "et")}
        gsb = P.sb("gsb", [128, 16])
        U8 = [(r, h) for r in range(2) for h in range(4)]
        mk = lambda nm: {u: P.sb("%s%d%d" % (nm, u[0], u[1]), [128, 128]) for u in U8}
        Dt, E2, a1 = mk("Dt"), mk("E2"), mk("a1")
        Pp = [[mk("Pp%d%d" % (q, i)) for i in range(2)] for q in range(2)]
        PT = [[mk("PT%d%d" % (q, i)) for i in range(2)] for q in range(2)]
        R = [[mk("R%d%d" % (q, i)) for i in range(2)] for q in range(2)]
        SCN = self.SCN.h.ap()
        SC2 = self.SC2.h.ap()
        hs = lambda h: slice(h * 128, (h + 1) * 128)

        def stage_a(n):
            b = n % 2
            scal = scalb[b]
            g = sm["g"]
            Pp0, PT0, R0 = Pp[b][0], PT[b][0], R[b][0]

            def a_load():
                P.sp.dma_start(out=kin[b][:], in_=SCN[n][:, 512:1024], _reads=[self.SCN])
                P.sp.dma_start(out=qs[b][:], in_=self.QS.h.ap()[n * 128:(n + 1) * 128, :], _reads=[self.QS])

            def a_gates1():
                P.act.activation(out=sm["e1"][:], in_=qs[b][:, 512:520], func=AF.Exp, scale=-1.0)
                P.dve.tensor_scalar(out=sm["d1"][:], in0=sm["e1"][:], scalar1=1.0, scalar2=None, op0=ALU.add)
                P.dve.reciprocal(out=scal[:, 0:8], in_=sm["d1"][:])
                P.dve.tensor_tensor(out=sm["x2"][:], in0=qs[b][:, 520:528], in1=dtb[:], op=ALU.add)
                P.act.activation(out=sm["e2"][:], in_=sm["x2"][:], func=AF.Exp)
                P.act.activation(out=sm["sp"][:], in_=sm["e2"][:], func=AF.Ln, bias=1.0)
                P.dve.tensor_tensor(out=g[:], in0=sm["sp"][:], in1=nea[:], op=ALU.mult)

            def a_gates2():
                P.pe.matmul(out=pg[:, 0:4], lhsT=c["U0"][:], rhs=g[:, 0:4], start=True, stop=True)
                P.pe.matmul(out=pg[:, 4:8], lhsT=c["U1"][:], rhs=g[:, 4:8], start=True, stop=True)
                P.pe.matmul(out=pg[:, 8:16], lhsT=c["ones"][:], rhs=g[:, 0:8], start=True, stop=True)

            def a_gates3():
                P.dve.tensor_copy(out=gsb[:], in_=pg[:])
                P.act.activation(out=scal[:, 8:16], in_=gsb[:, 0:8], func=AF.Exp)
                P.dve.tensor_tensor(out=sm["dl"][:], in0=gsb[:, 8:16], in1=gsb[:, 0:8], op=ALU.subtract)
                P.act.activation(out=sm["et"][:], in_=sm["dl"][:], func=AF.Exp)
                P.dve.tensor_copy(out=scal[:, 16:24], in_=sm["et"][:])
                P.act.activation(out=scal[:, 24:32], in_=gsb[:, 8:16], func=AF.Exp)
                for (r, h) in U8:
                    idx = r * 4 + h
                    P.act.activation(out=GU[r][:, hs(h)], in_=c["U%d" % r][:], func=AF.Copy, scale=g[:, idx:idx + 1])

            def a_tr():
                for h in range(4):
                    P.pe.transpose(out=ph[h][:], in_=kin[b][:, hs(h)], identity=c["ident"][:])
                for h in range(4):
                    P.pe.transpose(out=ph[4 + h][:], in_=qs[b][:, hs(h)], identity=c["ident"][:])

            def a_trc():
                for h in range(4):
                    P.act.copy(out=okT[b][:, hs(h)], in_=ph[h][:])
                for h in range(4):
                    P.dve.tensor_copy(out=oqT[b][:, hs(h)], in_=ph[4 + h][:])

            def a_kk():
                for h in range(4):
                    P.pe.matmul(out=ph[h][:], lhsT=okT[b][:, hs(h)], rhs=okT[b][:, hs(h)], start=True, stop=True)
                for h in range(4):
                    P.pe.matmul(out=ph[4 + h][:], lhsT=okT[b][:, hs(h)], rhs=oqT[b][:, hs(h)], start=True, stop=True)
                for r in range(2):
                    P.pe.matmul(out=pD[r][0].parent[:], lhsT=c["ones"][:], rhs=GU[r][:], start=True, stop=True)

            def a_dt():
                for (r, h) in U8:
                    idx = r * 4 + h
                    P.dve.scalar_tensor_tensor(out=Dt[(r, h)][:], in0=pD[r][h][:], scalar=gsb[:, idx:idx + 1], in1=c["NM%d" % r][:],
                                               op0=ALU.subtract, op1=ALU.add)
                for u in U8:
                    P.act.activation(out=E2[u][:], in_=Dt[u][:], func=AF.Exp)

            def a_n():
                for (r, h) in U8:
                    u = (r, h)
                    idx = r * 4 + h
                    P.dve.tensor_tensor(out=oQKT[b][r][:, hs(h)], in0=ph[4 + h][:], in1=E2[u][:], op=ALU.mult)
                    P.dve.scalar_tensor_tensor(out=a1[u][:], in0=ph[h][:], scalar=scal[:, idx:idx + 1], in1=E2[u][:],
                                               op0=ALU.mult, op1=ALU.mult)
                    P.pool.tensor_tensor(out=Pp0[u][:], in0=a1[u][:], in1=c["SU%d" % r][:], op=ALU.mult)
                for u in U8:
                    P.pool.tensor_tensor(out=R0[u][:], in0=c["ident"][:], in1=Pp0[u][:], op=ALU.subtract)

            def a_nt():
                for (r, h) in U8:
                    P.pe.transpose(out=pD[r][h][:], in_=Pp0[(r, h)][:], identity=c["ident"][:])

            def a_ntc():
                for (r, h) in U8:
                    (P.act.copy if r == 0 else P.dve.tensor_copy)(out=PT0[(r, h)][:], in_=pD[r][h][:])
            return [a_load, a_gates1, a_gates2, a_gates3, a_tr, a_trc, a_kk, a_dt, a_n, a_nt, a_ntc]

        def level_stages(n, r, lvl):
            b = n % 2
            cur, nxt = lvl % 2, 1 - (lvl % 2)
            Pc, Pn, Tc, Tn, Rc, Rn = Pp[b][cur], Pp[b][nxt], PT[b][cur], PT[b][nxt], R[b][cur], R[b][nxt]

            def s1():
                for h in range(4):
                    u = (r, h)
                    if lvl < 5:
                        P.pe.matmul(out=pA[r][h][:], lhsT=Tc[u][:], rhs=Pc[u][:], start=True, stop=True)
                    P.pe.matmul(out=pB[r][h][:], lhsT=Pc[u][:], rhs=Tc[u][:], start=True, stop=True)

            def s2():
                if lvl < 5:
                    for h in range(4):
                        (P.act.copy if r == 0 else P.dve.tensor_copy)(out=Pn[(r, h)][:], in_=pA[r][h][:])
                for h in range(4):
                    (P.dve.tensor_copy if r == 0 else P.act.copy)(out=Tn[(r, h)][:], in_=pB[r][h][:])

            def s3():
                for h in range(4):
                    u = (r, h)
                    P.pe.matmul(out=pA[r][h][:], lhsT=Tn[u][:], rhs=Rc[u][:], start=True, stop=True)

            def s4():
                for h in range(4):
                    u = (r, h)
                    dst = Rn[u][:] if lvl < 5 else oT2T[b][r][:, hs(h)]
                    P.dve.tensor_tensor(out=dst, in0=pA[r][h][:], in1=Rc[u][:], op=ALU.add)
            return [s1, s2, s3, s4]

        def stores(n):
            b = n % 2
            w = [self.SC2]
            P.sp.dma_start(out=SC2[n][:, 0:512], in_=okT[b][:], _writes=w)
            P.sp.dma_start(out=SC2[n][:, 512:1024], in_=oqT[b][:], _writes=w)
            for r in range(2):
                P.sp.dma_start(out=SC2[n][:, 1024 + r * 512:1536 + r * 512], in_=oT2T[b][r][:], _writes=w)
                P.sp.dma_start(out=SC2[n][:, 2048 + r * 512:2560 + r * 512], in_=oQKT[b][r][:], _writes=w)
            P.sp.dma_start(out=SC2[n][:, 3072:3104], in_=scalb[b][:], _writes=w)

        for f in stage_a(0):
            f()
        A_AT = {0: 0, 1: 1, 2: 4, 4: 5, 8: 2, 10: 3, 14: 6, 16: 7, 18: 8, 22: 9, 23: 10}
        for n in range(NTILE):
            stA = [st for lvl in range(6) for st in level_stages(n, 0, lvl)]
            stB = [st for lvl in range(6) for st in level_stages(n, 1, lvl)]
            nxtA = stage_a(n + 1) if n + 1 < NTILE else None
            for i in range(len(stA) + 1):
                if i < len(stA):
                    stA[i]()
                if i >= 1:
                    stB[i - 1]()
                if nxtA is not None and i in A_AT:
                    nxtA[A_AT[i]]()
            stores(n)
        P.end()

    def gdn_scan(self, l):
        P = self.P
        P.begin()
        SCN = self.SCN.h.ap()
        SC2 = self.SC2.h.ap()
        Forder = list(range(NTILE))
        Border = [1, 0] + list(range(NTILE - 1, 1, -1))
        names = ("kT", "k", "qT", "v", "T2T", "QKT")
        col0 = {"kT": 0, "k": 512, "qT": 1024, "v": 1536}
        inb = [[{nm: P.sb("i%s%d%d" % (nm, r, i), [128, 512]) for nm in names} for i in range(2)] for r in range(2)]
        scb = [[P.sb("isc%d%d" % (r, i), [128, 32]) for i in range(2)] for r in range(2)]
        ngc = [P.sb("ngc%d" % r, [128, 8]) for r in range(2)]
        S = [[[P.sb("S%d%d%d" % (r, h, i), [128, 128]) for i in range(2)] for h in range(4)] for r in range(2)]
        pa = [P.psb("pa%d" % r, 4, 128) for r in range(2)]
        po = [P.psb("po%d" % r, 4, 128) for r in range(2)]
        rr = [[P.sb("rr%d%d" % (r, h), [128, 128]) for h in range(4)] for r in range(2)]
        vn = [[P.sb("vn%d%d" % (r, h), [128, 128]) for h in range(4)] for r in range(2)]
        vt = [[P.sb("vt%d%d" % (r, h), [128, 128]) for h in range(4)] for r in range(2)]
        t1 = [[P.sb("t1%d%d" % (r, h), [128, 128]) for h in range(4)] for r in range(2)]
        oo = [[P.sb("oo%d%d" % (r, i), [128, 512]) for i in range(2)] for r in range(2)]
        for r in range(2):
            for h in range(4):
                P.pool.memset(ap=S[r][h][0][:], constant=0.0)
        units = [(r, h) for r in range(2) for h in range(4)]
        cur = 0
        for i in range(NTILE):
            b = i % 2
            nxt = 1 - cur
            tl = (Forder[i], Border[i])
            for r in range(2):
                n = tl[r]
                d = inb[r][b]
                for nm in ("k", "v"):
                    P.sp.dma_start(out=d[nm][:], in_=SCN[n][:, col0[nm]:col0[nm] + 512], _reads=[self.SCN])
                P.sp.dma_start(out=d["kT"][:], in_=SC2[n][:, 0:512], _reads=[self.SC2])
                P.sp.dma_start(out=d["qT"][:], in_=SC2[n][:, 512:1024], _reads=[self.SC2])
                P.sp.dma_start(out=d["T2T"][:], in_=SC2[n][:, 1024 + r * 512:1536 + r * 512], _reads=[self.SC2])
                P.sp.dma_start(out=d["QKT"][:], in_=SC2[n][:, 2048 + r * 512:2560 + r * 512], _reads=[self.SC2])
                P.sp.dma_start(out=scb[r][b][:], in_=SC2[n][:, 3072:3104], _reads=[self.SC2])
                P.pool.tensor_scalar(out=ngc[r][:], in0=scb[r][b][:, 8:16], scalar1=-1.0, scalar2=None, op0=ALU.mult)
            hs = lambda h: slice(h * 128, (h + 1) * 128)

            def scan_stages(r):
                d = inb[r][b]
                sc = scb[r][b]
                def t1_():
                    for h in range(4):
                        P.pe.matmul(out=pa[r][h][:], lhsT=d["kT"][:, hs(h)], rhs=S[r][h][cur][:], start=True, stop=True)
                    for h in range(4):
                        P.pe.matmul(out=po[r][h][:], lhsT=d["qT"][:, hs(h)], rhs=S[r][h][cur][:], start=True, stop=True)
                def t2_():
                    for h in range(4):
                        idx = r * 4 + h
                        P.dve.scalar_tensor_tensor(out=rr[r][h][:], in0=pa[r][h][:], scalar=ngc[r][:, idx:idx + 1], in1=d["v"][:, hs(h)],
                                                   op0=ALU.mult, op1=ALU.add)
                    for h in range(4):
                        idx = r * 4 + h
                        P.act.activation(out=t1[r][h][:], in_=po[r][h][:], func=AF.Copy, scale=sc[:, 8 + idx:9 + idx])
                def t3_():
                    for h in range(4):
                        P.pe.matmul(out=pa[r][h][:], lhsT=d["T2T"][:, hs(h)], rhs=rr[r][h][:], start=True, stop=True)
                def t4_():
                    for h in range(4):
                        idx = r * 4 + h
                        P.act.activation(out=vn[r][h][:], in_=pa[r][h][:], func=AF.Copy, scale=sc[:, idx:idx + 1])
                    for h in range(4):
                        idx = r * 4 + h
                        P.pool.tensor_scalar(out=vt[r][h][:], in0=vn[r][h][:], scalar1=sc[:, 16 + idx:17 + idx], scalar2=None, op0=ALU.mult)
                def t5_():
                    for h in range(4):
                        P.pe.matmul(out=pa[r][h][:], lhsT=d["k"][:, hs(h)], rhs=vt[r][h][:], start=True, stop=True)
                    for h in range(4):
                        P.pe.matmul(out=po[r][h][:], lhsT=d["QKT"][:, hs(h)], rhs=vn[r][h][:], start=True, stop=True)
                def t6_():
                    for h in range(4):
                        idx = r * 4 + h
                        P.dve.scalar_tensor_tensor(out=S[r][h][nxt][:], in0=S[r][h][cur][:], scalar=sc[:, 24 + idx:25 + idx],
                                                   in1=pa[r][h][:], op0=ALU.mult, op1=ALU.add)
                    for h in range(4):
                        P.dve.tensor_tensor(out=oo[r][b][:, hs(h)], in0=po[r][h][:], in1=t1[r][h][:], op=ALU.add)
                return [t1_, t2_, t3_, t4_, t5_, t6_]
            sA, sB = scan_stages(0), scan_stages(1)
            for k_ in range(len(sA) + 1):
                if k_ < len(sA):
                    sA[k_]()
                if k_ >= 1:
                    sB[k_ - 1]()
            for r in range(2):
                n = tl[r]
                dst = (self.OF, self.OB)[r]
                P.sp.dma_start(out=dst.h.ap()[n * 128:(n + 1) * 128, :], in_=oo[r][b][:], _writes=[dst])
            cur = nxt
        P.end()

    def mixer_a(self, l):
        P = self.P
        P.begin()
        PFM = self.PFM.h.ap()
        YT = self.YT.h.ap()
        cwa = P.sb("cwa", [128, 2, 3])
        for ct in range(2):
            P.sp.dma_start(out=cwa[:, ct, :], in_=self.conv_a.h.ap()[l][:, ct * 128:(ct + 1) * 128].rearrange("j c -> c j"),
                           allow_slow_non_contiguous=True)
        ca = [P.sb("ca%d" % i, [128, T]) for i in range(2)]
        Bt = [P.sb("Bt%d" % i, [128, T]) for i in range(2)]
        acc = [P.sb("acc%d" % i, [128, T]) for i in range(2)]
        ya = [P.sb("ya%d" % i, [128, T], BF16) for i in range(2)]
        it = 0
        for s, (c0, n) in enumerate(((0, TC), (TC, T))):
            if s == 0 and l == DEPTH - 1:
                continue
            for ct in range(2):
                b = it % 2
                it += 1
                P.sp.dma_start(out=ca[b][:, 0:n], in_=PFM[256 + ct * 128:256 + (ct + 1) * 128, c0:c0 + n], _reads=[self.PFM])
                P.sp.dma_start(out=Bt[b][:, 0:n], in_=PFM[ct * 128:(ct + 1) * 128, c0:c0 + n], _reads=[self.PFM])
                w = lambda j: cwa[:, ct, j:j + 1]
                P.pool.tensor_scalar(out=acc[b][:, 0:n], in0=ca[b][:, 0:n], scalar1=w(1), scalar2=None, op0=ALU.mult)
                if s == 0:
                    sh = [(acc[b][:, 1:n], ca[b][:, 0:n - 1], 0), (acc[b][:, 0:n - 1], ca[b][:, 1:n], 2)]
                elif ct == 0:
                    av = acc[b][:, 0:n].rearrange("p (r c) -> p r c", c=64)
                    cv = ca[b][:, 0:n].rearrange("p (r c) -> p r c", c=64)
                    sh = [(av[:, :, 1:64], cv[:, :, 0:63], 0), (av[:, :, 0:63], cv[:, :, 1:64], 2)]
                else:
                    sh = [(acc[b][:, 64:n], ca[b][:, 0:n - 64], 0), (acc[b][:, 0:n - 64], ca[b][:, 64:n], 2)]
                for (dst, src, j) in sh:
                    P.dve.scalar_tensor_tensor(out=dst, in0=src, scalar=w(j), in1=dst, op0=ALU.mult, op1=ALU.add)
                P.pool.tensor_tensor(out=ya[b][:, 0:n], in0=acc[b][:, 0:n], in1=Bt[b][:, 0:n], op=ALU.mult)
                P.sp.dma_start(out=YT[ct * 128:(ct + 1) * 128, c0:c0 + n], in_=ya[b][:, 0:n], _writes=[self.YT])
        P.end()

    def post_norm_residual(self, py, xt, G, tmp, xo, ssq, junk):
        P = self.P
        for hf in range(2):
            P.act.activation(out=junk[:, 0:512], in_=py[hf][:], func=AF.Square, accum_out=ssq[:, hf:hf + 1])
        ss = P.sm("pss", 1)
        P.dve.tensor_tensor(out=ss[:], in0=ssq[:, 0:1], in1=ssq[:, 1:2], op=ALU.add)
        rstd = self.rstd_chain(ss[:], 1, 1.0 / D, "prs")
        for hf in range(2):
            P.dve.scalar_tensor_tensor(out=tmp[:, hf * 512:(hf + 1) * 512], in0=py[hf][:], scalar=rstd[:, 0:1],
                                       in1=G[:, hf * 512:(hf + 1) * 512], op0=ALU.mult, op1=ALU.mult)
        P.pool.tensor_tensor(out=xo[:], in0=tmp[:], in1=xt[:], op=ALU.add)

    def mix_out(self, l, src):
        P = self.P
        c = self.c
        last = l == DEPTH - 1
        P.keep_begin()
        Wo = P.sbk("Wo", [128, 8, D], BF16)
        P.begin()
        stg = [P.sb("stgo%d" % i, [128, D]) for i in range(2)]
        for kc in range(8):
            P.sp.dma_start(out=stg[kc % 2][:], in_=self.w_o.h.ap()[l][kc * 128:(kc + 1) * 128, :])
            (P.act.copy if kc % 2 else P.dve.tensor_copy)(out=Wo[:, kc, :], in_=stg[kc % 2][:])
        P.end()
        P.begin()
        gon = P.sb("gon", [128, 128])
        P.sp.dma_start(out=gon[:], in_=self.g_onorm.h.ap()[l:l + 1, :].broadcast_to([128, 128]))
        G = [self.load_mod(l, 2, 1, "Gpc"), self.load_mod(l, 2, 0, "Gpx")]
        of = [P.sb("of%d" % i, [128, 512]) for i in range(2)]
        ob = [P.sb("ob%d" % i, [128, 512]) for i in range(2)]
        zs = [P.sb("zs%d" % i, [128, 512]) for i in range(2)]
        xt = [P.sb("xt%d" % i, [128, D]) for i in range(2)]
        yac = [P.sb("yac%d" % i, [128, 4, 128], BF16) for i in range(2)]
        o = P.sb("o", [128, 512])
        sq = P.sb("sq", [128, 512])
        y1 = P.sb("y1", [128, 512])
        y2 = P.sb("y2", [128, 512])
        yb = P.sb("yb", [128, 512], BF16)
        ybT = [P.sb("ybT%d" % i, [128, 4, 128], BF16) for i in range(2)]
        tmp = P.sb("tmp", [128, D])
        junk = P.sb("junk", [128, 512])
        xo = [P.sb("xo%d" % i, [128, D]) for i in range(2)]
        pT = P.ps("pT", [128, 1024], BF16)
        py = [[P.ps("py%d%d" % (i, hf), [128, 512]) for hf in range(2)] for i in range(2)]
        YT = self.YT.h.ap()
        h4 = lambda ap: ap.rearrange("p (h d) -> p h d", h=4)
        tl = list(range(2 if last else 0, NTILE))

        def front(i):
            n = tl[i]
            b = i % 2
            r0 = n * 128
            P.sp.dma_start(out=of[b][:], in_=self.OF.h.ap()[r0:r0 + 128, :], _reads=[self.OF])
            P.sp.dma_start(out=ob[b][:], in_=self.OB.h.ap()[r0:r0 + 128, :], _reads=[self.OB])
            P.sp.dma_start(out=zs[b][:], in_=self.ZS.h.ap()[r0:r0 + 128, :], _reads=[self.ZS])
            P.sp.dma_start(out=xt[b][:], in_=src.h.ap()[r0:r0 + 128, :], _reads=[src])
            P.sp.dma_start(out=yac[b][:, 0:2, :], in_=YT[0:256, r0:r0 + 128].rearrange("(ct p) t -> p ct t", p=128), _reads=[self.YT])
            P.sp.dma_start(out=yac[b][:, 2:4, :], in_=YT[768:1024, r0:r0 + 128].rearrange("(ct p) t -> p ct t", p=128), _reads=[self.YT])
            P.pool.tensor_tensor(out=o[:], in0=of[b][:], in1=ob[b][:], op=ALU.add)
            P.pool.tensor_tensor(out=sq[:], in0=o[:], in1=o[:], op=ALU.mult)
            ss4 = P.sm("ss4", 4)
            P.dve.tensor_reduce(out=ss4[:], in_=h4(sq[:]), axis=AX.X, op=ALU.add)
            rs = self.rstd_chain(ss4[:], 4, 1.0 / 128, "rso")
            P.dve.tensor_tensor(out=h4(y1[:]), in0=h4(o[:]), in1=rs[:].unsqueeze(2).broadcast_to([128, 4, 128]), op=ALU.mult)
            P.pool.tensor_tensor(out=h4(y2[:]), in0=h4(y1[:]), in1=gon[:].unsqueeze(1).broadcast_to([128, 4, 128]), op=ALU.mult)
            P.dve.tensor_tensor(out=yb[:], in0=y2[:], in1=zs[b][:], op=ALU.mult)
            for h in range(4):
                P.pe.transpose(out=pT[:, h * 128:(h + 1) * 128], in_=yb[:, h * 128:(h + 1) * 128], identity=c["ident_bf"][:])
            P.act.copy(out=ybT[b][:], in_=pT[:, 0:512].rearrange("p (h t) -> p h t", h=4))
            lhs = [yac[b][:, 0, :], yac[b][:, 1, :]] + [ybT[b][:, h, :] for h in range(4)] + [yac[b][:, 2, :], yac[b][:, 3, :]]
            for hf in range(2):
                for kc in range(8):
                    P.pe.matmul(out=py[b][hf][:], lhsT=lhs[kc], rhs=Wo[:, kc, hf * 512:(hf + 1) * 512], start=(kc == 0), stop=(kc == 7))

        def back(i):
            n = tl[i]
            b = i % 2
            s_ = 0 if n < 2 else 1
            r0 = n * 128
            ssq = P.sm("ssq", 2)
            self.post_norm_residual(py[b], xt[b], G[s_], tmp, xo[b], ssq, junk)
            P.sp.dma_start(out=self.XS.h.ap()[r0:r0 + 128, :], in_=xo[b][:], _writes=[self.XS])
        front(0)
        for i in range(len(tl)):
            if i + 1 < len(tl):
                front(i + 1)
            back(i)
        P.end()
        P.keep_end()

    def ffn(self, l):
        P = self.P
        c = self.c
        last = l == DEPTH - 1
        NJ = FFN_H // 128
        P.keep_begin()
        W1 = P.sbk("W1", [128, 8, 2 * FFN_H], BF16)
        W2 = P.sbk("W2", [128, NJ, D], BF16)
        P.begin()
        stg = [P.sb("stgf%d" % i, [128, 2 * FFN_H]) for i in range(2)]
        w1v = self.w_ffn_in.h.ap()[l]
        engs = (P.act.copy, P.dve.tensor_copy, P.pool.tensor_copy, P.act.copy)
        for kc in range(8):
            st = stg[kc % 2]
            P.sp.dma_start(out=st[:], in_=w1v[kc * 128:(kc + 1) * 128, :])
            for q in range(4):
                engs[q](out=W1[:, kc, q * 1408:(q + 1) * 1408], in_=st[:, q * 1408:(q + 1) * 1408])
        w2v = self.w_ffn_out.h.ap()[l].rearrange("(j p) n -> p j n", p=128)
        for jj, (j0, j1) in enumerate(((0, 5), (5, 10), (10, 15), (15, 20), (20, 22))):
            st = stg[jj % 2]
            nj = j1 - j0
            P.sp.dma_start(out=st[:, 0:nj * D].rearrange("p (j n) -> p j n", n=D), in_=w2v[:, j0:j1, :])
            for q in range(nj):
                engs[q % 3](out=W2[:, j0 + q, :], in_=st[:, q * D:(q + 1) * D])
        P.end()

        P.begin()
        S = [P.sb("Sf", [128, D]), None]
        Gm = [P.sb("Gf", [128, D]), None]
        Gp = [P.sb("Gpf", [128, D]), None]
        MBv = self.MB.h.ap()

        def load_mods(s):
            ms = 1 - s
            for t, k in ((S[0], 3), (Gm[0], 4), (Gp[0], 5)):
                P.sp.dma_start(out=t[:], in_=MBv[l, k, ms:ms + 1, :].broadcast_to([128, D]), _reads=[self.MB])
        xt = [[P.sb("xt%d%d" % (i, t), [128, D]) for t in range(2)] for i in range(2)]
        h1 = P.sb("h1", [128, D])
        hb = P.sb("hb", [128, D], BF16)
        hT = [P.sb("hT%d" % i, [128, 8, 256], BF16) for i in range(2)]
        sg = [P.sb("sg%d" % i, [128, 256]) for i in range(2)]
        tmp = P.sb("tmp", [128, D])
        xo = [P.sb("xo%d" % i, [128, D]) for i in range(2)]
        pT = P.ps("pT", [128, 1024], BF16)
        pg = P.ps("pg", [128, 512])
        pu = P.ps("pu", [128, 512])
        py = [[P.ps("py%d%d" % (t, hf), [128, 512]) for hf in range(2)] for t in range(2)]
        groups = list(range(1 if last else 0, NT // 256))
        LAG = 3
        aT = [P.sb("aTr%d" % i, [128, 256], BF16) for i in range(LAG + 2)]
        state = {"s": None}

        def pre_elem(gi, t):
            g = groups[gi]
            b = gi % 2
            s_ = 0 if g == 0 else 1
            if s_ != state["s"]:
                load_mods(s_)
                state["s"] = s_
            r0 = g * 256 + t * 128
            P.sp.dma_start(out=xt[b][t][:], in_=self.XS.h.ap()[r0:r0 + 128, :], _reads=[self.XS])
            ss = P.sm("ss", 1)
            P.act.activation(out=tmp[:], in_=xt[b][t][:], func=AF.Square, accum_out=ss[:])
            rstd = self.rstd_chain(ss[:], 1, 1.0 / D, "rsf")
            P.dve.scalar_tensor_tensor(out=h1[:], in0=xt[b][t][:], scalar=rstd[:, 0:1], in1=Gm[0][:], op0=ALU.mult, op1=ALU.mult)
            P.pool.tensor_tensor(out=hb[:], in0=h1[:], in1=S[0][:], op=ALU.add)

        def pre_tr(gi, t):
            b = gi % 2
            for kc in range(8):
                P.pe.transpose(out=pT[:, kc * 128:(kc + 1) * 128], in_=hb[:, kc * 128:(kc + 1) * 128], identity=c["ident_bf"][:])
            P.act.copy(out=hT[b][:, :, t * 128:(t + 1) * 128], in_=pT[:].rearrange("p (kc t) -> p kc t", kc=8))

        def second(j):
            a = aT[j % (LAG + 2)]
            for t in range(2):
                for hf in range(2):
                    P.pe.matmul(out=py[t][hf][:], lhsT=a[:, t * 128:(t + 1) * 128], rhs=W2[:, j, hf * 512:(hf + 1) * 512],
                                start=(j == 0), stop=(j == NJ - 1))

        def post(gi):
            g = groups[gi]
            b = gi % 2
            for t in range(2):
                r0 = g * 256 + t * 128
                ssq = P.sm("ssqf", 2)
                self.post_norm_residual(py[t], xt[b][t], Gp[0], tmp, xo[t], ssq, h1)
                if last:
                    P.sp.dma_start(out=self.out.h.ap()[r0 - TC:r0 - TC + 128, :], in_=xo[t][:], _writes=[self.out])
                else:
                    P.sp.dma_start(out=self.XB.h.ap()[r0:r0 + 128, :], in_=xo[t][:], _writes=[self.XB])

        for t in range(2):
            pre_elem(0, t)
            pre_tr(0, t)
        for gi, g in enumerate(groups):
            b = gi % 2
            nxt_ok = gi + 1 < len(groups)
            same_mods = nxt_ok and ((0 if groups[gi + 1] == 0 else 1) == state["s"])
            for j in range(NJ):
                for kc in range(8):
                    P.pe.matmul(out=pg[:, 0:256], lhsT=W1[:, kc, j * 128:(j + 1) * 128], rhs=hT[b][:, kc, :], start=(kc == 0), stop=(kc == 7))
                for kc in range(8):
                    P.pe.matmul(out=pu[:, 0:256], lhsT=W1[:, kc, FFN_H + j * 128:FFN_H + (j + 1) * 128], rhs=hT[b][:, kc, :], start=(kc == 0), stop=(kc == 7))
                if j >= LAG:
                    second(j - LAG)
                P.act.activation(out=sg[j % 2][:], in_=pg[:, 0:256], func=AF.Silu)
                P.dve.tensor_tensor(out=aT[j % (LAG + 2)][:], in0=pu[:, 0:256], in1=sg[j % 2][:], op=ALU.mult)
                if same_mods:
                    if j == 5:
                        pre_elem(gi + 1, 0)
                    elif j == 10:
                        pre_tr(gi + 1, 0)
                    elif j == 12:
                        pre_elem(gi + 1, 1)
                    elif j == 17:
                        pre_tr(gi + 1, 1)
            for j in range(NJ - LAG, NJ):
                second(j)
            post(gi)
            if nxt_ok and not same_mods:
                for t in range(2):
                    pre_elem(gi + 1, t)
                    pre_tr(gi + 1, t)
        P.end()
        P.keep_end()

    def forward(self, upto=None):
        self.consts()
        for l in range(DEPTH):
            src = self.xs_in if l == 0 else self.XB
            self.mods(l)
            self.prenorm(l, 0, 1, src)
            self.proj(l)
            self.gdn_prep(l)
            self.gdn_scan(l)
            self.mixer_a(l)
            self.mix_out(l, src)
            self.ffn(l)
        self.P.close()


W_NAMES = ["w_mod", "b_mod", "g_pre_mix", "g_post_mix", "g_pre_ffn", "g_post_ffn", "w_in", "conv_a", "conv_qkv",
           "a_log", "dt_bias", "g_onorm", "ln_c_g", "ln_c_b", "w_s", "b_s", "w_o", "w_ffn_in", "w_ffn_out"]


def make_in_maps(inputs, cores=range(8)):
    f = lambda a: np.ascontiguousarray(np.asarray(a, dtype=np.float32))
    shared = {n: f(inputs[n]) for n in W_NAMES}
    shared["a_log"] = shared["a_log"].reshape(DEPTH, 8)
    shared["dt_bias"] = shared["dt_bias"].reshape(DEPTH, 8)
    x, c, ctx, c_ctx = f(inputs["x"]), f(inputs["c"]), f(inputs["ctx"]), f(inputs["c_ctx"])
    maps = []
    for b in cores:
        m = dict(shared)
        m["xs"] = np.ascontiguousarray(np.concatenate([ctx[b], x[b]], axis=0))
        cc = np.stack([c[b], c_ctx], axis=0)
        m["ccT"] = np.ascontiguousarray(cc.reshape(2, 8, 128).transpose(2, 1, 0))
        maps.append(m)
    return maps


_CACHE = {}


def kernel(**inputs):
    if "nc" not in _CACHE:
        nc = bass.Bass("TRN2", target_bir_lowering=False)
        Model(nc).forward()
        _CACHE["nc"] = nc
    nc = _CACHE["nc"]
    maps = make_in_maps(inputs)
    res = run_bass_kernel_spmd(nc, maps, core_ids=list(range(8)))
    return np.stack([np.asarray(r["out"], dtype=np.float32) for r in res.results], axis=0)
```

```python
import numpy as np
from contextlib import ExitStack
import concourse.bass as bass
import concourse.mybir as mybir
from concourse.bass_utils import run_bass_kernel_spmd

F32 = mybir.dt.float32
F32R = mybir.dt.float32r
BF16 = mybir.dt.bfloat16
AF = mybir.ActivationFunctionType
ALU = mybir.AluOpType
AX = mybir.AxisListType

ENGS = ("tensor", "vector", "scalar", "gpsimd", "sync")
SAME_ENGINE_SYNC = True


class Tile:
    def __init__(self, prog, h, name, dram=False):
        self.prog = prog
        self.h = h
        self.name = name
        self.dram = dram
        self.lastw = None
        self.readers = []
        prog.tiles[name] = self

    def __getitem__(self, idx):
        return self.h[idx]

    def ap(self):
        return self.h.ap() if self.dram else self.h[:]


class SubTile:
    def __init__(self, parent, c0, w, name):
        self.parent = parent
        self.c0 = c0
        self.w = w
        self.name = name
        self.lastw = None
        self.readers = []

    def __getitem__(self, idx):
        return self.parent.h[:, self.c0:self.c0 + self.w][idx]


class Op:
    __slots__ = ("eng", "fn", "deps", "waits", "sig", "sigval", "is_dma", "dsem", "dval", "idx", "is_load", "F")

    def __init__(self, eng, fn, is_dma=False):
        self.eng = eng
        self.fn = fn
        self.deps = []
        self.waits = []
        self.sig = False
        self.sigval = None
        self.is_dma = is_dma
        self.dsem = None
        self.dval = None


class EngProxy:
    def __init__(self, prog, eng):
        self.prog = prog
        self.eng = eng

    def __getattr__(self, meth):
        prog, eng = self.prog, self.eng

        def call(*args, **kwargs):
            reads, writes = [], []
            for k, v in kwargs.items():
                if isinstance(v, bass.AP):
                    t = prog.tiles.get(v.tensor.name)
                    if t is None:
                        continue
                    if getattr(t, "subs", None):
                        t = t.subs[(v.offset % t.rowlen) // t.subw]
                    if k in ("out", "accum_out") or (k == "ap" and meth == "memset"):
                        writes.append(t)
                    else:
                        reads.append(t)
            extra_r = kwargs.pop("_reads", [])
            extra_w = kwargs.pop("_writes", [])
            reads += extra_r
            writes += extra_w
            is_dma = meth in ("dma_start",)
            if meth == "matmul" and kwargs.get("start", True) is False:
                pass
            fn = lambda e, meth=meth, args=args, kwargs=kwargs: getattr(e, meth)(*args, **kwargs)
            return prog.record(eng, fn, reads, writes, is_dma)

        return call


class Prog:
    def __init__(self, nc):
        self.nc = nc
        self.es = ExitStack()
        self.tiles = {}
        self.csem = {e: self.es.enter_context(nc.semaphore("c_" + e)) for e in ENGS}
        self.cnt = {e: 0 for e in ENGS}
        self.known = {e: {} for e in ENGS}
        self.ops = []
        self.dma_sems = []
        self.ndma_sems = 24
        for i in range(self.ndma_sems):
            self.dma_sems.append([self.es.enter_context(nc.semaphore("d%d" % i)), 0, None])
        self.dma_rr = 0
        self.phase_es = None
        for e in ENGS:
            setattr(self, e[0] if e != "sync" else "sp", EngProxy(self, e))
        self.pe = EngProxy(self, "tensor")
        self.dve = EngProxy(self, "vector")
        self.act = EngProxy(self, "scalar")
        self.pool = EngProxy(self, "gpsimd")
        self.sp = EngProxy(self, "sync")
        self.uid = 0

    def sb(self, name, shape, dtype=F32):
        self.uid += 1
        name = "%s_%d" % (name, self.uid)
        h = self.phase_es.enter_context(self.nc.sbuf_tensor(name, list(shape), dtype))
        return Tile(self, h, name)

    def sbc(self, name, shape, dtype=F32):
        self.uid += 1
        name = "%s_%d" % (name, self.uid)
        h = self.es.enter_context(self.nc.sbuf_tensor(name, list(shape), dtype))
        return Tile(self, h, name)

    def sm(self, name, cols, depth=4):
        key = (name, cols)
        ring = self.rings.setdefault(key, [[], 0])
        if len(ring[0]) < depth:
            ring[0].append(self.sb(name, [128, cols]))
            return ring[0][-1]
        ring[1] += 1
        return ring[0][ring[1] % depth]

    def keep_begin(self):
        self.keep_es = ExitStack()

    def keep_end(self):
        self.keep_es.close()
        self.keep_es = None

    def sbk(self, name, shape, dtype=F32):
        self.uid += 1
        name = "%s_%d" % (name, self.uid)
        h = self.keep_es.enter_context(self.nc.sbuf_tensor(name, list(shape), dtype))
        return Tile(self, h, name)

    def ps(self, name, shape, dtype=F32):
        self.uid += 1
        name = "%s_%d" % (name, self.uid)
        h = self.phase_es.enter_context(self.nc.psum_tensor(name, list(shape), dtype))
        t = Tile(self, h, name)
        t.psum = True
        return t

    def psb(self, name, n, w, dtype=F32):
        rowlen = 512 if dtype == F32 else 1024
        assert n * w <= rowlen
        t = self.ps(name, [128, rowlen], dtype)
        return [SubTile(t, i * w, w, "%s.%d" % (t.name, i)) for i in range(n)]

    def dram(self, name, shape, dtype=F32, kind="Internal"):
        h = self.nc.dram_tensor(name, list(shape), dtype, kind=kind)
        return Tile(self, h, name, dram=True)

    def record(self, eng, fn, reads, writes, is_dma=False):
        op = Op(eng, fn, is_dma)
        op.idx = len(self.ops)
        op.is_load = is_dma and any(not getattr(t, "dram", False) for t in writes)
        deps = []
        for t in reads:
            if t.lastw is not None:
                deps.append(t.lastw)
            if getattr(t, "psum", False):
                deps.extend(rd for rd in t.readers if rd.eng != eng)
        for t in writes:
            if t.lastw is not None:
                deps.append(t.lastw)
            deps.extend(t.readers)
        for t in reads:
            t.readers.append(op)
        for t in writes:
            t.lastw = op
            t.readers = []
        seen = set()
        for d in deps:
            if id(d) in seen or d is op:
                continue
            seen.add(id(d))
            op.deps.append(d)
        if is_dma:
            slot = self.dma_sems[self.dma_rr % self.ndma_sems]
            self.dma_rr += 1
            prev = slot[2]
            if prev is not None:
                op.deps.append(prev)
            slot[1] += 16
            slot[2] = op
            op.dsem = slot[0]
            op.dval = slot[1]
        self.ops.append(op)
        return op

    def begin(self):
        self.phase_es = ExitStack()
        self.ops = []
        self.rings = {}

    def end(self, final=False):
        nc = self.nc
        ops = self.ops
        last = {e: None for e in ENGS}
        for op in ops:
            if not op.is_dma:
                last[op.eng] = op
        pend_dma = [s[2] for s in self.dma_sems if s[2] is not None]
        for op in ops:
            for d in op.deps:
                if d.is_dma:
                    continue
                if d.eng != op.eng or (SAME_ENGINE_SYNC and d.eng != "tensor") or op.is_dma:
                    d.sig = True
        for e in ENGS:
            if last[e] is not None:
                last[e].sig = True
        for op in ops:
            if op.is_dma:
                continue
            if op.sig:
                self.cnt[op.eng] += 1
                op.sigval = self.cnt[op.eng]
        sp_ops = [op for op in ops if op.eng == "sync"]
        spidx = {id(op): i for i, op in enumerate(sp_ops)}
        lastF = {e: -1 for e in ENGS}
        for op in ops:
            f = -1
            for d in op.deps:
                if id(d) in spidx:
                    f = max(f, spidx[id(d)])
                elif getattr(d, "F", None) is not None:
                    f = max(f, d.F)
            if op.eng != "sync":
                f = max(f, lastF[op.eng])
                lastF[op.eng] = f
            op.F = f
        keys = {}
        prev_load_key = -1.0
        for i, op in enumerate(sp_ops):
            if op.is_load:
                k = max(op.F + 0.5, prev_load_key)
                k = min(k, float(i))
                prev_load_key = k
                keys[id(op)] = k
            else:
                keys[id(op)] = float(i)
        sp_sorted = sorted(range(len(sp_ops)), key=lambda i: (keys[id(sp_ops[i])], i))
        sp_new = [sp_ops[i] for i in sp_sorted]
        per = {e: [op for op in ops if op.eng == e] for e in ENGS}
        per["sync"] = sp_new
        for e in ENGS:
            kn = self.known[e]
            for op in per[e]:
                for d in op.deps:
                    if d.is_dma:
                        key, val, sem = ("d", id(d.dsem)), d.dval, d.dsem
                    else:
                        if d.sigval is None:
                            continue
                        if d.eng == op.eng and not op.is_dma and (d.eng == "tensor" or not SAME_ENGINE_SYNC):
                            continue
                        key, val, sem = ("c", d.eng), d.sigval, self.csem[d.eng]
                    if kn.get(key, 0) >= val:
                        continue
                    kn[key] = val
                    op.waits.append((sem, val))
        bar = {}
        for e in ENGS:
            w = []
            kn = self.known[e]
            for e2 in ENGS:
                if last[e2] is None:
                    continue
                v = last[e2].sigval
                if kn.get(("c", e2), 0) < v:
                    kn[("c", e2)] = v
                    w.append((self.csem[e2], v))
            for d in pend_dma:
                key = ("d", id(d.dsem))
                if kn.get(key, 0) < d.dval:
                    kn[key] = d.dval
                    w.append((d.dsem, d.dval))
            bar[e] = w
        for s in self.dma_sems:
            s[2] = None

        with nc.Block() as block:
            def emit(e):
                def body(eng):
                    for op in per[e]:
                        for (sem, val) in op.waits:
                            eng.wait_ge(sem, val)
                        ins = op.fn(eng)
                        if op.is_dma:
                            ins.then_inc(op.dsem, 16)
                        elif op.sig:
                            ins.then_inc(self.csem[e], 1)
                    for (sem, val) in bar[e]:
                        eng.wait_ge(sem, val)
                return body
            block.tensor(emit("tensor"))
            block.vector(emit("vector"))
            block.scalar(emit("scalar"))
            block.gpsimd(emit("gpsimd"))
            block.sync(emit("sync"))
        for t in self.tiles.values():
            t.lastw = None
            t.readers = []
            for st in (getattr(t, "subs", None) or []):
                st.lastw = None
                st.readers = []
        self.phase_es.close()
        self.phase_es = None
        self.ops = []

    def close(self):
        self.es.close()


D = 1024
T = 4096
TC = 256
NT = TC + T
NTILE = NT // 128
DEPTH = 2
EPS = 1e-6
IN_COLS = 3344
OFF_A_B, OFF_A_C, OFF_A_H, OFF_Q, OFF_K, OFF_V = 0, 256, 512, 768, 1280, 1792
OFF_AB, OFF_Z, OFF_CU, OFF_CV = 2304, 2320, 2832, 3088
FFN_H = 2816
SCN_W = 8 * 512 + 32
GROUPS = [(0, 256, 0, 0)] + [(256 + 512 * i, 512, 1, 512 * i) for i in range(8)]


class Model:
    def __init__(self, nc, debug=False):
        self.nc = nc
        self.P = P = Prog(nc)
        self.debug = debug
        k_in = "ExternalInput"
        k_sc = "ExternalOutput" if debug else "Internal"
        self.xs_in = P.dram("xs", [NT, D], F32, k_in)
        self.ccT = P.dram("ccT", [128, 8, 2], F32, k_in)
        self.w_mod = P.dram("w_mod", [DEPTH, D, 6 * D], F32, k_in)
        self.b_mod = P.dram("b_mod", [DEPTH, 6 * D], F32, k_in)
        self.g_pre_mix = P.dram("g_pre_mix", [DEPTH, D], F32, k_in)
        self.g_post_mix = P.dram("g_post_mix", [DEPTH, D], F32, k_in)
        self.g_pre_ffn = P.dram("g_pre_ffn", [DEPTH, D], F32, k_in)
        self.g_post_ffn = P.dram("g_post_ffn", [DEPTH, D], F32, k_in)
        self.w_in = P.dram("w_in", [DEPTH, D, IN_COLS], F32, k_in)
        self.conv_a = P.dram("conv_a", [DEPTH, 3, 256], F32, k_in)
        self.conv_qkv = P.dram("conv_qkv", [DEPTH, 3, 1536], F32, k_in)
        self.a_log = P.dram("a_log", [DEPTH, 8], F32, k_in)
        self.dt_bias = P.dram("dt_bias", [DEPTH, 8], F32, k_in)
        self.g_onorm = P.dram("g_onorm", [DEPTH, 128], F32, k_in)
        self.ln_c_g = P.dram("ln_c_g", [DEPTH, 256], F32, k_in)
        self.ln_c_b = P.dram("ln_c_b", [DEPTH, 256], F32, k_in)
        self.w_s = P.dram("w_s", [DEPTH, 4, 128, 128], F32, k_in)
        self.b_s = P.dram("b_s", [DEPTH, 4, 128], F32, k_in)
        self.w_o = P.dram("w_o", [DEPTH, D, D], F32, k_in)
        self.w_ffn_in = P.dram("w_ffn_in", [DEPTH, D, 2 * FFN_H], F32, k_in)
        self.w_ffn_out = P.dram("w_ffn_out", [DEPTH, FFN_H, D], F32, k_in)
        self.out = P.dram("out", [T, D], F32, "ExternalOutput")
        self.XS = P.dram("XS", [NT, D], F32, k_sc)
        self.MB = P.dram("MB", [DEPTH, 6, 2, D], F32, k_sc)
        self.HT = [P.dram("HTc", [8, 128, TC + 2], BF16, k_sc), P.dram("HTx", [8, 128, T + 2], BF16, k_sc)]
        self.PFM = P.dram("PFM", [512, NT], F32, k_sc)
        self.YT = P.dram("YT", [1024, NT], BF16, k_sc)
        self.ZS = P.dram("ZS", [NT, 512], F32, k_sc)
        self.QS = P.dram("QS", [NT, 528], F32, k_sc)
        self.SCN = P.dram("SCN", [NTILE, 128, SCN_W], F32, k_sc)
        self.SC2 = P.dram("SC2", [NTILE, 128, 6 * 512 + 32], F32, k_sc)
        self.XB = P.dram("XB", [NT, D], F32, k_sc)
        self.OF = P.dram("OF", [NT, 512], F32, k_sc)
        self.OB = P.dram("OB", [NT, 512], F32, k_sc)

    def consts(self):
        P = self.P
        c = self.c = {}
        for nm in ("ones", "U0", "U1", "SU0", "SU1", "NM0", "NM1", "ident", "NU0", "NU1"):
            c[nm] = P.sbc(nm, [128, 128])
        c["ident_bf"] = P.sbc("ident_bf", [128, 128], BF16)
        P.begin()
        ones = c["ones"]
        zer = P.sb("zer", [128, 128])
        P.pool.memset(ap=ones[:], constant=1.0)
        P.pool.memset(ap=zer[:], constant=0.0)

        def sel(name, src, step, cm, base, op, fill):
            t = c[name]
            P.pool.affine_select(out=t[:], in_=src[:], pattern=[[step, 128]], compare_op=op,
                                 fill=fill, base=base, channel_multiplier=cm)
            return t
        sel("U0", ones, 1, -1, 0, ALU.is_ge, 0.0)
        sel("U1", ones, -1, 1, 0, ALU.is_ge, 0.0)
        sel("SU0", ones, 1, -1, -1, ALU.is_ge, 0.0)
        sel("SU1", ones, -1, 1, -1, ALU.is_ge, 0.0)
        sel("NM0", zer, 1, -1, 0, ALU.is_ge, -30000.0)
        sel("NM1", zer, -1, 1, 0, ALU.is_ge, -30000.0)
        sel("ident", ones, 1, -1, 0, ALU.is_equal, 0.0)
        for r in (0, 1):
            P.pool.tensor_scalar(out=c["NU%d" % r][:], in0=c["U%d" % r][:], scalar1=-1.0, scalar2=None, op0=ALU.mult)
        P.pool.tensor_copy(out=c["ident_bf"][:], in_=c["ident"][:])
        zb = P.sb("zb", [128, 8, 1], BF16)
        P.pool.memset(ap=zb[:], constant=0.0)
        for s, n in ((0, TC), (1, T)):
            v = self.HT[s].h.ap().rearrange("kc p t -> p kc t")
            P.sp.dma_start(out=v[:, :, 0:1], in_=zb[:], _writes=[self.HT[s]], allow_slow_non_contiguous=True)
            P.sp.dma_start(out=v[:, :, n + 1:n + 2], in_=zb[:], _writes=[self.HT[s]], allow_slow_non_contiguous=True)
        P.end()

    def mods(self, l):
        P = self.P
        P.begin()
        cT = P.sb("cT", [128, 8, 2])
        sT = P.sb("sT", [128, 8, 2])
        P.sp.dma_start(out=cT[:], in_=self.ccT.ap())
        P.act.activation(out=sT[:], in_=cT[:], func=AF.Silu)
        bm = P.sb("bm", [2, 6 * D])
        P.sp.dma_start(out=bm[:], in_=self.b_mod.h.ap()[l:l + 1, :].broadcast_to([2, 6 * D]))
        mods = P.sb("mods", [2, 6 * D])
        wbuf = [P.sb("wm%d" % i, [128, 8, 512]) for i in range(2)]
        pm = [P.ps("pm%d" % i, [2, 512]) for i in range(2)]
        wv = self.w_mod.h.ap()[l].rearrange("(kc p) n -> p kc n", p=128)
        for nb in range(12):
            wb = wbuf[nb % 2]
            P.sp.dma_start(out=wb[:], in_=wv[:, :, nb * 512:(nb + 1) * 512])
            pp = pm[nb % 2]
            for kc in range(8):
                P.pe.matmul(out=pp[:], lhsT=sT[:, kc, :], rhs=wb[:, kc, :], start=(kc == 0), stop=(kc == 7))
            P.dve.tensor_tensor(out=mods[:, nb * 512:(nb + 1) * 512], in0=pp[:], in1=bm[:, nb * 512:(nb + 1) * 512], op=ALU.add)
        gv = P.sb("gv", [2, 4, D])
        for i, g in enumerate((self.g_pre_mix, self.g_post_mix, self.g_pre_ffn, self.g_post_ffn)):
            P.sp.dma_start(out=gv[:, i, :], in_=g.h.ap()[l:l + 1, :].broadcast_to([2, D]))
        mb = P.sb("mb", [2, 6, D])
        m = lambda i: mods[:, i * D:(i + 1) * D]
        P.dve.tensor_copy(out=mb[:, 0, :], in_=m(0))
        P.dve.scalar_tensor_tensor(out=mb[:, 1, :], in0=m(1), scalar=1.0, in1=gv[:, 0, :], op0=ALU.add, op1=ALU.mult)
        P.dve.tensor_tensor(out=mb[:, 2, :], in0=m(2), in1=gv[:, 1, :], op=ALU.mult)
        P.dve.tensor_copy(out=mb[:, 3, :], in_=m(3))
        P.dve.scalar_tensor_tensor(out=mb[:, 4, :], in0=m(4), scalar=1.0, in1=gv[:, 2, :], op0=ALU.add, op1=ALU.mult)
        P.dve.tensor_tensor(out=mb[:, 5, :], in0=m(5), in1=gv[:, 3, :], op=ALU.mult)
        P.sp.dma_start(out=self.MB.h.ap()[l].rearrange("k s d -> s k d"), in_=mb[:], _writes=[self.MB])
        P.end()

    def load_mod(self, l, k, s, name):
        P = self.P
        t = P.sb(name, [128, D])
        P.sp.dma_start(out=t[:], in_=self.MB.h.ap()[l, k, s:s + 1, :].broadcast_to([128, D]), _reads=[self.MB])
        return t

    def rstd_chain(self, ss, n, scale, name, post=None):
        P = self.P
        t1 = P.sm(name + "a", n)
        t2 = P.sm(name + "b", n)
        t3 = P.sm(name + "c", n)
        P.dve.tensor_scalar(out=t1[:], in0=ss, scalar1=scale, scalar2=EPS, op0=ALU.mult, op1=ALU.add)
        P.act.activation(out=t2[:], in_=t1[:], func=AF.Sqrt)
        P.dve.reciprocal(out=t3[:], in_=t2[:])
        return t3

    def prenorm(self, l, ks, kg, src):
        P = self.P
        c = self.c
        P.begin()
        Sx = [self.load_mod(l, ks, 1, "Sc"), self.load_mod(l, ks, 0, "Sx")]
        Gx = [self.load_mod(l, kg, 1, "Gc"), self.load_mod(l, kg, 0, "Gx")]
        xt = [P.sb("xt%d" % i, [128, D]) for i in range(2)]
        junk = P.sb("junk", [128, D])
        h1 = [P.sb("h1%d" % i, [128, D]) for i in range(2)]
        hb = [P.sb("hb%d" % i, [128, D], BF16) for i in range(2)]
        pT = [P.ps("pT%d" % i, [128, D], BF16) for i in range(2)]
        hT = [P.sb("hT%d" % i, [128, 8, 512], BF16) for i in range(2)]
        tiles = []
        for gi, (row0, ntok, s, t0) in enumerate(GROUPS):
            for j in range(ntok // 128):
                tiles.append((gi, row0, ntok, s, t0, j))

        def front(i):
            gi, row0, ntok, s, t0, j = tiles[i]
            b = i % 2
            r0 = row0 + j * 128
            P.sp.dma_start(out=xt[b][:], in_=src.h.ap()[r0:r0 + 128, :], _reads=[src])
            ss = P.sm("ss", 1)
            P.act.activation(out=junk[:], in_=xt[b][:], func=AF.Square, accum_out=ss[:])
            rstd = self.rstd_chain(ss[:], 1, 1.0 / D, "rs")
            P.dve.scalar_tensor_tensor(out=h1[b][:], in0=xt[b][:], scalar=rstd[:, 0:1], in1=Gx[s][:], op0=ALU.mult, op1=ALU.mult)
            P.pool.tensor_tensor(out=hb[b][:], in0=h1[b][:], in1=Sx[s][:], op=ALU.add)

        def back(i):
            gi, row0, ntok, s, t0, j = tiles[i]
            b = i % 2
            hTg = hT[gi % 2]
            for kc in range(8):
                P.pe.transpose(out=pT[b][:, kc * 128:(kc + 1) * 128], in_=hb[b][:, kc * 128:(kc + 1) * 128], identity=c["ident_bf"][:])
            P.act.copy(out=hTg[:, :, j * 128:(j + 1) * 128], in_=pT[b][:].rearrange("p (kc t) -> p kc t", kc=8))
            if j == ntok // 128 - 1:
                dst = self.HT[s].h.ap().rearrange("kc p t -> p kc t")[:, :, 1 + t0:1 + t0 + ntok]
                P.sp.dma_start(out=dst, in_=hTg[:, :, 0:ntok], _writes=[self.HT[s]])
        front(0)
        for i in range(len(tiles)):
            if i + 1 < len(tiles):
                front(i + 1)
            back(i)
        P.end()

    def proj(self, l):
        P = self.P
        c = self.c
        P.keep_begin()
        Wfm = P.sbk("Wfm", [128, 8, 1024], BF16)
        Wrest = P.sbk("Wrest", [128, 8, 784], BF16)
        Wqkv = [P.sbk("Wq%d" % j, [128, 8, 1536], BF16) for j in range(3)]
        P.begin()
        cw = P.sb("cw", [128, 3, 1536])
        P.sp.dma_start(out=cw[:], in_=self.conv_qkv.h.ap()[l:l + 1].broadcast_to([128, 3, 1536]))
        stg = [P.sb("stg%d" % i, [128, IN_COLS]) for i in range(2)]
        wv = self.w_in.h.ap()[l]
        for kc in range(8):
            st = stg[kc % 2]
            P.sp.dma_start(out=st[:], in_=wv[kc * 128:(kc + 1) * 128, :])
            P.act.copy(out=Wfm[:, kc, 0:768], in_=st[:, 0:768])
            P.act.copy(out=Wfm[:, kc, 768:1024], in_=st[:, OFF_CU:OFF_CV])
            P.act.copy(out=Wrest[:, kc, 0:512], in_=st[:, OFF_Z:OFF_CU])
            P.act.copy(out=Wrest[:, kc, 512:528], in_=st[:, OFF_AB:OFF_Z])
            P.act.copy(out=Wrest[:, kc, 528:784], in_=st[:, OFF_CV:IN_COLS])
            for j in range(3):
                eng = (P.dve, P.pool, P.dve)[j]
                eng.tensor_tensor(out=Wqkv[j][:, kc, :], in0=st[:, OFF_Q:OFF_AB], in1=cw[:, j, :], op=ALU.mult)
        P.end()

        P.begin()
        pfm = [P.ps("pfm%d" % i, [128, 512]) for i in range(2)]
        prot = [P.ps("prot%d" % i, [128, 512]) for i in range(3)]
        pr = P.ps("pr", [128, 512])
        pc = [P.ps("pc%d" % i, [128, 128]) for i in range(2)]
        wsl = P.sb("wsl", [128, 4, 128])
        wsT = P.sb("wsT", [128, 4, 128])
        P.sp.dma_start(out=wsl[:], in_=self.w_s.h.ap()[l].rearrange("g p q -> p g q"))
        for g in range(4):
            P.pe.transpose(out=pr[:, g * 128:(g + 1) * 128], in_=wsl[:, g, :], identity=c["ident"][:])
        P.dve.tensor_copy(out=wsT[:], in_=pr[:].rearrange("p (g q) -> p g q", g=4))
        Bs = P.sb("Bs", [128, 2, 128])
        for g in range(4):
            P.sp.dma_start(out=Bs[(g % 2) * 64:(g % 2) * 64 + 64, g // 2, :],
                           in_=self.b_s.h.ap()[l, g:g + 1, :].broadcast_to([64, 128]))
        lng = P.sb("lng", [128, 256])
        lnb = P.sb("lnb", [128, 256])
        P.sp.dma_start(out=lng[:], in_=self.ln_c_g.h.ap()[l:l + 1, :].broadcast_to([128, 256]))
        P.sp.dma_start(out=lnb[:], in_=self.ln_c_b.h.ap()[l:l + 1, :].broadcast_to([128, 256]))
        hTb = [P.sb("hTg%d" % i, [128, 8, 514], BF16) for i in range(2)]
        evb = [P.sb("evb%d" % i, [128, 512]) for i in range(2)]
        Csb = [P.sb("Csb%d" % i, [128, 512]) for i in range(2)]
        uTb = [P.sb("uT%d" % i, [128, 2, 512]) for i in range(2)]
        ycTb = [P.sb("ycT%d" % i, [128, 2, 512], BF16) for i in range(2)]
        qsb = [P.sb("qsb%d" % i, [128, 512]) for i in range(2)]
        ksb = [P.sb("ksb%d" % i, [128, 512]) for i in range(2)]
        sq = [P.sb("sq%d" % i, [128, 512]) for i in range(2)]
        kvst = [P.sb("kvst%d" % i, [128, 2, 512]) for i in range(2)]
        qst = [P.sb("qst%d" % i, [128, 528]) for i in range(2)]
        zsb = [P.sb("zsb%d" % i, [128, 512]) for i in range(2)]
        cv = P.sb("cv", [128, 256])
        vn1 = P.sb("vn1", [128, 256])
        vn2 = P.sb("vn2", [128, 256])
        vnb = [P.sb("vn%d" % i, [128, 256]) for i in range(2)]
        pending = []
        tmpc = P.sb("tmpc", [128, 128])
        PFM = self.PFM.h.ap()
        it = 0
        rot = 0
        for gi, (row0, ntok, s, t0) in enumerate(GROUPS):
            hTg = hTb[gi % 2]
            uT = uTb[gi % 2]
            ycT = ycTb[gi % 2]
            src = self.HT[s].h.ap().rearrange("kc p t -> p kc t")[:, :, t0:t0 + ntok + 2]
            P.sp.dma_start(out=hTg[:, :, 0:ntok + 2], in_=src, _reads=[self.HT[s]])
            for cb in range(8):
                pf = pfm[cb % 2]
                for kc in range(8):
                    P.pe.matmul(out=pf[:, 0:ntok], lhsT=Wfm[:, kc, cb * 128:(cb + 1) * 128], rhs=hTg[:, kc, 1:1 + ntok],
                                start=(kc == 0), stop=(kc == 7))
                if cb < 2:
                    ev = evb[cb % 2]
                    P.act.copy(out=ev[:, 0:ntok], in_=pf[:, 0:ntok])
                    P.sp.dma_start(out=PFM[cb * 128:(cb + 1) * 128, row0:row0 + ntok], in_=ev[:, 0:ntok], _writes=[self.PFM])
                elif cb < 4:
                    P.act.copy(out=Csb[cb - 2][:, 0:ntok], in_=pf[:, 0:ntok])
                elif cb < 6:
                    ev = evb[cb % 2]
                    P.dve.tensor_tensor(out=ev[:, 0:ntok], in0=pf[:, 0:ntok], in1=Csb[cb - 4][:, 0:ntok], op=ALU.mult)
                    P.sp.dma_start(out=PFM[256 + (cb - 4) * 128:256 + (cb - 3) * 128, row0:row0 + ntok], in_=ev[:, 0:ntok], _writes=[self.PFM])
                else:
                    P.act.activation(out=uT[:, cb - 6, 0:ntok], in_=pf[:, 0:ntok], func=AF.Gelu)
            for j in range(ntok // 128):
                b = it % 2
                it += 1
                n = (row0 + j * 128) // 128
                off = j * 128
                r0 = row0 + off
                pq = []
                for which in range(4):
                    pp = prot[rot % 3]
                    rot += 1
                    pq.append(pp)
                    if which < 3:
                        first = True
                        for tap in range(3):
                            for kc in range(8):
                                P.pe.matmul(out=pp[:], lhsT=hTg[:, kc, off + tap:off + tap + 128],
                                            rhs=Wqkv[tap][:, kc, which * 512:(which + 1) * 512],
                                            start=first, stop=(tap == 2 and kc == 7))
                                first = False
                    else:
                        for kc in range(8):
                            P.pe.matmul(out=pp[:], lhsT=hTg[:, kc, off + 1:off + 129], rhs=Wrest[:, kc, 0:512],
                                        start=(kc == 0), stop=(kc == 7))
                    if which == 0:
                        P.act.activation(out=qsb[b][:], in_=pp[:], func=AF.Silu)
                    elif which == 1:
                        P.act.activation(out=ksb[b][:], in_=pp[:], func=AF.Silu)
                    elif which == 2:
                        P.act.activation(out=kvst[b][:, 1, :], in_=pp[:], func=AF.Silu)
                    else:
                        P.act.activation(out=zsb[b][:], in_=pp[:], func=AF.Silu)
                        P.sp.dma_start(out=self.ZS.h.ap()[r0:r0 + 128, :], in_=zsb[b][:], _writes=[self.ZS])
                while pending:
                    pending.pop(0)()
                vn = vnb[it % 2]
                for kc in range(8):
                    P.pe.matmul(out=pr[:, 0:272], lhsT=hTg[:, kc, off + 1:off + 129], rhs=Wrest[:, kc, 512:784],
                                start=(kc == 0), stop=(kc == 7))
                P.dve.tensor_copy(out=qst[b][:, 512:528], in_=pr[:, 0:16])
                P.act.activation(out=cv[:], in_=pr[:, 16:272], func=AF.Gelu)
                ss8 = P.sm("ss8", 8)
                P.pool.tensor_tensor(out=sq[0][:], in0=qsb[b][:], in1=qsb[b][:], op=ALU.mult)
                P.dve.tensor_reduce(out=ss8[:, 0:4], in_=sq[0][:].rearrange("p (h d) -> p h d", h=4), axis=AX.X, op=ALU.add)
                P.pool.tensor_tensor(out=sq[1][:], in0=ksb[b][:], in1=ksb[b][:], op=ALU.mult)
                P.dve.tensor_reduce(out=ss8[:, 4:8], in_=sq[1][:].rearrange("p (h d) -> p h d", h=4), axis=AX.X, op=ALU.add)
                rs = self.rstd_chain(ss8[:], 8, 1.0, "rsqk")
                rq = P.sm("rq", 4)
                P.dve.tensor_scalar(out=rq[:], in0=rs[:, 0:4], scalar1=128.0 ** -0.5, scalar2=None, op0=ALU.mult)
                P.dve.tensor_tensor(out=qst[b][:, 0:512].rearrange("p (h d) -> p h d", h=4),
                                    in0=qsb[b][:].rearrange("p (h d) -> p h d", h=4),
                                    in1=rq[:].unsqueeze(2).broadcast_to([128, 4, 128]), op=ALU.mult)
                P.dve.tensor_tensor(out=kvst[b][:, 0, :].rearrange("p (h d) -> p h d", h=4),
                                    in0=ksb[b][:].rearrange("p (h d) -> p h d", h=4),
                                    in1=rs[:, 4:8].unsqueeze(2).broadcast_to([128, 4, 128]), op=ALU.mult)
                dst = self.SCN.h.ap()[n][:, 512:2560].rearrange("p (a b) -> p a b", b=1024)[:, :, 0:512]
                P.sp.dma_start(out=dst, in_=kvst[b][:], _writes=[self.SCN])
                P.sp.dma_start(out=self.QS.h.ap()[r0:r0 + 128, :], in_=qst[b][:], _writes=[self.QS])
                st6 = P.sm("st6", 6)
                mv = P.sm("mv", 2)
                P.dve.bn_stats(out=st6[:], in_=cv[:])
                P.dve.bn_aggr(out=mv[:], in_=st6[:])
                rl = self.rstd_chain(mv[:, 1:2], 1, 1.0, "rln")
                P.dve.tensor_scalar(out=vn1[:], in0=cv[:], scalar1=mv[:, 0:1], scalar2=rl[:, 0:1], op0=ALU.subtract, op1=ALU.mult)
                P.pool.tensor_tensor(out=vn2[:], in0=vn1[:], in1=lng[:], op=ALU.mult)
                P.pool.tensor_tensor(out=vn[:], in0=vn2[:], in1=lnb[:], op=ALU.add)
                def tail(vn=vn, uT=uT, ycT=ycT, off=off, lastj=(j == ntok // 128 - 1), row0=row0, ntok=ntok):
                    for g in range(4):
                        P.pe.matmul(out=pc[g // 2][(g % 2) * 64:(g % 2) * 64 + 64, :], lhsT=vn[:, g * 64:(g + 1) * 64],
                                    rhs=wsT[:, g, :], start=True, stop=True)
                    for ct in range(2):
                        P.dve.tensor_tensor(out=tmpc[:], in0=pc[ct][:], in1=Bs[:, ct, :], op=ALU.add)
                        P.pool.tensor_tensor(out=ycT[:, ct, off:off + 128], in0=tmpc[:], in1=uT[:, ct, off:off + 128], op=ALU.mult)
                    if lastj:
                        dst = self.YT.h.ap()[768:1024, row0:row0 + ntok].rearrange("(ct p) t -> p ct t", p=128)
                        P.sp.dma_start(out=dst, in_=ycT[:, :, 0:ntok], _writes=[self.YT])
                pending.append(tail)
        while pending:
            pending.pop(0)()
        P.end()
        P.keep_end()

    def gdn_prep(self, l):
        P = self.P
        c = self.c
        P.begin()
        al = P.sb("al", [128, 8])
        dtb = P.sb("dtb", [128, 8])
        ea = P.sb("ea", [128, 8])
        nea = P.sb("nea", [128, 8])
        P.sp.dma_start(out=al[:], in_=self.a_log.h.ap()[l:l + 1, :].broadcast_to([128, 8]))
        P.sp.dma_start(out=dtb[:], in_=self.dt_bias.h.ap()[l:l + 1, :].broadcast_to([128, 8]))
        P.act.activation(out=ea[:], in_=al[:], func=AF.Exp)
        P.dve.tensor_scalar(out=nea[:], in0=ea[:], scalar1=-1.0, scalar2=None, op0=ALU.mult)
        ph = P.psb("pha", 4, 128) + P.psb("phb", 4, 128)
        pA = [P.psb("pA%d" % r, 4, 128) for r in range(2)]
        pB = [P.psb("pB%d" % r, 4, 128) for r in range(2)]
        pD = [P.psb("pD%d" % r, 4, 128) for r in range(2)]
        pg = SubTile(ph[4].parent, 0, 16, "pgv")
        GU = [P.sb("GU%d" % r, [128, 512]) for r in range(2)]
        kin = [P.sb("kin%d" % i, [128, 512]) for i in range(2)]
        qs = [P.sb("qs%d" % i, [128, 528]) for i in range(2)]
        okT = [P.sb("okT%d" % i, [128, 512]) for i in range(2)]
        oqT = [P.sb("oqT%d" % i, [128, 512]) for i in range(2)]
        oT2T = [[P.sb("oT2T%d_%d" % (i, r), [128, 512]) for r in range(2)] for i in range(2)]
        oQKT = [[P.sb("oQKT%d_%d" % (i, r), [128, 512]) for r in range(2)] for i in range(2)]
        scalb = [P.sb("scal%d" % i, [128, 32]) for i in range(2)]
        sm = {nm: P.sb(nm, [128, 8]) for nm in ("e1", "d1", "x2", "e2", "sp", "g", "dl", "et")}
        gsb = P.sb("gsb", [128, 16])
        U8 = [(r, h) for r in range(2) for h in range(4)]
        mk = lambda nm: {u: P.sb("%s%d%d" % (nm, u[0], u[1]), [128, 128]) for u in U8}
        Dt, E2, a1 = mk("Dt"), mk("E2"), mk("a1")
        Pp = [[mk("Pp%d%d" % (q, i)) for i in range(2)] for q in range(2)]
        PT = [[mk("PT%d%d" % (q, i)) for i in range(2)] for q in range(2)]
        R = [[mk("R%d%d" % (q, i)) for i in range(2)] for q in range(2)]
        SCN = self.SCN.h.ap()
        SC2 = self.SC2.h.ap()
        hs = lambda h: slice(h * 128, (h + 1) * 128)

        def stage_a(n):
            b = n % 2
            scal = scalb[b]
            g = sm["g"]
            Pp0, PT0, R0 = Pp[b][0], PT[b][0], R[b][0]

            def a_load():
                P.sp.dma_start(out=kin[b][:], in_=SCN[n][:, 512:1024], _reads=[self.SCN])
                P.sp.dma_start(out=qs[b][:], in_=self.QS.h.ap()[n * 128:(n + 1) * 128, :], _reads=[self.QS])

            def a_gates1():
                P.act.activation(out=sm["e1"][:], in_=qs[b][:, 512:520], func=AF.Exp, scale=-1.0)
                P.dve.tensor_scalar(out=sm["d1"][:], in0=sm["e1"][:], scalar1=1.0, scalar2=None, op0=ALU.add)
                P.dve.reciprocal(out=scal[:, 0:8], in_=sm["d1"][:])
                P.dve.tensor_tensor(out=sm["x2"][:], in0=qs[b][:, 520:528], in1=dtb[:], op=ALU.add)
                P.act.activation(out=sm["e2"][:], in_=sm["x2"][:], func=AF.Exp)
                P.act.activation(out=sm["sp"][:], in_=sm["e2"][:], func=AF.Ln, bias=1.0)
                P.dve.tensor_tensor(out=g[:], in0=sm["sp"][:], in1=nea[:], op=ALU.mult)

            def a_gates2():
                P.pe.matmul(out=pg[:, 0:4], lhsT=c["U0"][:], rhs=g[:, 0:4], start=True, stop=True)
                P.pe.matmul(out=pg[:, 4:8], lhsT=c["U1"][:], rhs=g[:, 4:8], start=True, stop=True)
                P.pe.matmul(out=pg[:, 8:16], lhsT=c["ones"][:], rhs=g[:, 0:8], start=True, stop=True)

            def a_gates3():
                P.dve.tensor_copy(out=gsb[:], in_=pg[:])
                P.act.activation(out=scal[:, 8:16], in_=gsb[:, 0:8], func=AF.Exp)
                P.dve.tensor_tensor(out=sm["dl"][:], in0=gsb[:, 8:16], in1=gsb[:, 0:8], op=ALU.subtract)
                P.act.activation(out=sm["et"][:], in_=sm["dl"][:], func=AF.Exp)
                P.dve.tensor_tensor(out=scal[:, 16:24], in0=scal[:, 0:8], in1=sm["et"][:], op=ALU.mult)
                P.act.activation(out=scal[:, 24:32], in_=gsb[:, 8:16], func=AF.Exp)
                for (r, h) in U8:
                    idx = r * 4 + h
                    P.act.activation(out=GU[r][:, hs(h)], in_=c["U%d" % r][:], func=AF.Copy, scale=g[:, idx:idx + 1])

            def a_tr():
                for h in range(4):
                    P.pe.transpose(out=ph[h][:], in_=kin[b][:, hs(h)], identity=c["ident"][:])
                for h in range(4):
                    P.pe.transpose(out=ph[4 + h][:], in_=qs[b][:, hs(h)], identity=c["ident"][:])

            def a_trc():
                for h in range(4):
                    P.act.copy(out=okT[b][:, hs(h)], in_=ph[h][:])
                for h in range(4):
                    P.dve.tensor_copy(out=oqT[b][:, hs(h)], in_=ph[4 + h][:])

            def a_kk():
                for h in range(4):
                    P.pe.matmul(out=ph[h][:], lhsT=okT[b][:, hs(h)], rhs=okT[b][:, hs(h)], start=True, stop=True)
                for h in range(4):
                    P.pe.matmul(out=ph[4 + h][:], lhsT=okT[b][:, hs(h)], rhs=oqT[b][:, hs(h)], start=True, stop=True)
                for r in range(2):
                    P.pe.matmul(out=pD[r][0].parent[:], lhsT=c["ones"][:], rhs=GU[r][:], start=True, stop=True)

            def a_dt():
                for (r, h) in U8:
                    idx = r * 4 + h
                    P.dve.scalar_tensor_tensor(out=Dt[(r, h)][:], in0=pD[r][h][:], scalar=gsb[:, idx:idx + 1], in1=c["NM%d" % r][:],
                                               op0=ALU.subtract, op1=ALU.add)
                for u in U8:
                    P.act.activation(out=E2[u][:], in_=Dt[u][:], func=AF.Exp)

            def a_n():
                for (r, h) in U8:
                    u = (r, h)
                    idx = r * 4 + h
                    P.dve.tensor_tensor(out=oQKT[b][r][:, hs(h)], in0=ph[4 + h][:], in1=E2[u][:], op=ALU.mult)
                    P.dve.scalar_tensor_tensor(out=a1[u][:], in0=ph[h][:], scalar=scal[:, idx:idx + 1], in1=E2[u][:],
                                               op0=ALU.mult, op1=ALU.mult)
                    P.pool.tensor_tensor(out=Pp0[u][:], in0=a1[u][:], in1=c["SU%d" % r][:], op=ALU.mult)
                for u in U8:
                    P.pool.tensor_tensor(out=R0[u][:], in0=c["ident"][:], in1=Pp0[u][:], op=ALU.subtract)

            def a_nt():
                for (r, h) in U8:
                    P.pe.transpose(out=pD[r][h][:], in_=Pp0[(r, h)][:], identity=c["ident"][:])

            def a_ntc():
                for (r, h) in U8:
                    (P.act.copy if r == 0 else P.dve.tensor_copy)(out=PT0[(r, h)][:], in_=pD[r][h][:])
            return [a_load, a_gates1, a_gates2, a_gates3, a_tr, a_trc, a_kk, a_dt, a_n, a_nt, a_ntc]

        def level_stages(n, r, lvl):
            b = n % 2
            cur, nxt = lvl % 2, 1 - (lvl % 2)
            Pc, Pn, Tc, Tn, Rc, Rn = Pp[b][cur], Pp[b][nxt], PT[b][cur], PT[b][nxt], R[b][cur], R[b][nxt]

            def s1():
                for h in range(4):
                    u = (r, h)
                    if lvl < 5:
                        P.pe.matmul(out=pA[r][h][:], lhsT=Tc[u][:], rhs=Pc[u][:], start=True, stop=True)
                    P.pe.matmul(out=pB[r][h][:], lhsT=Pc[u][:], rhs=Tc[u][:], start=True, stop=True)

            def s2():
                if lvl < 5:
                    for h in range(4):
                        (P.act.copy if r == 0 else P.dve.tensor_copy)(out=Pn[(r, h)][:], in_=pA[r][h][:])
                for h in range(4):
                    (P.dve.tensor_copy if r == 0 else P.act.copy)(out=Tn[(r, h)][:], in_=pB[r][h][:])

            def s3():
                for h in range(4):
                    u = (r, h)
                    P.pe.matmul(out=pA[r][h][:], lhsT=Tn[u][:], rhs=Rc[u][:], start=True, stop=True)

            def s4():
                for h in range(4):
                    u = (r, h)
                    dst = Rn[u][:] if lvl < 5 else oT2T[b][r][:, hs(h)]
                    P.dve.tensor_tensor(out=dst, in0=pA[r][h][:], in1=Rc[u][:], op=ALU.add)
            return [s1, s2, s3, s4]

        def stores(n):
            b = n % 2
            w = [self.SC2]
            P.sp.dma_start(out=SC2[n][:, 0:512], in_=okT[b][:], _writes=w)
            P.sp.dma_start(out=SC2[n][:, 512:1024], in_=oqT[b][:], _writes=w)
            for r in range(2):
                P.sp.dma_start(out=SC2[n][:, 1024 + r * 512:1536 + r * 512], in_=oT2T[b][r][:], _writes=w)
                P.sp.dma_start(out=SC2[n][:, 2048 + r * 512:2560 + r * 512], in_=oQKT[b][r][:], _writes=w)
            P.sp.dma_start(out=SC2[n][:, 3072:3104], in_=scalb[b][:], _writes=w)

        for f in stage_a(0):
            f()
        A_AT = {0: 0, 1: 1, 2: 4, 4: 5, 8: 2, 10: 3, 14: 6, 16: 7, 18: 8, 22: 9, 23: 10}
        for n in range(NTILE):
            stA = [st for lvl in range(6) for st in level_stages(n, 0, lvl)]
            stB = [st for lvl in range(6) for st in level_stages(n, 1, lvl)]
            nxtA = stage_a(n + 1) if n + 1 < NTILE else None
            for i in range(len(stA) + 1):
                if i < len(stA):
                    stA[i]()
                if i >= 1:
                    stB[i - 1]()
                if nxtA is not None and i in A_AT:
                    nxtA[A_AT[i]]()
            stores(n)
        P.end()

    def gdn_scan(self, l):
        P = self.P
        P.begin()
        SCN = self.SCN.h.ap()
        SC2 = self.SC2.h.ap()
        Forder = list(range(NTILE))
        Border = [1, 0] + list(range(NTILE - 1, 1, -1))
        names = ("kT", "k", "qT", "v", "T2T", "QKT")
        col0 = {"kT": 0, "k": 512, "qT": 1024, "v": 1536}
        inb = [[{nm: P.sb("i%s%d%d" % (nm, r, i), [128, 512]) for nm in names} for i in range(2)] for r in range(2)]
        scb = [[P.sb("isc%d%d" % (r, i), [128, 32]) for i in range(2)] for r in range(2)]
        ngc = [P.sb("ngc%d" % r, [128, 8]) for r in range(2)]
        S = [[[P.sb("S%d%d%d" % (r, h, i), [128, 128]) for i in range(2)] for h in range(4)] for r in range(2)]
        pa = [P.psb("pa%d" % r, 4, 128) for r in range(2)]
        po = [P.psb("po%d" % r, 4, 128) for r in range(2)]
        rr = [[P.sb("rr%d%d" % (r, h), [128, 128]) for h in range(4)] for r in range(2)]
        vn = [[P.sb("vn%d%d" % (r, h), [128, 128]) for h in range(4)] for r in range(2)]
        vt = [[P.sb("vt%d%d" % (r, h), [128, 128]) for h in range(4)] for r in range(2)]
        t1 = [[P.sb("t1%d%d" % (r, h), [128, 128]) for h in range(4)] for r in range(2)]
        oo = [[P.sb("oo%d%d" % (r, i), [128, 512]) for i in range(2)] for r in range(2)]
        for r in range(2):
            for h in range(4):
                P.pool.memset(ap=S[r][h][0][:], constant=0.0)
        units = [(r, h) for r in range(2) for h in range(4)]
        cur = 0
        for i in range(NTILE):
            b = i % 2
            nxt = 1 - cur
            tl = (Forder[i], Border[i])
            for r in range(2):
                n = tl[r]
                d = inb[r][b]
                for nm in ("k", "v"):
                    P.sp.dma_start(out=d[nm][:], in_=SCN[n][:, col0[nm]:col0[nm] + 512], _reads=[self.SCN])
                P.sp.dma_start(out=d["kT"][:], in_=SC2[n][:, 0:512], _reads=[self.SC2])
                P.sp.dma_start(out=d["qT"][:], in_=SC2[n][:, 512:1024], _reads=[self.SC2])
                P.sp.dma_start(out=d["T2T"][:], in_=SC2[n][:, 1024 + r * 512:1536 + r * 512], _reads=[self.SC2])
                P.sp.dma_start(out=d["QKT"][:], in_=SC2[n][:, 2048 + r * 512:2560 + r * 512], _reads=[self.SC2])
                P.sp.dma_start(out=scb[r][b][:], in_=SC2[n][:, 3072:3104], _reads=[self.SC2])
                P.pool.tensor_scalar(out=ngc[r][:], in0=scb[r][b][:, 8:16], scalar1=-1.0, scalar2=None, op0=ALU.mult)
            hs = lambda h: slice(h * 128, (h + 1) * 128)

            def scan_stages(r):
                d = inb[r][b]
                sc = scb[r][b]
                def t1_():
                    for h in range(4):
                        P.pe.matmul(out=pa[r][h][:], lhsT=d["kT"][:, hs(h)], rhs=S[r][h][cur][:], start=True, stop=True)
                    for h in range(4):
                        P.pe.matmul(out=po[r][h][:], lhsT=d["qT"][:, hs(h)], rhs=S[r][h][cur][:], start=True, stop=True)
                def t2_():
                    for h in range(4):
                        idx = r * 4 + h
                        P.dve.scalar_tensor_tensor(out=rr[r][h][:], in0=pa[r][h][:], scalar=ngc[r][:, idx:idx + 1], in1=d["v"][:, hs(h)],
                                                   op0=ALU.mult, op1=ALU.add)
                    for h in range(4):
                        idx = r * 4 + h
                        P.act.activation(out=t1[r][h][:], in_=po[r][h][:], func=AF.Copy, scale=sc[:, 8 + idx:9 + idx])
                def t3_():
                    for h in range(4):
                        P.pe.matmul(out=pa[r][h][:], lhsT=d["T2T"][:, hs(h)], rhs=rr[r][h][:], start=True, stop=True)
                def t4_():
                    for h in range(4):
                        idx = r * 4 + h
                        P.act.activation(out=vn[r][h][:], in_=pa[r][h][:], func=AF.Copy, scale=sc[:, idx:idx + 1])
                    for h in range(4):
                        idx = r * 4 + h
                        P.act.activation(out=vt[r][h][:], in_=pa[r][h][:], func=AF.Copy, scale=sc[:, 16 + idx:17 + idx])
                def t5_():
                    for h in range(4):
                        P.pe.matmul(out=pa[r][h][:], lhsT=d["k"][:, hs(h)], rhs=vt[r][h][:], start=True, stop=True)
                    for h in range(4):
                        P.pe.matmul(out=po[r][h][:], lhsT=d["QKT"][:, hs(h)], rhs=vn[r][h][:], start=True, stop=True)
                def t6_():
                    for h in range(4):
                        idx = r * 4 + h
                        P.dve.scalar_tensor_tensor(out=S[r][h][nxt][:], in0=S[r][h][cur][:], scalar=sc[:, 24 + idx:25 + idx],
                                                   in1=pa[r][h][:], op0=ALU.mult, op1=ALU.add)
                    for h in range(4):
                        P.dve.tensor_tensor(out=oo[r][b][:, hs(h)], in0=po[r][h][:], in1=t1[r][h][:], op=ALU.add)
                return [t1_, t2_, t3_, t4_, t5_, t6_]
            sA, sB = scan_stages(0), scan_stages(1)
            for k_ in range(len(sA) + 1):
                if k_ < len(sA):
                    sA[k_]()
                if k_ >= 1:
                    sB[k_ - 1]()
            for r in range(2):
                n = tl[r]
                dst = (self.OF, self.OB)[r]
                P.sp.dma_start(out=dst.h.ap()[n * 128:(n + 1) * 128, :], in_=oo[r][b][:], _writes=[dst])
            cur = nxt
        P.end()

    def mixer_a(self, l):
        P = self.P
        P.begin()
        PFM = self.PFM.h.ap()
        YT = self.YT.h.ap()
        cwa = P.sb("cwa", [128, 2, 3])
        for ct in range(2):
            P.sp.dma_start(out=cwa[:, ct, :], in_=self.conv_a.h.ap()[l][:, ct * 128:(ct + 1) * 128].rearrange("j c -> c j"),
                           allow_slow_non_contiguous=True)
        ca = [P.sb("ca%d" % i, [128, T]) for i in range(2)]
        Bt = [P.sb("Bt%d" % i, [128, T]) for i in range(2)]
        acc = [P.sb("acc%d" % i, [128, T]) for i in range(2)]
        ya = [P.sb("ya%d" % i, [128, T], BF16) for i in range(2)]
        it = 0
        for s, (c0, n) in enumerate(((0, TC), (TC, T))):
            if s == 0 and l == DEPTH - 1:
                continue
            for ct in range(2):
                b = it % 2
                it += 1
                P.sp.dma_start(out=ca[b][:, 0:n], in_=PFM[256 + ct * 128:256 + (ct + 1) * 128, c0:c0 + n], _reads=[self.PFM])
                P.sp.dma_start(out=Bt[b][:, 0:n], in_=PFM[ct * 128:(ct + 1) * 128, c0:c0 + n], _reads=[self.PFM])
                w = lambda j: cwa[:, ct, j:j + 1]
                P.pool.tensor_scalar(out=acc[b][:, 0:n], in0=ca[b][:, 0:n], scalar1=w(1), scalar2=None, op0=ALU.mult)
                if s == 0:
                    sh = [(acc[b][:, 1:n], ca[b][:, 0:n - 1], 0), (acc[b][:, 0:n - 1], ca[b][:, 1:n], 2)]
                elif ct == 0:
                    av = acc[b][:, 0:n].rearrange("p (r c) -> p r c", c=64)
                    cv = ca[b][:, 0:n].rearrange("p (r c) -> p r c", c=64)
                    sh = [(av[:, :, 1:64], cv[:, :, 0:63], 0), (av[:, :, 0:63], cv[:, :, 1:64], 2)]
                else:
                    sh = [(acc[b][:, 64:n], ca[b][:, 0:n - 64], 0), (acc[b][:, 0:n - 64], ca[b][:, 64:n], 2)]
                for (dst, src, j) in sh:
                    P.dve.scalar_tensor_tensor(out=dst, in0=src, scalar=w(j), in1=dst, op0=ALU.mult, op1=ALU.add)
                P.pool.tensor_tensor(out=ya[b][:, 0:n], in0=acc[b][:, 0:n], in1=Bt[b][:, 0:n], op=ALU.mult)
                P.sp.dma_start(out=YT[ct * 128:(ct + 1) * 128, c0:c0 + n], in_=ya[b][:, 0:n], _writes=[self.YT])
        P.end()

    def post_norm_residual(self, py, xt, G, tmp, xo, ssq, junk):
        P = self.P
        for hf in range(2):
            P.act.activation(out=junk[:, 0:512], in_=py[hf][:], func=AF.Square, accum_out=ssq[:, hf:hf + 1])
        ss = P.sm("pss", 1)
        P.dve.tensor_tensor(out=ss[:], in0=ssq[:, 0:1], in1=ssq[:, 1:2], op=ALU.add)
        rstd = self.rstd_chain(ss[:], 1, 1.0 / D, "prs")
        for hf in range(2):
            P.dve.scalar_tensor_tensor(out=tmp[:, hf * 512:(hf + 1) * 512], in0=py[hf][:], scalar=rstd[:, 0:1],
                                       in1=G[:, hf * 512:(hf + 1) * 512], op0=ALU.mult, op1=ALU.mult)
        P.pool.tensor_tensor(out=xo[:], in0=tmp[:], in1=xt[:], op=ALU.add)

    def mix_out(self, l, src):
        P = self.P
        c = self.c
        last = l == DEPTH - 1
        P.keep_begin()
        Wo = P.sbk("Wo", [128, 8, D], BF16)
        P.begin()
        stg = [P.sb("stgo%d" % i, [128, D]) for i in range(2)]
        for kc in range(8):
            P.sp.dma_start(out=stg[kc % 2][:], in_=self.w_o.h.ap()[l][kc * 128:(kc + 1) * 128, :])
            (P.act.copy if kc % 2 else P.dve.tensor_copy)(out=Wo[:, kc, :], in_=stg[kc % 2][:])
        P.end()
        P.begin()
        gon = P.sb("gon", [128, 128])
        P.sp.dma_start(out=gon[:], in_=self.g_onorm.h.ap()[l:l + 1, :].broadcast_to([128, 128]))
        G = [self.load_mod(l, 2, 1, "Gpc"), self.load_mod(l, 2, 0, "Gpx")]
        of = [P.sb("of%d" % i, [128, 512]) for i in range(2)]
        ob = [P.sb("ob%d" % i, [128, 512]) for i in range(2)]
        zs = [P.sb("zs%d" % i, [128, 512]) for i in range(2)]
        xt = [P.sb("xt%d" % i, [128, D]) for i in range(2)]
        yac = [P.sb("yac%d" % i, [128, 4, 128], BF16) for i in range(2)]
        o = P.sb("o", [128, 512])
        sq = P.sb("sq", [128, 512])
        y1 = P.sb("y1", [128, 512])
        y2 = P.sb("y2", [128, 512])
        yb = P.sb("yb", [128, 512], BF16)
        ybT = [P.sb("ybT%d" % i, [128, 4, 128], BF16) for i in range(2)]
        tmp = P.sb("tmp", [128, D])
        junk = P.sb("junk", [128, 512])
        xo = [P.sb("xo%d" % i, [128, D]) for i in range(2)]
        pT = P.ps("pT", [128, 1024], BF16)
        py = [[P.ps("py%d%d" % (i, hf), [128, 512]) for hf in range(2)] for i in range(2)]
        YT = self.YT.h.ap()
        h4 = lambda ap: ap.rearrange("p (h d) -> p h d", h=4)
        tl = list(range(2 if last else 0, NTILE))

        def front(i):
            n = tl[i]
            b = i % 2
            r0 = n * 128
            P.sp.dma_start(out=of[b][:], in_=self.OF.h.ap()[r0:r0 + 128, :], _reads=[self.OF])
            P.sp.dma_start(out=ob[b][:], in_=self.OB.h.ap()[r0:r0 + 128, :], _reads=[self.OB])
            P.sp.dma_start(out=zs[b][:], in_=self.ZS.h.ap()[r0:r0 + 128, :], _reads=[self.ZS])
            P.sp.dma_start(out=xt[b][:], in_=src.h.ap()[r0:r0 + 128, :], _reads=[src])
            P.sp.dma_start(out=yac[b][:, 0:2, :], in_=YT[0:256, r0:r0 + 128].rearrange("(ct p) t -> p ct t", p=128), _reads=[self.YT])
            P.sp.dma_start(out=yac[b][:, 2:4, :], in_=YT[768:1024, r0:r0 + 128].rearrange("(ct p) t -> p ct t", p=128), _reads=[self.YT])
            P.pool.tensor_tensor(out=o[:], in0=of[b][:], in1=ob[b][:], op=ALU.add)
            P.pool.tensor_tensor(out=sq[:], in0=o[:], in1=o[:], op=ALU.mult)
            ss4 = P.sm("ss4", 4)
            P.dve.tensor_reduce(out=ss4[:], in_=h4(sq[:]), axis=AX.X, op=ALU.add)
            rs = self.rstd_chain(ss4[:], 4, 1.0 / 128, "rso")
            P.dve.tensor_tensor(out=h4(y1[:]), in0=h4(o[:]), in1=rs[:].unsqueeze(2).broadcast_to([128, 4, 128]), op=ALU.mult)
            P.pool.tensor_tensor(out=h4(y2[:]), in0=h4(y1[:]), in1=gon[:].unsqueeze(1).broadcast_to([128, 4, 128]), op=ALU.mult)
            P.dve.tensor_tensor(out=yb[:], in0=y2[:], in1=zs[b][:], op=ALU.mult)
            for h in range(4):
                P.pe.transpose(out=pT[:, h * 128:(h + 1) * 128], in_=yb[:, h * 128:(h + 1) * 128], identity=c["ident_bf"][:])
            P.act.copy(out=ybT[b][:], in_=pT[:, 0:512].rearrange("p (h t) -> p h t", h=4))
            lhs = [yac[b][:, 0, :], yac[b][:, 1, :]] + [ybT[b][:, h, :] for h in range(4)] + [yac[b][:, 2, :], yac[b][:, 3, :]]
            for hf in range(2):
                for kc in range(8):
                    P.pe.matmul(out=py[b][hf][:], lhsT=lhs[kc], rhs=Wo[:, kc, hf * 512:(hf + 1) * 512], start=(kc == 0), stop=(kc == 7))

        def back(i):
            n = tl[i]
            b = i % 2
            s_ = 0 if n < 2 else 1
            r0 = n * 128
            ssq = P.sm("ssq", 2)
            self.post_norm_residual(py[b], xt[b], G[s_], tmp, xo[b], ssq, junk)
            P.sp.dma_start(out=self.XS.h.ap()[r0:r0 + 128, :], in_=xo[b][:], _writes=[self.XS])
        front(0)
        for i in range(len(tl)):
            if i + 1 < len(tl):
                front(i + 1)
            back(i)
        P.end()
        P.keep_end()

    def ffn(self, l):
        P = self.P
        c = self.c
        last = l == DEPTH - 1
        NJ = FFN_H // 128
        P.keep_begin()
        W1 = P.sbk("W1", [128, 8, 2 * FFN_H], BF16)
        W2 = P.sbk("W2", [128, NJ, D], BF16)
        P.begin()
        stg = [P.sb("stgf%d" % i, [128, 2 * FFN_H]) for i in range(2)]
        w1v = self.w_ffn_in.h.ap()[l]
        engs = (P.act.copy, P.dve.tensor_copy, P.pool.tensor_copy, P.act.copy)
        for kc in range(8):
            st = stg[kc % 2]
            P.sp.dma_start(out=st[:], in_=w1v[kc * 128:(kc + 1) * 128, :])
            for q in range(4):
                engs[q](out=W1[:, kc, q * 1408:(q + 1) * 1408], in_=st[:, q * 1408:(q + 1) * 1408])
        w2v = self.w_ffn_out.h.ap()[l].rearrange("(j p) n -> p j n", p=128)
        for jj, (j0, j1) in enumerate(((0, 5), (5, 10), (10, 15), (15, 20), (20, 22))):
            st = stg[jj % 2]
            nj = j1 - j0
            P.sp.dma_start(out=st[:, 0:nj * D].rearrange("p (j n) -> p j n", n=D), in_=w2v[:, j0:j1, :])
            for q in range(nj):
                engs[q % 3](out=W2[:, j0 + q, :], in_=st[:, q * D:(q + 1) * D])
        P.end()

        P.begin()
        S = [P.sb("Sf", [128, D]), None]
        Gm = [P.sb("Gf", [128, D]), None]
        Gp = [P.sb("Gpf", [128, D]), None]
        MBv = self.MB.h.ap()

        def load_mods(s):
            ms = 1 - s
            for t, k in ((S[0], 3), (Gm[0], 4), (Gp[0], 5)):
                P.sp.dma_start(out=t[:], in_=MBv[l, k, ms:ms + 1, :].broadcast_to([128, D]), _reads=[self.MB])
        xt = [[P.sb("xt%d%d" % (i, t), [128, D]) for t in range(2)] for i in range(2)]
        h1 = P.sb("h1", [128, D])
        hb = P.sb("hb", [128, D], BF16)
        hT = [P.sb("hT%d" % i, [128, 8, 256], BF16) for i in range(2)]
        sg = [P.sb("sg%d" % i, [128, 256]) for i in range(2)]
        tmp = P.sb("tmp", [128, D])
        xo = [P.sb("xo%d" % i, [128, D]) for i in range(2)]
        pT = P.ps("pT", [128, 1024], BF16)
        pg = P.ps("pg", [128, 512])
        pu = P.ps("pu", [128, 512])
        py = [[P.ps("py%d%d" % (t, hf), [128, 512]) for hf in range(2)] for t in range(2)]
        groups = list(range(1 if last else 0, NT // 256))
        LAG = 3
        aT = [P.sb("aTr%d" % i, [128, 256], BF16) for i in range(LAG + 2)]
        state = {"s": None}

        def pre_elem(gi, t):
            g = groups[gi]
            b = gi % 2
            s_ = 0 if g == 0 else 1
            if s_ != state["s"]:
                load_mods(s_)
                state["s"] = s_
            r0 = g * 256 + t * 128
            P.sp.dma_start(out=xt[b][t][:], in_=self.XS.h.ap()[r0:r0 + 128, :], _reads=[self.XS])
            ss = P.sm("ss", 1)
            P.act.activation(out=tmp[:], in_=xt[b][t][:], func=AF.Square, accum_out=ss[:])
            rstd = self.rstd_chain(ss[:], 1, 1.0 / D, "rsf")
            P.dve.scalar_tensor_tensor(out=h1[:], in0=xt[b][t][:], scalar=rstd[:, 0:1], in1=Gm[0][:], op0=ALU.mult, op1=ALU.mult)
            P.pool.tensor_tensor(out=hb[:], in0=h1[:], in1=S[0][:], op=ALU.add)

        def pre_tr(gi, t):
            b = gi % 2
            for kc in range(8):
                P.pe.transpose(out=pT[:, kc * 128:(kc + 1) * 128], in_=hb[:, kc * 128:(kc + 1) * 128], identity=c["ident_bf"][:])
            P.act.copy(out=hT[b][:, :, t * 128:(t + 1) * 128], in_=pT[:].rearrange("p (kc t) -> p kc t", kc=8))

        def second(j):
            a = aT[j % (LAG + 2)]
            for t in range(2):
                for hf in range(2):
                    P.pe.matmul(out=py[t][hf][:], lhsT=a[:, t * 128:(t + 1) * 128], rhs=W2[:, j, hf * 512:(hf + 1) * 512],
                                start=(j == 0), stop=(j == NJ - 1))

        def post(gi):
            g = groups[gi]
            b = gi % 2
            for t in range(2):
                r0 = g * 256 + t * 128
                ssq = P.sm("ssqf", 2)
                self.post_norm_residual(py[t], xt[b][t], Gp[0], tmp, xo[t], ssq, h1)
                if last:
                    P.sp.dma_start(out=self.out.h.ap()[r0 - TC:r0 - TC + 128, :], in_=xo[t][:], _writes=[self.out])
                else:
                    P.sp.dma_start(out=self.XB.h.ap()[r0:r0 + 128, :], in_=xo[t][:], _writes=[self.XB])

        for t in range(2):
            pre_elem(0, t)
            pre_tr(0, t)
        for gi, g in enumerate(groups):
            b = gi % 2
            nxt_ok = gi + 1 < len(groups)
            same_mods = nxt_ok and ((0 if groups[gi + 1] == 0 else 1) == state["s"])
            for j in range(NJ):
                for kc in range(8):
                    P.pe.matmul(out=pg[:, 0:256], lhsT=W1[:, kc, j * 128:(j + 1) * 128], rhs=hT[b][:, kc, :], start=(kc == 0), stop=(kc == 7))
                for kc in range(8):
                    P.pe.matmul(out=pu[:, 0:256], lhsT=W1[:, kc, FFN_H + j * 128:FFN_H + (j + 1) * 128], rhs=hT[b][:, kc, :], start=(kc == 0), stop=(kc == 7))
                if j >= LAG:
                    second(j - LAG)
                P.act.activation(out=sg[j % 2][:], in_=pg[:, 0:256], func=AF.Silu)
                P.dve.tensor_tensor(out=aT[j % (LAG + 2)][:], in0=pu[:, 0:256], in1=sg[j % 2][:], op=ALU.mult)
                if same_mods:
                    if j == 5:
                        pre_elem(gi + 1, 0)
                    elif j == 10:
                        pre_tr(gi + 1, 0)
                    elif j == 12:
                        pre_elem(gi + 1, 1)
                    elif j == 17:
                        pre_tr(gi + 1, 1)
            for j in range(NJ - LAG, NJ):
                second(j)
            post(gi)
            if nxt_ok and not same_mods:
                for t in range(2):
                    pre_elem(gi + 1, t)
                    pre_tr(gi + 1, t)
        P.end()
        P.keep_end()

    def forward(self, upto=None):
        self.consts()
        for l in range(DEPTH):
            src = self.xs_in if l == 0 else self.XB
            self.mods(l)
            self.prenorm(l, 0, 1, src)
            self.proj(l)
            self.gdn_prep(l)
            self.gdn_scan(l)
            self.mixer_a(l)
            self.mix_out(l, src)
            self.ffn(l)
        self.P.close()


W_NAMES = ["w_mod", "b_mod", "g_pre_mix", "g_post_mix", "g_pre_ffn", "g_post_ffn", "w_in", "conv_a", "conv_qkv",
           "a_log", "dt_bias", "g_onorm", "ln_c_g", "ln_c_b", "w_s", "b_s", "w_o", "w_ffn_in", "w_ffn_out"]


def make_in_maps(inputs, cores=range(8)):
    f = lambda a: np.ascontiguousarray(np.asarray(a, dtype=np.float32))
    shared = {n: f(inputs[n]) for n in W_NAMES}
    shared["a_log"] = shared["a_log"].reshape(DEPTH, 8)
    shared["dt_bias"] = shared["dt_bias"].reshape(DEPTH, 8)
    x, c, ctx, c_ctx = f(inputs["x"]), f(inputs["c"]), f(inputs["ctx"]), f(inputs["c_ctx"])
    maps = []
    for b in cores:
        m = dict(shared)
        m["xs"] = np.ascontiguousarray(np.concatenate([ctx[b], x[b]], axis=0))
        cc = np.stack([c[b], c_ctx], axis=0)
        m["ccT"] = np.ascontiguousarray(cc.reshape(2, 8, 128).transpose(2, 1, 0))
        maps.append(m)
    return maps


_CACHE = {}


def kernel(**inputs):
    if "nc" not in _CACHE:
        nc = bass.Bass("TRN2", target_bir_lowering=False)
        Model(nc).forward()
        _CACHE["nc"] = nc
    nc = _CACHE["nc"]
    maps = make_in_maps(inputs)
    res = run_bass_kernel_spmd(nc, maps, core_ids=list(range(8)))
    return np.stack([np.asarray(r["out"], dtype=np.float32) for r in res.results], axis=0)
```
